# Optimizing a Trainium2 kernel written in Bass

```python
import math
import jax, jax.numpy as jnp
from jax import lax
import numpy as np

D_MODEL = 1024
BATCH = 4
SEQ = 8192
DEPTH = 4

MEM_LEN = 256
QBLK = 128
ROPE_THETA = 10000.0
EPS = 1e-6
N_EVEN = (DEPTH + 1) // 2
N_ODD = DEPTH // 2
DIFF_HEADS = 4
DIFF_QK = 64
DIFF_V = 2 * DIFF_QK
FOX_HEADS = 8
FOX_DIM = 64
NSA_HEADS = 8
NSA_GROUPS = 2
NSA_HPG = NSA_HEADS // NSA_GROUPS
NSA_DIM = 64
CMP_LEN = 32
CMP_STRIDE = 16
CMP_HIDDEN = 128
SLC_LEN = 64
SLC_TOPK = 16
WIN = 512
FORCE_SCORE = 1e4
MLA_HEADS = 8
MLA_NOPE = 64
MLA_ROPE = 32
MLA_V = 64
MLA_Q_RANK = 384
MLA_KV_RANK = 256
XA_HEADS = 4
XA_DIM = 128
D_FF = -(-8 * D_MODEL // (3 * 256)) * 256

EV_SIZES = (DIFF_HEADS * 2 * DIFF_QK, DIFF_HEADS * 2 * DIFF_QK, DIFF_HEADS * DIFF_V,
            FOX_HEADS * FOX_DIM, FOX_HEADS * FOX_DIM, FOX_HEADS * FOX_DIM, FOX_HEADS)
OD_SIZES = (NSA_HEADS * NSA_DIM,) + (NSA_GROUPS * NSA_DIM,) * 6 + (NSA_HEADS * 3, MLA_Q_RANK, MLA_KV_RANK, MLA_ROPE)
EV_OUT = DIFF_HEADS * DIFF_V + FOX_HEADS * FOX_DIM
OD_OUT = NSA_HEADS * NSA_DIM + MLA_HEADS * MLA_V

kernel_name = "hybrid_diff_fox_nsa_mla_trunk"


def _rms(x, g):
    xf = x.astype(jnp.float32)
    y = xf * lax.rsqrt(jnp.mean(xf * xf, axis=-1, keepdims=True) + EPS)
    return (y * g.astype(jnp.float32)).astype(x.dtype)


def _rope(x, pos):
    d = x.shape[-1]
    half = d // 2
    inv = 1.0 / (ROPE_THETA ** (jnp.arange(half, dtype=jnp.float32) / half))
    ang = pos.astype(jnp.float32)[:, None] * inv[None, :]
    cos = jnp.cos(ang)[:, None, :]
    sin = jnp.sin(ang)[:, None, :]
    xf = x.astype(jnp.float32)
    x1, x2 = xf[..., :half], xf[..., half:]
    return jnp.concatenate([x1 * cos - x2 * sin, x2 * cos + x1 * sin], axis=-1).astype(x.dtype)


def _split(h, sizes):
    return jnp.split(h, [int(v) for v in np.cumsum(sizes)[:-1]], axis=-1)


def _causal_attn(q, k, v, scale, logf_cum=None):
    B, S, H, _ = q.shape
    dv = v.shape[-1]
    nq = S // QBLK
    kpos = jnp.arange(S)
    ck = None if logf_cum is None else jnp.transpose(logf_cum, (0, 2, 1))

    def block(i):
        start = i * QBLK
        qi = lax.dynamic_slice_in_dim(q, start, QBLK, axis=1)
        s = jnp.einsum('bqhd,bkhd->bhqk', qi, k).astype(jnp.float32) * scale
        if ck is not None:
            cq = lax.dynamic_slice_in_dim(ck, start, QBLK, axis=2)
            s = s + cq[..., None] - ck[:, :, None, :]
        tq = start + jnp.arange(QBLK)
        s = jnp.where(kpos[None, :] <= tq[:, None], s, -jnp.inf)
        p = jax.nn.softmax(s, axis=-1).astype(v.dtype)
        return jnp.einsum('bhqk,bkhd->bqhd', p, v)

    out = lax.map(block, jnp.arange(nq))
    return jnp.moveaxis(out, 0, 1).reshape(B, S, H, dv)


def _even_mixer(xn, w_in, b_f, lam, subln, w_out, layer_idx, pos):
    B, S, _ = xn.shape
    aq, ak, av, fq, fk, fv, fz = _split(xn @ w_in, EV_SIZES)
    aq = _rope(aq.reshape(B, S, 2 * DIFF_HEADS, DIFF_QK), pos).reshape(B, S, DIFF_HEADS, 2, DIFF_QK)
    ak = _rope(ak.reshape(B, S, 2 * DIFF_HEADS, DIFF_QK), pos).reshape(B, S, DIFF_HEADS, 2, DIFF_QK)
    av = av.reshape(B, S, DIFF_HEADS, DIFF_V)
    lam_init = 0.8 - 0.6 * math.exp(-0.3 * layer_idx)
    lf = lam.astype(jnp.float32)
    lam_full = jnp.exp(jnp.sum(lf[0] * lf[1])) - jnp.exp(jnp.sum(lf[2] * lf[3])) + lam_init
    sc = DIFF_QK ** -0.5
    o1 = _causal_attn(aq[:, :, :, 0], ak[:, :, :, 0], av, sc)
    o2 = _causal_attn(aq[:, :, :, 1], ak[:, :, :, 1], av, sc)
    oa = (o1.astype(jnp.float32) - lam_full * o2.astype(jnp.float32)).astype(xn.dtype)
    oa = _rms(oa, subln) * (1.0 - lam_init)
    logf = jax.nn.log_sigmoid((fz + b_f).astype(jnp.float32))
    cum = jnp.cumsum(logf, axis=1)
    of = _causal_attn(fq.reshape(B, S, FOX_HEADS, FOX_DIM), fk.reshape(B, S, FOX_HEADS, FOX_DIM),
                      fv.reshape(B, S, FOX_HEADS, FOX_DIM), FOX_DIM ** -0.5, cum)
    o = jnp.concatenate([oa.reshape(B, S, -1), of.reshape(B, S, -1)], axis=-1)
    return o @ w_out


def _nsa_compress(kt, pos_emb, w1, w2):
    B, S, G, d = kt.shape
    nc = (S - CMP_LEN) // CMP_STRIDE + 1
    idx = np.arange(nc)[:, None] * CMP_STRIDE + np.arange(CMP_LEN)[None, :]
    blk = kt[:, idx] + pos_emb[None, None, :, None, :]
    blk = jnp.transpose(blk, (0, 1, 3, 2, 4)).reshape(B, nc, G, CMP_LEN * d)
    return jax.nn.silu(blk @ w1) @ w2


def _nsa(q, kc_t, vc_t, ks, vs, kw, vw, gates, cmp_pos, cmp_w1, cmp_w2):
    B, S, G, Hg, d = q.shape
    nq = S // QBLK
    kc = _nsa_compress(kc_t, cmp_pos[0], cmp_w1[0], cmp_w2[0])
    vc = _nsa_compress(vc_t, cmp_pos[1], cmp_w1[1], cmp_w2[1])
    nc = kc.shape[1]
    cmp_end = jnp.arange(nc) * CMP_STRIDE + CMP_LEN - 1
    nblk = S // SLC_LEN
    n_sel = min(SLC_TOPK, nblk)
    cs = np.arange(nc) * CMP_STRIDE
    bs = np.arange(nblk) * SLC_LEN
    ov = jnp.asarray(((cs[:, None] < bs[None, :] + SLC_LEN) & (cs[:, None] + CMP_LEN > bs[None, :])).astype(np.float32))
    ks_blk = jnp.transpose(ks, (0, 2, 1, 3)).reshape(B, G, nblk, SLC_LEN, d)
    vs_blk = jnp.transpose(vs, (0, 2, 1, 3)).reshape(B, G, nblk, SLC_LEN, d)
    kw_pad = jnp.pad(kw, ((0, 0), (WIN, 0), (0, 0), (0, 0)))
    vw_pad = jnp.pad(vw, ((0, 0), (WIN, 0), (0, 0), (0, 0)))
    bidx = jnp.arange(B)[:, None, None, None]
    gidx = jnp.arange(G)[None, :, None, None]
    blk_ids = jnp.arange(nblk)
    scale = d ** -0.5

    def block(i):
        start = i * QBLK
        qi = lax.dynamic_slice_in_dim(q, start, QBLK, axis=1)
        tq = start + jnp.arange(QBLK)
        s = jnp.einsum('bqghd,bcgd->bghqc', qi, kc).astype(jnp.float32) * scale
        s = jnp.where(cmp_end[None, :] <= tq[:, None], s, -jnp.inf)
        m = jnp.max(s, axis=-1, keepdims=True)
        e = jnp.exp(s - jnp.where(jnp.isfinite(m), m, 0.0))
        p = e / jnp.maximum(jnp.sum(e, axis=-1, keepdims=True), 1e-30)
        o_cmp = jnp.einsum('bghqc,bcgd->bqghd', p.astype(vc.dtype), vc)
        imp = jnp.einsum('bghqc,cn->bgqn', p, ov)
        cur = tq // SLC_LEN
        bid = blk_ids[None, :]
        forced = (bid == 0) | (bid == cur[:, None]) | (bid == cur[:, None] - 1)
        imp = jnp.where(forced, FORCE_SCORE, imp)
        imp = jnp.where(bid > cur[:, None], -jnp.inf, imp)
        _, sel = lax.top_k(imp, n_sel)
        kg = ks_blk[bidx, gidx, sel]
        vg = vs_blk[bidx, gidx, sel]
        tpos = sel[..., None] * SLC_LEN + jnp.arange(SLC_LEN)
        s = jnp.einsum('bqghd,bgqnld->bghqnl', qi, kg).astype(jnp.float32) * scale
        s = jnp.where((tpos <= tq[:, None, None])[:, :, None], s, -jnp.inf)
        p = jax.nn.softmax(s.reshape(B, G, Hg, QBLK, n_sel * SLC_LEN), axis=-1)
        o_slc = jnp.einsum('bghqm,bgqmd->bqghd', p.astype(vg.dtype), vg.reshape(B, G, QBLK, n_sel * SLC_LEN, d))
        kwi = lax.dynamic_slice_in_dim(kw_pad, start, WIN + QBLK, axis=1)
        vwi = lax.dynamic_slice_in_dim(vw_pad, start, WIN + QBLK, axis=1)
        kp = start - WIN + jnp.arange(WIN + QBLK)
        s = jnp.einsum('bqghd,bkgd->bghqk', qi, kwi).astype(jnp.float32) * scale
        dlt = tq[:, None] - kp[None, :]
        s = jnp.where((dlt >= 0) & (dlt < WIN) & (kp[None, :] >= 0), s, -jnp.inf)
        p = jax.nn.softmax(s, axis=-1)
        o_win = jnp.einsum('bghqk,bkgd->bqghd', p.astype(vwi.dtype), vwi)
        gi = lax.dynamic_slice_in_dim(gates, start, QBLK, axis=1)
        return gi[..., 0:1] * o_cmp + gi[..., 1:2] * o_slc + gi[..., 2:3] * o_win

    out = lax.map(block, jnp.arange(nq))
    return jnp.moveaxis(out, 0, 1).reshape(B, S, G * Hg * d)


def _odd_mixer(xn, w_in, cmp_pos, cmp_w1, cmp_w2, q_norm, kv_norm, w_uq, w_ukv, w_out, pos):
    B, S, _ = xn.shape
    nq_, kc, vc, ks, vs, kw, vw, gz, cq, ckv, kr = _split(xn @ w_in, OD_SIZES)
    kvs = (B, S, NSA_GROUPS, NSA_DIM)
    q = _rope(nq_.reshape(B, S, NSA_HEADS, NSA_DIM), pos).reshape(B, S, NSA_GROUPS, NSA_HPG, NSA_DIM)
    kc = _rope(kc.reshape(kvs), pos)
    ks = _rope(ks.reshape(kvs), pos)
    kw = _rope(kw.reshape(kvs), pos)
    gates = jax.nn.sigmoid(gz).reshape(B, S, NSA_GROUPS, NSA_HPG, 3)
    o_nsa = _nsa(q, kc, vc.reshape(kvs), ks, vs.reshape(kvs), kw, vw.reshape(kvs), gates, cmp_pos, cmp_w1, cmp_w2)
    qf = (_rms(cq, q_norm) @ w_uq).reshape(B, S, MLA_HEADS, MLA_NOPE + MLA_ROPE)
    q_nope, q_pe = qf[..., :MLA_NOPE], _rope(qf[..., MLA_NOPE:], pos)
    kvf = (_rms(ckv, kv_norm) @ w_ukv).reshape(B, S, MLA_HEADS, MLA_NOPE + MLA_V)
    k_nope, v = kvf[..., :MLA_NOPE], kvf[..., MLA_NOPE:]
    k_pe = _rope(kr.reshape(B, S, 1, MLA_ROPE), pos)
    qm = jnp.concatenate([q_nope, q_pe], axis=-1)
    km = jnp.concatenate([k_nope, jnp.broadcast_to(k_pe, (B, S, MLA_HEADS, MLA_ROPE))], axis=-1)
    o_mla = _causal_attn(qm, km, v, (MLA_NOPE + MLA_ROPE) ** -0.5)
    o = jnp.concatenate([o_nsa, o_mla.reshape(B, S, -1)], axis=-1)
    return o @ w_out


def _mem_attn(xn, mem_n, wq, wkv, wo):
    B, S, _ = xn.shape
    M = mem_n.shape[1]
    q = (xn @ wq).reshape(B, S, XA_HEADS, XA_DIM)
    k, v = jnp.split(mem_n @ wkv, 2, axis=-1)
    k = k.reshape(B, M, XA_HEADS, XA_DIM)
    v = v.reshape(B, M, XA_HEADS, XA_DIM)
    s = jnp.einsum('bshd,bmhd->bhsm', q, k).astype(jnp.float32) * (XA_DIM ** -0.5)
    p = jax.nn.softmax(s, axis=-1).astype(v.dtype)
    o = jnp.einsum('bhsm,bmhd->bshd', p, v).reshape(B, S, XA_HEADS * XA_DIM)
    return o @ wo


def _swiglu(xn, w13, w2):
    g, u = jnp.split(xn @ w13, 2, axis=-1)
    return (jax.nn.silu(g) * u) @ w2


def setup_inputs(seed: int = 0) -> dict:
    key = jax.random.key(seed)
    k = jax.random.split(key, 32)
    f32 = jnp.float32

    def w(kk, shape, fan_in):
        return jax.random.normal(kk, shape, f32) * (fan_in ** -0.5)

    def gain(kk, shape):
        return 1.0 + 0.02 * jax.random.normal(kk, shape, f32)

    D = D_MODEL
    return {
        "x": jax.random.normal(k[0], (BATCH, SEQ, D), f32),
        "mem": jax.random.normal(k[1], (BATCH, MEM_LEN, D), f32),
        "mem_norm": gain(k[2], (D,)),
        "norm_mix": gain(k[3], (DEPTH, D)),
        "norm_mem": gain(k[4], (DEPTH, D)),
        "norm_ffn": gain(k[5], (DEPTH, D)),
        "ev_w_in": w(k[6], (N_EVEN, D, sum(EV_SIZES)), D),
        "ev_b_f": 0.1 * jax.random.normal(k[7], (N_EVEN, FOX_HEADS), f32),
        "ev_lam": 0.1 * jax.random.normal(k[8], (N_EVEN, 4, DIFF_QK), f32),
        "ev_subln": gain(k[9], (N_EVEN, DIFF_V)),
        "ev_w_out": w(k[10], (N_EVEN, EV_OUT, D), EV_OUT),
        "od_w_in": w(k[11], (N_ODD, D, sum(OD_SIZES)), D),
        "nsa_cmp_pos": 0.02 * jax.random.normal(k[12], (N_ODD, 2, CMP_LEN, NSA_DIM), f32),
        "nsa_cmp_w1": w(k[13], (N_ODD, 2, CMP_LEN * NSA_DIM, CMP_HIDDEN), CMP_LEN * NSA_DIM),
        "nsa_cmp_w2": w(k[14], (N_ODD, 2, CMP_HIDDEN, NSA_DIM), CMP_HIDDEN),
        "mla_q_norm": gain(k[15], (N_ODD, MLA_Q_RANK)),
        "mla_kv_norm": gain(k[16], (N_ODD, MLA_KV_RANK)),
        "mla_w_uq": w(k[17], (N_ODD, MLA_Q_RANK, MLA_HEADS * (MLA_NOPE + MLA_ROPE)), MLA_Q_RANK),
        "mla_w_ukv": w(k[18], (N_ODD, MLA_KV_RANK, MLA_HEADS * (MLA_NOPE + MLA_V)), MLA_KV_RANK),
        "od_w_out": w(k[19], (N_ODD, OD_OUT, D), OD_OUT),
        "xa_wq": w(k[20], (DEPTH, D, XA_HEADS * XA_DIM), D),
        "xa_wkv": w(k[21], (DEPTH, D, 2 * XA_HEADS * XA_DIM), D),
        "xa_wo": w(k[22], (DEPTH, XA_HEADS * XA_DIM, D), XA_HEADS * XA_DIM),
        "ffn_w13": w(k[23], (DEPTH, D, 2 * D_FF), D),
        "ffn_w2": w(k[24], (DEPTH, D_FF, D), D_FF),
        "final_norm": gain(k[25], (D,)),
    }


def reference(x, mem, mem_norm, norm_mix, norm_mem, norm_ffn,
              ev_w_in, ev_b_f, ev_lam, ev_subln, ev_w_out,
              od_w_in, nsa_cmp_pos, nsa_cmp_w1, nsa_cmp_w2, mla_q_norm, mla_kv_norm, mla_w_uq, mla_w_ukv, od_w_out,
              xa_wq, xa_wkv, xa_wo, ffn_w13, ffn_w2, final_norm):
    S = x.shape[1]
    pos = jnp.arange(S)
    mem_n = _rms(mem, mem_norm)
    for li in range(DEPTH):
        j = li // 2
        h = _rms(x, norm_mix[li])
        if li % 2 == 0:
            x = x + _even_mixer(h, ev_w_in[j], ev_b_f[j], ev_lam[j], ev_subln[j], ev_w_out[j], li, pos)
        else:
            x = x + _odd_mixer(h, od_w_in[j], nsa_cmp_pos[j], nsa_cmp_w1[j], nsa_cmp_w2[j],
                               mla_q_norm[j], mla_kv_norm[j], mla_w_uq[j], mla_w_ukv[j], od_w_out[j], pos)
        x = x + _mem_attn(_rms(x, norm_mem[li]), mem_n, xa_wq[li], xa_wkv[li], xa_wo[li])
        x = x + _swiglu(_rms(x, norm_ffn[li]), ffn_w13[li], ffn_w2[li])
    return _rms(x, final_norm)
```

```python
import math
import numpy as np
import ml_dtypes
import concourse.bass as bass
import concourse.mybir as mybir
from concourse.bass_utils import run_bass_kernel_spmd

F32 = mybir.dt.float32
BF16 = mybir.dt.bfloat16
AF = mybir.ActivationFunctionType
ALU = mybir.AluOpType

D = 1024
KC = 8
DEPTH = 4
MEM = 256
EPS = 1e-6
THETA = 10000.0
DFF = 2816
EV_W = 3080
OD_W = 1976
BIG = 1.0e30

ENGS = ("sync", "scalar", "vector", "gpsimd", "tensor")


class Buf:
    __slots__ = ("name", "w", "r")

    def __init__(self, name):
        self.name = name
        self.w = None
        self.r = []


class Tile:
    __slots__ = ("ap", "buf")

    def __init__(self, ap, name="t"):
        self.ap = ap
        self.buf = Buf(name)

    def __getitem__(self, k):
        return self.ap[k]


class Prog:
    NDMA = 12

    def __init__(self, nc):
        self.nc = nc
        self.q = {e: [] for e in ENGS}
        self.cnt = {e: 0 for e in ENGS}
        self.sem = {e: nc.alloc_semaphore("es_" + e) for e in ENGS}
        self.semid = {e: ("E", e) for e in ENGS}
        self.waited = {e: {} for e in ENGS}
        self.dsem = {}
        self.dval = {}
        self.dnext = {}
        for qn in ("sync", "gpsimd"):
            self.dsem[qn] = [nc.alloc_semaphore("ds_%s_%d" % (qn, i)) for i in range(self.NDMA)]
            self.dval[qn] = [0] * self.NDMA
            self.dnext[qn] = 0
        self.ninst = 0

    def _need(self, eng, tok, kind):
        key, sh, val, teng = tok
        if teng == eng:
            if eng == "tensor" and kind == "waw":
                return
        if self.waited[eng].get(key, 0) >= val:
            return
        self.waited[eng][key] = val
        self.q[eng].append(lambda e, sh=sh, val=val: e.wait_ge(sh, val))

    def _deps(self, eng, reads, writes):
        for t in reads:
            b = t.buf
            if b.w is not None:
                self._need(eng, b.w, "raw")
        for t in writes:
            b = t.buf
            if b.w is not None:
                self._need(eng, b.w, "waw")
            for tok in b.r:
                self._need(eng, tok, "war")

    def _mark(self, tok, reads, writes):
        for t in reads:
            t.buf.r.append(tok)
        for t in writes:
            t.buf.w = tok
            t.buf.r = []

    def op(self, eng, fn, reads=(), writes=()):
        self._deps(eng, reads, writes)
        self.cnt[eng] += 1
        sh = self.sem[eng]
        self.q[eng].append(lambda e, fn=fn, sh=sh: fn(e).then_inc(sh, 1))
        tok = (self.semid[eng], sh, self.cnt[eng], eng)
        self._mark(tok, reads, writes)
        self.ninst += 1

    def dma(self, qn, out, in_, reads=(), writes=()):
        self._deps(qn, reads, writes)
        i = self.dnext[qn]
        self.dnext[qn] = (i + 1) % self.NDMA
        sh = self.dsem[qn][i]
        key = ("D", qn, i)
        prev = self.dval[qn][i]
        if prev > 0 and self.waited[qn].get(key, 0) < prev:
            self.waited[qn][key] = prev
            self.q[qn].append(lambda e, sh=sh, prev=prev: e.wait_ge(sh, prev))
        val = prev + 16
        self.dval[qn][i] = val
        self.q[qn].append(lambda e, out=out, in_=in_, sh=sh: e.dma_start(out=out, in_=in_).then_inc(sh, 16))
        tok = (key, sh, val, None)
        self._mark(tok, reads, writes)
        self.ninst += 1

    def barrier(self):
        toks = []
        for e in ENGS:
            if self.cnt[e] > 0:
                toks.append((self.semid[e], self.sem[e], self.cnt[e], e))
        for qn in self.dsem:
            for i in range(self.NDMA):
                if self.dval[qn][i] > 0:
                    toks.append((("D", qn, i), self.dsem[qn][i], self.dval[qn][i], None))
        for e in ENGS:
            for tok in toks:
                if tok[3] == e:
                    continue
                self._need(e, tok, "raw")

    def emit(self):
        nc = self.nc
        with nc.Block() as block:
            for e in ENGS:
                lst = self.q[e]

                def body(eng, lst=lst):
                    for f in lst:
                        f(eng)
                getattr(block, e)(body)


class Arena:
    def __init__(self, nc, nbytes):
        self.t = nc.alloc_sbuf_tensor("arena", [128, nbytes // 4], F32)
        self.cap = nbytes // 4
        self.off = 0

    def reset(self, off=0):
        self.off = off

    def alloc(self, free_elems, dtype, name="t"):
        nb = free_elems * (2 if dtype == BF16 else 4)
        nw = (nb + 3) // 4
        nw = (nw + 7) // 8 * 8
        assert self.off + nw <= self.cap, ("SBUF arena overflow", name, self.off, nw, self.cap)
        ap = self.t[:, self.off:self.off + nw]
        self.off += nw
        if dtype == BF16:
            ap = ap.bitcast(BF16)[:, 0:free_elems]
        else:
            ap = ap[:, 0:free_elems]
        return Tile(ap, name)


def rope_tables(S, rows, half, pairs_per):
    inv = (1.0 / (np.float32(THETA) ** (np.arange(half, dtype=np.float32) / np.float32(half)))).astype(np.float32)
    pos = np.arange(S, dtype=np.float32)
    ang = (pos[None, :] * inv[:, None]).astype(np.float32)
    c = np.cos(ang).astype(np.float32)
    s = np.sin(ang).astype(np.float32)
    blk_c = np.concatenate([c, c], axis=0)
    blk_s = np.concatenate([-s, s], axis=0)
    reps = rows // (2 * half)
    return np.tile(blk_c, (reps, 1)), np.tile(blk_s, (reps, 1))


class Builder:
    def __init__(self, S, layers, with_final=True):
        self.S = S
        self.NCH = S // 512
        self.NT = S // 128
        self.layers = layers
        self.with_final = with_final
        nc = bass.Bass("TRN2", target_bir_lowering=False)
        self.nc = nc
        self.P = Prog(nc)
        self.A = Arena(nc, 206 * 1024)
        self.psall = nc.alloc_psum_tensor("psall", [128, 8 * 512], F32).ap()
        self.ps = [Tile(self.psall[:, i * 512:(i + 1) * 512], "ps%d" % i) for i in range(8)]
        self.nsp = 0
        self.psn = 0
        self.npt = 0
        self.NC = (S - 32) // 16 + 1
        self.NCT = (self.NC + 127) // 128
        self.NCP = self.NCT * 128
        self.din = {}
        self.scr = {}

    def inp(self, name, shape, dt=F32):
        t = self.nc.dram_tensor(name, list(shape), dt, kind="ExternalInput")
        self.din[name] = t
        return t

    def scratch(self, name, shape, dt):
        t = self.nc.dram_tensor(name, list(shape), dt)
        self.scr[name] = t
        return t

    def psum(self, lo=0, hi=8):
        n = hi - lo
        i = lo + (self.psn % n)
        self.psn += 1
        return self.ps[i]

    def mm(self, out_t, out_ap, lhsT_t, lhsT_ap, rhs_t, rhs_ap, start, stop):
        self.P.op("tensor",
                  lambda e: e.matmul(out_ap, lhsT=lhsT_ap, rhs=rhs_ap, start=start, stop=stop),
                  reads=[lhsT_t, rhs_t], writes=[out_t])

    def act(self, out_t, out_ap, in_t, in_ap, func, scale=1.0, bias=0.0, extra_reads=()):
        self.P.op("scalar",
                  lambda e: e.activation(out=out_ap, in_=in_ap, func=func, bias=bias, scale=scale),
                  reads=[in_t] + list(extra_reads), writes=[out_t])

    def tt(self, eng, out_t, out_ap, a_t, a_ap, b_t, b_ap, op):
        self.P.op(eng, lambda e: e.tensor_tensor(out=out_ap, in0=a_ap, in1=b_ap, op=op),
                  reads=[a_t, b_t], writes=[out_t])

    def ts(self, eng, out_t, out_ap, a_t, a_ap, s1, op0, s2=None, op1=None, extra_reads=()):
        if op1 is None:
            self.P.op(eng, lambda e: e.tensor_scalar(out=out_ap, in0=a_ap, scalar1=s1, scalar2=None, op0=op0),
                      reads=[a_t] + list(extra_reads), writes=[out_t])
        else:
            self.P.op(eng, lambda e: e.tensor_scalar(out=out_ap, in0=a_ap, scalar1=s1, scalar2=s2, op0=op0, op1=op1),
                      reads=[a_t] + list(extra_reads), writes=[out_t])

    def stt(self, out_t, out_ap, a_t, a_ap, scalar, b_t, b_ap, op0, op1, extra_reads=()):
        self.P.op("vector",
                  lambda e: e.scalar_tensor_tensor(out=out_ap, in0=a_ap, scalar=scalar, in1=b_ap, op0=op0, op1=op1),
                  reads=[a_t, b_t] + list(extra_reads), writes=[out_t])

    def copy(self, eng, out_t, out_ap, in_t, in_ap):
        if eng == "scalar":
            self.P.op("scalar", lambda e: e.copy(out=out_ap, in_=in_ap), reads=[in_t], writes=[out_t])
        else:
            self.P.op(eng, lambda e: e.tensor_copy(out=out_ap, in_=in_ap), reads=[in_t], writes=[out_t])

    def memset(self, eng, t, ap, val):
        self.P.op(eng, lambda e: e.memset(ap, val), reads=[], writes=[t])

    def load(self, t, ap, src):
        self.P.dma("sync", ap, src, reads=[], writes=[t])

    def store(self, dst, t, ap):
        self.P.dma("gpsimd", dst, ap, reads=[t], writes=[])

    def setup_consts(self):
        A = self.A
        S = self.S
        self.c_ones_bf = A.alloc(128, BF16, "ones_bf")
        self.c_ones_f = A.alloc(128, F32, "ones_f")
        self.c_ident_f = A.alloc(128, F32, "ident_f")
        self.c_ident_bf = A.alloc(128, BF16, "ident_bf")
        self.c_maskc = A.alloc(4 * 512, F32, "maskc")
        self.memset("vector", self.c_ones_bf, self.c_ones_bf.ap, 1.0)
        self.memset("vector", self.c_ones_f, self.c_ones_f.ap, 1.0)
        self.load(self.c_ident_f, self.c_ident_f.ap, self.din["c_ident"].ap())
        self.copy("vector", self.c_ident_bf, self.c_ident_bf.ap, self.c_ident_f, self.c_ident_f.ap)
        self.load(self.c_maskc, self.c_maskc.ap, self.din["c_maskc"].ap())
        self.c_gain = A.alloc(3 * DEPTH * KC + 2 * KC, F32, "gains")
        self.load(self.c_gain, self.c_gain.ap[:, 0:(3 * DEPTH + 1) * KC], self.din["c_gains"].ap())
        self.base_off = A.off

    def gain(self, kind, li):
        o = (kind * DEPTH + li) * KC
        return self.c_gain.ap[:, o:o + KC]

    def rmsnorm_fm(self, X, nk, ntok, gain_ap, hT, sq, rstd, nfeat, gain_t=None):
        gt = gain_t if gain_t is not None else self.c_gain
        self.P.op("scalar", lambda e: e.activation(out=sq.ap[:, 0:nk * ntok], in_=X.ap[:, 0:nk * ntok], func=AF.Square),
                  reads=[X], writes=[sq])
        ps = self.psum()
        for k in range(nk):
            self.mm(ps, ps.ap[:, 0:ntok], self.c_ones_bf, self.c_ones_bf.ap[:, 0:128], sq, sq.ap[:, k * ntok:(k + 1) * ntok],
                    start=(k == 0), stop=(k == nk - 1))
        self.act(rstd, rstd.ap[:, 0:ntok], ps, ps.ap[:, 0:ntok], AF.Ln, scale=1.0 / nfeat, bias=self.c_eps.ap[:, 0:1],
                 extra_reads=[self.c_eps])
        self.act(rstd, rstd.ap[:, 0:ntok], rstd, rstd.ap[:, 0:ntok], AF.Exp, scale=-0.5)
        for k in range(nk):
            self.stt(hT, hT.ap[:, k * ntok:(k + 1) * ntok], X, X.ap[:, k * ntok:(k + 1) * ntok], gain_ap[:, k:k + 1],
                     rstd, rstd.ap[:, 0:ntok], ALU.mult, ALU.mult, extra_reads=[gt])

    def load_weight(self, W, wcols, src_ap, nk, ncols, col_off=0, stage=None):
        i = 0
        c0 = 0
        while c0 < ncols:
            cw = min(1024, ncols - c0)
            for k in range(nk):
                st = stage[i % 2]
                i += 1
                self.load(st, st.ap[:, 0:cw], src_ap[k * 128:(k + 1) * 128, c0:c0 + cw])
                dst = W.ap[:, k * wcols + col_off + c0: k * wcols + col_off + c0 + cw]
                eng = ("vector", "gpsimd")[i % 2]
                self.copy(eng, W, dst, st, st.ap[:, 0:cw])
            c0 += cw

    def phase_init(self):
        A, P, S = self.A, self.P, self.S
        A.reset(self.base_off)
        xin = self.din["x"].ap()
        xT = self.scr["xT"].ap()
        xa = [A.alloc(D, F32, "xin%d" % i) for i in range(2)]
        st = [A.alloc(KC * 512, F32, "xst%d" % i) for i in range(2)]
        for c in range(self.NCH):
            stg = st[c % 2]
            for s in range(4):
                t = c * 4 + s
                xt = xa[t % 2]
                self.load(xt, xt.ap, xin[t * 128:(t + 1) * 128, :])
                for kk in range(2):
                    ps = self.psum()
                    for j in range(4):
                        k = kk * 4 + j
                        P.op("tensor", lambda e, o=ps.ap[:, j * 128:(j + 1) * 128], i=xt.ap[:, k * 128:(k + 1) * 128]:
                             e.transpose(o, i, self.c_ident_f.ap[:, 0:128]), reads=[xt, self.c_ident_f], writes=[ps])
                    dst = stg.ap.rearrange("p (k t) -> p k t", k=KC)[:, kk * 4:(kk + 1) * 4, s * 128:(s + 1) * 128]
                    src = ps.ap.rearrange("p (j t) -> p j t", j=4)
                    self.copy(("vector", "scalar")[kk], stg, dst, ps, src)
            self.store(xT[:, :, c * 512:(c + 1) * 512].rearrange("k p t -> p k t"), stg,
                       stg.ap.rearrange("p (k t) -> p k t", k=KC))
        mem = self.din["mem"].ap()
        gB = A.alloc(D, F32, "memgain")
        self.load(gB, gB.ap, self.din["mem_norm"].ap().partition_broadcast(128))
        for s in range(2):
            mt = xa[s]
            self.load(mt, mt.ap, mem[s * 128:(s + 1) * 128, :])
            sqj = st[0]
            ss = A.alloc(8, F32, "memss%d" % s)
            P.op("scalar", lambda e, o=sqj.ap[:, 0:D], i=mt.ap, a=ss.ap[:, 0:1]: e.activation(out=o, in_=i, func=AF.Square, accum_out=a),
                 reads=[mt], writes=[sqj, ss])
            self.act(ss, ss.ap[:, 1:2], ss, ss.ap[:, 0:1], AF.Ln, scale=1.0 / D, bias=self.c_eps.ap[:, 0:1], extra_reads=[self.c_eps])
            self.act(ss, ss.ap[:, 2:3], ss, ss.ap[:, 1:2], AF.Exp, scale=-0.5)
            mn = st[1]
            self.stt(mn, mn.ap[:, 0:D], mt, mt.ap, ss.ap[:, 2:3], gB, gB.ap, ALU.mult, ALU.mult, extra_reads=[ss])
            for kk in range(2):
                ps = self.psum()
                for j in range(4):
                    k = kk * 4 + j
                    P.op("tensor", lambda e, o=ps.ap[:, j * 128:(j + 1) * 128], i=mn.ap[:, k * 128:(k + 1) * 128]:
                         e.transpose(o, i, self.c_ident_f.ap[:, 0:128]), reads=[mn, self.c_ident_f], writes=[ps])
                dst = self.c_memT.ap.rearrange("p (k t) -> p k t", k=KC)[:, kk * 4:(kk + 1) * 4, s * 128:(s + 1) * 128]
                self.copy("vector", self.c_memT, dst, ps, ps.ap.rearrange("p (j t) -> p j t", j=4))
        P.barrier()

    def phase_final(self):
        A, P, S = self.A, self.P, self.S
        A.reset(self.base_off)
        xT = self.scr["xT"].ap()
        out = self.dout.ap()
        X = [A.alloc(KC * 512, F32, "fx%d" % i) for i in range(2)]
        Y = A.alloc(KC * 512, F32, "fy")
        sq = A.alloc(KC * 512, BF16, "fsq")
        rstd = A.alloc(512, F32, "frstd")
        ot = [A.alloc(D, F32, "fo%d" % i) for i in range(2)]
        g = self.c_gain.ap[:, 3 * DEPTH * KC: 3 * DEPTH * KC + KC]
        for c in range(self.NCH):
            x = X[c % 2]
            self.load(x, x.ap.rearrange("p (k t) -> p k t", k=KC), xT[:, :, c * 512:(c + 1) * 512].rearrange("k p t -> p k t"))
            self.rmsnorm_fm(x, KC, 512, g, Y, sq, rstd, D)
            for s in range(4):
                o = ot[s % 2]
                for kk in range(2):
                    ps = self.psum()
                    for j in range(4):
                        k = kk * 4 + j
                        P.op("tensor", lambda e, oo=ps.ap[:, j * 128:(j + 1) * 128], i=Y.ap[:, k * 512 + s * 128: k * 512 + (s + 1) * 128]:
                             e.transpose(oo, i, self.c_ident_f.ap[:, 0:128]), reads=[Y, self.c_ident_f], writes=[ps])
                    self.copy(("vector", "scalar")[kk], o, o.ap[:, kk * 512:(kk + 1) * 512], ps, ps.ap)
                t = c * 4 + s
                self.store(out[t * 128:(t + 1) * 128, :], o, o.ap)
        P.barrier()

    def phase_ffn(self, li):
        A, P, S = self.A, self.P, self.S
        A.reset(self.base_off)
        NTK = 256
        NF = DFF // 128
        xT = self.scr["xT"].ap()
        W13 = A.alloc(KC * 2 * DFF, BF16, "w13")
        W2 = A.alloc(NF * D, BF16, "w2")
        stage = [A.alloc(1024, F32, "wst%d" % i) for i in range(2)]
        self.load_weight(W13, 2 * DFF, self.din["ffn_w13"].ap()[li], KC, 2 * DFF, stage=stage)
        self.load_weight(W2, D, self.din["ffn_w2"].ap()[li], NF, D, stage=stage)
        X = [A.alloc(KC * NTK, F32, "x%d" % i) for i in range(2)]
        hT = A.alloc(KC * NTK, BF16, "hT")
        sq = A.alloc(KC * NTK, BF16, "sq")
        rstd = A.alloc(NTK, F32, "rstd")
        act = A.alloc(NF * NTK, BF16, "act")
        sg = [A.alloc(NTK, F32, "sg%d" % i) for i in range(2)]
        g = self.gain(2, li)
        for c in range(S // NTK):
            x = X[c % 2]
            t0 = c * NTK
            self.load(x, x.ap.rearrange("p (k t) -> p k t", k=KC), xT[:, :, t0:t0 + NTK].rearrange("k p t -> p k t"))
            self.rmsnorm_fm(x, KC, NTK, g, hT, sq, rstd, D)
            for f in range(NF):
                pg = self.psum()
                for k in range(KC):
                    self.mm(pg, pg.ap[:, 0:NTK], W13, W13.ap[:, k * 2 * DFF + f * 128: k * 2 * DFF + (f + 1) * 128],
                            hT, hT.ap[:, k * NTK:(k + 1) * NTK], k == 0, k == KC - 1)
                pu = self.psum()
                for k in range(KC):
                    self.mm(pu, pu.ap[:, 0:NTK], W13, W13.ap[:, k * 2 * DFF + DFF + f * 128: k * 2 * DFF + DFF + (f + 1) * 128],
                            hT, hT.ap[:, k * NTK:(k + 1) * NTK], k == 0, k == KC - 1)
                s_ = sg[f % 2]
                self.act(s_, s_.ap[:, 0:NTK], pg, pg.ap[:, 0:NTK], AF.Silu)
                self.tt("vector", act, act.ap[:, f * NTK:(f + 1) * NTK], pu, pu.ap[:, 0:NTK], s_, s_.ap[:, 0:NTK], ALU.mult)
            for n in range(KC):
                po = self.psum()
                for f in range(NF):
                    self.mm(po, po.ap[:, 0:NTK], W2, W2.ap[:, f * D + n * 128: f * D + (n + 1) * 128],
                            act, act.ap[:, f * NTK:(f + 1) * NTK], f == 0, f == NF - 1)
                self.tt("vector", x, x.ap[:, n * NTK:(n + 1) * NTK], po, po.ap[:, 0:NTK], x, x.ap[:, n * NTK:(n + 1) * NTK], ALU.add)
            self.store(xT[:, :, t0:t0 + NTK].rearrange("k p t -> p k t"), x, x.ap.rearrange("p (k t) -> p k t", k=KC))
        P.barrier()

    def phase_out_xattn(self, li, w_out_ap):
        A, P, S = self.A, self.P, self.S
        A.reset(self.base_off)
        xT = self.scr["xT"].ap()
        oT = self.scr["oT"].ap()
        Wo = A.alloc(KC * D, BF16, "wo")
        Wq = A.alloc(KC * 512, BF16, "wq")
        Wx = A.alloc(4 * D, BF16, "wxo")
        Wkv = A.alloc(KC * D, BF16, "wkv")
        stage = [A.alloc(1024, F32, "wst%d" % i) for i in range(2)]
        self.load_weight(Wo, D, w_out_ap, KC, D, stage=stage)
        self.load_weight(Wq, 512, self.din["xa_wq"].ap()[li], KC, 512, stage=stage)
        self.load_weight(Wx, D, self.din["xa_wo"].ap()[li], 4, D, stage=stage)
        self.load_weight(Wkv, D, self.din["xa_wkv"].ap()[li], KC, D, stage=stage)
        KmT = A.alloc(4 * MEM, BF16, "kmT")
        Vm = A.alloc(2 * 512, BF16, "vm")
        memT = self.c_memT
        for h in range(4):
            ps = self.psum()
            for k in range(KC):
                self.mm(ps, ps.ap[:, 0:MEM], Wkv, Wkv.ap[:, k * D + h * 128: k * D + (h + 1) * 128],
                        memT, memT.ap[:, k * MEM:(k + 1) * MEM], k == 0, k == KC - 1)
            self.copy("vector", KmT, KmT.ap[:, h * MEM:(h + 1) * MEM], ps, ps.ap[:, 0:MEM])
        for s in range(2):
            ps = self.psum()
            for k in range(KC):
                self.mm(ps, ps.ap[:, 0:512], memT, memT.ap[:, k * MEM + s * 128: k * MEM + (s + 1) * 128],
                        Wkv, Wkv.ap[:, k * D + 512: k * D + 1024], k == 0, k == KC - 1)
            self.copy("vector", Vm, Vm.ap[:, s * 512:(s + 1) * 512], ps, ps.ap[:, 0:512])
        X = [A.alloc(KC * 512, F32, "x%d" % i) for i in range(2)]
        O = [A.alloc(KC * 512, BF16, "o%d" % i) for i in range(2)]
        hT = A.alloc(KC * 512, BF16, "hT")
        sq = A.alloc(KC * 512, BF16, "sq")
        rstd = A.alloc(512, F32, "rstd")
        qT = A.alloc(4 * 512, BF16, "qT")
        pT = [A.alloc(512, BF16, "pT%d" % i) for i in range(4)]
        rec = A.alloc(512, F32, "rec")
        xo = A.alloc(4 * 512, BF16, "xo")
        g = self.gain(1, li)
        sc = 128.0 ** -0.5
        for c in range(self.NCH):
            x = X[c % 2]
            o = O[c % 2]
            t0 = c * 512
            self.load(x, x.ap.rearrange("p (k t) -> p k t", k=KC), xT[:, :, t0:t0 + 512].rearrange("k p t -> p k t"))
            self.load(o, o.ap.rearrange("p (k t) -> p k t", k=KC), oT[:, :, t0:t0 + 512].rearrange("k p t -> p k t"))
            for n in range(KC):
                ps = self.psum()
                for k in range(KC):
                    self.mm(ps, ps.ap, Wo, Wo.ap[:, k * D + n * 128: k * D + (n + 1) * 128], o, o.ap[:, k * 512:(k + 1) * 512],
                            k == 0, k == KC - 1)
                self.tt("vector", x, x.ap[:, n * 512:(n + 1) * 512], ps, ps.ap, x, x.ap[:, n * 512:(n + 1) * 512], ALU.add)
            self.rmsnorm_fm(x, KC, 512, g, hT, sq, rstd, D)
            for h in range(4):
                ps = self.psum()
                for k in range(KC):
                    self.mm(ps, ps.ap, Wq, Wq.ap[:, k * 512 + h * 128: k * 512 + (h + 1) * 128], hT, hT.ap[:, k * 512:(k + 1) * 512],
                            k == 0, k == KC - 1)
                self.copy("scalar", qT, qT.ap[:, h * 512:(h + 1) * 512], ps, ps.ap)
            for h in range(4):
                pts = []
                for m in range(2):
                    ps = self.psum()
                    self.mm(ps, ps.ap, KmT, KmT.ap[:, h * MEM + m * 128: h * MEM + (m + 1) * 128], qT, qT.ap[:, h * 512:(h + 1) * 512],
                            True, True)
                    pt = pT[(h * 2 + m) % 4]
                    self.act(pt, pt.ap, ps, ps.ap, AF.Exp, scale=sc)
                    pts.append(pt)
                po = self.psum()
                for m in range(2):
                    self.mm(po, po.ap, Vm, Vm.ap[:, m * 512 + h * 128: m * 512 + (h + 1) * 128], pts[m], pts[m].ap, m == 0, m == 1)
                pz = self.psum()
                for m in range(2):
                    self.mm(pz, pz.ap, self.c_ones_bf, self.c_ones_bf.ap[:, 0:128], pts[m], pts[m].ap, m == 0, m == 1)
                P.op("vector", lambda e, o_=rec.ap, i_=pz.ap: e.reciprocal(out=o_, in_=i_), reads=[pz], writes=[rec])
                self.tt("vector", xo, xo.ap[:, h * 512:(h + 1) * 512], po, po.ap, rec, rec.ap, ALU.mult)
            for n in range(KC):
                ps = self.psum()
                for h in range(4):
                    self.mm(ps, ps.ap, Wx, Wx.ap[:, h * D + n * 128: h * D + (n + 1) * 128], xo, xo.ap[:, h * 512:(h + 1) * 512],
                            h == 0, h == 3)
                self.tt("vector", x, x.ap[:, n * 512:(n + 1) * 512], ps, ps.ap, x, x.ap[:, n * 512:(n + 1) * 512], ALU.add)
            self.store(xT[:, :, t0:t0 + 512].rearrange("k p t -> p k t"), x, x.ap.rearrange("p (k t) -> p k t", k=KC))
        P.barrier()

    def attn_pass(self, k_t, k_ap_fn, q_t, q_ap, v_t, v_ap_fn, vcols, scale, Oacc, pairs, accs=None, ptiles=None, ssb=None):
        P = self.P
        total = 2 * len(pairs)
        done = [0]
        used = [False, False]

        def emit_pv(item):
            kta, ktb, pt = item
            for half, kt in ((0, kta), (1, ktb)):
                self.mm(Oacc, Oacc.ap[0:vcols, :], v_t, v_ap_fn(kt), pt, pt.ap[:, half * 512:(half + 1) * 512], done[0] == 0, done[0] == total - 1)
                done[0] += 1

        pend = None
        for pi, (kta, ktb, minfo) in enumerate(pairs):
            b0 = 2 * (self.nsp % 2)
            self.nsp += 1
            pa, pb = self.ps[b0], self.ps[b0 + 1]
            self.mm(pa, pa.ap, k_t, k_ap_fn(kta), q_t, q_ap, True, True)
            self.mm(pb, pb.ap, k_t, k_ap_fn(ktb), q_t, q_ap, True, True)
            pair_ap = self.psall[:, b0 * 512:(b0 + 2) * 512]
            pt = ptiles[self.npt % len(ptiles)]
            self.npt += 1
            if minfo is not None:
                sb = ssb[self.npt % len(ssb)]
                P.op("vector", lambda e, o=sb.ap, i0=pair_ap, i1=minfo[1]: e.tensor_tensor(out=o, in0=i0, in1=i1, op=ALU.min),
                     reads=[pa, pb, minfo[0]], writes=[sb])
                P.op("scalar", lambda e, o=pt.ap, i=sb.ap: e.activation(out=o, in_=i, func=AF.Exp, scale=scale), reads=[sb], writes=[pt])
            else:
                P.op("scalar", lambda e, o=pt.ap, i=pair_ap: e.activation(out=o, in_=i, func=AF.Exp, scale=scale), reads=[pa, pb], writes=[pt])
            if accs is not None:
                accA, tmpA, accB, tmpB = accs
                if pi % 3 != 2:
                    eng, acc, tmp, ui = "vector", accA, tmpA, 0
                else:
                    eng, acc, tmp, ui = "gpsimd", accB, tmpB, 1
                if not used[ui]:
                    self.tt(eng, acc, acc.ap, pt, pt.ap[:, 0:512], pt, pt.ap[:, 512:1024], ALU.add)
                    used[ui] = True
                else:
                    self.tt(eng, tmp, tmp.ap, pt, pt.ap[:, 0:512], pt, pt.ap[:, 512:1024], ALU.add)
                    self.tt(eng, acc, acc.ap, acc, acc.ap, tmp, tmp.ap, ALU.add)
            if pend is not None:
                emit_pv(pend)
            pend = (kta, ktb, pt)
        emit_pv(pend)
        return used

    def causal_pairs(self, qc):
        lst = [(2 * i, 2 * i + 1, None) for i in range(2 * qc)]
        lst.append((4 * qc, 4 * qc + 1, (self.c_maskc, self.c_maskc.ap[:, 0:1024])))
        lst.append((4 * qc + 2, 4 * qc + 3, (self.c_maskc, self.c_maskc.ap[:, 1024:2048])))
        return lst

    def causal_list(self, qc):
        lst = [(kt, None) for kt in range(4 * qc)]
        for o in range(4):
            lst.append((4 * qc + o, (self.c_maskc, self.c_maskc.ap[:, o * 512:(o + 1) * 512])))
        return lst

    def phase_even_in(self, li):
        A, P, S = self.A, self.P, self.S
        j = li // 2
        A.reset(self.base_off)
        xT = self.scr["xT"].ap()
        w_in = self.din["ev_w_in"].ap()[j]
        W = A.alloc(KC * EV_W, BF16, "w_in")
        Wp = A.alloc(KC * 1024, BF16, "w_perm")
        stage = [A.alloc(1024, F32, "wst%d" % i) for i in range(2)]
        self.load_weight(W, EV_W, w_in, KC, EV_W, stage=stage)
        wv = w_in[:, 0:1024].rearrange("k (b two d) -> k b two d", two=2, d=32)
        i = 0
        for k in range(KC):
            for two in range(2):
                st = stage[i % 2]
                i += 1
                self.load(st, st.ap[:, 0:512].rearrange("p (b d) -> p b d", d=32), wv[k * 128:(k + 1) * 128, :, 1 - two, :])
                dst = Wp.ap[:, k * 1024:(k + 1) * 1024].rearrange("p (b two d) -> p b two d", two=2, d=32)[:, :, two, :]
                self.copy(("vector", "gpsimd")[i % 2], Wp, dst, st, st.ap[:, 0:512].rearrange("p (b d) -> p b d", d=32))
        nbf = A.alloc(8, F32, "nbf")
        self.load(nbf, nbf.ap[0:8, 0:1], self.din["ev_b_f"].ap()[j].rearrange("(h o) -> h o", o=1))
        self.ts("vector", nbf, nbf.ap[0:8, 1:2], nbf, nbf.ap[0:8, 0:1], -1.0, ALU.mult)
        X = [A.alloc(KC * 512, F32, "x%d" % i) for i in range(2)]
        hT = A.alloc(KC * 512, BF16, "hT")
        sq = A.alloc(KC * 512, BF16, "sq")
        rstd = A.alloc(512, F32, "rstd")
        COS = [A.alloc(512, F32, "cos%d" % i) for i in range(2)]
        SIN = [A.alloc(512, F32, "sin%d" % i) for i in range(2)]
        t1 = [A.alloc(512, F32, "t1_%d" % i) for i in range(2)]
        t2 = [A.alloc(512, F32, "t2_%d" % i) for i in range(2)]
        qo = [A.alloc(512, BF16, "qo%d" % i) for i in range(4)]
        vo = [A.alloc(512, BF16, "vo%d" % i) for i in range(2)]
        lf = [A.alloc(512, F32, "lf%d" % i) for i in range(2)]
        g = self.gain(0, li)
        dq, dk, dv = self.scr["dq"].ap(), self.scr["dk"].ap(), self.scr["dv"].ap()
        fq, fk, fv = self.scr["fq"].ap(), self.scr["fk"].ap(), self.scr["fv"].ap()
        lfp = self.scr["lfp"].ap()
        nq = 0
        for c in range(self.NCH):
            x = X[c % 2]
            t0 = c * 512
            self.load(x, x.ap.rearrange("p (k t) -> p k t", k=KC), xT[:, :, t0:t0 + 512].rearrange("k p t -> p k t"))
            cs, sn = COS[c % 2], SIN[c % 2]
            self.load(cs, cs.ap, self.din["c_cos64"].ap()[:, t0:t0 + 512])
            self.load(sn, sn.ap, self.din["c_sin64"].ap()[:, t0:t0 + 512])
            self.rmsnorm_fm(x, KC, 512, g, hT, sq, rstd, D)

            def proj(Wt, wcols, col0, ncols):
                ps = self.psum()
                for k in range(KC):
                    self.mm(ps, ps.ap[0:ncols, :], Wt, Wt.ap[:, k * wcols + col0: k * wcols + col0 + ncols], hT, hT.ap[:, k * 512:(k + 1) * 512],
                            k == 0, k == KC - 1)
                return ps
            for which, dst in ((0, dq), (1, dk)):
                for h in range(4):
                    col = which * 512 + h * 128
                    pa = proj(W, EV_W, col, 128)
                    pb = proj(Wp, 1024, col, 128)
                    a1, a2 = t1[nq % 2], t2[nq % 2]
                    q_ = qo[nq % 4]
                    nq += 1
                    self.tt("vector", a1, a1.ap, pa, pa.ap, cs, cs.ap, ALU.mult)
                    self.tt("vector", a2, a2.ap, pb, pb.ap, sn, sn.ap, ALU.mult)
                    self.tt("gpsimd", q_, q_.ap, a1, a1.ap, a2, a2.ap, ALU.add)
                    self.store(dst[h, :, t0:t0 + 512], q_, q_.ap)
            for which, dst in ((0, fq), (1, fk)):
                for pr in range(4):
                    col = 1536 + which * 512 + pr * 128
                    pa = proj(W, EV_W, col, 128)
                    q_ = qo[nq % 4]
                    nq += 1
                    self.copy("scalar", q_, q_.ap, pa, pa.ap)
                    for two in range(2):
                        self.store(dst[pr * 2 + two, 0:64, t0:t0 + 512], q_, q_.ap[two * 64:(two + 1) * 64, :])
            for s in range(4):
                for which, col, dst in ((0, 1024, dv), (1, 2560, fv)):
                    ps = self.psum()
                    for k in range(KC):
                        self.mm(ps, ps.ap, hT, hT.ap[:, k * 512 + s * 128: k * 512 + (s + 1) * 128], W, W.ap[:, k * EV_W + col: k * EV_W + col + 512],
                                k == 0, k == KC - 1)
                    v_ = vo[which]
                    self.copy(("scalar", "vector")[which], v_, v_.ap, ps, ps.ap)
                    tt0 = t0 + s * 128
                    if which == 0:
                        self.store(dst[:, tt0:tt0 + 128, :].rearrange("h t d -> t h d"), v_, v_.ap.rearrange("p (h d) -> p h d", h=4))
                    else:
                        self.store(dst[:, tt0:tt0 + 128, :].rearrange("h t d -> t h d"), v_, v_.ap.rearrange("p (h d) -> p h d", h=8))
            pz = proj(W, EV_W, 3072, 8)
            l_ = lf[c % 2]
            self.act(l_, l_.ap[0:8, :], pz, pz.ap[0:8, :], AF.Exp, scale=-1.0, bias=nbf.ap[0:8, 1:2], extra_reads=[nbf])
            self.act(l_, l_.ap[0:8, :], l_, l_.ap[0:8, :], AF.Ln, scale=1.0, bias=self.c_one.ap[0:8, 0:1], extra_reads=[self.c_one])
            self.store(lfp[:, t0:t0 + 512], l_, l_.ap[0:8, :])
        P.barrier()

    def phase_even_scan(self, li):
        A, P, S = self.A, self.P, self.S
        A.reset(self.base_off)
        PW = min(2048, S)
        npieces = S // PW
        lfp = self.scr["lfp"].ap()
        fq, fk = self.scr["fq"].ap(), self.scr["fk"].ap()
        ones = A.alloc(PW, BF16, "ones")
        self.memset("vector", ones, ones.ap[0:8, :], 1.0)
        self.c_onesrow = A.alloc(PW, F32, "onesrow")
        self.memset("gpsimd", self.c_onesrow, self.c_onesrow.ap[0:8, :], 1.0)
        L = [A.alloc(PW, F32, "L%d" % i) for i in range(2)]
        C = [A.alloc(PW, F32, "C%d" % i) for i in range(2)]
        R = A.alloc(PW, F32, "R")
        hi = [[A.alloc(PW, BF16, "sp%d_%d" % (i, jj)) for jj in range(3)] for i in range(2)]
        ng = [[A.alloc(PW, BF16, "ng%d_%d" % (i, jj)) for jj in range(3)] for i in range(2)]
        zero = A.alloc(8, F32, "zero")
        self.memset("vector", zero, zero.ap[0:8, 0:1], 0.0)
        prev = None
        for pc in range(npieces):
            l_, c_ = L[pc % 2], C[pc % 2]
            t0 = pc * PW
            self.load(l_, l_.ap[0:8, :], lfp[:, t0:t0 + PW])
            init_t = zero if prev is None else prev
            init_ap = zero.ap[0:8, 0:1] if prev is None else prev.ap[0:8, PW - 1:PW]
            P.op("vector", lambda e, o=c_.ap[0:8, :], d1=l_.ap[0:8, :], ia=init_ap, d0=self.c_onesrow.ap[0:8, 0:PW]:
                 e.tensor_tensor_scan(out=o, data0=d0, data1=d1, initial=ia, op0=ALU.mult, op1=ALU.add),
                 reads=[l_, init_t, self.c_onesrow], writes=[c_])
            prev = c_
            h3 = hi[pc % 2]
            n3 = ng[pc % 2]
            self.ts("vector", R, R.ap[0:8, :], c_, c_.ap[0:8, :], 8.0, ALU.mult)
            for jj in range(3):
                self.copy("vector", h3[jj], h3[jj].ap[0:8, :], R, R.ap[0:8, :])
                if jj < 2:
                    self.tt("vector", R, R.ap[0:8, :], R, R.ap[0:8, :], h3[jj], h3[jj].ap[0:8, :], ALU.subtract)
                self.ts("vector", n3[jj], n3[jj].ap[0:8, :], h3[jj], h3[jj].ap[0:8, :], -1.0, ALU.mult)
            for jj in range(3):
                self.store(fq[:, 64 + jj, t0:t0 + PW], n3[jj], n3[jj].ap[0:8, :])
                self.store(fq[:, 67 + jj, t0:t0 + PW], ones, ones.ap[0:8, :])
                self.store(fk[:, 64 + jj, t0:t0 + PW], ones, ones.ap[0:8, :])
                self.store(fk[:, 67 + jj, t0:t0 + PW], h3[jj], h3[jj].ap[0:8, :])
        P.barrier()

    def phase_even_attn(self, li):
        A, P, S, NT = self.A, self.P, self.S, self.NT
        j = li // 2
        A.reset(self.base_off)
        lam_init = 0.8 - 0.6 * math.exp(-0.3 * li)
        dq, dk, dv = self.scr["dq"].ap(), self.scr["dk"].ap(), self.scr["dv"].ap()
        fq, fk, fv = self.scr["fq"].ap(), self.scr["fk"].ap(), self.scr["fv"].ap()
        oT = self.scr["oT"].ap()
        KT = [A.alloc(S, BF16, "KT%d" % i) for i in range(2)]
        V = [A.alloc(NT * 129, BF16, "V%d" % i) for i in range(2)]
        Q = [A.alloc(512, BF16, "Q%d" % i) for i in range(2)]
        pts = [A.alloc(1024, BF16, "pt%d" % i) for i in range(4)]
        ssb = [A.alloc(1024, F32, "ssb%d" % i) for i in range(2)]
        rec = [A.alloc(512, F32, "rec%d" % i) for i in range(2)]
        recB = A.alloc(512, F32, "recB")
        on = [A.alloc(512, F32, "on%d" % i) for i in range(2)]
        oa = A.alloc(512, F32, "oa")
        osq = A.alloc(512, BF16, "osq")
        orstd = A.alloc(512, F32, "orstd")
        ob = [A.alloc(512, BF16, "ob%d" % i) for i in range(2)]
        accA = A.alloc(512, F32, "accA")
        accB = A.alloc(512, F32, "accB")
        tmpA = A.alloc(512, F32, "tmpA")
        tmpB = A.alloc(512, F32, "tmpB")
        lam = A.alloc(4 * 64 + 8, F32, "lam")
        self.load(lam, lam.ap[:, 0:256], self.din["ev_lam"].ap()[j].rearrange("a d -> (a d)").partition_broadcast(128))
        lp = A.alloc(128, F32, "lamp")
        LS = 256
        self.tt("vector", lp, lp.ap[:, 0:64], lam, lam.ap[:, 0:64], lam, lam.ap[:, 64:128], ALU.mult)
        self.tt("vector", lp, lp.ap[:, 64:128], lam, lam.ap[:, 128:192], lam, lam.ap[:, 192:256], ALU.mult)
        P.op("vector", lambda e: e.reduce_sum(out=lam.ap[:, LS:LS + 1], in_=lp.ap[:, 0:64], axis=mybir.AxisListType.X), reads=[lp], writes=[lam])
        P.op("vector", lambda e: e.reduce_sum(out=lam.ap[:, LS + 1:LS + 2], in_=lp.ap[:, 64:128], axis=mybir.AxisListType.X), reads=[lp], writes=[lam])
        self.act(lam, lam.ap[:, LS + 2:LS + 4], lam, lam.ap[:, LS:LS + 2], AF.Exp)
        self.tt("vector", lam, lam.ap[:, LS + 4:LS + 5], lam, lam.ap[:, LS + 2:LS + 3], lam, lam.ap[:, LS + 3:LS + 4], ALU.subtract)
        self.ts("vector", lam, lam.ap[:, LS + 5:LS + 6], lam, lam.ap[:, LS + 4:LS + 5], lam_init, ALU.add, -1.0, ALU.mult)
        neg_lam = lam.ap[:, LS + 5:LS + 6]
        sub = A.alloc(8, F32, "subln")
        self.load(sub, sub.ap[:, 0:1], self.din["ev_subln"].ap()[j].rearrange("(p o) -> p o", o=1))
        self.ts("vector", sub, sub.ap[:, 1:2], sub, sub.ap[:, 0:1], 1.0 - lam_init, ALU.mult)
        for i in range(2):
            self.memset("gpsimd", V[i], V[i].ap, 1.0)
        units = [("f", h) for h in range(8)] + [("d", h) for h in range(4)]

        def load_unit(u, slot):
            kind, h = u
            kt_, v_ = KT[slot], V[slot]
            if kind == "f":
                self.load(kt_, kt_.ap[0:70, :], fk[h])
                self.load(v_, v_.ap.rearrange("p (t c) -> p t c", c=129)[:, :, 0:64], fv[h].rearrange("(t p) d -> p t d", p=128))
            else:
                self.load(kt_, kt_.ap, dk[h])
                self.load(v_, v_.ap.rearrange("p (t c) -> p t c", c=129)[:, :, 0:128], dv[h].rearrange("(t p) d -> p t d", p=128))

        load_unit(units[0], 0)
        nqi = 0
        npass = 0
        for ui, u in enumerate(units):
            kind, h = u
            slot = ui % 2
            if ui + 1 < len(units):
                load_unit(units[ui + 1], 1 - slot)
            kt_, v_ = KT[slot], V[slot]
            for qc in range(self.NCH):
                q_ = Q[nqi % 2]
                nqi += 1
                t0 = qc * 512
                kl = self.causal_pairs(qc)
                if kind == "f":
                    self.load(q_, q_.ap[0:70, :], fq[h, :, t0:t0 + 512])
                    Oacc = self.ps[4 + npass % 2]
                    npass += 1
                    self.attn_pass(kt_, lambda kt, kt_=kt_: kt_.ap[0:70, kt * 128:(kt + 1) * 128], q_, q_.ap[0:70, :],
                                   v_, lambda kt, v_=v_: v_.ap[:, kt * 129: kt * 129 + 65], 65, 0.125, Oacc, kl, ptiles=pts, ssb=ssb)
                    r_ = rec[0]
                    P.op("vector", lambda e, o=r_.ap[64:65, :], i=Oacc.ap[64:65, :]: e.reciprocal(out=o, in_=i), reads=[Oacc], writes=[r_])
                    pb = self.ps[6]
                    self.mm(pb, pb.ap[0:64, :], self.c_ones_f, self.c_ones_f.ap[64:65, 0:64], r_, r_.ap[64:65, :], True, True)
                    self.copy("scalar", recB, recB.ap[0:64, :], pb, pb.ap[0:64, :])
                    o_ = ob[npass % 2]
                    self.tt("vector", o_, o_.ap[0:64, :], Oacc, Oacc.ap[0:64, :], recB, recB.ap[0:64, :], ALU.mult)
                    self.store(oT[4 + h // 2, (h % 2) * 64:(h % 2) * 64 + 64, t0:t0 + 512], o_, o_.ap[0:64, :])
                else:
                    self.load(q_, q_.ap, dq[h, :, t0:t0 + 512])
                    for sub_j in range(2):
                        Oacc = self.ps[4 + npass % 2]
                        npass += 1
                        sacc = self.ps[6]
                        r0 = 64 * sub_j
                        used = self.attn_pass(kt_, lambda kt, kt_=kt_, r0=r0: kt_.ap[r0:r0 + 64, kt * 128:(kt + 1) * 128], q_, q_.ap[r0:r0 + 64, :],
                                              v_, lambda kt, v_=v_: v_.ap[:, kt * 129: kt * 129 + 128], 128, 0.125, Oacc, kl,
                                              accs=(accA, tmpA, accB, tmpB), ptiles=pts, ssb=ssb)
                        if used[1]:
                            self.tt("vector", accA, accA.ap, accA, accA.ap, accB, accB.ap, ALU.add)
                        self.mm(sacc, sacc.ap[0:1, :], self.c_ones_f, self.c_ones_f.ap[:, 0:1], accA, accA.ap, True, True)
                        r_ = rec[sub_j]
                        P.op("vector", lambda e, o=r_.ap[0:1, :], i=sacc.ap[0:1, :]: e.reciprocal(out=o, in_=i), reads=[sacc], writes=[r_])
                        if sub_j == 1:
                            self.ts("vector", r_, r_.ap[0:1, :], r_, r_.ap[0:1, :], neg_lam[0:1, :], ALU.mult, extra_reads=[lam])
                        pb = self.ps[7]
                        self.mm(pb, pb.ap, self.c_ones_f, self.c_ones_f.ap[0:1, 0:128], r_, r_.ap[0:1, :], True, True)
                        self.copy("scalar", recB, recB.ap, pb, pb.ap)
                        self.tt("vector", on[sub_j], on[sub_j].ap, Oacc, Oacc.ap, recB, recB.ap, ALU.mult)
                    self.tt("gpsimd", oa, oa.ap, on[0], on[0].ap, on[1], on[1].ap, ALU.add)
                    self.act(osq, osq.ap, oa, oa.ap, AF.Square)
                    pz = self.ps[7]
                    self.mm(pz, pz.ap, self.c_ones_bf, self.c_ones_bf.ap[:, 0:128], osq, osq.ap, True, True)
                    self.act(orstd, orstd.ap, pz, pz.ap, AF.Ln, scale=1.0 / 128, bias=self.c_eps.ap[:, 0:1], extra_reads=[self.c_eps])
                    self.act(orstd, orstd.ap, orstd, orstd.ap, AF.Exp, scale=-0.5)
                    o_ = ob[npass % 2]
                    self.stt(o_, o_.ap, oa, oa.ap, sub.ap[:, 1:2], orstd, orstd.ap, ALU.mult, ALU.mult, extra_reads=[sub])
                    self.store(oT[h, :, t0:t0 + 512], o_, o_.ap)
        P.barrier()

    def load_perm(self, Wp, wpcols, dst_col, src_cols_ap, n, half, stage, nk=KC):
        nb = n // (2 * half)
        sv = src_cols_ap.rearrange("k (b two d) -> k b two d", two=2, d=half)
        i = 0
        for k in range(nk):
            for two in range(2):
                st = stage[i % 2]
                i += 1
                self.load(st, st.ap[:, 0:nb * half].rearrange("p (b d) -> p b d", d=half), sv[k * 128:(k + 1) * 128, :, 1 - two, :])
                dst = Wp.ap[:, k * wpcols + dst_col: k * wpcols + dst_col + n].rearrange("p (b two d) -> p b two d", two=2, d=half)[:, :, two, :]
                self.copy(("vector", "gpsimd")[i % 2], Wp, dst, st, st.ap[:, 0:nb * half].rearrange("p (b d) -> p b d", d=half))

    def rope_evac(self, pa, pb, rows, cs, sn, t1, t2, out_t, out_ap):
        self.tt("vector", t1, t1.ap[0:rows, :], pa, pa.ap[0:rows, :], cs, cs.ap[0:rows, :], ALU.mult)
        self.tt("vector", t2, t2.ap[0:rows, :], pb, pb.ap[0:rows, :], sn, sn.ap[0:rows, :], ALU.mult)
        self.tt("gpsimd", out_t, out_ap, t1, t1.ap[0:rows, :], t2, t2.ap[0:rows, :], ALU.add)

    def phase_odd_in(self, li):
        A, P, S = self.A, self.P, self.S
        j = li // 2
        A.reset(self.base_off)
        xT = self.scr["xT"].ap()
        w_in = self.din["od_w_in"].ap()[j]
        NPERM = 928
        W = A.alloc(KC * OD_W, BF16, "w_in")
        Wp = A.alloc(KC * NPERM, BF16, "w_perm")
        stage = [A.alloc(1024, F32, "wst%d" % i) for i in range(2)]
        self.load_weight(W, OD_W, w_in, KC, OD_W, stage=stage)
        self.load_perm(Wp, NPERM, 0, w_in[:, 0:512], 512, 32, stage)
        self.load_perm(Wp, NPERM, 512, w_in[:, 512:640], 128, 32, stage)
        self.load_perm(Wp, NPERM, 640, w_in[:, 768:896], 128, 32, stage)
        self.load_perm(Wp, NPERM, 768, w_in[:, 1024:1152], 128, 32, stage)
        self.load_perm(Wp, NPERM, 896, w_in[:, 1944:1976], 32, 16, stage)
        w_uq = self.din["mla_w_uq"].ap()[j]
        w_ukv = self.din["mla_w_ukv"].ap()[j]
        Wuq = A.alloc(3 * 768, BF16, "wuq")
        Wuqp = A.alloc(3 * 768, BF16, "wuqp")
        self.load_weight(Wuq, 768, w_uq, 3, 768, stage=stage)
        for k in range(3):
            self.copy("gpsimd", Wuqp, Wuqp.ap[:, k * 768:(k + 1) * 768], Wuq, Wuq.ap[:, k * 768:(k + 1) * 768])
        sv = w_uq.rearrange("k (h c) -> k h c", c=96)[:, :, 64:96].rearrange("k h (two d) -> k h two d", two=2)
        i = 0
        for k in range(3):
            for two in range(2):
                st = stage[i % 2]
                i += 1
                self.load(st, st.ap[:, 0:128].rearrange("p (h d) -> p h d", d=16), sv[k * 128:(k + 1) * 128, :, 1 - two, :])
                dst = Wuqp.ap[:, k * 768:(k + 1) * 768].rearrange("p (h c) -> p h c", c=96)[:, :, 64 + two * 16: 64 + two * 16 + 16]
                self.copy("vector", Wuqp, dst, st, st.ap[:, 0:128].rearrange("p (h d) -> p h d", d=16))
        Wkn = A.alloc(2 * 512, BF16, "wkn")
        Wkv = A.alloc(2 * 512, BF16, "wkv")
        kvv = w_ukv.rearrange("k (h two d) -> k h two d", two=2, d=64)
        for k in range(2):
            for two, Wt in ((0, Wkn), (1, Wkv)):
                st = stage[i % 2]
                i += 1
                self.load(st, st.ap[:, 0:512].rearrange("p (h d) -> p h d", d=64), kvv[k * 128:(k + 1) * 128, :, two, :])
                self.copy("vector", Wt, Wt.ap[:, k * 512:(k + 1) * 512], st, st.ap[:, 0:512])
        mlan = A.alloc(8, F32, "mlan")
        self.load(mlan, mlan.ap[:, 0:5], self.din["c_mlan"].ap()[j])
        X = [A.alloc(KC * 512, F32, "x%d" % i) for i in range(2)]
        hT = A.alloc(KC * 512, BF16, "hT")
        sq = A.alloc(KC * 512, BF16, "sq")
        rstd = A.alloc(512, F32, "rstd")
        COS = [A.alloc(512, F32, "cos%d" % i) for i in range(2)]
        SIN = [A.alloc(512, F32, "sin%d" % i) for i in range(2)]
        COSM = [A.alloc(512, F32, "cosm%d" % i) for i in range(2)]
        SINM = [A.alloc(512, F32, "sinm%d" % i) for i in range(2)]
        COSK = [A.alloc(512, F32, "cosk%d" % i) for i in range(2)]
        SINK = [A.alloc(512, F32, "sink%d" % i) for i in range(2)]
        t1 = [A.alloc(512, F32, "t1_%d" % i) for i in range(2)]
        t2 = [A.alloc(512, F32, "t2_%d" % i) for i in range(2)]
        qo = [A.alloc(512, BF16, "qo%d" % i) for i in range(4)]
        vo = [A.alloc(512, BF16, "vo%d" % i) for i in range(2)]
        gt = [A.alloc(512, F32, "gt%d" % i) for i in range(2)]
        CQ = A.alloc(3 * 512, F32, "cq")
        CQn = A.alloc(3 * 512, BF16, "cqn")
        CKV = A.alloc(2 * 512, F32, "ckv")
        CKVn = A.alloc(2 * 512, BF16, "ckvn")
        g = self.gain(0, li)
        scr = self.scr
        nsq, kcT, vcT, ksT, kwT = scr["nsq"].ap(), scr["kcT"].ap(), scr["vcT"].ap(), scr["ksT"].ap(), scr["kwT"].ap()
        vsv, vwv, gT = scr["vsv"].ap(), scr["vwv"].ap(), scr["gT"].ap()
        mq, mk, mv = scr["mq"].ap(), scr["mk"].ap(), scr["mv"].ap()
        nq = 0
        for c in range(self.NCH):
            x = X[c % 2]
            t0 = c * 512
            self.load(x, x.ap.rearrange("p (k t) -> p k t", k=KC), xT[:, :, t0:t0 + 512].rearrange("k p t -> p k t"))
            cs, sn = COS[c % 2], SIN[c % 2]
            csm, snm = COSM[c % 2], SINM[c % 2]
            csk, snk = COSK[c % 2], SINK[c % 2]
            self.load(cs, cs.ap, self.din["c_cos64"].ap()[:, t0:t0 + 512])
            self.load(sn, sn.ap, self.din["c_sin64"].ap()[:, t0:t0 + 512])
            self.load(csm, csm.ap[0:96, :], self.din["c_cosm"].ap()[:, t0:t0 + 512])
            self.load(snm, snm.ap[0:96, :], self.din["c_sinm"].ap()[:, t0:t0 + 512])
            self.load(csk, csk.ap[0:32, :], self.din["c_cosm"].ap()[64:96, t0:t0 + 512])
            self.load(snk, snk.ap[0:32, :], self.din["c_sinm"].ap()[64:96, t0:t0 + 512])
            self.rmsnorm_fm(x, KC, 512, g, hT, sq, rstd, D)

            def proj(Wt, wcols, col0, ncols, rhs=hT, nk=KC):
                ps = self.psum()
                for k in range(nk):
                    self.mm(ps, ps.ap[0:ncols, :], Wt, Wt.ap[:, k * wcols + col0: k * wcols + col0 + ncols], rhs, rhs.ap[:, k * 512:(k + 1) * 512],
                            k == 0, k == nk - 1)
                return ps

            def roped(col, pcol, rows, cst, snt):
                nonlocal nq
                pa = proj(W, OD_W, col, rows)
                pb = proj(Wp, NPERM, pcol, rows)
                q_ = qo[nq % 4]
                a1, a2 = t1[nq % 2], t2[nq % 2]
                nq += 1
                self.rope_evac(pa, pb, rows, cst, snt, a1, a2, q_, q_.ap[0:rows, :])
                return q_
            for t in range(4):
                q_ = roped(128 * t, 128 * t, 128, cs, sn)
                for two in range(2):
                    self.store(nsq[2 * t + two, :, t0:t0 + 512], q_, q_.ap[two * 64:(two + 1) * 64, :])
            for col, pcol, dst in ((512, 512, kcT), (768, 640, ksT), (1024, 768, kwT)):
                q_ = roped(col, pcol, 128, cs, sn)
                for two in range(2):
                    self.store(dst[two, :, t0:t0 + 512], q_, q_.ap[two * 64:(two + 1) * 64, :])
            pa = proj(W, OD_W, 640, 128)
            q_ = qo[nq % 4]
            nq += 1
            self.copy("scalar", q_, q_.ap, pa, pa.ap)
            for two in range(2):
                self.store(vcT[two, :, t0:t0 + 512], q_, q_.ap[two * 64:(two + 1) * 64, :])
            for s in range(4):
                for which, col, dst in ((0, 896, vsv), (1, 1152, vwv)):
                    ps = self.psum()
                    for k in range(KC):
                        self.mm(ps, ps.ap[:, 0:128], hT, hT.ap[:, k * 512 + s * 128: k * 512 + (s + 1) * 128], W, W.ap[:, k * OD_W + col: k * OD_W + col + 128],
                                k == 0, k == KC - 1)
                    v_ = vo[which]
                    self.copy(("scalar", "vector")[which], v_, v_.ap[:, 0:128], ps, ps.ap[:, 0:128])
                    tt0 = t0 + s * 128
                    self.store(dst[:, tt0:tt0 + 128, :].rearrange("g t d -> t g d"), v_, v_.ap[:, 0:128].rearrange("p (g d) -> p g d", g=2))
            pz = proj(W, OD_W, 1280, 24)
            g_ = gt[c % 2]
            self.act(g_, g_.ap[0:24, :], pz, pz.ap[0:24, :], AF.Sigmoid)
            self.store(gT[:, t0:t0 + 512], g_, g_.ap[0:24, :])
            for i3 in range(3):
                pa = proj(W, OD_W, 1304 + 128 * i3, 128)
                self.copy("scalar", CQ, CQ.ap[:, i3 * 512:(i3 + 1) * 512], pa, pa.ap)
            self.rmsnorm_fm(CQ, 3, 512, mlan.ap[:, 0:3], CQn, sq, rstd, 384, gain_t=mlan)
            for i2 in range(2):
                pa = proj(W, OD_W, 1688 + 128 * i2, 128)
                self.copy("scalar", CKV, CKV.ap[:, i2 * 512:(i2 + 1) * 512], pa, pa.ap)
            self.rmsnorm_fm(CKV, 2, 512, mlan.ap[:, 3:5], CKVn, sq, rstd, 256, gain_t=mlan)
            q_ = roped(1944, 896, 32, csk, snk)
            for h in range(8):
                self.store(mk[h, 64:96, t0:t0 + 512], q_, q_.ap[0:32, :])
            for h in range(8):
                pa = proj(Wuq, 768, 96 * h, 96, rhs=CQn, nk=3)
                pb = proj(Wuqp, 768, 96 * h, 96, rhs=CQn, nk=3)
                q_ = qo[nq % 4]
                a1, a2 = t1[nq % 2], t2[nq % 2]
                nq += 1
                self.rope_evac(pa, pb, 96, csm, snm, a1, a2, q_, q_.ap[0:96, :])
                self.store(mq[h, :, t0:t0 + 512], q_, q_.ap[0:96, :])
            for pr in range(4):
                pa = proj(Wkn, 512, 128 * pr, 128, rhs=CKVn, nk=2)
                q_ = qo[nq % 4]
                nq += 1
                self.copy("scalar", q_, q_.ap, pa, pa.ap)
                for two in range(2):
                    self.store(mk[2 * pr + two, 0:64, t0:t0 + 512], q_, q_.ap[two * 64:(two + 1) * 64, :])
            for s in range(4):
                ps = self.psum()
                for k in range(2):
                    self.mm(ps, ps.ap, CKVn, CKVn.ap[:, k * 512 + s * 128: k * 512 + (s + 1) * 128], Wkv, Wkv.ap[:, k * 512:(k + 1) * 512], k == 0, k == 1)
                v_ = vo[s % 2]
                self.copy("vector", v_, v_.ap, ps, ps.ap)
                tt0 = t0 + s * 128
                self.store(mv[:, tt0:tt0 + 128, :].rearrange("h t d -> t h d"), v_, v_.ap.rearrange("p (h d) -> p h d", h=8))
        P.barrier()

    def phase_odd_cmp(self, li):
        A, P, S = self.A, self.P, self.S
        j = li // 2
        A.reset(self.base_off)
        NC, NCT, NCP = self.NC, self.NCT, self.NCP
        scr = self.scr
        src = {0: scr["kcT"].ap(), 1: scr["vcT"].ap()}
        kcmp, vcmp = scr["kcmp"].ap(), scr["vcmp"].ap()
        stage = [A.alloc(1024, F32, "wst%d" % i) for i in range(2)]
        SRC = [A.alloc(2 * S, BF16, "src%d" % kv) for kv in range(2)]
        for kv in range(2):
            for g in range(2):
                self.load(SRC[kv], SRC[kv].ap[0:64, g * S:(g + 1) * S], src[kv][g])
        W1 = [A.alloc(32 * 128, BF16, "w1_%d" % kv) for kv in range(2)]
        W2 = [A.alloc(64, BF16, "w2_%d" % kv) for kv in range(2)]
        posT = A.alloc(2 * 32, F32, "posT")
        posTb = A.alloc(2 * 32, BF16, "posTb")
        self.load(posT, posT.ap[0:64, 0:64], self.din["c_posT"].ap()[j])
        self.copy("vector", posTb, posTb.ap[0:64, 0:64], posT, posT.ap[0:64, 0:64])
        i = 0
        for kv in range(2):
            w1 = self.din["nsa_cmp_w1"].ap()[j, kv].rearrange("(l d) h -> d l h", d=64)
            for l4 in range(4):
                st = stage[i % 2]
                i += 1
                self.load(st, st.ap[0:64, 0:1024].rearrange("p (l h) -> p l h", h=128), w1[:, l4 * 8:(l4 + 1) * 8, :])
                self.copy("vector", W1[kv], W1[kv].ap[0:64, l4 * 1024:(l4 + 1) * 1024], st, st.ap[0:64, 0:1024])
            st = stage[i % 2]
            i += 1
            self.load(st, st.ap[:, 0:64], self.din["nsa_cmp_w2"].ap()[j, kv])
            self.copy("vector", W2[kv], W2[kv].ap[:, 0:64], st, st.ap[:, 0:64])
        bias = A.alloc(8, F32, "cbias")
        H = [A.alloc(NCP, BF16, "H%d" % i) for i in range(2)]
        for i in range(2):
            self.memset("vector", H[i], H[i].ap, 0.0)
        ko = [A.alloc(NCP, BF16, "ko%d" % i) for i in range(2)]
        for i in range(2):
            self.memset("vector", ko[i], ko[i].ap, 0.0)
        vo = [A.alloc(64, BF16, "cvo%d" % i) for i in range(2)]
        for kv in range(2):
            pb = self.psum()
            for l in range(32):
                self.mm(pb, pb.ap[:, 0:1], W1[kv], W1[kv].ap[0:64, l * 128:(l + 1) * 128], posTb, posTb.ap[0:64, kv * 32 + l: kv * 32 + l + 1], l == 0, l == 31)
            self.copy("vector", bias, bias.ap[:, kv:kv + 1], pb, pb.ap[:, 0:1])
        n = 0
        for kv in range(2):
            for g in range(2):
                ps = self.psum()
                for l in range(32):
                    rhs = SRC[kv].ap[0:64, g * S + l: g * S + l + 16 * (NC - 1) + 1: 16]
                    self.mm(ps, ps.ap[:, 0:NC], W1[kv], W1[kv].ap[0:64, l * 128:(l + 1) * 128], SRC[kv], rhs, l == 0, l == 31)
                h_ = H[n % 2]
                self.act(h_, h_.ap[:, 0:NC], ps, ps.ap[:, 0:NC], AF.Silu, bias=bias.ap[:, kv:kv + 1], extra_reads=[bias])
                if kv == 0:
                    p2 = self.psum()
                    self.mm(p2, p2.ap[0:64, 0:NC], W2[0], W2[0].ap[:, 0:64], h_, h_.ap[:, 0:NC], True, True)
                    k_ = ko[g]
                    self.copy("vector", k_, k_.ap[0:64, 0:NC], p2, p2.ap[0:64, 0:NC])
                    self.store(kcmp[g], k_, k_.ap[0:64, :])
                else:
                    for ct in range(NCT):
                        p2 = self.psum()
                        self.mm(p2, p2.ap[:, 0:64], h_, h_.ap[:, ct * 128:(ct + 1) * 128], W2[1], W2[1].ap[:, 0:64], True, True)
                        v_ = vo[ct % 2]
                        self.copy("vector", v_, v_.ap[:, 0:64], p2, p2.ap[:, 0:64])
                        self.store(vcmp[g, ct * 128:(ct + 1) * 128, :], v_, v_.ap[:, 0:64])
                n += 1
        P.barrier()

    def combine(self, Oacc, gr, b, dst, first, rec_t, recB, tmp_t, bank_lo, bank_hi):
        P = self.P
        r_ = rec_t
        self.ts("vector", r_, r_.ap[64:65, :], Oacc, Oacc.ap[64:65, :], 1e-30, ALU.max)
        P.op("vector", lambda e, o=r_.ap[64:65, :]: e.reciprocal(out=o, in_=o), reads=[r_], writes=[r_])
        if gr is not None:
            self.tt("vector", r_, r_.ap[64:65, :], r_, r_.ap[64:65, :], gr, gr.ap[64:65, b * 512:(b + 1) * 512], ALU.mult)
        pb = self.psum(bank_lo, bank_hi)
        self.mm(pb, pb.ap[0:64, :], self.c_ones_f, self.c_ones_f.ap[64:65, 0:64], r_, r_.ap[64:65, :], True, True)
        self.copy("scalar", recB, recB.ap[0:64, :], pb, pb.ap[0:64, :])
        if first:
            self.tt("vector", dst, dst.ap[0:64, :], Oacc, Oacc.ap[0:64, :], recB, recB.ap[0:64, :], ALU.mult)
        else:
            self.tt("vector", tmp_t, tmp_t.ap[0:64, :], Oacc, Oacc.ap[0:64, :], recB, recB.ap[0:64, :], ALU.mult)
            self.tt("gpsimd", dst, dst.ap[0:64, :], dst, dst.ap[0:64, :], tmp_t, tmp_t.ap[0:64, :], ALU.add)

    def phase_odd_nsa(self, li):
        A, P, S, NT = self.A, self.P, self.S, self.NT
        A.reset(self.base_off)
        NB = S // 64
        NC, NCT, NCP = self.NC, self.NCT, self.NCP
        scr = self.scr
        nsq, ksT, kwT, vsv, vwv, gT = scr["nsq"].ap(), scr["ksT"].ap(), scr["kwT"].ap(), scr["vsv"].ap(), scr["vwv"].ap(), scr["gT"].ap()
        kcmp, vcmp, oT = scr["kcmp"].ap(), scr["vcmp"].ap(), scr["oT"].ap()
        sc = 0.125
        stage = [A.alloc(1024, F32, "wst%d" % i) for i in range(2)]
        KS = A.alloc(2 * S, BF16, "KS")
        VS = A.alloc(2 * NT * 65, BF16, "VS")
        KCM = A.alloc(2 * NCP, BF16, "KCM")
        VCM = A.alloc(2 * NCT * 65, BF16, "VCM")
        self.memset("gpsimd", VS, VS.ap, 1.0)
        self.memset("gpsimd", VCM, VCM.ap, 1.0)
        for g in range(2):
            self.load(KS, KS.ap[0:64, g * S:(g + 1) * S], ksT[g])
            self.load(VS, VS.ap.rearrange("p (g t c) -> p g t c", g=2, c=65)[:, g, :, 0:64], vsv[g].rearrange("(t p) d -> p t d", p=128))
            self.load(KCM, KCM.ap[0:64, g * NCP:(g + 1) * NCP], kcmp[g])
            self.load(VCM, VCM.ap.rearrange("p (g t c) -> p g t c", g=2, c=65)[:, g, :, 0:64], vcmp[g].rearrange("(t p) d -> p t d", p=128))
        OV1 = A.alloc(NCT * (NB + 1), BF16, "OV1")
        st = stage[0]
        self.load(st, st.ap[:, 0:NCT * (NB + 1)], self.din["c_ov1"].ap())
        self.copy("vector", OV1, OV1.ap, st, st.ap[:, 0:NCT * (NB + 1)])
        EXP = A.alloc(S, BF16, "EXP")
        for i in range(S // 1024):
            st = stage[(i + 1) % 2]
            self.load(st, st.ap, self.din["c_expand"].ap()[:, i * 1024:(i + 1) * 1024])
            self.copy("vector", EXP, EXP.ap[:, i * 1024:(i + 1) * 1024], st, st.ap)
        MW = A.alloc(4 * 512, F32, "MW")
        MM = A.alloc(5 * 512, F32, "MM")
        self.load(MW, MW.ap, self.din["c_maskw"].ap())
        self.load(MM, MM.ap, self.din["c_maskm"].ap())
        TWW = 2 * (NT - 1) + NB
        TW1 = A.alloc(TWW, F32, "TW1")
        TW2 = A.alloc(TWW, F32, "TW2")
        self.load(TW1, TW1.ap, self.din["c_tw1"].ap())
        self.load(TW2, TW2.ap, self.din["c_tw2"].ap())
        Q8 = A.alloc(8 * 512, BF16, "Q8")
        GR = [A.alloc(3 * 512, F32, "GR%d" % i) for i in range(4)]
        KW = [A.alloc(2 * 1024, BF16, "KW%d" % i) for i in range(2)]
        VW = [A.alloc(2 * 8 * 65, BF16, "VW%d" % i) for i in range(2)]
        for i in range(2):
            self.memset("gpsimd", VW[i], VW[i].ap, 1.0)
        OACC = [A.alloc(512, F32, "OACC%d" % i) for i in range(4)]
        ET = [A.alloc(512, BF16, "ET%d" % i) for i in range(4)]
        pts = [A.alloc(1024, BF16, "pt%d" % i) for i in range(4)]
        ssb = [A.alloc(1024, F32, "ssb%d" % i) for i in range(2)]
        rec = [A.alloc(512, F32, "rec%d" % i) for i in range(2)]
        recB = A.alloc(512, F32, "recB")
        ctmp = A.alloc(512, F32, "ctmp")
        ACC = [A.alloc(4 * NB, F32, "ACC%d" % i) for i in range(2)]
        adj = [A.alloc(NB, F32, "adj%d" % i) for i in range(2)]
        tmpv = A.alloc(NB, F32, "tmpv")
        m8 = A.alloc(16, F32, "m8")
        selb = [A.alloc(NB, BF16, "selb%d" % i) for i in range(2)]
        selT = [A.alloc(512, BF16, "selT%d" % i) for i in range(2)]
        mks = [A.alloc(1024, BF16, "mk%d" % i) for i in range(2)]
        rI = A.alloc(8, F32, "rI")
        ob = [A.alloc(512, BF16, "ob%d" % i) for i in range(2)]
        npt = 0
        ncomb = 0
        for qc in range(self.NCH):
            t0 = qc * 512
            self.load(Q8, Q8.ap[0:64, :].rearrange("p (h t) -> p h t", h=8), nsq[:, :, t0:t0 + 512].rearrange("h r t -> r h t"))
            kw, vw = KW[qc % 2], VW[qc % 2]
            lo = max(t0 - 512, 0)
            nwt = (t0 + 512 - lo) // 128
            slot0 = 8 - nwt
            for g in range(2):
                self.load(kw, kw.ap[0:64, g * 1024 + slot0 * 128:(g + 1) * 1024], kwT[g][:, lo:t0 + 512])
                self.load(vw, vw.ap.rearrange("p (g t c) -> p g t c", g=2, c=65)[:, g, slot0:8, 0:64],
                          vwv[g][lo:t0 + 512, :].rearrange("(t p) d -> p t d", p=128))
            for g in range(2):
                nct = min(NCT, (32 * qc + 31 + 127) // 128)
                for hh in range(4):
                    h = 4 * g + hh
                    gr = GR[hh]
                    self.load(gr, gr.ap[64:65, :].rearrange("p (r t) -> p r t", r=3),
                              gT[3 * h:3 * h + 3, t0:t0 + 512].rearrange("(o r) t -> o r t", o=1))
                    qh = Q8.ap[0:64, h * 512:(h + 1) * 512]
                    Oc = self.ps[4 + hh % 2]
                    for ct in range(nct):
                        ps = self.psum(0, 4)
                        self.mm(ps, ps.ap, KCM, KCM.ap[0:64, g * NCP + ct * 128: g * NCP + (ct + 1) * 128], Q8, qh, True, True)
                        Dv = 512 * qc - 2048 * ct
                        e_ = ET[ct]
                        if Dv <= 2048:
                            sb = ssb[ct % 2]
                            mi = Dv // 512
                            self.tt("vector", sb, sb.ap[:, 0:512], ps, ps.ap, MM, MM.ap[:, mi * 512:(mi + 1) * 512], ALU.min)
                            self.act(e_, e_.ap, sb, sb.ap[:, 0:512], AF.Exp, scale=sc)
                        else:
                            self.act(e_, e_.ap, ps, ps.ap, AF.Exp, scale=sc)
                        self.mm(Oc, Oc.ap[0:65, :], VCM, VCM.ap[:, (g * NCT + ct) * 65:(g * NCT + ct) * 65 + 65], e_, e_.ap, ct == 0, ct == nct - 1)
                    for s in range(4):
                        ib = self.ps[6 + s // 2]
                        for ct in range(nct):
                            self.mm(ib, ib.ap[:, (s % 2) * (NB + 1):(s % 2 + 1) * (NB + 1)], ET[ct], ET[ct].ap[:, s * 128:(s + 1) * 128],
                                    OV1, OV1.ap[:, ct * (NB + 1):(ct + 1) * (NB + 1)], ct == 0, ct == nct - 1)
                    for bnk in range(2):
                        ib = self.ps[6 + bnk]
                        v3 = ib.ap[:, 0:2 * (NB + 1)].rearrange("p (s n) -> p s n", s=2)
                        ri = rI.ap[:, bnk * 2:(bnk + 1) * 2]
                        self.ts("vector", rI, ri, ib, v3[:, :, NB], 1e-30, ALU.max)
                        P.op("vector", lambda e, o=ri: e.reciprocal(out=o, in_=o), reads=[rI], writes=[rI])
                        for s2 in range(2):
                            s = bnk * 2 + s2
                            dst = ACC[g].ap[:, s * NB:(s + 1) * NB]
                            if hh == 0:
                                self.ts("vector", ACC[g], dst, ib, v3[:, s2, 0:NB], rI.ap[:, s:s + 1], ALU.mult, extra_reads=[rI])
                            else:
                                self.stt(ACC[g], dst, ib, v3[:, s2, 0:NB], rI.ap[:, s:s + 1], ACC[g], dst, ALU.mult, ALU.add, extra_reads=[rI])
                    self.combine(Oc, gr, 0, OACC[hh], True, rec[ncomb % 2], recB, ctmp, 0, 4)
                    ncomb += 1
                psT = self.psum(0, 4)
                psTb = psT.ap.bitcast(BF16)
                for s in range(4):
                    gs = 4 * qc + s
                    off = 2 * (NT - 1) - 2 * gs
                    a_ = adj[s % 2]
                    self.tt("vector", a_, a_.ap, ACC[g], ACC[g].ap[:, s * NB:(s + 1) * NB], TW1, TW1.ap[:, off:off + NB], ALU.mult)
                    self.tt("vector", a_, a_.ap, a_, a_.ap, TW2, TW2.ap[:, off:off + NB], ALU.add)
                    self.memset("vector", a_, a_.ap[:, 0:1], 1.0e4)
                    P.op("vector", lambda e, o=m8.ap[:, 0:8], i=a_.ap: e.max(out=o, in_=i), reads=[a_], writes=[m8])
                    P.op("vector", lambda e, o=tmpv.ap, r=m8.ap[:, 0:8], i=a_.ap: e.match_replace(out=o, in_to_replace=r, in_values=i, imm_value=-2.0e30),
                         reads=[m8, a_], writes=[tmpv])
                    P.op("vector", lambda e, o=m8.ap[:, 8:16], i=tmpv.ap: e.max(out=o, in_=i), reads=[tmpv], writes=[m8])
                    sb_ = selb[s % 2]
                    self.ts("vector", sb_, sb_.ap, a_, a_.ap, m8.ap[:, 15:16], ALU.is_ge, extra_reads=[m8])
                    P.op("tensor", lambda e, o=psTb[0:NB, s * 128:(s + 1) * 128], i=sb_.ap[:, 0:NB]: e.transpose(o, i, self.c_ident_bf.ap[:, 0:128]),
                         reads=[sb_, self.c_ident_bf], writes=[psT])
                self.copy("scalar", selT[g], selT[g].ap[0:NB, :], psT, psTb[0:NB, 0:512])
                OS = [self.ps[4 + hh] for hh in range(4)]
                pend = []
                pairs = self.causal_pairs(qc)
                npr = len(pairs)

                def emit_pv(item):
                    hh_, kta_, ktb_, pt_, first_, last_ = item
                    for half, kt_ in ((0, kta_), (1, ktb_)):
                        self.mm(OS[hh_], OS[hh_].ap[0:65, :], VS, VS.ap[:, (g * NT + kt_) * 65:(g * NT + kt_) * 65 + 65], pt_,
                                pt_.ap[:, half * 512:(half + 1) * 512], first_ and half == 0, last_ and half == 1)
                for pi, (kta, ktb, minfo) in enumerate(pairs):
                    b0 = 2 * (self.nsp % 2)
                    self.nsp += 1
                    ma, mb = self.ps[b0], self.ps[b0 + 1]
                    self.mm(ma, ma.ap, EXP, EXP.ap[0:NB, kta * 128:(kta + 1) * 128], selT[g], selT[g].ap[0:NB, :], True, True)
                    self.mm(mb, mb.ap, EXP, EXP.ap[0:NB, ktb * 128:(ktb + 1) * 128], selT[g], selT[g].ap[0:NB, :], True, True)
                    mk = mks[pi % 2]
                    P.op("scalar", lambda e, o=mk.ap, i=self.psall[:, b0 * 512:(b0 + 2) * 512]: e.copy(out=o, in_=i), reads=[ma, mb], writes=[mk])
                    for hh in range(4):
                        h = 4 * g + hh
                        b0 = 2 * (self.nsp % 2)
                        self.nsp += 1
                        pa, pb = self.ps[b0], self.ps[b0 + 1]
                        qh = Q8.ap[0:64, h * 512:(h + 1) * 512]
                        self.mm(pa, pa.ap, KS, KS.ap[0:64, g * S + kta * 128: g * S + (kta + 1) * 128], Q8, qh, True, True)
                        self.mm(pb, pb.ap, KS, KS.ap[0:64, g * S + ktb * 128: g * S + (ktb + 1) * 128], Q8, qh, True, True)
                        pair_ap = self.psall[:, b0 * 512:(b0 + 2) * 512]
                        pt = pts[npt % 4]
                        npt += 1
                        if minfo is not None:
                            sb = ssb[npt % 2]
                            P.op("vector", lambda e, o=sb.ap, i0=pair_ap, i1=minfo[1]: e.tensor_tensor(out=o, in0=i0, in1=i1, op=ALU.min),
                                 reads=[pa, pb, minfo[0]], writes=[sb])
                            P.op("scalar", lambda e, o=pt.ap, i=sb.ap: e.activation(out=o, in_=i, func=AF.Exp, scale=sc), reads=[sb], writes=[pt])
                        else:
                            P.op("scalar", lambda e, o=pt.ap, i=pair_ap: e.activation(out=o, in_=i, func=AF.Exp, scale=sc), reads=[pa, pb], writes=[pt])
                        self.tt("vector", pt, pt.ap, pt, pt.ap, mk, mk.ap, ALU.mult)
                        pend.append((hh, kta, ktb, pt, pi == 0, pi == npr - 1))
                        if len(pend) > 1:
                            emit_pv(pend.pop(0))
                while pend:
                    emit_pv(pend.pop(0))
                for hh in range(4):
                    self.combine(OS[hh], GR[hh], 1, OACC[hh], False, rec[ncomb % 2], recB, ctmp, 0, 4)
                    ncomb += 1
                for hh in range(4):
                    h = 4 * g + hh
                    wl = []
                    if qc > 0:
                        wl.append((0, 1, (MW, MW.ap[:, 0:1024])))
                        wl.append((2, 3, (MW, MW.ap[:, 1024:2048])))
                    wl.append((4, 5, (self.c_maskc, self.c_maskc.ap[:, 0:1024])))
                    wl.append((6, 7, (self.c_maskc, self.c_maskc.ap[:, 1024:2048])))
                    Ow = self.ps[4 + hh % 2]
                    self.attn_pass(kw, lambda sl, g=g: kw.ap[0:64, g * 1024 + sl * 128: g * 1024 + (sl + 1) * 128],
                                   Q8, Q8.ap[0:64, h * 512:(h + 1) * 512],
                                   vw, lambda sl, g=g: vw.ap[:, (g * 8 + sl) * 65:(g * 8 + sl) * 65 + 65], 65, sc, Ow, wl, ptiles=pts, ssb=ssb)
                    self.combine(Ow, GR[hh], 2, OACC[hh], False, rec[ncomb % 2], recB, ctmp, 6, 8)
                    ncomb += 1
                    o_ = ob[hh % 2]
                    self.copy("scalar", o_, o_.ap[0:64, :], OACC[hh], OACC[hh].ap[0:64, :])
                    self.store(oT[h // 2, (h % 2) * 64:(h % 2) * 64 + 64, t0:t0 + 512], o_, o_.ap[0:64, :])
        P.barrier()

    def aug_head_chunk(self, kt_, v_, q_, rows, scale, qc, Oacc, pts, ssb, r_, recB, o_, dst_ap):
        P = self.P
        kl = self.causal_pairs(qc)
        self.attn_pass(kt_, lambda kt: kt_.ap[0:rows, kt * 128:(kt + 1) * 128], q_, q_.ap[0:rows, :],
                       v_, lambda kt: v_.ap[:, kt * 129: kt * 129 + 65], 65, scale, Oacc, kl, ptiles=pts, ssb=ssb)
        P.op("vector", lambda e, o=r_.ap[64:65, :], i=Oacc.ap[64:65, :]: e.reciprocal(out=o, in_=i), reads=[Oacc], writes=[r_])
        pb = self.ps[6]
        self.mm(pb, pb.ap[0:64, :], self.c_ones_f, self.c_ones_f.ap[64:65, 0:64], r_, r_.ap[64:65, :], True, True)
        self.copy("scalar", recB, recB.ap[0:64, :], pb, pb.ap[0:64, :])
        self.tt("vector", o_, o_.ap[0:64, :], Oacc, Oacc.ap[0:64, :], recB, recB.ap[0:64, :], ALU.mult)
        self.store(dst_ap, o_, o_.ap[0:64, :])

    def phase_odd_mla(self, li):
        A, P, S, NT = self.A, self.P, self.S, self.NT
        A.reset(self.base_off)
        mq, mk, mv, oT = self.scr["mq"].ap(), self.scr["mk"].ap(), self.scr["mv"].ap(), self.scr["oT"].ap()
        KT = [A.alloc(S, BF16, "KT%d" % i) for i in range(2)]
        V = [A.alloc(NT * 129, BF16, "V%d" % i) for i in range(2)]
        Q = [A.alloc(512, BF16, "Q%d" % i) for i in range(2)]
        pts = [A.alloc(1024, BF16, "pt%d" % i) for i in range(4)]
        ssb = [A.alloc(1024, F32, "ssb%d" % i) for i in range(2)]
        rec = A.alloc(512, F32, "rec")
        recB = A.alloc(512, F32, "recB")
        ob = [A.alloc(512, BF16, "ob%d" % i) for i in range(2)]
        for i in range(2):
            self.memset("gpsimd", V[i], V[i].ap, 1.0)

        def load_unit(h, slot):
            self.load(KT[slot], KT[slot].ap[0:96, :], mk[h])
            self.load(V[slot], V[slot].ap.rearrange("p (t c) -> p t c", c=129)[:, :, 0:64], mv[h].rearrange("(t p) d -> p t d", p=128))
        load_unit(0, 0)
        n = 0
        for h in range(8):
            slot = h % 2
            if h + 1 < 8:
                load_unit(h + 1, 1 - slot)
            for qc in range(self.NCH):
                q_ = Q[n % 2]
                t0 = qc * 512
                self.load(q_, q_.ap[0:96, :], mq[h, :, t0:t0 + 512])
                self.aug_head_chunk(KT[slot], V[slot], q_, 96, 96.0 ** -0.5, qc, self.ps[4 + n % 2], pts, ssb, rec, recB, ob[n % 2],
                                    oT[4 + h // 2, (h % 2) * 64:(h % 2) * 64 + 64, t0:t0 + 512])
                n += 1
        P.barrier()

    def build(self):
        S = self.S
        nc = self.nc
        self.inp("x", [S, D])
        self.inp("mem", [MEM, D])
        self.inp("mem_norm", [D])
        for nm in ("norm_mix", "norm_mem", "norm_ffn"):
            self.inp(nm, [DEPTH, D])
        self.inp("ev_w_in", [2, D, EV_W])
        self.inp("ev_b_f", [2, 8])
        self.inp("ev_lam", [2, 4, 64])
        self.inp("ev_subln", [2, 128])
        self.inp("ev_w_out", [2, D, D])
        self.inp("od_w_in", [2, D, OD_W])
        self.inp("nsa_cmp_pos", [2, 2, 32, 64])
        self.inp("nsa_cmp_w1", [2, 2, 2048, 128])
        self.inp("nsa_cmp_w2", [2, 2, 128, 64])
        self.inp("mla_q_norm", [2, 384])
        self.inp("mla_kv_norm", [2, 256])
        self.inp("mla_w_uq", [2, 384, 768])
        self.inp("mla_w_ukv", [2, 256, 1024])
        self.inp("od_w_out", [2, D, D])
        self.inp("xa_wq", [DEPTH, D, 512])
        self.inp("xa_wkv", [DEPTH, D, D])
        self.inp("xa_wo", [DEPTH, 512, D])
        self.inp("ffn_w13", [DEPTH, D, 2 * DFF])
        self.inp("ffn_w2", [DEPTH, DFF, D])
        self.inp("final_norm", [D])
        self.inp("c_gains", [128, (3 * DEPTH + 1) * KC])
        self.inp("c_ident", [128, 128])
        self.inp("c_maskc", [128, 4 * 512])
        self.inp("c_cos64", [128, S])
        self.inp("c_sin64", [128, S])
        NB = S // 64
        self.inp("c_cosm", [96, S])
        self.inp("c_sinm", [96, S])
        self.inp("c_mlan", [2, 128, 5])
        self.inp("c_posT", [2, 64, 64])
        self.inp("c_ov1", [128, self.NCT * (NB + 1)])
        self.inp("c_expand", [128, S])
        self.inp("c_maskw", [128, 4 * 512])
        self.inp("c_maskm", [128, 5 * 512])
        self.inp("c_tw1", [128, 2 * (self.NT - 1) + NB])
        self.inp("c_tw2", [128, 2 * (self.NT - 1) + NB])
        self.dout = nc.dram_tensor("out", [S, D], F32, kind="ExternalOutput")
        self.scratch("xT", [KC, 128, S], F32)
        self.scratch("oT", [KC, 128, S], BF16)
        self.scratch("dq", [4, 128, S], BF16)
        self.scratch("dk", [4, 128, S], BF16)
        self.scratch("dv", [4, S, 128], BF16)
        self.scratch("fq", [8, 70, S], BF16)
        self.scratch("fk", [8, 70, S], BF16)
        self.scratch("fv", [8, S, 64], BF16)
        self.scratch("lfp", [8, S], F32)
        self.scratch("nsq", [8, 64, S], BF16)
        for nm in ("kcT", "vcT", "ksT", "kwT"):
            self.scratch(nm, [2, 64, S], BF16)
        self.scratch("vsv", [2, S, 64], BF16)
        self.scratch("vwv", [2, S, 64], BF16)
        self.scratch("gT", [24, S], F32)
        self.scratch("mq", [8, 96, S], BF16)
        self.scratch("mk", [8, 96, S], BF16)
        self.scratch("mv", [8, S, 64], BF16)
        self.scratch("kcmp", [2, 64, self.NCP], BF16)
        self.scratch("vcmp", [2, self.NCP, 64], BF16)

        A = self.A
        self.c_eps = A.alloc(8, F32, "eps")
        self.c_one = A.alloc(8, F32, "one")
        self.memset("vector", self.c_eps, self.c_eps.ap, EPS)
        self.memset("vector", self.c_one, self.c_one.ap, 1.0)
        self.c_memT = A.alloc(KC * MEM, BF16, "memT")
        self.setup_consts()
        self.phase_init()
        for li in self.layers:
            if li % 2 == 0:
                self.phase_even_in(li)
                self.phase_even_scan(li)
                self.phase_even_attn(li)
                self.phase_out_xattn(li, self.din["ev_w_out"].ap()[li // 2])
            else:
                self.phase_odd_in(li)
                self.phase_odd_cmp(li)
                self.phase_odd_nsa(li)
                self.phase_odd_mla(li)
                self.phase_out_xattn(li, self.din["od_w_out"].ap()[li // 2])
            self.phase_ffn(li)
        self.phase_final()
        self.P.emit()
        return nc


def ones_f_ap(b):
    return b.c_ones_f.ap


def host_gains(inputs):
    cols = []
    for nm in ("norm_mix", "norm_mem", "norm_ffn"):
        g = np.asarray(inputs[nm], dtype=np.float32)
        for li in range(DEPTH):
            cols.append(g[li].reshape(KC, 128).T)
    cols.append(np.asarray(inputs["final_norm"], dtype=np.float32).reshape(KC, 128).T)
    return np.ascontiguousarray(np.concatenate(cols, axis=1))


def host_consts(S):
    c = {}
    c["c_ident"] = np.eye(128, dtype=np.float32)
    m = np.zeros((128, 4, 512), np.float32)
    p = np.arange(128)[:, None]
    q = np.arange(512)[None, :]
    for o in range(4):
        m[:, o, :] = np.where(o * 128 + p <= q, BIG, -BIG)
    c["c_maskc"] = m.reshape(128, 2048)
    cs, sn = rope_tables(S, 128, 32, 2)
    c["c_cos64"] = cs
    c["c_sin64"] = sn
    c32, s32 = rope_tables(S, 32, 16, 1)
    c["c_cosm"] = np.ascontiguousarray(np.concatenate([np.ones((64, S), np.float32), c32], axis=0))
    c["c_sinm"] = np.ascontiguousarray(np.concatenate([np.zeros((64, S), np.float32), s32], axis=0))
    NB = S // 64
    NT = S // 128
    NC = (S - 32) // 16 + 1
    NCT = (NC + 127) // 128
    cidx = np.arange(NCT * 128)
    ov = ((cidx[:, None] * 16 < np.arange(NB)[None, :] * 64 + 64) & (cidx[:, None] * 16 + 32 > np.arange(NB)[None, :] * 64)
          & (cidx[:, None] < NC)).astype(np.float32)
    ov1 = np.concatenate([ov, np.ones((NCT * 128, 1), np.float32)], axis=1)
    c["c_ov1"] = np.ascontiguousarray(ov1.reshape(NCT, 128, NB + 1).transpose(1, 0, 2).reshape(128, NCT * (NB + 1)))
    c["c_expand"] = (np.arange(128)[:, None] == (np.arange(S)[None, :] // 64)).astype(np.float32)
    mw = np.zeros((128, 4, 512), np.float32)
    for o in range(4):
        mw[:, o, :] = np.where(q < 128 * o + p, BIG, -BIG)
    c["c_maskw"] = mw.reshape(128, 2048)
    mm_ = np.zeros((128, 5, 512), np.float32)
    for i in range(5):
        mm_[:, i, :] = np.where(16 * p + 31 - q <= 512 * i, BIG, -BIG)
    c["c_maskm"] = mm_.reshape(128, 2560)
    TWW = 2 * (NT - 1) + NB
    RO = 2 * (NT - 1)
    r = np.arange(TWW)[None, :] - RO
    e = (np.arange(128)[:, None] >= 64).astype(np.int64)
    fut = r > e
    forced = (r == e) | (r == e - 1)
    c["c_tw1"] = np.where(fut | forced, 0.0, 1.0).astype(np.float32)
    c["c_tw2"] = np.where(fut, -BIG, np.where(forced, 1.0e4, 0.0)).astype(np.float32)
    return c


def host_layouts(inputs):
    c = {}
    qn = np.asarray(inputs["mla_q_norm"], dtype=np.float32)
    kn = np.asarray(inputs["mla_kv_norm"], dtype=np.float32)
    ml = np.zeros((2, 128, 5), np.float32)
    for j in range(2):
        ml[j, :, 0:3] = qn[j].reshape(3, 128).T
        ml[j, :, 3:5] = kn[j].reshape(2, 128).T
    c["c_mlan"] = ml
    pos = np.asarray(inputs["nsa_cmp_pos"], dtype=np.float32)
    c["c_posT"] = np.ascontiguousarray(pos.transpose(0, 3, 1, 2).reshape(2, 64, 64))
    return c


_CACHE = {}


def run(inputs, S, layers, n_cores, batch_ids):
    key = (S, tuple(layers))
    if key not in _CACHE:
        import time as _t
        _t0 = _t.time()
        _b = Builder(S, layers)
        _CACHE[key] = _b.build()
        print("[build] %.1fs ninst=%d" % (_t.time() - _t0, _b.P.ninst), {e: len(v) for e, v in _b.P.q.items()}, flush=True)
    nc = _CACHE[key]
    consts = host_consts(S)
    consts["c_gains"] = host_gains(inputs)
    consts.update(host_layouts(inputs))
    in_maps = []
    for b in batch_ids:
        m = {}
        for k, v in inputs.items():
            v = np.asarray(v)
            if k == "x" or k == "mem":
                m[k] = np.ascontiguousarray(v[b], dtype=np.float32)
            else:
                m[k] = np.ascontiguousarray(v, dtype=np.float32)
        m.update(consts)
        in_maps.append(m)
    import time as _t
    _t0 = _t.time()
    res = run_bass_kernel_spmd(nc, in_maps, core_ids=list(range(n_cores)))
    print("[run] spmd launch+compile %.1fs" % (_t.time() - _t0), flush=True)
    return [r["out"] for r in res.results]


def kernel(**inputs):
    x = np.asarray(inputs["x"])
    B, S, _ = x.shape
    outs = run(inputs, S, list(range(DEPTH)), 8, [0, 1, 2, 3, 0, 1, 2, 3])
    return np.stack(outs[:4], axis=0).astype(np.float32)
```

```python
import math
import numpy as np
import ml_dtypes
import concourse.bass as bass
import concourse.mybir as mybir
from concourse.bass_utils import run_bass_kernel_spmd

F32 = mybir.dt.float32
BF16 = mybir.dt.bfloat16
AF = mybir.ActivationFunctionType
ALU = mybir.AluOpType

D = 1024
KC = 8
DEPTH = 4
MEM = 256
EPS = 1e-6
THETA = 10000.0
DFF = 2816
EV_W = 3080
OD_W = 1976
BIG = 1.0e30

ENGS = ("sync", "scalar", "vector", "gpsimd", "tensor")


class Buf:
    __slots__ = ("name", "w", "r")

    def __init__(self, name):
        self.name = name
        self.w = None
        self.r = []


class Tile:
    __slots__ = ("ap", "buf")

    def __init__(self, ap, name="t"):
        self.ap = ap
        self.buf = Buf(name)

    def __getitem__(self, k):
        return self.ap[k]


class Prog:
    NDMA = 12

    def __init__(self, nc):
        self.nc = nc
        self.q = {e: [] for e in ENGS}
        self.cnt = {e: 0 for e in ENGS}
        self.sem = {e: nc.alloc_semaphore("es_" + e) for e in ENGS}
        self.semid = {e: ("E", e) for e in ENGS}
        self.waited = {e: {} for e in ENGS}
        self.dsem = {}
        self.dval = {}
        self.dnext = {}
        for qn in ("sync", "gpsimd"):
            self.dsem[qn] = [nc.alloc_semaphore("ds_%s_%d" % (qn, i)) for i in range(self.NDMA)]
            self.dval[qn] = [0] * self.NDMA
            self.dnext[qn] = 0
        self.ninst = 0

    def _need(self, eng, tok, kind):
        key, sh, val, teng = tok
        if teng == eng:
            if eng == "tensor" and kind == "waw":
                return
        if self.waited[eng].get(key, 0) >= val:
            return
        self.waited[eng][key] = val
        self.q[eng].append(lambda e, sh=sh, val=val: e.wait_ge(sh, val))

    def _deps(self, eng, reads, writes):
        for t in reads:
            b = t.buf
            if b.w is not None:
                self._need(eng, b.w, "raw")
        for t in writes:
            b = t.buf
            if b.w is not None:
                self._need(eng, b.w, "waw")
            for tok in b.r:
                self._need(eng, tok, "war")

    def _mark(self, tok, reads, writes):
        for t in reads:
            t.buf.r.append(tok)
        for t in writes:
            t.buf.w = tok
            t.buf.r = []

    def op(self, eng, fn, reads=(), writes=()):
        self._deps(eng, reads, writes)
        self.cnt[eng] += 1
        sh = self.sem[eng]
        self.q[eng].append(lambda e, fn=fn, sh=sh: fn(e).then_inc(sh, 1))
        tok = (self.semid[eng], sh, self.cnt[eng], eng)
        self._mark(tok, reads, writes)
        self.ninst += 1

    def dma(self, qn, out, in_, reads=(), writes=()):
        self._deps(qn, reads, writes)
        i = self.dnext[qn]
        self.dnext[qn] = (i + 1) % self.NDMA
        sh = self.dsem[qn][i]
        key = ("D", qn, i)
        prev = self.dval[qn][i]
        if prev > 0 and self.waited[qn].get(key, 0) < prev:
            self.waited[qn][key] = prev
            self.q[qn].append(lambda e, sh=sh, prev=prev: e.wait_ge(sh, prev))
        val = prev + 16
        self.dval[qn][i] = val
        self.q[qn].append(lambda e, out=out, in_=in_, sh=sh: e.dma_start(out=out, in_=in_).then_inc(sh, 16))
        tok = (key, sh, val, None)
        self._mark(tok, reads, writes)
        self.ninst += 1

    def barrier(self):
        toks = []
        for e in ENGS:
            if self.cnt[e] > 0:
                toks.append((self.semid[e], self.sem[e], self.cnt[e], e))
        for qn in self.dsem:
            for i in range(self.NDMA):
                if self.dval[qn][i] > 0:
                    toks.append((("D", qn, i), self.dsem[qn][i], self.dval[qn][i], None))
        for e in ENGS:
            for tok in toks:
                if tok[3] == e:
                    continue
                self._need(e, tok, "raw")

    def emit(self):
        nc = self.nc
        with nc.Block() as block:
            for e in ENGS:
                lst = self.q[e]

                def body(eng, lst=lst):
                    for f in lst:
                        f(eng)
                getattr(block, e)(body)


class Arena:
    def __init__(self, nc, nbytes):
        self.t = nc.alloc_sbuf_tensor("arena", [128, nbytes // 4], F32)
        self.cap = nbytes // 4
        self.off = 0

    def reset(self, off=0):
        self.off = off

    def alloc(self, free_elems, dtype, name="t"):
        nb = free_elems * (2 if dtype == BF16 else 4)
        nw = (nb + 3) // 4
        nw = (nw + 7) // 8 * 8
        assert self.off + nw <= self.cap, ("SBUF arena overflow", name, self.off, nw, self.cap)
        ap = self.t[:, self.off:self.off + nw]
        self.off += nw
        if dtype == BF16:
            ap = ap.bitcast(BF16)[:, 0:free_elems]
        else:
            ap = ap[:, 0:free_elems]
        return Tile(ap, name)


def rope_tables(S, rows, half, pairs_per):
    inv = (1.0 / (np.float32(THETA) ** (np.arange(half, dtype=np.float32) / np.float32(half)))).astype(np.float32)
    pos = np.arange(S, dtype=np.float32)
    ang = (pos[None, :] * inv[:, None]).astype(np.float32)
    c = np.cos(ang).astype(np.float32)
    s = np.sin(ang).astype(np.float32)
    blk_c = np.concatenate([c, c], axis=0)
    blk_s = np.concatenate([-s, s], axis=0)
    reps = rows // (2 * half)
    return np.tile(blk_c, (reps, 1)), np.tile(blk_s, (reps, 1))


class Builder:
    def __init__(self, S, layers, with_final=True):
        self.S = S
        self.NCH = S // 512
        self.NT = S // 128
        self.layers = layers
        self.with_final = with_final
        nc = bass.Bass("TRN2", target_bir_lowering=False)
        self.nc = nc
        self.P = Prog(nc)
        self.A = Arena(nc, 206 * 1024)
        self.psall = nc.alloc_psum_tensor("psall", [128, 8 * 512], F32).ap()
        self.ps = [Tile(self.psall[:, i * 512:(i + 1) * 512], "ps%d" % i) for i in range(8)]
        self.nsp = 0
        self.psn = 0
        self.npt = 0
        self.NC = (S - 32) // 16 + 1
        self.NCT = (self.NC + 127) // 128
        self.NCP = self.NCT * 128
        self.din = {}
        self.scr = {}

    def inp(self, name, shape, dt=F32):
        t = self.nc.dram_tensor(name, list(shape), dt, kind="ExternalInput")
        self.din[name] = t
        return t

    def scratch(self, name, shape, dt):
        t = self.nc.dram_tensor(name, list(shape), dt)
        self.scr[name] = t
        return t

    def psum(self, lo=0, hi=8):
        n = hi - lo
        i = lo + (self.psn % n)
        self.psn += 1
        return self.ps[i]

    def mm(self, out_t, out_ap, lhsT_t, lhsT_ap, rhs_t, rhs_ap, start, stop):
        self.P.op("tensor",
                  lambda e: e.matmul(out_ap, lhsT=lhsT_ap, rhs=rhs_ap, start=start, stop=stop),
                  reads=[lhsT_t, rhs_t], writes=[out_t])

    def act(self, out_t, out_ap, in_t, in_ap, func, scale=1.0, bias=0.0, extra_reads=()):
        self.P.op("scalar",
                  lambda e: e.activation(out=out_ap, in_=in_ap, func=func, bias=bias, scale=scale),
                  reads=[in_t] + list(extra_reads), writes=[out_t])

    def tt(self, eng, out_t, out_ap, a_t, a_ap, b_t, b_ap, op):
        self.P.op(eng, lambda e: e.tensor_tensor(out=out_ap, in0=a_ap, in1=b_ap, op=op),
                  reads=[a_t, b_t], writes=[out_t])

    def ts(self, eng, out_t, out_ap, a_t, a_ap, s1, op0, s2=None, op1=None, extra_reads=()):
        if op1 is None:
            self.P.op(eng, lambda e: e.tensor_scalar(out=out_ap, in0=a_ap, scalar1=s1, scalar2=None, op0=op0),
                      reads=[a_t] + list(extra_reads), writes=[out_t])
        else:
            self.P.op(eng, lambda e: e.tensor_scalar(out=out_ap, in0=a_ap, scalar1=s1, scalar2=s2, op0=op0, op1=op1),
                      reads=[a_t] + list(extra_reads), writes=[out_t])

    def stt(self, out_t, out_ap, a_t, a_ap, scalar, b_t, b_ap, op0, op1, extra_reads=()):
        self.P.op("vector",
                  lambda e: e.scalar_tensor_tensor(out=out_ap, in0=a_ap, scalar=scalar, in1=b_ap, op0=op0, op1=op1),
                  reads=[a_t, b_t] + list(extra_reads), writes=[out_t])

    def copy(self, eng, out_t, out_ap, in_t, in_ap):
        if eng == "scalar":
            self.P.op("scalar", lambda e: e.copy(out=out_ap, in_=in_ap), reads=[in_t], writes=[out_t])
        else:
            self.P.op(eng, lambda e: e.tensor_copy(out=out_ap, in_=in_ap), reads=[in_t], writes=[out_t])

    def memset(self, eng, t, ap, val):
        self.P.op(eng, lambda e: e.memset(ap, val), reads=[], writes=[t])

    def load(self, t, ap, src):
        self.P.dma("sync", ap, src, reads=[], writes=[t])

    def store(self, dst, t, ap):
        self.P.dma("gpsimd", dst, ap, reads=[t], writes=[])

    def setup_consts(self):
        A = self.A
        S = self.S
        self.c_ones_bf = A.alloc(128, BF16, "ones_bf")
        self.c_ones_f = A.alloc(128, F32, "ones_f")
        self.c_ident_f = A.alloc(128, F32, "ident_f")
        self.c_ident_bf = A.alloc(128, BF16, "ident_bf")
        self.c_maskc = A.alloc(4 * 512, F32, "maskc")
        self.memset("vector", self.c_ones_bf, self.c_ones_bf.ap, 1.0)
        self.memset("vector", self.c_ones_f, self.c_ones_f.ap, 1.0)
        self.load(self.c_ident_f, self.c_ident_f.ap, self.din["c_ident"].ap())
        self.copy("vector", self.c_ident_bf, self.c_ident_bf.ap, self.c_ident_f, self.c_ident_f.ap)
        self.load(self.c_maskc, self.c_maskc.ap, self.din["c_maskc"].ap())
        self.c_gain = A.alloc(3 * DEPTH * KC + 2 * KC, F32, "gains")
        self.load(self.c_gain, self.c_gain.ap[:, 0:(3 * DEPTH + 1) * KC], self.din["c_gains"].ap())
        self.base_off = A.off

    def gain(self, kind, li):
        o = (kind * DEPTH + li) * KC
        return self.c_gain.ap[:, o:o + KC]

    def rmsnorm_fm(self, X, nk, ntok, gain_ap, hT, sq, rstd, nfeat, gain_t=None):
        gt = gain_t if gain_t is not None else self.c_gain
        self.P.op("scalar", lambda e: e.activation(out=sq.ap[:, 0:nk * ntok], in_=X.ap[:, 0:nk * ntok], func=AF.Square),
                  reads=[X], writes=[sq])
        ps = self.psum()
        for k in range(nk):
            self.mm(ps, ps.ap[:, 0:ntok], self.c_ones_bf, self.c_ones_bf.ap[:, 0:128], sq, sq.ap[:, k * ntok:(k + 1) * ntok],
                    start=(k == 0), stop=(k == nk - 1))
        self.act(rstd, rstd.ap[:, 0:ntok], ps, ps.ap[:, 0:ntok], AF.Ln, scale=1.0 / nfeat, bias=self.c_eps.ap[:, 0:1],
                 extra_reads=[self.c_eps])
        self.act(rstd, rstd.ap[:, 0:ntok], rstd, rstd.ap[:, 0:ntok], AF.Exp, scale=-0.5)
        for k in range(nk):
            self.stt(hT, hT.ap[:, k * ntok:(k + 1) * ntok], X, X.ap[:, k * ntok:(k + 1) * ntok], gain_ap[:, k:k + 1],
                     rstd, rstd.ap[:, 0:ntok], ALU.mult, ALU.mult, extra_reads=[gt])

    def load_weight(self, W, wcols, src_ap, nk, ncols, col_off=0, stage=None):
        i = 0
        c0 = 0
        while c0 < ncols:
            cw = min(1024, ncols - c0)
            for k in range(nk):
                st = stage[i % 2]
                i += 1
                self.load(st, st.ap[:, 0:cw], src_ap[k * 128:(k + 1) * 128, c0:c0 + cw])
                dst = W.ap[:, k * wcols + col_off + c0: k * wcols + col_off + c0 + cw]
                eng = ("vector", "gpsimd")[i % 2]
                self.copy(eng, W, dst, st, st.ap[:, 0:cw])
            c0 += cw

    def phase_init(self):
        A, P, S = self.A, self.P, self.S
        A.reset(self.base_off)
        xin = self.din["x"].ap()
        xT = self.scr["xT"].ap()
        xa = [A.alloc(D, F32, "xin%d" % i) for i in range(2)]
        st = [A.alloc(KC * 512, F32, "xst%d" % i) for i in range(2)]
        for c in range(self.NCH):
            stg = st[c % 2]
            for s in range(4):
                t = c * 4 + s
                xt = xa[t % 2]
                self.load(xt, xt.ap, xin[t * 128:(t + 1) * 128, :])
                for kk in range(2):
                    ps = self.psum()
                    for j in range(4):
                        k = kk * 4 + j
                        P.op("tensor", lambda e, o=ps.ap[:, j * 128:(j + 1) * 128], i=xt.ap[:, k * 128:(k + 1) * 128]:
                             e.transpose(o, i, self.c_ident_f.ap[:, 0:128]), reads=[xt, self.c_ident_f], writes=[ps])
                    dst = stg.ap.rearrange("p (k t) -> p k t", k=KC)[:, kk * 4:(kk + 1) * 4, s * 128:(s + 1) * 128]
                    src = ps.ap.rearrange("p (j t) -> p j t", j=4)
                    self.copy(("vector", "scalar")[kk], stg, dst, ps, src)
            self.store(xT[:, :, c * 512:(c + 1) * 512].rearrange("k p t -> p k t"), stg,
                       stg.ap.rearrange("p (k t) -> p k t", k=KC))
        mem = self.din["mem"].ap()
        gB = A.alloc(D, F32, "memgain")
        self.load(gB, gB.ap, self.din["mem_norm"].ap().partition_broadcast(128))
        for s in range(2):
            mt = xa[s]
            self.load(mt, mt.ap, mem[s * 128:(s + 1) * 128, :])
            sqj = st[0]
            ss = A.alloc(8, F32, "memss%d" % s)
            P.op("scalar", lambda e, o=sqj.ap[:, 0:D], i=mt.ap, a=ss.ap[:, 0:1]: e.activation(out=o, in_=i, func=AF.Square, accum_out=a),
                 reads=[mt], writes=[sqj, ss])
            self.act(ss, ss.ap[:, 1:2], ss, ss.ap[:, 0:1], AF.Ln, scale=1.0 / D, bias=self.c_eps.ap[:, 0:1], extra_reads=[self.c_eps])
            self.act(ss, ss.ap[:, 2:3], ss, ss.ap[:, 1:2], AF.Exp, scale=-0.5)
            mn = st[1]
            self.stt(mn, mn.ap[:, 0:D], mt, mt.ap, ss.ap[:, 2:3], gB, gB.ap, ALU.mult, ALU.mult, extra_reads=[ss])
            for kk in range(2):
                ps = self.psum()
                for j in range(4):
                    k = kk * 4 + j
                    P.op("tensor", lambda e, o=ps.ap[:, j * 128:(j + 1) * 128], i=mn.ap[:, k * 128:(k + 1) * 128]:
                         e.transpose(o, i, self.c_ident_f.ap[:, 0:128]), reads=[mn, self.c_ident_f], writes=[ps])
                dst = self.c_memT.ap.rearrange("p (k t) -> p k t", k=KC)[:, kk * 4:(kk + 1) * 4, s * 128:(s + 1) * 128]
                self.copy("vector", self.c_memT, dst, ps, ps.ap.rearrange("p (j t) -> p j t", j=4))
        P.barrier()

    def phase_final(self):
        A, P, S = self.A, self.P, self.S
        A.reset(self.base_off)
        xT = self.scr["xT"].ap()
        out = self.dout.ap()
        X = [A.alloc(KC * 512, F32, "fx%d" % i) for i in range(2)]
        Y = A.alloc(KC * 512, F32, "fy")
        sq = A.alloc(KC * 512, BF16, "fsq")
        rstd = A.alloc(512, F32, "frstd")
        ot = [A.alloc(D, F32, "fo%d" % i) for i in range(2)]
        g = self.c_gain.ap[:, 3 * DEPTH * KC: 3 * DEPTH * KC + KC]
        for c in range(self.NCH):
            x = X[c % 2]
            self.load(x, x.ap.rearrange("p (k t) -> p k t", k=KC), xT[:, :, c * 512:(c + 1) * 512].rearrange("k p t -> p k t"))
            self.rmsnorm_fm(x, KC, 512, g, Y, sq, rstd, D)
            for s in range(4):
                o = ot[s % 2]
                for kk in range(2):
                    ps = self.psum()
                    for j in range(4):
                        k = kk * 4 + j
                        P.op("tensor", lambda e, oo=ps.ap[:, j * 128:(j + 1) * 128], i=Y.ap[:, k * 512 + s * 128: k * 512 + (s + 1) * 128]:
                             e.transpose(oo, i, self.c_ident_f.ap[:, 0:128]), reads=[Y, self.c_ident_f], writes=[ps])
                    self.copy(("vector", "scalar")[kk], o, o.ap[:, kk * 512:(kk + 1) * 512], ps, ps.ap)
                t = c * 4 + s
                self.store(out[t * 128:(t + 1) * 128, :], o, o.ap)
        P.barrier()

    def phase_ffn(self, li):
        A, P, S = self.A, self.P, self.S
        A.reset(self.base_off)
        NTK = 256
        NF = DFF // 128
        xT = self.scr["xT"].ap()
        W13 = A.alloc(KC * 2 * DFF, BF16, "w13")
        W2 = A.alloc(NF * D, BF16, "w2")
        stage = [A.alloc(1024, F32, "wst%d" % i) for i in range(2)]
        self.load_weight(W13, 2 * DFF, self.din["ffn_w13"].ap()[li], KC, 2 * DFF, stage=stage)
        self.load_weight(W2, D, self.din["ffn_w2"].ap()[li], NF, D, stage=stage)
        X = [A.alloc(KC * NTK, F32, "x%d" % i) for i in range(2)]
        hT = A.alloc(KC * NTK, BF16, "hT")
        sq = A.alloc(KC * NTK, BF16, "sq")
        rstd = A.alloc(NTK, F32, "rstd")
        act = A.alloc(NF * NTK, BF16, "act")
        sg = [A.alloc(NTK, F32, "sg%d" % i) for i in range(2)]
        g = self.gain(2, li)
        for c in range(S // NTK):
            x = X[c % 2]
            t0 = c * NTK
            self.load(x, x.ap.rearrange("p (k t) -> p k t", k=KC), xT[:, :, t0:t0 + NTK].rearrange("k p t -> p k t"))
            self.rmsnorm_fm(x, KC, NTK, g, hT, sq, rstd, D)
            for f in range(NF):
                pg = self.psum()
                for k in range(KC):
                    self.mm(pg, pg.ap[:, 0:NTK], W13, W13.ap[:, k * 2 * DFF + f * 128: k * 2 * DFF + (f + 1) * 128],
                            hT, hT.ap[:, k * NTK:(k + 1) * NTK], k == 0, k == KC - 1)
                pu = self.psum()
                for k in range(KC):
                    self.mm(pu, pu.ap[:, 0:NTK], W13, W13.ap[:, k * 2 * DFF + DFF + f * 128: k * 2 * DFF + DFF + (f + 1) * 128],
                            hT, hT.ap[:, k * NTK:(k + 1) * NTK], k == 0, k == KC - 1)
                s_ = sg[f % 2]
                self.act(s_, s_.ap[:, 0:NTK], pg, pg.ap[:, 0:NTK], AF.Silu)
                self.tt("vector", act, act.ap[:, f * NTK:(f + 1) * NTK], pu, pu.ap[:, 0:NTK], s_, s_.ap[:, 0:NTK], ALU.mult)
            for n in range(KC):
                po = self.psum()
                for f in range(NF):
                    self.mm(po, po.ap[:, 0:NTK], W2, W2.ap[:, f * D + n * 128: f * D + (n + 1) * 128],
                            act, act.ap[:, f * NTK:(f + 1) * NTK], f == 0, f == NF - 1)
                self.tt("vector", x, x.ap[:, n * NTK:(n + 1) * NTK], po, po.ap[:, 0:NTK], x, x.ap[:, n * NTK:(n + 1) * NTK], ALU.add)
            self.store(xT[:, :, t0:t0 + NTK].rearrange("k p t -> p k t"), x, x.ap.rearrange("p (k t) -> p k t", k=KC))
        P.barrier()

    def phase_out_xattn(self, li, w_out_ap):
        A, P, S = self.A, self.P, self.S
        A.reset(self.base_off)
        xT = self.scr["xT"].ap()
        oT = self.scr["oT"].ap()
        Wo = A.alloc(KC * D, BF16, "wo")
        Wq = A.alloc(KC * 512, BF16, "wq")
        Wx = A.alloc(4 * D, BF16, "wxo")
        Wkv = A.alloc(KC * D, BF16, "wkv")
        stage = [A.alloc(1024, F32, "wst%d" % i) for i in range(2)]
        self.load_weight(Wo, D, w_out_ap, KC, D, stage=stage)
        self.load_weight(Wq, 512, self.din["xa_wq"].ap()[li], KC, 512, stage=stage)
        self.load_weight(Wx, D, self.din["xa_wo"].ap()[li], 4, D, stage=stage)
        self.load_weight(Wkv, D, self.din["xa_wkv"].ap()[li], KC, D, stage=stage)
        KmT = A.alloc(4 * MEM, BF16, "kmT")
        Vm = A.alloc(2 * 512, BF16, "vm")
        memT = self.c_memT
        for h in range(4):
            ps = self.psum()
            for k in range(KC):
                self.mm(ps, ps.ap[:, 0:MEM], Wkv, Wkv.ap[:, k * D + h * 128: k * D + (h + 1) * 128],
                        memT, memT.ap[:, k * MEM:(k + 1) * MEM], k == 0, k == KC - 1)
            self.copy("vector", KmT, KmT.ap[:, h * MEM:(h + 1) * MEM], ps, ps.ap[:, 0:MEM])
        for s in range(2):
            ps = self.psum()
            for k in range(KC):
                self.mm(ps, ps.ap[:, 0:512], memT, memT.ap[:, k * MEM + s * 128: k * MEM + (s + 1) * 128],
                        Wkv, Wkv.ap[:, k * D + 512: k * D + 1024], k == 0, k == KC - 1)
            self.copy("vector", Vm, Vm.ap[:, s * 512:(s + 1) * 512], ps, ps.ap[:, 0:512])
        X = [A.alloc(KC * 512, F32, "x%d" % i) for i in range(2)]
        O = [A.alloc(KC * 512, BF16, "o%d" % i) for i in range(2)]
        hT = A.alloc(KC * 512, BF16, "hT")
        sq = A.alloc(KC * 512, BF16, "sq")
        rstd = A.alloc(512, F32, "rstd")
        qT = A.alloc(4 * 512, BF16, "qT")
        pT = [A.alloc(512, BF16, "pT%d" % i) for i in range(4)]
        rec = A.alloc(512, F32, "rec")
        xo = A.alloc(4 * 512, BF16, "xo")
        g = self.gain(1, li)
        sc = 128.0 ** -0.5
        for c in range(self.NCH):
            x = X[c % 2]
            o = O[c % 2]
            t0 = c * 512
            self.load(x, x.ap.rearrange("p (k t) -> p k t", k=KC), xT[:, :, t0:t0 + 512].rearrange("k p t -> p k t"))
            self.load(o, o.ap.rearrange("p (k t) -> p k t", k=KC), oT[:, :, t0:t0 + 512].rearrange("k p t -> p k t"))
            for n in range(KC):
                ps = self.psum()
                for k in range(KC):
                    self.mm(ps, ps.ap, Wo, Wo.ap[:, k * D + n * 128: k * D + (n + 1) * 128], o, o.ap[:, k * 512:(k + 1) * 512],
                            k == 0, k == KC - 1)
                self.tt("vector", x, x.ap[:, n * 512:(n + 1) * 512], ps, ps.ap, x, x.ap[:, n * 512:(n + 1) * 512], ALU.add)
            self.rmsnorm_fm(x, KC, 512, g, hT, sq, rstd, D)
            for h in range(4):
                ps = self.psum()
                for k in range(KC):
                    self.mm(ps, ps.ap, Wq, Wq.ap[:, k * 512 + h * 128: k * 512 + (h + 1) * 128], hT, hT.ap[:, k * 512:(k + 1) * 512],
                            k == 0, k == KC - 1)
                self.copy("scalar", qT, qT.ap[:, h * 512:(h + 1) * 512], ps, ps.ap)
            for h in range(4):
                pts = []
                for m in range(2):
                    ps = self.psum()
                    self.mm(ps, ps.ap, KmT, KmT.ap[:, h * MEM + m * 128: h * MEM + (m + 1) * 128], qT, qT.ap[:, h * 512:(h + 1) * 512],
                            True, True)
                    pt = pT[(h * 2 + m) % 4]
                    self.act(pt, pt.ap, ps, ps.ap, AF.Exp, scale=sc)
                    pts.append(pt)
                po = self.psum()
                for m in range(2):
                    self.mm(po, po.ap, Vm, Vm.ap[:, m * 512 + h * 128: m * 512 + (h + 1) * 128], pts[m], pts[m].ap, m == 0, m == 1)
                pz = self.psum()
                for m in range(2):
                    self.mm(pz, pz.ap, self.c_ones_bf, self.c_ones_bf.ap[:, 0:128], pts[m], pts[m].ap, m == 0, m == 1)
                P.op("vector", lambda e, o_=rec.ap, i_=pz.ap: e.reciprocal(out=o_, in_=i_), reads=[pz], writes=[rec])
                self.tt("vector", xo, xo.ap[:, h * 512:(h + 1) * 512], po, po.ap, rec, rec.ap, ALU.mult)
            for n in range(KC):
                ps = self.psum()
                for h in range(4):
                    self.mm(ps, ps.ap, Wx, Wx.ap[:, h * D + n * 128: h * D + (n + 1) * 128], xo, xo.ap[:, h * 512:(h + 1) * 512],
                            h == 0, h == 3)
                self.tt("vector", x, x.ap[:, n * 512:(n + 1) * 512], ps, ps.ap, x, x.ap[:, n * 512:(n + 1) * 512], ALU.add)
            self.store(xT[:, :, t0:t0 + 512].rearrange("k p t -> p k t"), x, x.ap.rearrange("p (k t) -> p k t", k=KC))
        P.barrier()

    def attn_pass(self, k_t, k_ap_fn, q_t, q_ap, v_t, v_ap_fn, vcols, scale, Oacc, pairs, accs=None, ptiles=None, ssb=None):
        P = self.P
        total = 2 * len(pairs)
        done = [0]
        used = [False, False]

        def emit_pv(item):
            kta, ktb, pt = item
            for half, kt in ((0, kta), (1, ktb)):
                self.mm(Oacc, Oacc.ap[0:vcols, :], v_t, v_ap_fn(kt), pt, pt.ap[:, half * 512:(half + 1) * 512], done[0] == 0, done[0] == total - 1)
                done[0] += 1

        pend = None
        for pi, (kta, ktb, minfo) in enumerate(pairs):
            b0 = 2 * (self.nsp % 2)
            self.nsp += 1
            pa, pb = self.ps[b0], self.ps[b0 + 1]
            self.mm(pa, pa.ap, k_t, k_ap_fn(kta), q_t, q_ap, True, True)
            self.mm(pb, pb.ap, k_t, k_ap_fn(ktb), q_t, q_ap, True, True)
            pair_ap = self.psall[:, b0 * 512:(b0 + 2) * 512]
            pt = ptiles[self.npt % len(ptiles)]
            self.npt += 1
            if minfo is not None:
                sb = ssb[self.npt % len(ssb)]
                P.op("vector", lambda e, o=sb.ap, i0=pair_ap, i1=minfo[1]: e.tensor_tensor(out=o, in0=i0, in1=i1, op=ALU.min),
                     reads=[pa, pb, minfo[0]], writes=[sb])
                P.op("scalar", lambda e, o=pt.ap, i=sb.ap: e.activation(out=o, in_=i, func=AF.Exp, scale=scale), reads=[sb], writes=[pt])
            else:
                P.op("scalar", lambda e, o=pt.ap, i=pair_ap: e.activation(out=o, in_=i, func=AF.Exp, scale=scale), reads=[pa, pb], writes=[pt])
            if accs is not None:
                accA, tmpA, accB, tmpB = accs
                if pi % 3 != 2:
                    eng, acc, tmp, ui = "vector", accA, tmpA, 0
                else:
                    eng, acc, tmp, ui = "gpsimd", accB, tmpB, 1
                if not used[ui]:
                    self.tt(eng, acc, acc.ap, pt, pt.ap[:, 0:512], pt, pt.ap[:, 512:1024], ALU.add)
                    used[ui] = True
                else:
                    self.tt(eng, tmp, tmp.ap, pt, pt.ap[:, 0:512], pt, pt.ap[:, 512:1024], ALU.add)
                    self.tt(eng, acc, acc.ap, acc, acc.ap, tmp, tmp.ap, ALU.add)
            if pend is not None:
                emit_pv(pend)
            pend = (kta, ktb, pt)
        emit_pv(pend)
        return used

    def causal_pairs(self, qc):
        lst = [(2 * i, 2 * i + 1, None) for i in range(2 * qc)]
        lst.append((4 * qc, 4 * qc + 1, (self.c_maskc, self.c_maskc.ap[:, 0:1024])))
        lst.append((4 * qc + 2, 4 * qc + 3, (self.c_maskc, self.c_maskc.ap[:, 1024:2048])))
        return lst

    def attn_pass_dual(self, k_t, q_t, v_t, v_ap_fn, vcols, scale, O0, O1, kts, accs=None, ptiles=None, ssb=None, post=None):
        P = self.P
        total = len(kts)
        used = [False, False]
        cnt = [0]

        def emit_pv(item):
            kt, pt = item
            first, last = cnt[0] == 0, cnt[0] == total - 1
            self.mm(O0, O0.ap[0:vcols, :], v_t, v_ap_fn(kt, 0), pt, pt.ap[:, 0:512], first, last)
            self.mm(O1, O1.ap[0:vcols, :], v_t, v_ap_fn(kt, 1), pt, pt.ap[:, 512:1024], first, last)
            cnt[0] += 1

        pend = None
        for i, (kt, minfo) in enumerate(kts):
            b0 = 2 * (self.nsp % 2)
            self.nsp += 1
            pa, pb = self.ps[b0], self.ps[b0 + 1]
            ka, kb = k_t[0], k_t[1]
            self.mm(pa, pa.ap, ka[0], ka[1](kt), q_t[0], q_t[1], True, True)
            self.mm(pb, pb.ap, kb[0], kb[1](kt), q_t[0], q_t[2], True, True)
            pair_ap = self.psall[:, b0 * 512:(b0 + 2) * 512]
            pt = ptiles[self.npt % len(ptiles)]
            self.npt += 1
            if minfo is not None:
                sb = ssb[self.npt % len(ssb)]
                if len(minfo) > 2:
                    o3 = sb.ap.rearrange("p (a c) -> p a c", a=2)
                    i3 = pair_ap.rearrange("p (a c) -> p a c", a=2)
                    m3 = minfo[1].unsqueeze(1).to_broadcast([128, 2, 512])
                    P.op("vector", lambda e, o=o3, i0=i3, i1=m3: e.tensor_tensor(out=o, in0=i0, in1=i1, op=ALU.min),
                         reads=[pa, pb, minfo[0]], writes=[sb])
                else:
                    P.op("vector", lambda e, o=sb.ap, i0=pair_ap, i1=minfo[1]: e.tensor_tensor(out=o, in0=i0, in1=i1, op=ALU.min),
                         reads=[pa, pb, minfo[0]], writes=[sb])
                P.op("scalar", lambda e, o=pt.ap, i=sb.ap: e.activation(out=o, in_=i, func=AF.Exp, scale=scale), reads=[sb], writes=[pt])
            else:
                P.op("scalar", lambda e, o=pt.ap, i=pair_ap: e.activation(out=o, in_=i, func=AF.Exp, scale=scale), reads=[pa, pb], writes=[pt])
            if post is not None:
                mt, map512 = post(kt)
                p3 = pt.ap.rearrange("p (a c) -> p a c", a=2)
                m3 = map512.unsqueeze(1).to_broadcast([128, 2, 512])
                P.op("vector", lambda e, o=p3, i1=m3: e.tensor_tensor(out=o, in0=o, in1=i1, op=ALU.mult), reads=[pt, mt], writes=[pt])
            if accs is not None:
                accA, accB = accs
                if i % 3 != 2:
                    eng, acc, ui = "vector", accA, 0
                else:
                    eng, acc, ui = "gpsimd", accB, 1
                if not used[ui]:
                    self.copy(eng, acc, acc.ap, pt, pt.ap)
                    used[ui] = True
                else:
                    self.tt(eng, acc, acc.ap, acc, acc.ap, pt, pt.ap, ALU.add)
            if pend is not None:
                emit_pv(pend)
            pend = (kt, pt)
        emit_pv(pend)
        return used

    def causal_list(self, qc):
        lst = [(kt, None) for kt in range(4 * qc)]
        for o in range(4):
            lst.append((4 * qc + o, (self.c_maskc, self.c_maskc.ap[:, o * 512:(o + 1) * 512])))
        return lst

    def phase_even_in(self, li):
        A, P, S = self.A, self.P, self.S
        j = li // 2
        A.reset(self.base_off)
        xT = self.scr["xT"].ap()
        w_in = self.din["ev_w_in"].ap()[j]
        W = A.alloc(KC * EV_W, BF16, "w_in")
        Wp = A.alloc(KC * 1024, BF16, "w_perm")
        stage = [A.alloc(1024, F32, "wst%d" % i) for i in range(2)]
        self.load_weight(W, EV_W, w_in, KC, EV_W, stage=stage)
        wv = w_in[:, 0:1024].rearrange("k (b two d) -> k b two d", two=2, d=32)
        i = 0
        for k in range(KC):
            for two in range(2):
                st = stage[i % 2]
                i += 1
                self.load(st, st.ap[:, 0:512].rearrange("p (b d) -> p b d", d=32), wv[k * 128:(k + 1) * 128, :, 1 - two, :])
                dst = Wp.ap[:, k * 1024:(k + 1) * 1024].rearrange("p (b two d) -> p b two d", two=2, d=32)[:, :, two, :]
                self.copy(("vector", "gpsimd")[i % 2], Wp, dst, st, st.ap[:, 0:512].rearrange("p (b d) -> p b d", d=32))
        nbf = A.alloc(8, F32, "nbf")
        self.load(nbf, nbf.ap[0:8, 0:1], self.din["ev_b_f"].ap()[j].rearrange("(h o) -> h o", o=1))
        self.ts("vector", nbf, nbf.ap[0:8, 1:2], nbf, nbf.ap[0:8, 0:1], -1.0, ALU.mult)
        X = [A.alloc(KC * 512, F32, "x%d" % i) for i in range(2)]
        hT = A.alloc(KC * 512, BF16, "hT")
        sq = A.alloc(KC * 512, BF16, "sq")
        rstd = A.alloc(512, F32, "rstd")
        COS = [A.alloc(512, F32, "cos%d" % i) for i in range(2)]
        SIN = [A.alloc(512, F32, "sin%d" % i) for i in range(2)]
        t1 = [A.alloc(512, F32, "t1_%d" % i) for i in range(2)]
        t2 = [A.alloc(512, F32, "t2_%d" % i) for i in range(2)]
        qo = [A.alloc(512, BF16, "qo%d" % i) for i in range(4)]
        vo = [A.alloc(512, BF16, "vo%d" % i) for i in range(2)]
        lf = [A.alloc(512, F32, "lf%d" % i) for i in range(2)]
        g = self.gain(0, li)
        dq, dk, dv = self.scr["dq"].ap(), self.scr["dk"].ap(), self.scr["dv"].ap()
        fq, fk, fv = self.scr["fq"].ap(), self.scr["fk"].ap(), self.scr["fv"].ap()
        lfp = self.scr["lfp"].ap()
        nq = 0
        for c in range(self.NCH):
            x = X[c % 2]
            t0 = c * 512
            self.load(x, x.ap.rearrange("p (k t) -> p k t", k=KC), xT[:, :, t0:t0 + 512].rearrange("k p t -> p k t"))
            cs, sn = COS[c % 2], SIN[c % 2]
            self.load(cs, cs.ap, self.din["c_cos64"].ap()[:, t0:t0 + 512])
            self.load(sn, sn.ap, self.din["c_sin64"].ap()[:, t0:t0 + 512])
            self.rmsnorm_fm(x, KC, 512, g, hT, sq, rstd, D)

            def proj(Wt, wcols, col0, ncols):
                ps = self.psum()
                for k in range(KC):
                    self.mm(ps, ps.ap[0:ncols, :], Wt, Wt.ap[:, k * wcols + col0: k * wcols + col0 + ncols], hT, hT.ap[:, k * 512:(k + 1) * 512],
                            k == 0, k == KC - 1)
                return ps
            for which, dst in ((0, dq), (1, dk)):
                for h in range(4):
                    col = which * 512 + h * 128
                    pa = proj(W, EV_W, col, 128)
                    pb = proj(Wp, 1024, col, 128)
                    a1, a2 = t1[nq % 2], t2[nq % 2]
                    q_ = qo[nq % 4]
                    nq += 1
                    self.tt("vector", a1, a1.ap, pa, pa.ap, cs, cs.ap, ALU.mult)
                    self.tt("vector", a2, a2.ap, pb, pb.ap, sn, sn.ap, ALU.mult)
                    self.tt("gpsimd", q_, q_.ap, a1, a1.ap, a2, a2.ap, ALU.add)
                    self.store(dst[h, :, t0:t0 + 512], q_, q_.ap)
            for which, dst in ((0, fq), (1, fk)):
                for pr in range(4):
                    col = 1536 + which * 512 + pr * 128
                    pa = proj(W, EV_W, col, 128)
                    q_ = qo[nq % 4]
                    nq += 1
                    self.copy("scalar", q_, q_.ap, pa, pa.ap)
                    for two in range(2):
                        self.store(dst[pr * 2 + two, 0:64, t0:t0 + 512], q_, q_.ap[two * 64:(two + 1) * 64, :])
            for s in range(4):
                for which, col, dst in ((0, 1024, dv), (1, 2560, fv)):
                    ps = self.psum()
                    for k in range(KC):
                        self.mm(ps, ps.ap, hT, hT.ap[:, k * 512 + s * 128: k * 512 + (s + 1) * 128], W, W.ap[:, k * EV_W + col: k * EV_W + col + 512],
                                k == 0, k == KC - 1)
                    v_ = vo[which]
                    self.copy(("scalar", "vector")[which], v_, v_.ap, ps, ps.ap)
                    tt0 = t0 + s * 128
                    if which == 0:
                        self.store(dst[:, tt0:tt0 + 128, :].rearrange("h t d -> t h d"), v_, v_.ap.rearrange("p (h d) -> p h d", h=4))
                    else:
                        self.store(dst[:, tt0:tt0 + 128, :].rearrange("h t d -> t h d"), v_, v_.ap.rearrange("p (h d) -> p h d", h=8))
            pz = proj(W, EV_W, 3072, 8)
            l_ = lf[c % 2]
            self.act(l_, l_.ap[0:8, :], pz, pz.ap[0:8, :], AF.Exp, scale=-1.0, bias=nbf.ap[0:8, 1:2], extra_reads=[nbf])
            self.act(l_, l_.ap[0:8, :], l_, l_.ap[0:8, :], AF.Ln, scale=1.0, bias=self.c_one.ap[0:8, 0:1], extra_reads=[self.c_one])
            self.store(lfp[:, t0:t0 + 512], l_, l_.ap[0:8, :])
        P.barrier()

    def phase_even_scan(self, li):
        A, P, S = self.A, self.P, self.S
        A.reset(self.base_off)
        PW = min(2048, S)
        npieces = S // PW
        lfp = self.scr["lfp"].ap()
        fq, fk = self.scr["fq"].ap(), self.scr["fk"].ap()
        ones = A.alloc(PW, BF16, "ones")
        self.memset("vector", ones, ones.ap[0:8, :], 1.0)
        self.c_onesrow = A.alloc(PW, F32, "onesrow")
        self.memset("gpsimd", self.c_onesrow, self.c_onesrow.ap[0:8, :], 1.0)
        L = [A.alloc(PW, F32, "L%d" % i) for i in range(2)]
        C = [A.alloc(PW, F32, "C%d" % i) for i in range(2)]
        R = A.alloc(PW, F32, "R")
        hi = [[A.alloc(PW, BF16, "sp%d_%d" % (i, jj)) for jj in range(3)] for i in range(2)]
        ng = [[A.alloc(PW, BF16, "ng%d_%d" % (i, jj)) for jj in range(3)] for i in range(2)]
        zero = A.alloc(8, F32, "zero")
        self.memset("vector", zero, zero.ap[0:8, 0:1], 0.0)
        prev = None
        for pc in range(npieces):
            l_, c_ = L[pc % 2], C[pc % 2]
            t0 = pc * PW
            self.load(l_, l_.ap[0:8, :], lfp[:, t0:t0 + PW])
            init_t = zero if prev is None else prev
            init_ap = zero.ap[0:8, 0:1] if prev is None else prev.ap[0:8, PW - 1:PW]
            P.op("vector", lambda e, o=c_.ap[0:8, :], d1=l_.ap[0:8, :], ia=init_ap, d0=self.c_onesrow.ap[0:8, 0:PW]:
                 e.tensor_tensor_scan(out=o, data0=d0, data1=d1, initial=ia, op0=ALU.mult, op1=ALU.add),
                 reads=[l_, init_t, self.c_onesrow], writes=[c_])
            prev = c_
            h3 = hi[pc % 2]
            n3 = ng[pc % 2]
            self.ts("vector", R, R.ap[0:8, :], c_, c_.ap[0:8, :], 8.0, ALU.mult)
            for jj in range(3):
                self.copy("vector", h3[jj], h3[jj].ap[0:8, :], R, R.ap[0:8, :])
                if jj < 2:
                    self.tt("vector", R, R.ap[0:8, :], R, R.ap[0:8, :], h3[jj], h3[jj].ap[0:8, :], ALU.subtract)
                self.ts("vector", n3[jj], n3[jj].ap[0:8, :], h3[jj], h3[jj].ap[0:8, :], -1.0, ALU.mult)
            for jj in range(3):
                self.store(fq[:, 64 + jj, t0:t0 + PW], n3[jj], n3[jj].ap[0:8, :])
                self.store(fq[:, 67 + jj, t0:t0 + PW], ones, ones.ap[0:8, :])
                self.store(fk[:, 64 + jj, t0:t0 + PW], ones, ones.ap[0:8, :])
                self.store(fk[:, 67 + jj, t0:t0 + PW], h3[jj], h3[jj].ap[0:8, :])
        P.barrier()

    def phase_even_attn(self, li):
        A, P, S, NT = self.A, self.P, self.S, self.NT
        j = li // 2
        A.reset(self.base_off)
        lam_init = 0.8 - 0.6 * math.exp(-0.3 * li)
        dq, dk, dv = self.scr["dq"].ap(), self.scr["dk"].ap(), self.scr["dv"].ap()
        fq, fk, fv = self.scr["fq"].ap(), self.scr["fk"].ap(), self.scr["fv"].ap()
        oT = self.scr["oT"].ap()
        KT = [A.alloc(S, BF16, "KT%d" % i) for i in range(2)]
        V = [A.alloc(NT * 129, BF16, "V%d" % i) for i in range(2)]
        Q = [A.alloc(512, BF16, "Q%d" % i) for i in range(2)]
        pts = [A.alloc(1024, BF16, "pt%d" % i) for i in range(4)]
        ssb = [A.alloc(1024, F32, "ssb%d" % i) for i in range(2)]
        rec = [A.alloc(512, F32, "rec%d" % i) for i in range(2)]
        recB = A.alloc(512, F32, "recB")
        on = [A.alloc(512, F32, "on%d" % i) for i in range(2)]
        oa = A.alloc(512, F32, "oa")
        osq = A.alloc(512, BF16, "osq")
        orstd = A.alloc(512, F32, "orstd")
        ob = [A.alloc(512, BF16, "ob%d" % i) for i in range(2)]
        accA = A.alloc(1024, F32, "accA")
        accB = A.alloc(1024, F32, "accB")
        mask2 = A.alloc(4 * 1024, F32, "mask2")
        for o in range(4):
            for two in range(2):
                self.copy("gpsimd", mask2, mask2.ap[:, o * 1024 + two * 512: o * 1024 + (two + 1) * 512], self.c_maskc, self.c_maskc.ap[:, o * 512:(o + 1) * 512])
        lam = A.alloc(4 * 64 + 8, F32, "lam")
        self.load(lam, lam.ap[:, 0:256], self.din["ev_lam"].ap()[j].rearrange("a d -> (a d)").partition_broadcast(128))
        lp = A.alloc(128, F32, "lamp")
        LS = 256
        self.tt("vector", lp, lp.ap[:, 0:64], lam, lam.ap[:, 0:64], lam, lam.ap[:, 64:128], ALU.mult)
        self.tt("vector", lp, lp.ap[:, 64:128], lam, lam.ap[:, 128:192], lam, lam.ap[:, 192:256], ALU.mult)
        P.op("vector", lambda e: e.reduce_sum(out=lam.ap[:, LS:LS + 1], in_=lp.ap[:, 0:64], axis=mybir.AxisListType.X), reads=[lp], writes=[lam])
        P.op("vector", lambda e: e.reduce_sum(out=lam.ap[:, LS + 1:LS + 2], in_=lp.ap[:, 64:128], axis=mybir.AxisListType.X), reads=[lp], writes=[lam])
        self.act(lam, lam.ap[:, LS + 2:LS + 4], lam, lam.ap[:, LS:LS + 2], AF.Exp)
        self.tt("vector", lam, lam.ap[:, LS + 4:LS + 5], lam, lam.ap[:, LS + 2:LS + 3], lam, lam.ap[:, LS + 3:LS + 4], ALU.subtract)
        self.ts("vector", lam, lam.ap[:, LS + 5:LS + 6], lam, lam.ap[:, LS + 4:LS + 5], lam_init, ALU.add, -1.0, ALU.mult)
        neg_lam = lam.ap[:, LS + 5:LS + 6]
        sub = A.alloc(8, F32, "subln")
        self.load(sub, sub.ap[:, 0:1], self.din["ev_subln"].ap()[j].rearrange("(p o) -> p o", o=1))
        self.ts("vector", sub, sub.ap[:, 1:2], sub, sub.ap[:, 0:1], 1.0 - lam_init, ALU.mult)
        for i in range(2):
            self.memset("gpsimd", V[i], V[i].ap, 1.0)
        units = [("f", h) for h in range(8)] + [("d", h) for h in range(4)]

        def load_unit(u, slot):
            kind, h = u
            kt_, v_ = KT[slot], V[slot]
            if kind == "f":
                self.load(kt_, kt_.ap[0:70, :], fk[h])
                self.load(v_, v_.ap.rearrange("p (t c) -> p t c", c=129)[:, :, 0:64], fv[h].rearrange("(t p) d -> p t d", p=128))
            else:
                self.load(kt_, kt_.ap, dk[h])
                self.load(v_, v_.ap.rearrange("p (t c) -> p t c", c=129)[:, :, 0:128], dv[h].rearrange("(t p) d -> p t d", p=128))

        load_unit(units[0], 0)
        nqi = 0
        npass = 0
        for ui, u in enumerate(units):
            kind, h = u
            slot = ui % 2
            if ui + 1 < len(units):
                load_unit(units[ui + 1], 1 - slot)
            kt_, v_ = KT[slot], V[slot]
            for qc in range(self.NCH):
                q_ = Q[nqi % 2]
                nqi += 1
                t0 = qc * 512
                kl = self.causal_pairs(qc)
                if kind == "f":
                    self.load(q_, q_.ap[0:70, :], fq[h, :, t0:t0 + 512])
                    Oacc = self.ps[4 + npass % 2]
                    npass += 1
                    self.attn_pass(kt_, lambda kt, kt_=kt_: kt_.ap[0:70, kt * 128:(kt + 1) * 128], q_, q_.ap[0:70, :],
                                   v_, lambda kt, v_=v_: v_.ap[:, kt * 129: kt * 129 + 65], 65, 0.125, Oacc, kl, ptiles=pts, ssb=ssb)
                    r_ = rec[0]
                    P.op("vector", lambda e, o=r_.ap[64:65, :], i=Oacc.ap[64:65, :]: e.reciprocal(out=o, in_=i), reads=[Oacc], writes=[r_])
                    pb = self.ps[6]
                    self.mm(pb, pb.ap[0:64, :], self.c_ones_f, self.c_ones_f.ap[64:65, 0:64], r_, r_.ap[64:65, :], True, True)
                    self.copy("scalar", recB, recB.ap[0:64, :], pb, pb.ap[0:64, :])
                    o_ = ob[npass % 2]
                    self.tt("vector", o_, o_.ap[0:64, :], Oacc, Oacc.ap[0:64, :], recB, recB.ap[0:64, :], ALU.mult)
                    self.store(oT[4 + h // 2, (h % 2) * 64:(h % 2) * 64 + 64, t0:t0 + 512], o_, o_.ap[0:64, :])
                else:
                    self.load(q_, q_.ap, dq[h, :, t0:t0 + 512])
                    pb4 = 4 + 2 * (npass % 2)
                    npass += 1
                    O0, O1 = self.ps[pb4], self.ps[pb4 + 1]
                    kts = [(kt, None) for kt in range(4 * qc)]
                    for o in range(4):
                        kts.append((4 * qc + o, (mask2, mask2.ap[:, o * 1024:(o + 1) * 1024])))
                    used = self.attn_pass_dual(
                        ((kt_, lambda kt, kt_=kt_: kt_.ap[0:64, kt * 128:(kt + 1) * 128]),
                         (kt_, lambda kt, kt_=kt_: kt_.ap[64:128, kt * 128:(kt + 1) * 128])),
                        (q_, q_.ap[0:64, :], q_.ap[64:128, :]),
                        v_, lambda kt, j_, v_=v_: v_.ap[:, kt * 129: kt * 129 + 128], 128, 0.125, O0, O1, kts,
                        accs=(accA, accB), ptiles=pts, ssb=ssb)
                    if used[1]:
                        self.tt("vector", accA, accA.ap, accA, accA.ap, accB, accB.ap, ALU.add)
                    for sub_j, Oacc in ((0, O0), (1, O1)):
                        sacc = self.psum(0, 4)
                        self.mm(sacc, sacc.ap[0:1, :], self.c_ones_f, self.c_ones_f.ap[:, 0:1], accA, accA.ap[:, sub_j * 512:(sub_j + 1) * 512], True, True)
                        r_ = rec[sub_j]
                        P.op("vector", lambda e, o=r_.ap[0:1, :], i=sacc.ap[0:1, :]: e.reciprocal(out=o, in_=i), reads=[sacc], writes=[r_])
                        if sub_j == 1:
                            self.ts("vector", r_, r_.ap[0:1, :], r_, r_.ap[0:1, :], neg_lam[0:1, :], ALU.mult, extra_reads=[lam])
                        pb = self.psum(0, 4)
                        self.mm(pb, pb.ap, self.c_ones_f, self.c_ones_f.ap[0:1, 0:128], r_, r_.ap[0:1, :], True, True)
                        self.copy("scalar", recB, recB.ap, pb, pb.ap)
                        self.tt("vector", on[sub_j], on[sub_j].ap, Oacc, Oacc.ap, recB, recB.ap, ALU.mult)
                    self.tt("gpsimd", oa, oa.ap, on[0], on[0].ap, on[1], on[1].ap, ALU.add)
                    self.act(osq, osq.ap, oa, oa.ap, AF.Square)
                    pz = self.psum(0, 4)
                    self.mm(pz, pz.ap, self.c_ones_bf, self.c_ones_bf.ap[:, 0:128], osq, osq.ap, True, True)
                    self.act(orstd, orstd.ap, pz, pz.ap, AF.Ln, scale=1.0 / 128, bias=self.c_eps.ap[:, 0:1], extra_reads=[self.c_eps])
                    self.act(orstd, orstd.ap, orstd, orstd.ap, AF.Exp, scale=-0.5)
                    o_ = ob[npass % 2]
                    self.stt(o_, o_.ap, oa, oa.ap, sub.ap[:, 1:2], orstd, orstd.ap, ALU.mult, ALU.mult, extra_reads=[sub])
                    self.store(oT[h, :, t0:t0 + 512], o_, o_.ap)
        P.barrier()

    def load_perm(self, Wp, wpcols, dst_col, src_cols_ap, n, half, stage, nk=KC):
        nb = n // (2 * half)
        sv = src_cols_ap.rearrange("k (b two d) -> k b two d", two=2, d=half)
        i = 0
        for k in range(nk):
            for two in range(2):
                st = stage[i % 2]
                i += 1
                self.load(st, st.ap[:, 0:nb * half].rearrange("p (b d) -> p b d", d=half), sv[k * 128:(k + 1) * 128, :, 1 - two, :])
                dst = Wp.ap[:, k * wpcols + dst_col: k * wpcols + dst_col + n].rearrange("p (b two d) -> p b two d", two=2, d=half)[:, :, two, :]
                self.copy(("vector", "gpsimd")[i % 2], Wp, dst, st, st.ap[:, 0:nb * half].rearrange("p (b d) -> p b d", d=half))

    def rope_evac(self, pa, pb, rows, cs, sn, t1, t2, out_t, out_ap):
        self.tt("vector", t1, t1.ap[0:rows, :], pa, pa.ap[0:rows, :], cs, cs.ap[0:rows, :], ALU.mult)
        self.tt("vector", t2, t2.ap[0:rows, :], pb, pb.ap[0:rows, :], sn, sn.ap[0:rows, :], ALU.mult)
        self.tt("gpsimd", out_t, out_ap, t1, t1.ap[0:rows, :], t2, t2.ap[0:rows, :], ALU.add)

    def phase_odd_in(self, li):
        A, P, S = self.A, self.P, self.S
        j = li // 2
        A.reset(self.base_off)
        xT = self.scr["xT"].ap()
        w_in = self.din["od_w_in"].ap()[j]
        NPERM = 928
        W = A.alloc(KC * OD_W, BF16, "w_in")
        Wp = A.alloc(KC * NPERM, BF16, "w_perm")
        stage = [A.alloc(1024, F32, "wst%d" % i) for i in range(2)]
        self.load_weight(W, OD_W, w_in, KC, OD_W, stage=stage)
        self.load_perm(Wp, NPERM, 0, w_in[:, 0:512], 512, 32, stage)
        self.load_perm(Wp, NPERM, 512, w_in[:, 512:640], 128, 32, stage)
        self.load_perm(Wp, NPERM, 640, w_in[:, 768:896], 128, 32, stage)
        self.load_perm(Wp, NPERM, 768, w_in[:, 1024:1152], 128, 32, stage)
        self.load_perm(Wp, NPERM, 896, w_in[:, 1944:1976], 32, 16, stage)
        w_uq = self.din["mla_w_uq"].ap()[j]
        w_ukv = self.din["mla_w_ukv"].ap()[j]
        Wuq = A.alloc(3 * 768, BF16, "wuq")
        Wuqp = A.alloc(3 * 768, BF16, "wuqp")
        self.load_weight(Wuq, 768, w_uq, 3, 768, stage=stage)
        for k in range(3):
            self.copy("gpsimd", Wuqp, Wuqp.ap[:, k * 768:(k + 1) * 768], Wuq, Wuq.ap[:, k * 768:(k + 1) * 768])
        sv = w_uq.rearrange("k (h c) -> k h c", c=96)[:, :, 64:96].rearrange("k h (two d) -> k h two d", two=2)
        i = 0
        for k in range(3):
            for two in range(2):
                st = stage[i % 2]
                i += 1
                self.load(st, st.ap[:, 0:128].rearrange("p (h d) -> p h d", d=16), sv[k * 128:(k + 1) * 128, :, 1 - two, :])
                dst = Wuqp.ap[:, k * 768:(k + 1) * 768].rearrange("p (h c) -> p h c", c=96)[:, :, 64 + two * 16: 64 + two * 16 + 16]
                self.copy("vector", Wuqp, dst, st, st.ap[:, 0:128].rearrange("p (h d) -> p h d", d=16))
        Wkn = A.alloc(2 * 512, BF16, "wkn")
        Wkv = A.alloc(2 * 512, BF16, "wkv")
        kvv = w_ukv.rearrange("k (h two d) -> k h two d", two=2, d=64)
        for k in range(2):
            for two, Wt in ((0, Wkn), (1, Wkv)):
                st = stage[i % 2]
                i += 1
                self.load(st, st.ap[:, 0:512].rearrange("p (h d) -> p h d", d=64), kvv[k * 128:(k + 1) * 128, :, two, :])
                self.copy("vector", Wt, Wt.ap[:, k * 512:(k + 1) * 512], st, st.ap[:, 0:512])
        mlan = A.alloc(8, F32, "mlan")
        self.load(mlan, mlan.ap[:, 0:5], self.din["c_mlan"].ap()[j])
        X = [A.alloc(KC * 512, F32, "x%d" % i) for i in range(2)]
        hT = A.alloc(KC * 512, BF16, "hT")
        sq = A.alloc(KC * 512, BF16, "sq")
        rstd = A.alloc(512, F32, "rstd")
        COS = [A.alloc(512, F32, "cos%d" % i) for i in range(2)]
        SIN = [A.alloc(512, F32, "sin%d" % i) for i in range(2)]
        COSM = [A.alloc(512, F32, "cosm%d" % i) for i in range(2)]
        SINM = [A.alloc(512, F32, "sinm%d" % i) for i in range(2)]
        COSK = [A.alloc(512, F32, "cosk%d" % i) for i in range(2)]
        SINK = [A.alloc(512, F32, "sink%d" % i) for i in range(2)]
        t1 = [A.alloc(512, F32, "t1_%d" % i) for i in range(2)]
        t2 = [A.alloc(512, F32, "t2_%d" % i) for i in range(2)]
        qo = [A.alloc(512, BF16, "qo%d" % i) for i in range(4)]
        vo = [A.alloc(512, BF16, "vo%d" % i) for i in range(2)]
        gt = [A.alloc(512, F32, "gt%d" % i) for i in range(2)]
        CQ = A.alloc(3 * 512, F32, "cq")
        CQn = A.alloc(3 * 512, BF16, "cqn")
        CKV = A.alloc(2 * 512, F32, "ckv")
        CKVn = A.alloc(2 * 512, BF16, "ckvn")
        g = self.gain(0, li)
        scr = self.scr
        nsq, kcT, vcT, ksT, kwT = scr["nsq"].ap(), scr["kcT"].ap(), scr["vcT"].ap(), scr["ksT"].ap(), scr["kwT"].ap()
        vsv, vwv, gT = scr["vsv"].ap(), scr["vwv"].ap(), scr["gT"].ap()
        mq, mk, mv = scr["mq"].ap(), scr["mk"].ap(), scr["mv"].ap()
        nq = 0
        for c in range(self.NCH):
            x = X[c % 2]
            t0 = c * 512
            self.load(x, x.ap.rearrange("p (k t) -> p k t", k=KC), xT[:, :, t0:t0 + 512].rearrange("k p t -> p k t"))
            cs, sn = COS[c % 2], SIN[c % 2]
            csm, snm = COSM[c % 2], SINM[c % 2]
            csk, snk = COSK[c % 2], SINK[c % 2]
            self.load(cs, cs.ap, self.din["c_cos64"].ap()[:, t0:t0 + 512])
            self.load(sn, sn.ap, self.din["c_sin64"].ap()[:, t0:t0 + 512])
            self.load(csm, csm.ap[0:96, :], self.din["c_cosm"].ap()[:, t0:t0 + 512])
            self.load(snm, snm.ap[0:96, :], self.din["c_sinm"].ap()[:, t0:t0 + 512])
            self.load(csk, csk.ap[0:32, :], self.din["c_cosm"].ap()[64:96, t0:t0 + 512])
            self.load(snk, snk.ap[0:32, :], self.din["c_sinm"].ap()[64:96, t0:t0 + 512])
            self.rmsnorm_fm(x, KC, 512, g, hT, sq, rstd, D)

            def proj(Wt, wcols, col0, ncols, rhs=hT, nk=KC):
                ps = self.psum()
                for k in range(nk):
                    self.mm(ps, ps.ap[0:ncols, :], Wt, Wt.ap[:, k * wcols + col0: k * wcols + col0 + ncols], rhs, rhs.ap[:, k * 512:(k + 1) * 512],
                            k == 0, k == nk - 1)
                return ps

            def roped(col, pcol, rows, cst, snt):
                nonlocal nq
                pa = proj(W, OD_W, col, rows)
                pb = proj(Wp, NPERM, pcol, rows)
                q_ = qo[nq % 4]
                a1, a2 = t1[nq % 2], t2[nq % 2]
                nq += 1
                self.rope_evac(pa, pb, rows, cst, snt, a1, a2, q_, q_.ap[0:rows, :])
                return q_
            for t in range(4):
                q_ = roped(128 * t, 128 * t, 128, cs, sn)
                for two in range(2):
                    self.store(nsq[2 * t + two, :, t0:t0 + 512], q_, q_.ap[two * 64:(two + 1) * 64, :])
            for col, pcol, dst in ((512, 512, kcT), (768, 640, ksT), (1024, 768, kwT)):
                q_ = roped(col, pcol, 128, cs, sn)
                for two in range(2):
                    self.store(dst[two, :, t0:t0 + 512], q_, q_.ap[two * 64:(two + 1) * 64, :])
            pa = proj(W, OD_W, 640, 128)
            q_ = qo[nq % 4]
            nq += 1
            self.copy("scalar", q_, q_.ap, pa, pa.ap)
            for two in range(2):
                self.store(vcT[two, :, t0:t0 + 512], q_, q_.ap[two * 64:(two + 1) * 64, :])
            for s in range(4):
                for which, col, dst in ((0, 896, vsv), (1, 1152, vwv)):
                    ps = self.psum()
                    for k in range(KC):
                        self.mm(ps, ps.ap[:, 0:128], hT, hT.ap[:, k * 512 + s * 128: k * 512 + (s + 1) * 128], W, W.ap[:, k * OD_W + col: k * OD_W + col + 128],
                                k == 0, k == KC - 1)
                    v_ = vo[which]
                    self.copy(("scalar", "vector")[which], v_, v_.ap[:, 0:128], ps, ps.ap[:, 0:128])
                    tt0 = t0 + s * 128
                    self.store(dst[:, tt0:tt0 + 128, :].rearrange("g t d -> t g d"), v_, v_.ap[:, 0:128].rearrange("p (g d) -> p g d", g=2))
            pz = proj(W, OD_W, 1280, 24)
            g_ = gt[c % 2]
            self.act(g_, g_.ap[0:24, :], pz, pz.ap[0:24, :], AF.Sigmoid)
            self.store(gT[:, t0:t0 + 512], g_, g_.ap[0:24, :])
            for i3 in range(3):
                pa = proj(W, OD_W, 1304 + 128 * i3, 128)
                self.copy("scalar", CQ, CQ.ap[:, i3 * 512:(i3 + 1) * 512], pa, pa.ap)
            self.rmsnorm_fm(CQ, 3, 512, mlan.ap[:, 0:3], CQn, sq, rstd, 384, gain_t=mlan)
            for i2 in range(2):
                pa = proj(W, OD_W, 1688 + 128 * i2, 128)
                self.copy("scalar", CKV, CKV.ap[:, i2 * 512:(i2 + 1) * 512], pa, pa.ap)
            self.rmsnorm_fm(CKV, 2, 512, mlan.ap[:, 3:5], CKVn, sq, rstd, 256, gain_t=mlan)
            q_ = roped(1944, 896, 32, csk, snk)
            for h in range(8):
                self.store(mk[h, 64:96, t0:t0 + 512], q_, q_.ap[0:32, :])
            for h in range(8):
                pa = proj(Wuq, 768, 96 * h, 96, rhs=CQn, nk=3)
                pb = proj(Wuqp, 768, 96 * h, 96, rhs=CQn, nk=3)
                q_ = qo[nq % 4]
                a1, a2 = t1[nq % 2], t2[nq % 2]
                nq += 1
                self.rope_evac(pa, pb, 96, csm, snm, a1, a2, q_, q_.ap[0:96, :])
                self.store(mq[h, :, t0:t0 + 512], q_, q_.ap[0:96, :])
            for pr in range(4):
                pa = proj(Wkn, 512, 128 * pr, 128, rhs=CKVn, nk=2)
                q_ = qo[nq % 4]
                nq += 1
                self.copy("scalar", q_, q_.ap, pa, pa.ap)
                for two in range(2):
                    self.store(mk[2 * pr + two, 0:64, t0:t0 + 512], q_, q_.ap[two * 64:(two + 1) * 64, :])
            for s in range(4):
                ps = self.psum()
                for k in range(2):
                    self.mm(ps, ps.ap, CKVn, CKVn.ap[:, k * 512 + s * 128: k * 512 + (s + 1) * 128], Wkv, Wkv.ap[:, k * 512:(k + 1) * 512], k == 0, k == 1)
                v_ = vo[s % 2]
                self.copy("vector", v_, v_.ap, ps, ps.ap)
                tt0 = t0 + s * 128
                self.store(mv[:, tt0:tt0 + 128, :].rearrange("h t d -> t h d"), v_, v_.ap.rearrange("p (h d) -> p h d", h=8))
        P.barrier()

    def phase_odd_cmp(self, li):
        A, P, S = self.A, self.P, self.S
        j = li // 2
        A.reset(self.base_off)
        NC, NCT, NCP = self.NC, self.NCT, self.NCP
        scr = self.scr
        src = {0: scr["kcT"].ap(), 1: scr["vcT"].ap()}
        kcmp, vcmp = scr["kcmp"].ap(), scr["vcmp"].ap()
        stage = [A.alloc(1024, F32, "wst%d" % i) for i in range(2)]
        SRC = [A.alloc(2 * S, BF16, "src%d" % kv) for kv in range(2)]
        for kv in range(2):
            for g in range(2):
                self.load(SRC[kv], SRC[kv].ap[0:64, g * S:(g + 1) * S], src[kv][g])
        W1 = [A.alloc(32 * 128, BF16, "w1_%d" % kv) for kv in range(2)]
        W2 = [A.alloc(64, BF16, "w2_%d" % kv) for kv in range(2)]
        posT = A.alloc(2 * 32, F32, "posT")
        posTb = A.alloc(2 * 32, BF16, "posTb")
        self.load(posT, posT.ap[0:64, 0:64], self.din["c_posT"].ap()[j])
        self.copy("vector", posTb, posTb.ap[0:64, 0:64], posT, posT.ap[0:64, 0:64])
        i = 0
        for kv in range(2):
            w1 = self.din["nsa_cmp_w1"].ap()[j, kv].rearrange("(l d) h -> d l h", d=64)
            for l4 in range(4):
                st = stage[i % 2]
                i += 1
                self.load(st, st.ap[0:64, 0:1024].rearrange("p (l h) -> p l h", h=128), w1[:, l4 * 8:(l4 + 1) * 8, :])
                self.copy("vector", W1[kv], W1[kv].ap[0:64, l4 * 1024:(l4 + 1) * 1024], st, st.ap[0:64, 0:1024])
            st = stage[i % 2]
            i += 1
            self.load(st, st.ap[:, 0:64], self.din["nsa_cmp_w2"].ap()[j, kv])
            self.copy("vector", W2[kv], W2[kv].ap[:, 0:64], st, st.ap[:, 0:64])
        bias = A.alloc(8, F32, "cbias")
        H = [A.alloc(NCP, BF16, "H%d" % i) for i in range(2)]
        for i in range(2):
            self.memset("vector", H[i], H[i].ap, 0.0)
        ko = [A.alloc(NCP, BF16, "ko%d" % i) for i in range(2)]
        for i in range(2):
            self.memset("vector", ko[i], ko[i].ap, 0.0)
        vo = [A.alloc(64, BF16, "cvo%d" % i) for i in range(2)]
        for kv in range(2):
            pb = self.psum()
            for l in range(32):
                self.mm(pb, pb.ap[:, 0:1], W1[kv], W1[kv].ap[0:64, l * 128:(l + 1) * 128], posTb, posTb.ap[0:64, kv * 32 + l: kv * 32 + l + 1], l == 0, l == 31)
            self.copy("vector", bias, bias.ap[:, kv:kv + 1], pb, pb.ap[:, 0:1])
        n = 0
        for kv in range(2):
            for g in range(2):
                ps = self.psum()
                for l in range(32):
                    rhs = SRC[kv].ap[0:64, g * S + l: g * S + l + 16 * (NC - 1) + 1: 16]
                    self.mm(ps, ps.ap[:, 0:NC], W1[kv], W1[kv].ap[0:64, l * 128:(l + 1) * 128], SRC[kv], rhs, l == 0, l == 31)
                h_ = H[n % 2]
                self.act(h_, h_.ap[:, 0:NC], ps, ps.ap[:, 0:NC], AF.Silu, bias=bias.ap[:, kv:kv + 1], extra_reads=[bias])
                if kv == 0:
                    p2 = self.psum()
                    self.mm(p2, p2.ap[0:64, 0:NC], W2[0], W2[0].ap[:, 0:64], h_, h_.ap[:, 0:NC], True, True)
                    k_ = ko[g]
                    self.copy("vector", k_, k_.ap[0:64, 0:NC], p2, p2.ap[0:64, 0:NC])
                    self.store(kcmp[g], k_, k_.ap[0:64, :])
                else:
                    for ct in range(NCT):
                        p2 = self.psum()
                        self.mm(p2, p2.ap[:, 0:64], h_, h_.ap[:, ct * 128:(ct + 1) * 128], W2[1], W2[1].ap[:, 0:64], True, True)
                        v_ = vo[ct % 2]
                        self.copy("vector", v_, v_.ap[:, 0:64], p2, p2.ap[:, 0:64])
                        self.store(vcmp[g, ct * 128:(ct + 1) * 128, :], v_, v_.ap[:, 0:64])
                n += 1
        P.barrier()

    def combine(self, Oacc, gr, b, dst, first, rec_t, recB, tmp_t, bank_lo, bank_hi):
        P = self.P
        r_ = rec_t
        self.ts("vector", r_, r_.ap[64:65, :], Oacc, Oacc.ap[64:65, :], 1e-30, ALU.max)
        P.op("vector", lambda e, o=r_.ap[64:65, :]: e.reciprocal(out=o, in_=o), reads=[r_], writes=[r_])
        if gr is not None:
            self.tt("vector", r_, r_.ap[64:65, :], r_, r_.ap[64:65, :], gr, gr.ap[64:65, b * 512:(b + 1) * 512], ALU.mult)
        pb = self.psum(bank_lo, bank_hi)
        self.mm(pb, pb.ap[0:64, :], self.c_ones_f, self.c_ones_f.ap[64:65, 0:64], r_, r_.ap[64:65, :], True, True)
        self.copy("scalar", recB, recB.ap[0:64, :], pb, pb.ap[0:64, :])
        if first:
            self.tt("vector", dst, dst.ap[0:64, :], Oacc, Oacc.ap[0:64, :], recB, recB.ap[0:64, :], ALU.mult)
        else:
            self.tt("vector", tmp_t, tmp_t.ap[0:64, :], Oacc, Oacc.ap[0:64, :], recB, recB.ap[0:64, :], ALU.mult)
            self.tt("gpsimd", dst, dst.ap[0:64, :], dst, dst.ap[0:64, :], tmp_t, tmp_t.ap[0:64, :], ALU.add)

    def phase_odd_nsa(self, li):
        A, P, S, NT = self.A, self.P, self.S, self.NT
        A.reset(self.base_off)
        NB = S // 64
        NC, NCT, NCP = self.NC, self.NCT, self.NCP
        scr = self.scr
        nsq, ksT, kwT, vsv, vwv, gT = scr["nsq"].ap(), scr["ksT"].ap(), scr["kwT"].ap(), scr["vsv"].ap(), scr["vwv"].ap(), scr["gT"].ap()
        kcmp, vcmp, oT = scr["kcmp"].ap(), scr["vcmp"].ap(), scr["oT"].ap()
        sc = 0.125
        stage = [A.alloc(1024, F32, "wst%d" % i) for i in range(2)]
        KS = A.alloc(2 * S, BF16, "KS")
        VS = A.alloc(2 * NT * 65, BF16, "VS")
        KCM = A.alloc(2 * NCP, BF16, "KCM")
        VCM = A.alloc(2 * NCT * 65, BF16, "VCM")
        self.memset("gpsimd", VS, VS.ap, 1.0)
        self.memset("gpsimd", VCM, VCM.ap, 1.0)
        for g in range(2):
            self.load(KS, KS.ap[0:64, g * S:(g + 1) * S], ksT[g])
            self.load(KS, KS.ap[64:128, g * S:(g + 1) * S], ksT[g])
            self.load(VS, VS.ap.rearrange("p (g t c) -> p g t c", g=2, c=65)[:, g, :, 0:64], vsv[g].rearrange("(t p) d -> p t d", p=128))
            self.load(KCM, KCM.ap[0:64, g * NCP:(g + 1) * NCP], kcmp[g])
            self.load(KCM, KCM.ap[64:128, g * NCP:(g + 1) * NCP], kcmp[g])
            self.load(VCM, VCM.ap.rearrange("p (g t c) -> p g t c", g=2, c=65)[:, g, :, 0:64], vcmp[g].rearrange("(t p) d -> p t d", p=128))
        OV1 = A.alloc(NCT * (NB + 1), BF16, "OV1")
        st = stage[0]
        self.load(st, st.ap[:, 0:NCT * (NB + 1)], self.din["c_ov1"].ap())
        self.copy("vector", OV1, OV1.ap, st, st.ap[:, 0:NCT * (NB + 1)])
        EXP = A.alloc(S, BF16, "EXP")
        for i in range(S // 1024):
            st = stage[(i + 1) % 2]
            self.load(st, st.ap, self.din["c_expand"].ap()[:, i * 1024:(i + 1) * 1024])
            self.copy("vector", EXP, EXP.ap[:, i * 1024:(i + 1) * 1024], st, st.ap)
        MW = A.alloc(4 * 512, F32, "MW")
        MM = A.alloc(5 * 512, F32, "MM")
        self.load(MW, MW.ap, self.din["c_maskw"].ap())
        self.load(MM, MM.ap, self.din["c_maskm"].ap())
        TWW = 2 * (NT - 1) + NB
        TW1 = A.alloc(TWW, F32, "TW1")
        TW2 = A.alloc(TWW, F32, "TW2")
        self.load(TW1, TW1.ap, self.din["c_tw1"].ap())
        self.load(TW2, TW2.ap, self.din["c_tw2"].ap())
        Q8 = A.alloc(4 * 512, BF16, "Q8")
        GR = [A.alloc(3 * 512, F32, "GR%d" % i) for i in range(4)]
        KW = [A.alloc(2 * 1024, BF16, "KW%d" % i) for i in range(2)]
        VW = [A.alloc(2 * 8 * 65, BF16, "VW%d" % i) for i in range(2)]
        for i in range(2):
            self.memset("gpsimd", VW[i], VW[i].ap, 1.0)
        OACC = [A.alloc(512, F32, "OACC%d" % i) for i in range(4)]
        ET = [A.alloc(512, BF16, "ET%d" % i) for i in range(4)]
        pts = [A.alloc(1024, BF16, "pt%d" % i) for i in range(4)]
        ssb = [A.alloc(1024, F32, "ssb%d" % i) for i in range(2)]
        rec = [A.alloc(512, F32, "rec%d" % i) for i in range(2)]
        recB = A.alloc(512, F32, "recB")
        ctmp = A.alloc(512, F32, "ctmp")
        ACC = [A.alloc(4 * NB, F32, "ACC%d" % i) for i in range(2)]
        adj = [A.alloc(NB, F32, "adj%d" % i) for i in range(2)]
        tmpv = A.alloc(NB, F32, "tmpv")
        m8 = A.alloc(16, F32, "m8")
        selb = [A.alloc(NB, BF16, "selb%d" % i) for i in range(2)]
        selT = [A.alloc(512, BF16, "selT%d" % i) for i in range(2)]
        mks = [A.alloc(1024, BF16, "mk%d" % i) for i in range(2)]
        rI = A.alloc(8, F32, "rI")
        ob = [A.alloc(512, BF16, "ob%d" % i) for i in range(2)]
        npt = 0
        ncomb = 0
        nwin = 0
        for qc in range(self.NCH):
            t0 = qc * 512
            for two in range(2):
                self.load(Q8, Q8.ap[two * 64:(two + 1) * 64, :].rearrange("p (h t) -> p h t", h=4),
                          nsq[two::2, :, t0:t0 + 512].rearrange("h r t -> r h t"))
            kw, vw = KW[qc % 2], VW[qc % 2]
            lo = max(t0 - 512, 0)
            nwt = (t0 + 512 - lo) // 128
            slot0 = 8 - nwt
            for g in range(2):
                self.load(kw, kw.ap[0:64, g * 1024 + slot0 * 128:(g + 1) * 1024], kwT[g][:, lo:t0 + 512])
                self.load(kw, kw.ap[64:128, g * 1024 + slot0 * 128:(g + 1) * 1024], kwT[g][:, lo:t0 + 512])
                self.load(vw, vw.ap.rearrange("p (g t c) -> p g t c", g=2, c=65)[:, g, slot0:8, 0:64],
                          vwv[g][lo:t0 + 512, :].rearrange("(t p) d -> p t d", p=128))
            for g in range(2):
                nct = min(NCT, (32 * qc + 31 + 127) // 128)
                for hh in range(4):
                    h = 4 * g + hh
                    gr = GR[hh]
                    self.load(gr, gr.ap[64:65, :].rearrange("p (r t) -> p r t", r=3),
                              gT[3 * h:3 * h + 3, t0:t0 + 512].rearrange("(o r) t -> o r t", o=1))
                    r0 = (h % 2) * 64
                    qh = Q8.ap[r0:r0 + 64, (h // 2) * 512:(h // 2 + 1) * 512]
                    Oc = self.ps[4 + hh % 2]
                    for ct in range(nct):
                        ps = self.psum(0, 4)
                        self.mm(ps, ps.ap, KCM, KCM.ap[r0:r0 + 64, g * NCP + ct * 128: g * NCP + (ct + 1) * 128], Q8, qh, True, True)
                        Dv = 512 * qc - 2048 * ct
                        e_ = ET[ct]
                        if Dv <= 2048:
                            sb = ssb[ct % 2]
                            mi = Dv // 512
                            self.tt("vector", sb, sb.ap[:, 0:512], ps, ps.ap, MM, MM.ap[:, mi * 512:(mi + 1) * 512], ALU.min)
                            self.act(e_, e_.ap, sb, sb.ap[:, 0:512], AF.Exp, scale=sc)
                        else:
                            self.act(e_, e_.ap, ps, ps.ap, AF.Exp, scale=sc)
                        self.mm(Oc, Oc.ap[0:65, :], VCM, VCM.ap[:, (g * NCT + ct) * 65:(g * NCT + ct) * 65 + 65], e_, e_.ap, ct == 0, ct == nct - 1)
                    for s in range(4):
                        ib = self.ps[6 + s // 2]
                        for ct in range(nct):
                            self.mm(ib, ib.ap[:, (s % 2) * (NB + 1):(s % 2 + 1) * (NB + 1)], ET[ct], ET[ct].ap[:, s * 128:(s + 1) * 128],
                                    OV1, OV1.ap[:, ct * (NB + 1):(ct + 1) * (NB + 1)], ct == 0, ct == nct - 1)
                    for bnk in range(2):
                        ib = self.ps[6 + bnk]
                        v3 = ib.ap[:, 0:2 * (NB + 1)].rearrange("p (s n) -> p s n", s=2)
                        ri = rI.ap[:, bnk * 2:(bnk + 1) * 2]
                        self.ts("vector", rI, ri, ib, v3[:, :, NB], 1e-30, ALU.max)
                        P.op("vector", lambda e, o=ri: e.reciprocal(out=o, in_=o), reads=[rI], writes=[rI])
                        for s2 in range(2):
                            s = bnk * 2 + s2
                            dst = ACC[g].ap[:, s * NB:(s + 1) * NB]
                            if hh == 0:
                                self.ts("vector", ACC[g], dst, ib, v3[:, s2, 0:NB], rI.ap[:, s:s + 1], ALU.mult, extra_reads=[rI])
                            else:
                                self.stt(ACC[g], dst, ib, v3[:, s2, 0:NB], rI.ap[:, s:s + 1], ACC[g], dst, ALU.mult, ALU.add, extra_reads=[rI])
                    self.combine(Oc, gr, 0, OACC[hh], True, rec[ncomb % 2], recB, ctmp, 0, 4)
                    ncomb += 1
                psT = self.psum(0, 4)
                psTb = psT.ap.bitcast(BF16)
                for s in range(4):
                    gs = 4 * qc + s
                    off = 2 * (NT - 1) - 2 * gs
                    a_ = adj[s % 2]
                    self.tt("vector", a_, a_.ap, ACC[g], ACC[g].ap[:, s * NB:(s + 1) * NB], TW1, TW1.ap[:, off:off + NB], ALU.mult)
                    self.tt("vector", a_, a_.ap, a_, a_.ap, TW2, TW2.ap[:, off:off + NB], ALU.add)
                    self.memset("vector", a_, a_.ap[:, 0:1], 1.0e4)
                    P.op("vector", lambda e, o=m8.ap[:, 0:8], i=a_.ap: e.max(out=o, in_=i), reads=[a_], writes=[m8])
                    P.op("vector", lambda e, o=tmpv.ap, r=m8.ap[:, 0:8], i=a_.ap: e.match_replace(out=o, in_to_replace=r, in_values=i, imm_value=-2.0e30),
                         reads=[m8, a_], writes=[tmpv])
                    P.op("vector", lambda e, o=m8.ap[:, 8:16], i=tmpv.ap: e.max(out=o, in_=i), reads=[tmpv], writes=[m8])
                    sb_ = selb[s % 2]
                    self.ts("vector", sb_, sb_.ap, a_, a_.ap, m8.ap[:, 15:16], ALU.is_ge, extra_reads=[m8])
                    P.op("tensor", lambda e, o=psTb[0:NB, s * 128:(s + 1) * 128], i=sb_.ap[:, 0:NB]: e.transpose(o, i, self.c_ident_bf.ap[:, 0:128]),
                         reads=[sb_, self.c_ident_bf], writes=[psT])
                self.copy("scalar", selT[g], selT[g].ap[0:NB, :], psT, psTb[0:NB, 0:512])
                OS = [self.ps[4 + hh] for hh in range(4)]
                pend = []
                pairs = self.causal_pairs(qc)
                npr = len(pairs)

                def emit_pv(item):
                    tp_, kt_, pt_, first_, last_ = item
                    vap = VS.ap[:, (g * NT + kt_) * 65:(g * NT + kt_) * 65 + 65]
                    self.mm(OS[2 * tp_], OS[2 * tp_].ap[0:65, :], VS, vap, pt_, pt_.ap[:, 0:512], first_, last_)
                    self.mm(OS[2 * tp_ + 1], OS[2 * tp_ + 1].ap[0:65, :], VS, vap, pt_, pt_.ap[:, 512:1024], first_, last_)
                for pi, (kta, ktb, minfo) in enumerate(pairs):
                    b0 = 2 * (self.nsp % 2)
                    self.nsp += 1
                    ma, mb = self.ps[b0], self.ps[b0 + 1]
                    self.mm(ma, ma.ap, EXP, EXP.ap[0:NB, kta * 128:(kta + 1) * 128], selT[g], selT[g].ap[0:NB, :], True, True)
                    self.mm(mb, mb.ap, EXP, EXP.ap[0:NB, ktb * 128:(ktb + 1) * 128], selT[g], selT[g].ap[0:NB, :], True, True)
                    mk = mks[pi % 2]
                    P.op("scalar", lambda e, o=mk.ap, i=self.psall[:, b0 * 512:(b0 + 2) * 512]: e.copy(out=o, in_=i), reads=[ma, mb], writes=[mk])
                    for half, kt in ((0, kta), (1, ktb)):
                        for tp in range(2):
                            t = 2 * g + tp
                            b0 = 2 * (self.nsp % 2)
                            self.nsp += 1
                            pa, pb = self.ps[b0], self.ps[b0 + 1]
                            self.mm(pa, pa.ap, KS, KS.ap[0:64, g * S + kt * 128: g * S + (kt + 1) * 128], Q8, Q8.ap[0:64, t * 512:(t + 1) * 512], True, True)
                            self.mm(pb, pb.ap, KS, KS.ap[64:128, g * S + kt * 128: g * S + (kt + 1) * 128], Q8, Q8.ap[64:128, t * 512:(t + 1) * 512], True, True)
                            pair_ap = self.psall[:, b0 * 512:(b0 + 2) * 512]
                            pt = pts[npt % 4]
                            npt += 1
                            if minfo is not None:
                                sb = ssb[npt % 2]
                                o3 = sb.ap.rearrange("p (a c) -> p a c", a=2)
                                i3 = pair_ap.rearrange("p (a c) -> p a c", a=2)
                                m3 = minfo[1][:, half * 512:(half + 1) * 512].unsqueeze(1).to_broadcast([128, 2, 512])
                                P.op("vector", lambda e, o=o3, i0=i3, i1=m3: e.tensor_tensor(out=o, in0=i0, in1=i1, op=ALU.min),
                                     reads=[pa, pb, minfo[0]], writes=[sb])
                                P.op("scalar", lambda e, o=pt.ap, i=sb.ap: e.activation(out=o, in_=i, func=AF.Exp, scale=sc), reads=[sb], writes=[pt])
                            else:
                                P.op("scalar", lambda e, o=pt.ap, i=pair_ap: e.activation(out=o, in_=i, func=AF.Exp, scale=sc), reads=[pa, pb], writes=[pt])
                            p3 = pt.ap.rearrange("p (a c) -> p a c", a=2)
                            k3 = mk.ap[:, half * 512:(half + 1) * 512].unsqueeze(1).to_broadcast([128, 2, 512])
                            P.op("vector", lambda e, o=p3, i1=k3: e.tensor_tensor(out=o, in0=o, in1=i1, op=ALU.mult), reads=[pt, mk], writes=[pt])
                            pend.append((tp, kt, pt, pi == 0 and half == 0, pi == npr - 1 and half == 1))
                            if len(pend) > 1:
                                emit_pv(pend.pop(0))
                while pend:
                    emit_pv(pend.pop(0))
                for hh in range(4):
                    self.combine(OS[hh], GR[hh], 1, OACC[hh], False, rec[ncomb % 2], recB, ctmp, 0, 4)
                    ncomb += 1
                for tp in range(2):
                    t = 2 * g + tp
                    wl = []
                    if qc > 0:
                        for o in range(4):
                            wl.append((o, (MW, MW.ap[:, o * 512:(o + 1) * 512], "b")))
                    for o in range(4):
                        wl.append((4 + o, (self.c_maskc, self.c_maskc.ap[:, o * 512:(o + 1) * 512], "b")))
                    pb4 = 4 + 2 * (nwin % 2)
                    nwin += 1
                    Ow = (self.ps[pb4], self.ps[pb4 + 1])
                    self.attn_pass_dual(
                        ((kw, lambda sl, g=g: kw.ap[0:64, g * 1024 + sl * 128: g * 1024 + (sl + 1) * 128]),
                         (kw, lambda sl, g=g: kw.ap[64:128, g * 1024 + sl * 128: g * 1024 + (sl + 1) * 128])),
                        (Q8, Q8.ap[0:64, t * 512:(t + 1) * 512], Q8.ap[64:128, t * 512:(t + 1) * 512]),
                        vw, lambda sl, j_, g=g: vw.ap[:, (g * 8 + sl) * 65:(g * 8 + sl) * 65 + 65], 65, sc, Ow[0], Ow[1], wl,
                        ptiles=pts, ssb=ssb)
                    for two in range(2):
                        hh = 2 * tp + two
                        h = 4 * g + hh
                        self.combine(Ow[two], GR[hh], 2, OACC[hh], False, rec[ncomb % 2], recB, ctmp, 0, 4)
                        ncomb += 1
                        o_ = ob[hh % 2]
                        self.copy("scalar", o_, o_.ap[0:64, :], OACC[hh], OACC[hh].ap[0:64, :])
                        self.store(oT[h // 2, (h % 2) * 64:(h % 2) * 64 + 64, t0:t0 + 512], o_, o_.ap[0:64, :])
        P.barrier()

    def aug_head_chunk(self, kt_, v_, q_, rows, scale, qc, Oacc, pts, ssb, r_, recB, o_, dst_ap):
        P = self.P
        kl = self.causal_pairs(qc)
        self.attn_pass(kt_, lambda kt: kt_.ap[0:rows, kt * 128:(kt + 1) * 128], q_, q_.ap[0:rows, :],
                       v_, lambda kt: v_.ap[:, kt * 129: kt * 129 + 65], 65, scale, Oacc, kl, ptiles=pts, ssb=ssb)
        P.op("vector", lambda e, o=r_.ap[64:65, :], i=Oacc.ap[64:65, :]: e.reciprocal(out=o, in_=i), reads=[Oacc], writes=[r_])
        pb = self.ps[6]
        self.mm(pb, pb.ap[0:64, :], self.c_ones_f, self.c_ones_f.ap[64:65, 0:64], r_, r_.ap[64:65, :], True, True)
        self.copy("scalar", recB, recB.ap[0:64, :], pb, pb.ap[0:64, :])
        self.tt("vector", o_, o_.ap[0:64, :], Oacc, Oacc.ap[0:64, :], recB, recB.ap[0:64, :], ALU.mult)
        self.store(dst_ap, o_, o_.ap[0:64, :])

    def phase_odd_mla(self, li):
        A, P, S, NT = self.A, self.P, self.S, self.NT
        A.reset(self.base_off)
        mq, mk, mv, oT = self.scr["mq"].ap(), self.scr["mk"].ap(), self.scr["mv"].ap(), self.scr["oT"].ap()
        KT = [A.alloc(S, BF16, "KT%d" % i) for i in range(2)]
        V = [A.alloc(NT * 129, BF16, "V%d" % i) for i in range(2)]
        Q = [A.alloc(512, BF16, "Q%d" % i) for i in range(2)]
        pts = [A.alloc(1024, BF16, "pt%d" % i) for i in range(4)]
        ssb = [A.alloc(1024, F32, "ssb%d" % i) for i in range(2)]
        rec = A.alloc(512, F32, "rec")
        recB = A.alloc(512, F32, "recB")
        ob = [A.alloc(512, BF16, "ob%d" % i) for i in range(2)]
        for i in range(2):
            self.memset("gpsimd", V[i], V[i].ap, 1.0)

        def load_unit(h, slot):
            self.load(KT[slot], KT[slot].ap[0:96, :], mk[h])
            self.load(V[slot], V[slot].ap.rearrange("p (t c) -> p t c", c=129)[:, :, 0:64], mv[h].rearrange("(t p) d -> p t d", p=128))
        load_unit(0, 0)
        n = 0
        for h in range(8):
            slot = h % 2
            if h + 1 < 8:
                load_unit(h + 1, 1 - slot)
            for qc in range(self.NCH):
                q_ = Q[n % 2]
                t0 = qc * 512
                self.load(q_, q_.ap[0:96, :], mq[h, :, t0:t0 + 512])
                self.aug_head_chunk(KT[slot], V[slot], q_, 96, 96.0 ** -0.5, qc, self.ps[4 + n % 2], pts, ssb, rec, recB, ob[n % 2],
                                    oT[4 + h // 2, (h % 2) * 64:(h % 2) * 64 + 64, t0:t0 + 512])
                n += 1
        P.barrier()

    def build(self):
        S = self.S
        nc = self.nc
        self.inp("x", [S, D])
        self.inp("mem", [MEM, D])
        self.inp("mem_norm", [D])
        for nm in ("norm_mix", "norm_mem", "norm_ffn"):
            self.inp(nm, [DEPTH, D])
        self.inp("ev_w_in", [2, D, EV_W])
        self.inp("ev_b_f", [2, 8])
        self.inp("ev_lam", [2, 4, 64])
        self.inp("ev_subln", [2, 128])
        self.inp("ev_w_out", [2, D, D])
        self.inp("od_w_in", [2, D, OD_W])
        self.inp("nsa_cmp_pos", [2, 2, 32, 64])
        self.inp("nsa_cmp_w1", [2, 2, 2048, 128])
        self.inp("nsa_cmp_w2", [2, 2, 128, 64])
        self.inp("mla_q_norm", [2, 384])
        self.inp("mla_kv_norm", [2, 256])
        self.inp("mla_w_uq", [2, 384, 768])
        self.inp("mla_w_ukv", [2, 256, 1024])
        self.inp("od_w_out", [2, D, D])
        self.inp("xa_wq", [DEPTH, D, 512])
        self.inp("xa_wkv", [DEPTH, D, D])
        self.inp("xa_wo", [DEPTH, 512, D])
        self.inp("ffn_w13", [DEPTH, D, 2 * DFF])
        self.inp("ffn_w2", [DEPTH, DFF, D])
        self.inp("final_norm", [D])
        self.inp("c_gains", [128, (3 * DEPTH + 1) * KC])
        self.inp("c_ident", [128, 128])
        self.inp("c_maskc", [128, 4 * 512])
        self.inp("c_cos64", [128, S])
        self.inp("c_sin64", [128, S])
        NB = S // 64
        self.inp("c_cosm", [96, S])
        self.inp("c_sinm", [96, S])
        self.inp("c_mlan", [2, 128, 5])
        self.inp("c_posT", [2, 64, 64])
        self.inp("c_ov1", [128, self.NCT * (NB + 1)])
        self.inp("c_expand", [128, S])
        self.inp("c_maskw", [128, 4 * 512])
        self.inp("c_maskm", [128, 5 * 512])
        self.inp("c_tw1", [128, 2 * (self.NT - 1) + NB])
        self.inp("c_tw2", [128, 2 * (self.NT - 1) + NB])
        self.dout = nc.dram_tensor("out", [S, D], F32, kind="ExternalOutput")
        self.scratch("xT", [KC, 128, S], F32)
        self.scratch("oT", [KC, 128, S], BF16)
        self.scratch("dq", [4, 128, S], BF16)
        self.scratch("dk", [4, 128, S], BF16)
        self.scratch("dv", [4, S, 128], BF16)
        self.scratch("fq", [8, 70, S], BF16)
        self.scratch("fk", [8, 70, S], BF16)
        self.scratch("fv", [8, S, 64], BF16)
        self.scratch("lfp", [8, S], F32)
        self.scratch("nsq", [8, 64, S], BF16)
        for nm in ("kcT", "vcT", "ksT", "kwT"):
            self.scratch(nm, [2, 64, S], BF16)
        self.scratch("vsv", [2, S, 64], BF16)
        self.scratch("vwv", [2, S, 64], BF16)
        self.scratch("gT", [24, S], F32)
        self.scratch("mq", [8, 96, S], BF16)
        self.scratch("mk", [8, 96, S], BF16)
        self.scratch("mv", [8, S, 64], BF16)
        self.scratch("kcmp", [2, 64, self.NCP], BF16)
        self.scratch("vcmp", [2, self.NCP, 64], BF16)

        A = self.A
        self.c_eps = A.alloc(8, F32, "eps")
        self.c_one = A.alloc(8, F32, "one")
        self.memset("vector", self.c_eps, self.c_eps.ap, EPS)
        self.memset("vector", self.c_one, self.c_one.ap, 1.0)
        self.c_memT = A.alloc(KC * MEM, BF16, "memT")
        self.setup_consts()
        self.phase_init()
        for li in self.layers:
            if li % 2 == 0:
                self.phase_even_in(li)
                self.phase_even_scan(li)
                self.phase_even_attn(li)
                self.phase_out_xattn(li, self.din["ev_w_out"].ap()[li // 2])
            else:
                self.phase_odd_in(li)
                self.phase_odd_cmp(li)
                self.phase_odd_nsa(li)
                self.phase_odd_mla(li)
                self.phase_out_xattn(li, self.din["od_w_out"].ap()[li // 2])
            self.phase_ffn(li)
        self.phase_final()
        self.P.emit()
        return nc


def ones_f_ap(b):
    return b.c_ones_f.ap


def host_gains(inputs):
    cols = []
    for nm in ("norm_mix", "norm_mem", "norm_ffn"):
        g = np.asarray(inputs[nm], dtype=np.float32)
        for li in range(DEPTH):
            cols.append(g[li].reshape(KC, 128).T)
    cols.append(np.asarray(inputs["final_norm"], dtype=np.float32).reshape(KC, 128).T)
    return np.ascontiguousarray(np.concatenate(cols, axis=1))


def host_consts(S):
    c = {}
    c["c_ident"] = np.eye(128, dtype=np.float32)
    m = np.zeros((128, 4, 512), np.float32)
    p = np.arange(128)[:, None]
    q = np.arange(512)[None, :]
    for o in range(4):
        m[:, o, :] = np.where(o * 128 + p <= q, BIG, -BIG)
    c["c_maskc"] = m.reshape(128, 2048)
    cs, sn = rope_tables(S, 128, 32, 2)
    c["c_cos64"] = cs
    c["c_sin64"] = sn
    c32, s32 = rope_tables(S, 32, 16, 1)
    c["c_cosm"] = np.ascontiguousarray(np.concatenate([np.ones((64, S), np.float32), c32], axis=0))
    c["c_sinm"] = np.ascontiguousarray(np.concatenate([np.zeros((64, S), np.float32), s32], axis=0))
    NB = S // 64
    NT = S // 128
    NC = (S - 32) // 16 + 1
    NCT = (NC + 127) // 128
    cidx = np.arange(NCT * 128)
    ov = ((cidx[:, None] * 16 < np.arange(NB)[None, :] * 64 + 64) & (cidx[:, None] * 16 + 32 > np.arange(NB)[None, :] * 64)
          & (cidx[:, None] < NC)).astype(np.float32)
    ov1 = np.concatenate([ov, np.ones((NCT * 128, 1), np.float32)], axis=1)
    c["c_ov1"] = np.ascontiguousarray(ov1.reshape(NCT, 128, NB + 1).transpose(1, 0, 2).reshape(128, NCT * (NB + 1)))
    c["c_expand"] = (np.arange(128)[:, None] == (np.arange(S)[None, :] // 64)).astype(np.float32)
    mw = np.zeros((128, 4, 512), np.float32)
    for o in range(4):
        mw[:, o, :] = np.where(q < 128 * o + p, BIG, -BIG)
    c["c_maskw"] = mw.reshape(128, 2048)
    mm_ = np.zeros((128, 5, 512), np.float32)
    for i in range(5):
        mm_[:, i, :] = np.where(16 * p + 31 - q <= 512 * i, BIG, -BIG)
    c["c_maskm"] = mm_.reshape(128, 2560)
    TWW = 2 * (NT - 1) + NB
    RO = 2 * (NT - 1)
    r = np.arange(TWW)[None, :] - RO
    e = (np.arange(128)[:, None] >= 64).astype(np.int64)
    fut = r > e
    forced = (r == e) | (r == e - 1)
    c["c_tw1"] = np.where(fut | forced, 0.0, 1.0).astype(np.float32)
    c["c_tw2"] = np.where(fut, -BIG, np.where(forced, 1.0e4, 0.0)).astype(np.float32)
    return c


def host_layouts(inputs):
    c = {}
    qn = np.asarray(inputs["mla_q_norm"], dtype=np.float32)
    kn = np.asarray(inputs["mla_kv_norm"], dtype=np.float32)
    ml = np.zeros((2, 128, 5), np.float32)
    for j in range(2):
        ml[j, :, 0:3] = qn[j].reshape(3, 128).T
        ml[j, :, 3:5] = kn[j].reshape(2, 128).T
    c["c_mlan"] = ml
    pos = np.asarray(inputs["nsa_cmp_pos"], dtype=np.float32)
    c["c_posT"] = np.ascontiguousarray(pos.transpose(0, 3, 1, 2).reshape(2, 64, 64))
    return c


_CACHE = {}


def run(inputs, S, layers, n_cores, batch_ids):
    key = (S, tuple(layers))
    if key not in _CACHE:
        import time as _t
        _t0 = _t.time()
        _b = Builder(S, layers)
        _CACHE[key] = _b.build()
        print("[build] %.1fs ninst=%d" % (_t.time() - _t0, _b.P.ninst), {e: len(v) for e, v in _b.P.q.items()}, flush=True)
    nc = _CACHE[key]
    consts = host_consts(S)
    consts["c_gains"] = host_gains(inputs)
    consts.update(host_layouts(inputs))
    in_maps = []
    for b in batch_ids:
        m = {}
        for k, v in inputs.items():
            v = np.asarray(v)
            if k == "x" or k == "mem":
                m[k] = np.ascontiguousarray(v[b], dtype=np.float32)
            else:
                m[k] = np.ascontiguousarray(v, dtype=np.float32)
        m.update(consts)
        in_maps.append(m)
    import time as _t
    _t0 = _t.time()
    res = run_bass_kernel_spmd(nc, in_maps, core_ids=list(range(n_cores)))
    print("[run] spmd launch+compile %.1fs" % (_t.time() - _t0), flush=True)
    return [r["out"] for r in res.results]


def kernel(**inputs):
    x = np.asarray(inputs["x"])
    B, S, _ = x.shape
    outs = run(inputs, S, list(range(DEPTH)), 8, [0, 1, 2, 3, 0, 1, 2, 3])
    return np.stack(outs[:4], axis=0).astype(np.float32)
```

```python
import math
import numpy as np
import ml_dtypes
import concourse.bass as bass
import concourse.mybir as mybir
from concourse.bass_utils import run_bass_kernel_spmd

F32 = mybir.dt.float32
BF16 = mybir.dt.bfloat16
AF = mybir.ActivationFunctionType
ALU = mybir.AluOpType

D = 1024
KC = 8
DEPTH = 4
MEM = 256
EPS = 1e-6
THETA = 10000.0
DFF = 2816
EV_W = 3080
OD_W = 1976
BIG = 1.0e30

ENGS = ("sync", "scalar", "vector", "gpsimd", "tensor")


class Buf:
    __slots__ = ("name", "w", "r")

    def __init__(self, name):
        self.name = name
        self.w = None
        self.r = []


class Tile:
    __slots__ = ("ap", "buf")

    def __init__(self, ap, name="t"):
        self.ap = ap
        self.buf = Buf(name)

    def __getitem__(self, k):
        return self.ap[k]


class Prog:
    NDMA = 12

    def __init__(self, nc):
        self.nc = nc
        self.q = {e: [] for e in ENGS}
        self.cnt = {e: 0 for e in ENGS}
        self.sem = {e: nc.alloc_semaphore("es_" + e) for e in ENGS}
        self.semid = {e: ("E", e) for e in ENGS}
        self.waited = {e: {} for e in ENGS}
        self.dsem = {}
        self.dval = {}
        self.dnext = {}
        for qn in ("sync", "gpsimd"):
            self.dsem[qn] = [nc.alloc_semaphore("ds_%s_%d" % (qn, i)) for i in range(self.NDMA)]
            self.dval[qn] = [0] * self.NDMA
            self.dnext[qn] = 0
        self.ninst = 0

    def _need(self, eng, tok, kind):
        key, sh, val, teng = tok
        if teng == eng:
            if eng == "tensor" and kind == "waw":
                return
        if self.waited[eng].get(key, 0) >= val:
            return
        self.waited[eng][key] = val
        self.q[eng].append(lambda e, sh=sh, val=val: e.wait_ge(sh, val))

    def _deps(self, eng, reads, writes):
        for t in reads:
            b = t.buf
            if b.w is not None:
                self._need(eng, b.w, "raw")
        for t in writes:
            b = t.buf
            if b.w is not None:
                self._need(eng, b.w, "waw")
            for tok in b.r:
                self._need(eng, tok, "war")

    def _mark(self, tok, reads, writes):
        for t in reads:
            t.buf.r.append(tok)
        for t in writes:
            t.buf.w = tok
            t.buf.r = []

    def op(self, eng, fn, reads=(), writes=()):
        self._deps(eng, reads, writes)
        self.cnt[eng] += 1
        sh = self.sem[eng]
        self.q[eng].append(lambda e, fn=fn, sh=sh: fn(e).then_inc(sh, 1))
        tok = (self.semid[eng], sh, self.cnt[eng], eng)
        self._mark(tok, reads, writes)
        self.ninst += 1

    def dma(self, qn, out, in_, reads=(), writes=()):
        self._deps(qn, reads, writes)
        i = self.dnext[qn]
        self.dnext[qn] = (i + 1) % self.NDMA
        sh = self.dsem[qn][i]
        key = ("D", qn, i)
        prev = self.dval[qn][i]
        if prev > 0 and self.waited[qn].get(key, 0) < prev:
            self.waited[qn][key] = prev
            self.q[qn].append(lambda e, sh=sh, prev=prev: e.wait_ge(sh, prev))
        val = prev + 16
        self.dval[qn][i] = val
        self.q[qn].append(lambda e, out=out, in_=in_, sh=sh: e.dma_start(out=out, in_=in_).then_inc(sh, 16))
        tok = (key, sh, val, None)
        self._mark(tok, reads, writes)
        self.ninst += 1

    def barrier(self):
        toks = []
        for e in ENGS:
            if self.cnt[e] > 0:
                toks.append((self.semid[e], self.sem[e], self.cnt[e], e))
        for qn in self.dsem:
            for i in range(self.NDMA):
                if self.dval[qn][i] > 0:
                    toks.append((("D", qn, i), self.dsem[qn][i], self.dval[qn][i], None))
        for e in ENGS:
            for tok in toks:
                if tok[3] == e:
                    continue
                self._need(e, tok, "raw")

    def emit(self):
        nc = self.nc
        with nc.Block() as block:
            for e in ENGS:
                lst = self.q[e]

                def body(eng, lst=lst):
                    for f in lst:
                        f(eng)
                getattr(block, e)(body)


class Arena:
    def __init__(self, nc, nbytes):
        self.t = nc.alloc_sbuf_tensor("arena", [128, nbytes // 4], F32)
        self.cap = nbytes // 4
        self.off = 0

    def reset(self, off=0):
        self.off = off

    def alloc(self, free_elems, dtype, name="t"):
        nb = free_elems * (2 if dtype == BF16 else 4)
        nw = (nb + 3) // 4
        nw = (nw + 7) // 8 * 8
        assert self.off + nw <= self.cap, ("SBUF arena overflow", name, self.off, nw, self.cap)
        ap = self.t[:, self.off:self.off + nw]
        self.off += nw
        if dtype == BF16:
            ap = ap.bitcast(BF16)[:, 0:free_elems]
        else:
            ap = ap[:, 0:free_elems]
        return Tile(ap, name)


def rope_tables(S, rows, half, pairs_per):
    inv = (1.0 / (np.float32(THETA) ** (np.arange(half, dtype=np.float32) / np.float32(half)))).astype(np.float32)
    pos = np.arange(S, dtype=np.float32)
    ang = (pos[None, :] * inv[:, None]).astype(np.float32)
    c = np.cos(ang).astype(np.float32)
    s = np.sin(ang).astype(np.float32)
    blk_c = np.concatenate([c, c], axis=0)
    blk_s = np.concatenate([-s, s], axis=0)
    reps = rows // (2 * half)
    return np.tile(blk_c, (reps, 1)), np.tile(blk_s, (reps, 1))


class Builder:
    def __init__(self, S, layers, with_final=True):
        self.S = S
        self.NCH = S // 512
        self.NT = S // 128
        self.layers = layers
        self.with_final = with_final
        nc = bass.Bass("TRN2", target_bir_lowering=False)
        self.nc = nc
        self.P = Prog(nc)
        self.A = Arena(nc, 206 * 1024)
        self.psall = nc.alloc_psum_tensor("psall", [128, 8 * 512], F32).ap()
        self.ps = [Tile(self.psall[:, i * 512:(i + 1) * 512], "ps%d" % i) for i in range(8)]
        self.nsp = 0
        self.deferred = []
        self.pass_id = 0
        self.psn = 0
        self.npt = 0
        self.NC = (S - 32) // 16 + 1
        self.NCT = (self.NC + 127) // 128
        self.NCP = self.NCT * 128
        self.din = {}
        self.scr = {}

    def inp(self, name, shape, dt=F32):
        t = self.nc.dram_tensor(name, list(shape), dt, kind="ExternalInput")
        self.din[name] = t
        return t

    def scratch(self, name, shape, dt):
        t = self.nc.dram_tensor(name, list(shape), dt)
        self.scr[name] = t
        return t

    def psum(self, lo=0, hi=8):
        n = hi - lo
        i = lo + (self.psn % n)
        self.psn += 1
        return self.ps[i]

    def defer(self, steps, fn):
        self.deferred.append([steps, fn, self.pass_id])

    def begin_pass(self):
        self.pass_id += 1
        run = [it for it in self.deferred if it[2] <= self.pass_id - 2]
        self.deferred = [it for it in self.deferred if it[2] > self.pass_id - 2]
        for it in run:
            it[1]()

    def tick(self):
        if not self.deferred:
            return
        run = []
        keep = []
        for it in self.deferred:
            it[0] -= 1
            (run if it[0] <= 0 else keep).append(it)
        self.deferred = keep
        for it in run:
            it[1]()

    def flush_deferred(self):
        while self.deferred:
            d = self.deferred
            self.deferred = []
            for it in d:
                it[1]()

    def mm(self, out_t, out_ap, lhsT_t, lhsT_ap, rhs_t, rhs_ap, start, stop):
        self.P.op("tensor",
                  lambda e: e.matmul(out_ap, lhsT=lhsT_ap, rhs=rhs_ap, start=start, stop=stop),
                  reads=[lhsT_t, rhs_t], writes=[out_t])

    def act(self, out_t, out_ap, in_t, in_ap, func, scale=1.0, bias=0.0, extra_reads=()):
        self.P.op("scalar",
                  lambda e: e.activation(out=out_ap, in_=in_ap, func=func, bias=bias, scale=scale),
                  reads=[in_t] + list(extra_reads), writes=[out_t])

    def tt(self, eng, out_t, out_ap, a_t, a_ap, b_t, b_ap, op):
        self.P.op(eng, lambda e: e.tensor_tensor(out=out_ap, in0=a_ap, in1=b_ap, op=op),
                  reads=[a_t, b_t], writes=[out_t])

    def ts(self, eng, out_t, out_ap, a_t, a_ap, s1, op0, s2=None, op1=None, extra_reads=()):
        if op1 is None:
            self.P.op(eng, lambda e: e.tensor_scalar(out=out_ap, in0=a_ap, scalar1=s1, scalar2=None, op0=op0),
                      reads=[a_t] + list(extra_reads), writes=[out_t])
        else:
            self.P.op(eng, lambda e: e.tensor_scalar(out=out_ap, in0=a_ap, scalar1=s1, scalar2=s2, op0=op0, op1=op1),
                      reads=[a_t] + list(extra_reads), writes=[out_t])

    def stt(self, out_t, out_ap, a_t, a_ap, scalar, b_t, b_ap, op0, op1, extra_reads=()):
        self.P.op("vector",
                  lambda e: e.scalar_tensor_tensor(out=out_ap, in0=a_ap, scalar=scalar, in1=b_ap, op0=op0, op1=op1),
                  reads=[a_t, b_t] + list(extra_reads), writes=[out_t])

    def copy(self, eng, out_t, out_ap, in_t, in_ap):
        if eng == "scalar":
            self.P.op("scalar", lambda e: e.copy(out=out_ap, in_=in_ap), reads=[in_t], writes=[out_t])
        else:
            self.P.op(eng, lambda e: e.tensor_copy(out=out_ap, in_=in_ap), reads=[in_t], writes=[out_t])

    def memset(self, eng, t, ap, val):
        self.P.op(eng, lambda e: e.memset(ap, val), reads=[], writes=[t])

    def load(self, t, ap, src):
        self.P.dma("sync", ap, src, reads=[], writes=[t])

    def store(self, dst, t, ap):
        self.P.dma("gpsimd", dst, ap, reads=[t], writes=[])

    def setup_consts(self):
        A = self.A
        S = self.S
        self.c_ones_bf = A.alloc(128, BF16, "ones_bf")
        self.c_ones_f = A.alloc(128, F32, "ones_f")
        self.c_ident_f = A.alloc(128, F32, "ident_f")
        self.c_ident_bf = A.alloc(128, BF16, "ident_bf")
        self.c_maskc = A.alloc(4 * 512, F32, "maskc")
        self.memset("vector", self.c_ones_bf, self.c_ones_bf.ap, 1.0)
        self.memset("vector", self.c_ones_f, self.c_ones_f.ap, 1.0)
        self.load(self.c_ident_f, self.c_ident_f.ap, self.din["c_ident"].ap())
        self.copy("vector", self.c_ident_bf, self.c_ident_bf.ap, self.c_ident_f, self.c_ident_f.ap)
        self.load(self.c_maskc, self.c_maskc.ap, self.din["c_maskc"].ap())
        self.c_gain = A.alloc(3 * DEPTH * KC + 2 * KC, F32, "gains")
        self.load(self.c_gain, self.c_gain.ap[:, 0:(3 * DEPTH + 1) * KC], self.din["c_gains"].ap())
        self.base_off = A.off

    def gain(self, kind, li):
        o = (kind * DEPTH + li) * KC
        return self.c_gain.ap[:, o:o + KC]

    def rmsnorm_fm(self, X, nk, ntok, gain_ap, hT, sq, rstd, nfeat, gain_t=None):
        gt = gain_t if gain_t is not None else self.c_gain
        self.P.op("scalar", lambda e: e.activation(out=sq.ap[:, 0:nk * ntok], in_=X.ap[:, 0:nk * ntok], func=AF.Square),
                  reads=[X], writes=[sq])
        ps = self.psum()
        for k in range(nk):
            self.mm(ps, ps.ap[:, 0:ntok], self.c_ones_bf, self.c_ones_bf.ap[:, 0:128], sq, sq.ap[:, k * ntok:(k + 1) * ntok],
                    start=(k == 0), stop=(k == nk - 1))
        self.act(rstd, rstd.ap[:, 0:ntok], ps, ps.ap[:, 0:ntok], AF.Ln, scale=1.0 / nfeat, bias=self.c_eps.ap[:, 0:1],
                 extra_reads=[self.c_eps])
        self.act(rstd, rstd.ap[:, 0:ntok], rstd, rstd.ap[:, 0:ntok], AF.Exp, scale=-0.5)
        for k in range(nk):
            self.stt(hT, hT.ap[:, k * ntok:(k + 1) * ntok], X, X.ap[:, k * ntok:(k + 1) * ntok], gain_ap[:, k:k + 1],
                     rstd, rstd.ap[:, 0:ntok], ALU.mult, ALU.mult, extra_reads=[gt])

    def load_weight(self, W, wcols, src_ap, nk, ncols, col_off=0, stage=None):
        i = 0
        c0 = 0
        while c0 < ncols:
            cw = min(1024, ncols - c0)
            for k in range(nk):
                st = stage[i % 2]
                i += 1
                self.load(st, st.ap[:, 0:cw], src_ap[k * 128:(k + 1) * 128, c0:c0 + cw])
                dst = W.ap[:, k * wcols + col_off + c0: k * wcols + col_off + c0 + cw]
                eng = ("vector", "gpsimd")[i % 2]
                self.copy(eng, W, dst, st, st.ap[:, 0:cw])
            c0 += cw

    def phase_init(self):
        A, P, S = self.A, self.P, self.S
        A.reset(self.base_off)
        xin = self.din["x"].ap()
        xT = self.scr["xT"].ap()
        xa = [A.alloc(D, F32, "xin%d" % i) for i in range(2)]
        st = [A.alloc(KC * 512, F32, "xst%d" % i) for i in range(2)]
        for c in range(self.NCH):
            stg = st[c % 2]
            for s in range(4):
                t = c * 4 + s
                xt = xa[t % 2]
                self.load(xt, xt.ap, xin[t * 128:(t + 1) * 128, :])
                for kk in range(2):
                    ps = self.psum()
                    for j in range(4):
                        k = kk * 4 + j
                        P.op("tensor", lambda e, o=ps.ap[:, j * 128:(j + 1) * 128], i=xt.ap[:, k * 128:(k + 1) * 128]:
                             e.transpose(o, i, self.c_ident_f.ap[:, 0:128]), reads=[xt, self.c_ident_f], writes=[ps])
                    dst = stg.ap.rearrange("p (k t) -> p k t", k=KC)[:, kk * 4:(kk + 1) * 4, s * 128:(s + 1) * 128]
                    src = ps.ap.rearrange("p (j t) -> p j t", j=4)
                    self.copy(("vector", "scalar")[kk], stg, dst, ps, src)
            self.store(xT[:, :, c * 512:(c + 1) * 512].rearrange("k p t -> p k t"), stg,
                       stg.ap.rearrange("p (k t) -> p k t", k=KC))
        mem = self.din["mem"].ap()
        gB = A.alloc(D, F32, "memgain")
        self.load(gB, gB.ap, self.din["mem_norm"].ap().partition_broadcast(128))
        for s in range(2):
            mt = xa[s]
            self.load(mt, mt.ap, mem[s * 128:(s + 1) * 128, :])
            sqj = st[0]
            ss = A.alloc(8, F32, "memss%d" % s)
            P.op("scalar", lambda e, o=sqj.ap[:, 0:D], i=mt.ap, a=ss.ap[:, 0:1]: e.activation(out=o, in_=i, func=AF.Square, accum_out=a),
                 reads=[mt], writes=[sqj, ss])
            self.act(ss, ss.ap[:, 1:2], ss, ss.ap[:, 0:1], AF.Ln, scale=1.0 / D, bias=self.c_eps.ap[:, 0:1], extra_reads=[self.c_eps])
            self.act(ss, ss.ap[:, 2:3], ss, ss.ap[:, 1:2], AF.Exp, scale=-0.5)
            mn = st[1]
            self.stt(mn, mn.ap[:, 0:D], mt, mt.ap, ss.ap[:, 2:3], gB, gB.ap, ALU.mult, ALU.mult, extra_reads=[ss])
            for kk in range(2):
                ps = self.psum()
                for j in range(4):
                    k = kk * 4 + j
                    P.op("tensor", lambda e, o=ps.ap[:, j * 128:(j + 1) * 128], i=mn.ap[:, k * 128:(k + 1) * 128]:
                         e.transpose(o, i, self.c_ident_f.ap[:, 0:128]), reads=[mn, self.c_ident_f], writes=[ps])
                dst = self.c_memT.ap.rearrange("p (k t) -> p k t", k=KC)[:, kk * 4:(kk + 1) * 4, s * 128:(s + 1) * 128]
                self.copy("vector", self.c_memT, dst, ps, ps.ap.rearrange("p (j t) -> p j t", j=4))
        self.flush_deferred()
        P.barrier()

    def phase_final(self):
        A, P, S = self.A, self.P, self.S
        A.reset(self.base_off)
        xT = self.scr["xT"].ap()
        out = self.dout.ap()
        X = [A.alloc(KC * 512, F32, "fx%d" % i) for i in range(2)]
        Y = A.alloc(KC * 512, F32, "fy")
        sq = A.alloc(KC * 512, BF16, "fsq")
        rstd = A.alloc(512, F32, "frstd")
        ot = [A.alloc(D, F32, "fo%d" % i) for i in range(2)]
        g = self.c_gain.ap[:, 3 * DEPTH * KC: 3 * DEPTH * KC + KC]
        for c in range(self.NCH):
            x = X[c % 2]
            self.load(x, x.ap.rearrange("p (k t) -> p k t", k=KC), xT[:, :, c * 512:(c + 1) * 512].rearrange("k p t -> p k t"))
            self.rmsnorm_fm(x, KC, 512, g, Y, sq, rstd, D)
            for s in range(4):
                o = ot[s % 2]
                for kk in range(2):
                    ps = self.psum()
                    for j in range(4):
                        k = kk * 4 + j
                        P.op("tensor", lambda e, oo=ps.ap[:, j * 128:(j + 1) * 128], i=Y.ap[:, k * 512 + s * 128: k * 512 + (s + 1) * 128]:
                             e.transpose(oo, i, self.c_ident_f.ap[:, 0:128]), reads=[Y, self.c_ident_f], writes=[ps])
                    self.copy(("vector", "scalar")[kk], o, o.ap[:, kk * 512:(kk + 1) * 512], ps, ps.ap)
                t = c * 4 + s
                self.store(out[t * 128:(t + 1) * 128, :], o, o.ap)
        self.flush_deferred()
        P.barrier()

    def phase_ffn(self, li):
        A, P, S = self.A, self.P, self.S
        A.reset(self.base_off)
        NTK = 256
        NF = DFF // 128
        xT = self.scr["xT"].ap()
        W13 = A.alloc(KC * 2 * DFF, BF16, "w13")
        W2 = A.alloc(NF * D, BF16, "w2")
        stage = [A.alloc(1024, F32, "wst%d" % i) for i in range(2)]
        self.load_weight(W13, 2 * DFF, self.din["ffn_w13"].ap()[li], KC, 2 * DFF, stage=stage)
        self.load_weight(W2, D, self.din["ffn_w2"].ap()[li], NF, D, stage=stage)
        X = [A.alloc(KC * NTK, F32, "x%d" % i) for i in range(2)]
        hT = A.alloc(KC * NTK, BF16, "hT")
        sq = A.alloc(KC * NTK, BF16, "sq")
        rstd = A.alloc(NTK, F32, "rstd")
        act = A.alloc(NF * NTK, BF16, "act")
        sg = [A.alloc(NTK, F32, "sg%d" % i) for i in range(2)]
        g = self.gain(2, li)
        for c in range(S // NTK):
            x = X[c % 2]
            t0 = c * NTK
            self.load(x, x.ap.rearrange("p (k t) -> p k t", k=KC), xT[:, :, t0:t0 + NTK].rearrange("k p t -> p k t"))
            self.rmsnorm_fm(x, KC, NTK, g, hT, sq, rstd, D)
            for f in range(NF):
                pg = self.psum()
                for k in range(KC):
                    self.mm(pg, pg.ap[:, 0:NTK], W13, W13.ap[:, k * 2 * DFF + f * 128: k * 2 * DFF + (f + 1) * 128],
                            hT, hT.ap[:, k * NTK:(k + 1) * NTK], k == 0, k == KC - 1)
                pu = self.psum()
                for k in range(KC):
                    self.mm(pu, pu.ap[:, 0:NTK], W13, W13.ap[:, k * 2 * DFF + DFF + f * 128: k * 2 * DFF + DFF + (f + 1) * 128],
                            hT, hT.ap[:, k * NTK:(k + 1) * NTK], k == 0, k == KC - 1)
                s_ = sg[f % 2]
                self.act(s_, s_.ap[:, 0:NTK], pg, pg.ap[:, 0:NTK], AF.Silu)
                self.tt("vector", act, act.ap[:, f * NTK:(f + 1) * NTK], pu, pu.ap[:, 0:NTK], s_, s_.ap[:, 0:NTK], ALU.mult)
            for n in range(KC):
                po = self.psum()
                for f in range(NF):
                    self.mm(po, po.ap[:, 0:NTK], W2, W2.ap[:, f * D + n * 128: f * D + (n + 1) * 128],
                            act, act.ap[:, f * NTK:(f + 1) * NTK], f == 0, f == NF - 1)
                self.tt("vector", x, x.ap[:, n * NTK:(n + 1) * NTK], po, po.ap[:, 0:NTK], x, x.ap[:, n * NTK:(n + 1) * NTK], ALU.add)
            self.store(xT[:, :, t0:t0 + NTK].rearrange("k p t -> p k t"), x, x.ap.rearrange("p (k t) -> p k t", k=KC))
        self.flush_deferred()
        P.barrier()

    def phase_out_xattn(self, li, w_out_ap):
        A, P, S = self.A, self.P, self.S
        A.reset(self.base_off)
        xT = self.scr["xT"].ap()
        oT = self.scr["oT"].ap()
        Wo = A.alloc(KC * D, BF16, "wo")
        Wq = A.alloc(KC * 512, BF16, "wq")
        Wx = A.alloc(4 * D, BF16, "wxo")
        Wkv = A.alloc(KC * D, BF16, "wkv")
        stage = [A.alloc(1024, F32, "wst%d" % i) for i in range(2)]
        self.load_weight(Wo, D, w_out_ap, KC, D, stage=stage)
        self.load_weight(Wq, 512, self.din["xa_wq"].ap()[li], KC, 512, stage=stage)
        self.load_weight(Wx, D, self.din["xa_wo"].ap()[li], 4, D, stage=stage)
        self.load_weight(Wkv, D, self.din["xa_wkv"].ap()[li], KC, D, stage=stage)
        KmT = A.alloc(4 * MEM, BF16, "kmT")
        Vm = A.alloc(2 * 512, BF16, "vm")
        memT = self.c_memT
        for h in range(4):
            ps = self.psum()
            for k in range(KC):
                self.mm(ps, ps.ap[:, 0:MEM], Wkv, Wkv.ap[:, k * D + h * 128: k * D + (h + 1) * 128],
                        memT, memT.ap[:, k * MEM:(k + 1) * MEM], k == 0, k == KC - 1)
            self.copy("vector", KmT, KmT.ap[:, h * MEM:(h + 1) * MEM], ps, ps.ap[:, 0:MEM])
        for s in range(2):
            ps = self.psum()
            for k in range(KC):
                self.mm(ps, ps.ap[:, 0:512], memT, memT.ap[:, k * MEM + s * 128: k * MEM + (s + 1) * 128],
                        Wkv, Wkv.ap[:, k * D + 512: k * D + 1024], k == 0, k == KC - 1)
            self.copy("vector", Vm, Vm.ap[:, s * 512:(s + 1) * 512], ps, ps.ap[:, 0:512])
        X = [A.alloc(KC * 512, F32, "x%d" % i) for i in range(2)]
        O = [A.alloc(KC * 512, BF16, "o%d" % i) for i in range(2)]
        hT = A.alloc(KC * 512, BF16, "hT")
        sq = A.alloc(KC * 512, BF16, "sq")
        rstd = A.alloc(512, F32, "rstd")
        qT = A.alloc(4 * 512, BF16, "qT")
        pT = [A.alloc(512, BF16, "pT%d" % i) for i in range(4)]
        rec = A.alloc(512, F32, "rec")
        xo = A.alloc(4 * 512, BF16, "xo")
        g = self.gain(1, li)
        sc = 128.0 ** -0.5
        for c in range(self.NCH):
            x = X[c % 2]
            o = O[c % 2]
            t0 = c * 512
            self.load(x, x.ap.rearrange("p (k t) -> p k t", k=KC), xT[:, :, t0:t0 + 512].rearrange("k p t -> p k t"))
            self.load(o, o.ap.rearrange("p (k t) -> p k t", k=KC), oT[:, :, t0:t0 + 512].rearrange("k p t -> p k t"))
            for n in range(KC):
                ps = self.psum()
                for k in range(KC):
                    self.mm(ps, ps.ap, Wo, Wo.ap[:, k * D + n * 128: k * D + (n + 1) * 128], o, o.ap[:, k * 512:(k + 1) * 512],
                            k == 0, k == KC - 1)
                self.tt("vector", x, x.ap[:, n * 512:(n + 1) * 512], ps, ps.ap, x, x.ap[:, n * 512:(n + 1) * 512], ALU.add)
            self.rmsnorm_fm(x, KC, 512, g, hT, sq, rstd, D)
            for h in range(4):
                ps = self.psum()
                for k in range(KC):
                    self.mm(ps, ps.ap, Wq, Wq.ap[:, k * 512 + h * 128: k * 512 + (h + 1) * 128], hT, hT.ap[:, k * 512:(k + 1) * 512],
                            k == 0, k == KC - 1)
                self.copy("scalar", qT, qT.ap[:, h * 512:(h + 1) * 512], ps, ps.ap)
            for h in range(4):
                pts = []
                for m in range(2):
                    ps = self.psum()
                    self.mm(ps, ps.ap, KmT, KmT.ap[:, h * MEM + m * 128: h * MEM + (m + 1) * 128], qT, qT.ap[:, h * 512:(h + 1) * 512],
                            True, True)
                    pt = pT[(h * 2 + m) % 4]
                    self.act(pt, pt.ap, ps, ps.ap, AF.Exp, scale=sc)
                    pts.append(pt)
                po = self.psum()
                for m in range(2):
                    self.mm(po, po.ap, Vm, Vm.ap[:, m * 512 + h * 128: m * 512 + (h + 1) * 128], pts[m], pts[m].ap, m == 0, m == 1)
                pz = self.psum()
                for m in range(2):
                    self.mm(pz, pz.ap, self.c_ones_bf, self.c_ones_bf.ap[:, 0:128], pts[m], pts[m].ap, m == 0, m == 1)
                P.op("vector", lambda e, o_=rec.ap, i_=pz.ap: e.reciprocal(out=o_, in_=i_), reads=[pz], writes=[rec])
                self.tt("vector", xo, xo.ap[:, h * 512:(h + 1) * 512], po, po.ap, rec, rec.ap, ALU.mult)
            for n in range(KC):
                ps = self.psum()
                for h in range(4):
                    self.mm(ps, ps.ap, Wx, Wx.ap[:, h * D + n * 128: h * D + (n + 1) * 128], xo, xo.ap[:, h * 512:(h + 1) * 512],
                            h == 0, h == 3)
                self.tt("vector", x, x.ap[:, n * 512:(n + 1) * 512], ps, ps.ap, x, x.ap[:, n * 512:(n + 1) * 512], ALU.add)
            self.store(xT[:, :, t0:t0 + 512].rearrange("k p t -> p k t"), x, x.ap.rearrange("p (k t) -> p k t", k=KC))
        self.flush_deferred()
        P.barrier()

    def attn_pass(self, k_t, k_ap_fn, q_t, q_ap, v_t, v_ap_fn, vcols, scale, Oacc, pairs, accs=None, ptiles=None, ssb=None):
        P = self.P
        self.begin_pass()
        total = 2 * len(pairs)
        done = [0]
        used = [False, False]

        def emit_pv(item):
            kta, ktb, pt = item
            for half, kt in ((0, kta), (1, ktb)):
                self.mm(Oacc, Oacc.ap[0:vcols, :], v_t, v_ap_fn(kt), pt, pt.ap[:, half * 512:(half + 1) * 512], done[0] == 0, done[0] == total - 1)
                done[0] += 1

        pend = None
        for pi, (kta, ktb, minfo) in enumerate(pairs):
            b0 = 2 * (self.nsp % 2)
            self.nsp += 1
            pa, pb = self.ps[b0], self.ps[b0 + 1]
            self.mm(pa, pa.ap, k_t, k_ap_fn(kta), q_t, q_ap, True, True)
            self.mm(pb, pb.ap, k_t, k_ap_fn(ktb), q_t, q_ap, True, True)
            pair_ap = self.psall[:, b0 * 512:(b0 + 2) * 512]
            pt = ptiles[self.npt % len(ptiles)]
            self.npt += 1
            if minfo is not None:
                sb = ssb[self.npt % len(ssb)]
                P.op("vector", lambda e, o=sb.ap, i0=pair_ap, i1=minfo[1]: e.tensor_tensor(out=o, in0=i0, in1=i1, op=ALU.min),
                     reads=[pa, pb, minfo[0]], writes=[sb])
                P.op("scalar", lambda e, o=pt.ap, i=sb.ap: e.activation(out=o, in_=i, func=AF.Exp, scale=scale), reads=[sb], writes=[pt])
            else:
                P.op("scalar", lambda e, o=pt.ap, i=pair_ap: e.activation(out=o, in_=i, func=AF.Exp, scale=scale), reads=[pa, pb], writes=[pt])
            if accs is not None:
                accA, tmpA, accB, tmpB = accs
                if pi % 3 != 2:
                    eng, acc, tmp, ui = "vector", accA, tmpA, 0
                else:
                    eng, acc, tmp, ui = "gpsimd", accB, tmpB, 1
                if not used[ui]:
                    self.tt(eng, acc, acc.ap, pt, pt.ap[:, 0:512], pt, pt.ap[:, 512:1024], ALU.add)
                    used[ui] = True
                else:
                    self.tt(eng, tmp, tmp.ap, pt, pt.ap[:, 0:512], pt, pt.ap[:, 512:1024], ALU.add)
                    self.tt(eng, acc, acc.ap, acc, acc.ap, tmp, tmp.ap, ALU.add)
            if pend is not None:
                emit_pv(pend)
            self.tick()
            pend = (kta, ktb, pt)
        emit_pv(pend)
        return used

    def causal_pairs(self, qc):
        lst = [(2 * i, 2 * i + 1, None) for i in range(2 * qc)]
        lst.append((4 * qc, 4 * qc + 1, (self.c_maskc, self.c_maskc.ap[:, 0:1024])))
        lst.append((4 * qc + 2, 4 * qc + 3, (self.c_maskc, self.c_maskc.ap[:, 1024:2048])))
        return lst

    def attn_pass_dual(self, k_t, q_t, v_t, v_ap_fn, vcols, scale, O0, O1, kts, accs=None, ptiles=None, ssb=None, post=None):
        P = self.P
        self.begin_pass()
        total = len(kts)
        used = [False, False]
        cnt = [0]

        def emit_pv(item):
            kt, pt = item
            first, last = cnt[0] == 0, cnt[0] == total - 1
            self.mm(O0, O0.ap[0:vcols, :], v_t, v_ap_fn(kt, 0), pt, pt.ap[:, 0:512], first, last)
            self.mm(O1, O1.ap[0:vcols, :], v_t, v_ap_fn(kt, 1), pt, pt.ap[:, 512:1024], first, last)
            cnt[0] += 1

        pend = None
        for i, (kt, minfo) in enumerate(kts):
            b0 = 2 * (self.nsp % 2)
            self.nsp += 1
            pa, pb = self.ps[b0], self.ps[b0 + 1]
            ka, kb = k_t[0], k_t[1]
            self.mm(pa, pa.ap, ka[0], ka[1](kt), q_t[0], q_t[1], True, True)
            self.mm(pb, pb.ap, kb[0], kb[1](kt), q_t[0], q_t[2], True, True)
            pair_ap = self.psall[:, b0 * 512:(b0 + 2) * 512]
            pt = ptiles[self.npt % len(ptiles)]
            self.npt += 1
            if minfo is not None:
                sb = ssb[self.npt % len(ssb)]
                if len(minfo) > 2:
                    o3 = sb.ap.rearrange("p (a c) -> p a c", a=2)
                    i3 = pair_ap.rearrange("p (a c) -> p a c", a=2)
                    m3 = minfo[1].unsqueeze(1).to_broadcast([128, 2, 512])
                    P.op("vector", lambda e, o=o3, i0=i3, i1=m3: e.tensor_tensor(out=o, in0=i0, in1=i1, op=ALU.min),
                         reads=[pa, pb, minfo[0]], writes=[sb])
                else:
                    P.op("vector", lambda e, o=sb.ap, i0=pair_ap, i1=minfo[1]: e.tensor_tensor(out=o, in0=i0, in1=i1, op=ALU.min),
                         reads=[pa, pb, minfo[0]], writes=[sb])
                P.op("scalar", lambda e, o=pt.ap, i=sb.ap: e.activation(out=o, in_=i, func=AF.Exp, scale=scale), reads=[sb], writes=[pt])
            else:
                P.op("scalar", lambda e, o=pt.ap, i=pair_ap: e.activation(out=o, in_=i, func=AF.Exp, scale=scale), reads=[pa, pb], writes=[pt])
            if post is not None:
                mt, map512 = post(kt)
                p3 = pt.ap.rearrange("p (a c) -> p a c", a=2)
                m3 = map512.unsqueeze(1).to_broadcast([128, 2, 512])
                P.op("vector", lambda e, o=p3, i1=m3: e.tensor_tensor(out=o, in0=o, in1=i1, op=ALU.mult), reads=[pt, mt], writes=[pt])
            if accs is not None:
                accA, accB = accs
                if i % 3 != 2:
                    eng, acc, ui = "vector", accA, 0
                else:
                    eng, acc, ui = "gpsimd", accB, 1
                if not used[ui]:
                    self.copy(eng, acc, acc.ap, pt, pt.ap)
                    used[ui] = True
                else:
                    self.tt(eng, acc, acc.ap, acc, acc.ap, pt, pt.ap, ALU.add)
            if pend is not None:
                emit_pv(pend)
            self.tick()
            pend = (kt, pt)
        emit_pv(pend)
        return used

    def causal_list(self, qc):
        lst = [(kt, None) for kt in range(4 * qc)]
        for o in range(4):
            lst.append((4 * qc + o, (self.c_maskc, self.c_maskc.ap[:, o * 512:(o + 1) * 512])))
        return lst

    def phase_even_in(self, li):
        A, P, S = self.A, self.P, self.S
        j = li // 2
        A.reset(self.base_off)
        xT = self.scr["xT"].ap()
        w_in = self.din["ev_w_in"].ap()[j]
        W = A.alloc(KC * EV_W, BF16, "w_in")
        Wp = A.alloc(KC * 1024, BF16, "w_perm")
        stage = [A.alloc(1024, F32, "wst%d" % i) for i in range(2)]
        self.load_weight(W, EV_W, w_in, KC, EV_W, stage=stage)
        wv = w_in[:, 0:1024].rearrange("k (b two d) -> k b two d", two=2, d=32)
        i = 0
        for k in range(KC):
            for two in range(2):
                st = stage[i % 2]
                i += 1
                self.load(st, st.ap[:, 0:512].rearrange("p (b d) -> p b d", d=32), wv[k * 128:(k + 1) * 128, :, 1 - two, :])
                dst = Wp.ap[:, k * 1024:(k + 1) * 1024].rearrange("p (b two d) -> p b two d", two=2, d=32)[:, :, two, :]
                self.copy(("vector", "gpsimd")[i % 2], Wp, dst, st, st.ap[:, 0:512].rearrange("p (b d) -> p b d", d=32))
        nbf = A.alloc(8, F32, "nbf")
        self.load(nbf, nbf.ap[0:8, 0:1], self.din["ev_b_f"].ap()[j].rearrange("(h o) -> h o", o=1))
        self.ts("vector", nbf, nbf.ap[0:8, 1:2], nbf, nbf.ap[0:8, 0:1], -1.0, ALU.mult)
        X = [A.alloc(KC * 512, F32, "x%d" % i) for i in range(2)]
        hT = A.alloc(KC * 512, BF16, "hT")
        sq = A.alloc(KC * 512, BF16, "sq")
        rstd = A.alloc(512, F32, "rstd")
        COS = [A.alloc(512, F32, "cos%d" % i) for i in range(2)]
        SIN = [A.alloc(512, F32, "sin%d" % i) for i in range(2)]
        t1 = [A.alloc(512, F32, "t1_%d" % i) for i in range(2)]
        t2 = [A.alloc(512, F32, "t2_%d" % i) for i in range(2)]
        qo = [A.alloc(512, BF16, "qo%d" % i) for i in range(4)]
        vo = [A.alloc(512, BF16, "vo%d" % i) for i in range(2)]
        lf = [A.alloc(512, F32, "lf%d" % i) for i in range(2)]
        g = self.gain(0, li)
        dq, dk, dv = self.scr["dq"].ap(), self.scr["dk"].ap(), self.scr["dv"].ap()
        fq, fk, fv = self.scr["fq"].ap(), self.scr["fk"].ap(), self.scr["fv"].ap()
        lfp = self.scr["lfp"].ap()
        nq = 0
        for c in range(self.NCH):
            x = X[c % 2]
            t0 = c * 512
            self.load(x, x.ap.rearrange("p (k t) -> p k t", k=KC), xT[:, :, t0:t0 + 512].rearrange("k p t -> p k t"))
            cs, sn = COS[c % 2], SIN[c % 2]
            self.load(cs, cs.ap, self.din["c_cos64"].ap()[:, t0:t0 + 512])
            self.load(sn, sn.ap, self.din["c_sin64"].ap()[:, t0:t0 + 512])
            self.rmsnorm_fm(x, KC, 512, g, hT, sq, rstd, D)

            def proj(Wt, wcols, col0, ncols):
                ps = self.psum()
                for k in range(KC):
                    self.mm(ps, ps.ap[0:ncols, :], Wt, Wt.ap[:, k * wcols + col0: k * wcols + col0 + ncols], hT, hT.ap[:, k * 512:(k + 1) * 512],
                            k == 0, k == KC - 1)
                return ps
            for which, dst in ((0, dq), (1, dk)):
                for h in range(4):
                    col = which * 512 + h * 128
                    pa = proj(W, EV_W, col, 128)
                    pb = proj(Wp, 1024, col, 128)
                    a1, a2 = t1[nq % 2], t2[nq % 2]
                    q_ = qo[nq % 4]
                    nq += 1
                    self.tt("vector", a1, a1.ap, pa, pa.ap, cs, cs.ap, ALU.mult)
                    self.tt("vector", a2, a2.ap, pb, pb.ap, sn, sn.ap, ALU.mult)
                    self.tt("gpsimd", q_, q_.ap, a1, a1.ap, a2, a2.ap, ALU.add)
                    self.store(dst[h, :, t0:t0 + 512], q_, q_.ap)
            for which, dst in ((0, fq), (1, fk)):
                for pr in range(4):
                    col = 1536 + which * 512 + pr * 128
                    pa = proj(W, EV_W, col, 128)
                    q_ = qo[nq % 4]
                    nq += 1
                    self.copy("scalar", q_, q_.ap, pa, pa.ap)
                    for two in range(2):
                        self.store(dst[pr * 2 + two, 0:64, t0:t0 + 512], q_, q_.ap[two * 64:(two + 1) * 64, :])
            for s in range(4):
                for which, col, dst in ((0, 1024, dv), (1, 2560, fv)):
                    ps = self.psum()
                    for k in range(KC):
                        self.mm(ps, ps.ap, hT, hT.ap[:, k * 512 + s * 128: k * 512 + (s + 1) * 128], W, W.ap[:, k * EV_W + col: k * EV_W + col + 512],
                                k == 0, k == KC - 1)
                    v_ = vo[which]
                    self.copy(("scalar", "vector")[which], v_, v_.ap, ps, ps.ap)
                    tt0 = t0 + s * 128
                    if which == 0:
                        self.store(dst[:, tt0:tt0 + 128, :].rearrange("h t d -> t h d"), v_, v_.ap.rearrange("p (h d) -> p h d", h=4))
                    else:
                        self.store(dst[:, tt0:tt0 + 128, :].rearrange("h t d -> t h d"), v_, v_.ap.rearrange("p (h d) -> p h d", h=8))
            pz = proj(W, EV_W, 3072, 8)
            l_ = lf[c % 2]
            self.act(l_, l_.ap[0:8, :], pz, pz.ap[0:8, :], AF.Exp, scale=-1.0, bias=nbf.ap[0:8, 1:2], extra_reads=[nbf])
            self.act(l_, l_.ap[0:8, :], l_, l_.ap[0:8, :], AF.Ln, scale=1.0, bias=self.c_one.ap[0:8, 0:1], extra_reads=[self.c_one])
            self.store(lfp[:, t0:t0 + 512], l_, l_.ap[0:8, :])
        self.flush_deferred()
        P.barrier()

    def phase_even_scan(self, li):
        A, P, S = self.A, self.P, self.S
        A.reset(self.base_off)
        PW = min(2048, S)
        npieces = S // PW
        lfp = self.scr["lfp"].ap()
        fq, fk = self.scr["fq"].ap(), self.scr["fk"].ap()
        ones = A.alloc(PW, BF16, "ones")
        self.memset("vector", ones, ones.ap[0:8, :], 1.0)
        self.c_onesrow = A.alloc(PW, F32, "onesrow")
        self.memset("gpsimd", self.c_onesrow, self.c_onesrow.ap[0:8, :], 1.0)
        L = [A.alloc(PW, F32, "L%d" % i) for i in range(2)]
        C = [A.alloc(PW, F32, "C%d" % i) for i in range(2)]
        R = A.alloc(PW, F32, "R")
        hi = [[A.alloc(PW, BF16, "sp%d_%d" % (i, jj)) for jj in range(3)] for i in range(2)]
        ng = [[A.alloc(PW, BF16, "ng%d_%d" % (i, jj)) for jj in range(3)] for i in range(2)]
        zero = A.alloc(8, F32, "zero")
        self.memset("vector", zero, zero.ap[0:8, 0:1], 0.0)
        prev = None
        for pc in range(npieces):
            l_, c_ = L[pc % 2], C[pc % 2]
            t0 = pc * PW
            self.load(l_, l_.ap[0:8, :], lfp[:, t0:t0 + PW])
            init_t = zero if prev is None else prev
            init_ap = zero.ap[0:8, 0:1] if prev is None else prev.ap[0:8, PW - 1:PW]
            P.op("vector", lambda e, o=c_.ap[0:8, :], d1=l_.ap[0:8, :], ia=init_ap, d0=self.c_onesrow.ap[0:8, 0:PW]:
                 e.tensor_tensor_scan(out=o, data0=d0, data1=d1, initial=ia, op0=ALU.mult, op1=ALU.add),
                 reads=[l_, init_t, self.c_onesrow], writes=[c_])
            prev = c_
            h3 = hi[pc % 2]
            n3 = ng[pc % 2]
            self.ts("vector", R, R.ap[0:8, :], c_, c_.ap[0:8, :], 8.0, ALU.mult)
            for jj in range(3):
                self.copy("vector", h3[jj], h3[jj].ap[0:8, :], R, R.ap[0:8, :])
                if jj < 2:
                    self.tt("vector", R, R.ap[0:8, :], R, R.ap[0:8, :], h3[jj], h3[jj].ap[0:8, :], ALU.subtract)
                self.ts("vector", n3[jj], n3[jj].ap[0:8, :], h3[jj], h3[jj].ap[0:8, :], -1.0, ALU.mult)
            for jj in range(3):
                self.store(fq[:, 64 + jj, t0:t0 + PW], n3[jj], n3[jj].ap[0:8, :])
                self.store(fq[:, 67 + jj, t0:t0 + PW], ones, ones.ap[0:8, :])
                self.store(fk[:, 64 + jj, t0:t0 + PW], ones, ones.ap[0:8, :])
                self.store(fk[:, 67 + jj, t0:t0 + PW], h3[jj], h3[jj].ap[0:8, :])
        self.flush_deferred()
        P.barrier()

    def phase_even_attn(self, li):
        A, P, S, NT = self.A, self.P, self.S, self.NT
        j = li // 2
        A.reset(self.base_off)
        lam_init = 0.8 - 0.6 * math.exp(-0.3 * li)
        dq, dk, dv = self.scr["dq"].ap(), self.scr["dk"].ap(), self.scr["dv"].ap()
        fq, fk, fv = self.scr["fq"].ap(), self.scr["fk"].ap(), self.scr["fv"].ap()
        oT = self.scr["oT"].ap()
        KT = [A.alloc(S, BF16, "KT%d" % i) for i in range(2)]
        V = [A.alloc(NT * 129, BF16, "V%d" % i) for i in range(2)]
        Q = [A.alloc(512, BF16, "Q%d" % i) for i in range(2)]
        pts = [A.alloc(1024, BF16, "pt%d" % i) for i in range(4)]
        ssb = [A.alloc(1024, F32, "ssb%d" % i) for i in range(2)]
        rec = [A.alloc(512, F32, "rec%d" % i) for i in range(2)]
        recB = A.alloc(512, F32, "recB")
        on = [A.alloc(512, F32, "on%d" % i) for i in range(2)]
        oa = A.alloc(512, F32, "oa")
        osq = A.alloc(512, BF16, "osq")
        orstd = A.alloc(512, F32, "orstd")
        ob = [A.alloc(512, BF16, "ob%d" % i) for i in range(2)]
        accAs = [A.alloc(1024, F32, "accA%d" % i) for i in range(2)]
        accBs = [A.alloc(1024, F32, "accB%d" % i) for i in range(2)]
        mask2 = A.alloc(4 * 1024, F32, "mask2")
        for o in range(4):
            for two in range(2):
                self.copy("gpsimd", mask2, mask2.ap[:, o * 1024 + two * 512: o * 1024 + (two + 1) * 512], self.c_maskc, self.c_maskc.ap[:, o * 512:(o + 1) * 512])
        lam = A.alloc(4 * 64 + 8, F32, "lam")
        self.load(lam, lam.ap[:, 0:256], self.din["ev_lam"].ap()[j].rearrange("a d -> (a d)").partition_broadcast(128))
        lp = A.alloc(128, F32, "lamp")
        LS = 256
        self.tt("vector", lp, lp.ap[:, 0:64], lam, lam.ap[:, 0:64], lam, lam.ap[:, 64:128], ALU.mult)
        self.tt("vector", lp, lp.ap[:, 64:128], lam, lam.ap[:, 128:192], lam, lam.ap[:, 192:256], ALU.mult)
        P.op("vector", lambda e: e.reduce_sum(out=lam.ap[:, LS:LS + 1], in_=lp.ap[:, 0:64], axis=mybir.AxisListType.X), reads=[lp], writes=[lam])
        P.op("vector", lambda e: e.reduce_sum(out=lam.ap[:, LS + 1:LS + 2], in_=lp.ap[:, 64:128], axis=mybir.AxisListType.X), reads=[lp], writes=[lam])
        self.act(lam, lam.ap[:, LS + 2:LS + 4], lam, lam.ap[:, LS:LS + 2], AF.Exp)
        self.tt("vector", lam, lam.ap[:, LS + 4:LS + 5], lam, lam.ap[:, LS + 2:LS + 3], lam, lam.ap[:, LS + 3:LS + 4], ALU.subtract)
        self.ts("vector", lam, lam.ap[:, LS + 5:LS + 6], lam, lam.ap[:, LS + 4:LS + 5], lam_init, ALU.add, -1.0, ALU.mult)
        neg_lam = lam.ap[:, LS + 5:LS + 6]
        sub = A.alloc(8, F32, "subln")
        self.load(sub, sub.ap[:, 0:1], self.din["ev_subln"].ap()[j].rearrange("(p o) -> p o", o=1))
        self.ts("vector", sub, sub.ap[:, 1:2], sub, sub.ap[:, 0:1], 1.0 - lam_init, ALU.mult)
        for i in range(2):
            self.memset("gpsimd", V[i], V[i].ap, 1.0)
        units = [("f", h) for h in range(8)] + [("d", h) for h in range(4)]

        def load_unit(u, slot):
            kind, h = u
            kt_, v_ = KT[slot], V[slot]
            if kind == "f":
                self.load(kt_, kt_.ap[0:70, :], fk[h])
                self.load(v_, v_.ap.rearrange("p (t c) -> p t c", c=129)[:, :, 0:64], fv[h].rearrange("(t p) d -> p t d", p=128))
            else:
                self.load(kt_, kt_.ap, dk[h])
                self.load(v_, v_.ap.rearrange("p (t c) -> p t c", c=129)[:, :, 0:128], dv[h].rearrange("(t p) d -> p t d", p=128))

        load_unit(units[0], 0)
        nqi = 0
        npass = 0
        for ui, u in enumerate(units):
            kind, h = u
            slot = ui % 2
            if ui + 1 < len(units):
                load_unit(units[ui + 1], 1 - slot)
            kt_, v_ = KT[slot], V[slot]
            for qc in range(self.NCH):
                q_ = Q[nqi % 2]
                nqi += 1
                t0 = qc * 512
                kl = self.causal_pairs(qc)
                if kind == "f":
                    self.load(q_, q_.ap[0:70, :], fq[h, :, t0:t0 + 512])
                    Oacc = self.ps[4 + npass % 2]
                    npass += 1
                    self.attn_pass(kt_, lambda kt, kt_=kt_: kt_.ap[0:70, kt * 128:(kt + 1) * 128], q_, q_.ap[0:70, :],
                                   v_, lambda kt, v_=v_: v_.ap[:, kt * 129: kt * 129 + 65], 65, 0.125, Oacc, kl, ptiles=pts, ssb=ssb)
                    r_ = rec[npass % 2]
                    P.op("vector", lambda e, o=r_.ap[64:65, :], i=Oacc.ap[64:65, :]: e.reciprocal(out=o, in_=i), reads=[Oacc], writes=[r_])
                    o_ = ob[npass % 2]

                    def fin(Oacc=Oacc, r_=r_, o_=o_, dst=oT[4 + h // 2, (h % 2) * 64:(h % 2) * 64 + 64, t0:t0 + 512]):
                        pb = self.ps[6]
                        self.mm(pb, pb.ap[0:64, :], self.c_ones_f, self.c_ones_f.ap[64:65, 0:64], r_, r_.ap[64:65, :], True, True)
                        self.copy("scalar", recB, recB.ap[0:64, :], pb, pb.ap[0:64, :])
                        self.tt("vector", o_, o_.ap[0:64, :], Oacc, Oacc.ap[0:64, :], recB, recB.ap[0:64, :], ALU.mult)
                        self.store(dst, o_, o_.ap[0:64, :])
                    self.defer(4, fin)
                else:
                    self.load(q_, q_.ap, dq[h, :, t0:t0 + 512])
                    if h == 0 and qc == 0:
                        self.flush_deferred()
                    pb4 = 4 + 2 * (npass % 2)
                    npass += 1
                    O0, O1 = self.ps[pb4], self.ps[pb4 + 1]
                    kts = [(kt, None) for kt in range(4 * qc)]
                    for o in range(4):
                        kts.append((4 * qc + o, (mask2, mask2.ap[:, o * 1024:(o + 1) * 1024])))
                    aA, aB = accAs[npass % 2], accBs[npass % 2]
                    used = self.attn_pass_dual(
                        ((kt_, lambda kt, kt_=kt_: kt_.ap[0:64, kt * 128:(kt + 1) * 128]),
                         (kt_, lambda kt, kt_=kt_: kt_.ap[64:128, kt * 128:(kt + 1) * 128])),
                        (q_, q_.ap[0:64, :], q_.ap[64:128, :]),
                        v_, lambda kt, j_, v_=v_: v_.ap[:, kt * 129: kt * 129 + 128], 128, 0.125, O0, O1, kts,
                        accs=(aA, aB), ptiles=pts, ssb=ssb)
                    if used[1]:
                        self.tt("vector", aA, aA.ap, aA, aA.ap, aB, aB.ap, ALU.add)
                    o_ = ob[npass % 2]

                    def fin1(aA=aA):
                        for sub_j in range(2):
                            sacc = self.psum(0, 4)
                            self.mm(sacc, sacc.ap[0:1, :], self.c_ones_f, self.c_ones_f.ap[:, 0:1], aA, aA.ap[:, sub_j * 512:(sub_j + 1) * 512], True, True)
                            r_ = rec[sub_j]
                            P.op("vector", lambda e, o=r_.ap[0:1, :], i=sacc.ap[0:1, :]: e.reciprocal(out=o, in_=i), reads=[sacc], writes=[r_])
                            if sub_j == 1:
                                self.ts("vector", r_, r_.ap[0:1, :], r_, r_.ap[0:1, :], neg_lam[0:1, :], ALU.mult, extra_reads=[lam])

                    def fin2(O0=O0, O1=O1, o_=o_, dst=oT[h, :, t0:t0 + 512]):
                        for sub_j, Oacc in ((0, O0), (1, O1)):
                            r_ = rec[sub_j]
                            pb = self.psum(0, 4)
                            self.mm(pb, pb.ap, self.c_ones_f, self.c_ones_f.ap[0:1, 0:128], r_, r_.ap[0:1, :], True, True)
                            self.copy("scalar", recB, recB.ap, pb, pb.ap)
                            self.tt("vector", on[sub_j], on[sub_j].ap, Oacc, Oacc.ap, recB, recB.ap, ALU.mult)
                        self.tt("gpsimd", oa, oa.ap, on[0], on[0].ap, on[1], on[1].ap, ALU.add)
                        self.act(osq, osq.ap, oa, oa.ap, AF.Square)
                        pz = self.psum(0, 4)
                        self.mm(pz, pz.ap, self.c_ones_bf, self.c_ones_bf.ap[:, 0:128], osq, osq.ap, True, True)
                        self.act(orstd, orstd.ap, pz, pz.ap, AF.Ln, scale=1.0 / 128, bias=self.c_eps.ap[:, 0:1], extra_reads=[self.c_eps])
                        self.act(orstd, orstd.ap, orstd, orstd.ap, AF.Exp, scale=-0.5)
                        self.stt(o_, o_.ap, oa, oa.ap, sub.ap[:, 1:2], orstd, orstd.ap, ALU.mult, ALU.mult, extra_reads=[sub])
                        self.store(dst, o_, o_.ap)
                    self.defer(3, fin1)
                    self.defer(12, fin2)
        self.flush_deferred()
        P.barrier()

    def load_perm(self, Wp, wpcols, dst_col, src_cols_ap, n, half, stage, nk=KC):
        nb = n // (2 * half)
        sv = src_cols_ap.rearrange("k (b two d) -> k b two d", two=2, d=half)
        i = 0
        for k in range(nk):
            for two in range(2):
                st = stage[i % 2]
                i += 1
                self.load(st, st.ap[:, 0:nb * half].rearrange("p (b d) -> p b d", d=half), sv[k * 128:(k + 1) * 128, :, 1 - two, :])
                dst = Wp.ap[:, k * wpcols + dst_col: k * wpcols + dst_col + n].rearrange("p (b two d) -> p b two d", two=2, d=half)[:, :, two, :]
                self.copy(("vector", "gpsimd")[i % 2], Wp, dst, st, st.ap[:, 0:nb * half].rearrange("p (b d) -> p b d", d=half))

    def rope_evac(self, pa, pb, rows, cs, sn, t1, t2, out_t, out_ap):
        self.tt("vector", t1, t1.ap[0:rows, :], pa, pa.ap[0:rows, :], cs, cs.ap[0:rows, :], ALU.mult)
        self.tt("vector", t2, t2.ap[0:rows, :], pb, pb.ap[0:rows, :], sn, sn.ap[0:rows, :], ALU.mult)
        self.tt("gpsimd", out_t, out_ap, t1, t1.ap[0:rows, :], t2, t2.ap[0:rows, :], ALU.add)

    def phase_odd_in(self, li):
        A, P, S = self.A, self.P, self.S
        j = li // 2
        A.reset(self.base_off)
        xT = self.scr["xT"].ap()
        w_in = self.din["od_w_in"].ap()[j]
        NPERM = 928
        W = A.alloc(KC * OD_W, BF16, "w_in")
        Wp = A.alloc(KC * NPERM, BF16, "w_perm")
        stage = [A.alloc(1024, F32, "wst%d" % i) for i in range(2)]
        self.load_weight(W, OD_W, w_in, KC, OD_W, stage=stage)
        self.load_perm(Wp, NPERM, 0, w_in[:, 0:512], 512, 32, stage)
        self.load_perm(Wp, NPERM, 512, w_in[:, 512:640], 128, 32, stage)
        self.load_perm(Wp, NPERM, 640, w_in[:, 768:896], 128, 32, stage)
        self.load_perm(Wp, NPERM, 768, w_in[:, 1024:1152], 128, 32, stage)
        self.load_perm(Wp, NPERM, 896, w_in[:, 1944:1976], 32, 16, stage)
        w_uq = self.din["mla_w_uq"].ap()[j]
        w_ukv = self.din["mla_w_ukv"].ap()[j]
        Wuq = A.alloc(3 * 768, BF16, "wuq")
        Wuqp = A.alloc(3 * 768, BF16, "wuqp")
        self.load_weight(Wuq, 768, w_uq, 3, 768, stage=stage)
        for k in range(3):
            self.copy("gpsimd", Wuqp, Wuqp.ap[:, k * 768:(k + 1) * 768], Wuq, Wuq.ap[:, k * 768:(k + 1) * 768])
        sv = w_uq.rearrange("k (h c) -> k h c", c=96)[:, :, 64:96].rearrange("k h (two d) -> k h two d", two=2)
        i = 0
        for k in range(3):
            for two in range(2):
                st = stage[i % 2]
                i += 1
                self.load(st, st.ap[:, 0:128].rearrange("p (h d) -> p h d", d=16), sv[k * 128:(k + 1) * 128, :, 1 - two, :])
                dst = Wuqp.ap[:, k * 768:(k + 1) * 768].rearrange("p (h c) -> p h c", c=96)[:, :, 64 + two * 16: 64 + two * 16 + 16]
                self.copy("vector", Wuqp, dst, st, st.ap[:, 0:128].rearrange("p (h d) -> p h d", d=16))
        Wkn = A.alloc(2 * 512, BF16, "wkn")
        Wkv = A.alloc(2 * 512, BF16, "wkv")
        kvv = w_ukv.rearrange("k (h two d) -> k h two d", two=2, d=64)
        for k in range(2):
            for two, Wt in ((0, Wkn), (1, Wkv)):
                st = stage[i % 2]
                i += 1
                self.load(st, st.ap[:, 0:512].rearrange("p (h d) -> p h d", d=64), kvv[k * 128:(k + 1) * 128, :, two, :])
                self.copy("vector", Wt, Wt.ap[:, k * 512:(k + 1) * 512], st, st.ap[:, 0:512])
        mlan = A.alloc(8, F32, "mlan")
        self.load(mlan, mlan.ap[:, 0:5], self.din["c_mlan"].ap()[j])
        X = [A.alloc(KC * 512, F32, "x%d" % i) for i in range(2)]
        hT = A.alloc(KC * 512, BF16, "hT")
        sq = A.alloc(KC * 512, BF16, "sq")
        rstd = A.alloc(512, F32, "rstd")
        COS = [A.alloc(512, F32, "cos%d" % i) for i in range(2)]
        SIN = [A.alloc(512, F32, "sin%d" % i) for i in range(2)]
        COSM = [A.alloc(512, F32, "cosm%d" % i) for i in range(2)]
        SINM = [A.alloc(512, F32, "sinm%d" % i) for i in range(2)]
        COSK = [A.alloc(512, F32, "cosk%d" % i) for i in range(2)]
        SINK = [A.alloc(512, F32, "sink%d" % i) for i in range(2)]
        t1 = [A.alloc(512, F32, "t1_%d" % i) for i in range(2)]
        t2 = [A.alloc(512, F32, "t2_%d" % i) for i in range(2)]
        qo = [A.alloc(512, BF16, "qo%d" % i) for i in range(4)]
        vo = [A.alloc(512, BF16, "vo%d" % i) for i in range(2)]
        gt = [A.alloc(512, F32, "gt%d" % i) for i in range(2)]
        CQ = A.alloc(3 * 512, F32, "cq")
        CQn = A.alloc(3 * 512, BF16, "cqn")
        CKV = A.alloc(2 * 512, F32, "ckv")
        CKVn = A.alloc(2 * 512, BF16, "ckvn")
        g = self.gain(0, li)
        scr = self.scr
        nsq, kcT, vcT, ksT, kwT = scr["nsq"].ap(), scr["kcT"].ap(), scr["vcT"].ap(), scr["ksT"].ap(), scr["kwT"].ap()
        vsv, vwv, gT = scr["vsv"].ap(), scr["vwv"].ap(), scr["gT"].ap()
        mq, mk, mv = scr["mq"].ap(), scr["mk"].ap(), scr["mv"].ap()
        nq = 0
        for c in range(self.NCH):
            x = X[c % 2]
            t0 = c * 512
            self.load(x, x.ap.rearrange("p (k t) -> p k t", k=KC), xT[:, :, t0:t0 + 512].rearrange("k p t -> p k t"))
            cs, sn = COS[c % 2], SIN[c % 2]
            csm, snm = COSM[c % 2], SINM[c % 2]
            csk, snk = COSK[c % 2], SINK[c % 2]
            self.load(cs, cs.ap, self.din["c_cos64"].ap()[:, t0:t0 + 512])
            self.load(sn, sn.ap, self.din["c_sin64"].ap()[:, t0:t0 + 512])
            self.load(csm, csm.ap[0:96, :], self.din["c_cosm"].ap()[:, t0:t0 + 512])
            self.load(snm, snm.ap[0:96, :], self.din["c_sinm"].ap()[:, t0:t0 + 512])
            self.load(csk, csk.ap[0:32, :], self.din["c_cosm"].ap()[64:96, t0:t0 + 512])
            self.load(snk, snk.ap[0:32, :], self.din["c_sinm"].ap()[64:96, t0:t0 + 512])
            self.rmsnorm_fm(x, KC, 512, g, hT, sq, rstd, D)

            def proj(Wt, wcols, col0, ncols, rhs=hT, nk=KC):
                ps = self.psum()
                for k in range(nk):
                    self.mm(ps, ps.ap[0:ncols, :], Wt, Wt.ap[:, k * wcols + col0: k * wcols + col0 + ncols], rhs, rhs.ap[:, k * 512:(k + 1) * 512],
                            k == 0, k == nk - 1)
                return ps

            def roped(col, pcol, rows, cst, snt):
                nonlocal nq
                pa = proj(W, OD_W, col, rows)
                pb = proj(Wp, NPERM, pcol, rows)
                q_ = qo[nq % 4]
                a1, a2 = t1[nq % 2], t2[nq % 2]
                nq += 1
                self.rope_evac(pa, pb, rows, cst, snt, a1, a2, q_, q_.ap[0:rows, :])
                return q_
            for t in range(4):
                q_ = roped(128 * t, 128 * t, 128, cs, sn)
                for two in range(2):
                    self.store(nsq[2 * t + two, :, t0:t0 + 512], q_, q_.ap[two * 64:(two + 1) * 64, :])
            for col, pcol, dst in ((512, 512, kcT), (768, 640, ksT), (1024, 768, kwT)):
                q_ = roped(col, pcol, 128, cs, sn)
                for two in range(2):
                    self.store(dst[two, :, t0:t0 + 512], q_, q_.ap[two * 64:(two + 1) * 64, :])
            pa = proj(W, OD_W, 640, 128)
            q_ = qo[nq % 4]
            nq += 1
            self.copy("scalar", q_, q_.ap, pa, pa.ap)
            for two in range(2):
                self.store(vcT[two, :, t0:t0 + 512], q_, q_.ap[two * 64:(two + 1) * 64, :])
            for s in range(4):
                for which, col, dst in ((0, 896, vsv), (1, 1152, vwv)):
                    ps = self.psum()
                    for k in range(KC):
                        self.mm(ps, ps.ap[:, 0:128], hT, hT.ap[:, k * 512 + s * 128: k * 512 + (s + 1) * 128], W, W.ap[:, k * OD_W + col: k * OD_W + col + 128],
                                k == 0, k == KC - 1)
                    v_ = vo[which]
                    self.copy(("scalar", "vector")[which], v_, v_.ap[:, 0:128], ps, ps.ap[:, 0:128])
                    tt0 = t0 + s * 128
                    self.store(dst[:, tt0:tt0 + 128, :].rearrange("g t d -> t g d"), v_, v_.ap[:, 0:128].rearrange("p (g d) -> p g d", g=2))
            pz = proj(W, OD_W, 1280, 24)
            g_ = gt[c % 2]
            self.act(g_, g_.ap[0:24, :], pz, pz.ap[0:24, :], AF.Sigmoid)
            self.store(gT[:, t0:t0 + 512], g_, g_.ap[0:24, :])
            for i3 in range(3):
                pa = proj(W, OD_W, 1304 + 128 * i3, 128)
                self.copy("scalar", CQ, CQ.ap[:, i3 * 512:(i3 + 1) * 512], pa, pa.ap)
            self.rmsnorm_fm(CQ, 3, 512, mlan.ap[:, 0:3], CQn, sq, rstd, 384, gain_t=mlan)
            for i2 in range(2):
                pa = proj(W, OD_W, 1688 + 128 * i2, 128)
                self.copy("scalar", CKV, CKV.ap[:, i2 * 512:(i2 + 1) * 512], pa, pa.ap)
            self.rmsnorm_fm(CKV, 2, 512, mlan.ap[:, 3:5], CKVn, sq, rstd, 256, gain_t=mlan)
            q_ = roped(1944, 896, 32, csk, snk)
            for h in range(8):
                self.store(mk[h, 64:96, t0:t0 + 512], q_, q_.ap[0:32, :])
            for h in range(8):
                pa = proj(Wuq, 768, 96 * h, 96, rhs=CQn, nk=3)
                pb = proj(Wuqp, 768, 96 * h, 96, rhs=CQn, nk=3)
                q_ = qo[nq % 4]
                a1, a2 = t1[nq % 2], t2[nq % 2]
                nq += 1
                self.rope_evac(pa, pb, 96, csm, snm, a1, a2, q_, q_.ap[0:96, :])
                self.store(mq[h, :, t0:t0 + 512], q_, q_.ap[0:96, :])
            for pr in range(4):
                pa = proj(Wkn, 512, 128 * pr, 128, rhs=CKVn, nk=2)
                q_ = qo[nq % 4]
                nq += 1
                self.copy("scalar", q_, q_.ap, pa, pa.ap)
                for two in range(2):
                    self.store(mk[2 * pr + two, 0:64, t0:t0 + 512], q_, q_.ap[two * 64:(two + 1) * 64, :])
            for s in range(4):
                ps = self.psum()
                for k in range(2):
                    self.mm(ps, ps.ap, CKVn, CKVn.ap[:, k * 512 + s * 128: k * 512 + (s + 1) * 128], Wkv, Wkv.ap[:, k * 512:(k + 1) * 512], k == 0, k == 1)
                v_ = vo[s % 2]
                self.copy("vector", v_, v_.ap, ps, ps.ap)
                tt0 = t0 + s * 128
                self.store(mv[:, tt0:tt0 + 128, :].rearrange("h t d -> t h d"), v_, v_.ap.rearrange("p (h d) -> p h d", h=8))
        self.flush_deferred()
        P.barrier()

    def phase_odd_cmp(self, li):
        A, P, S = self.A, self.P, self.S
        j = li // 2
        A.reset(self.base_off)
        NC, NCT, NCP = self.NC, self.NCT, self.NCP
        scr = self.scr
        src = {0: scr["kcT"].ap(), 1: scr["vcT"].ap()}
        kcmp, vcmp = scr["kcmp"].ap(), scr["vcmp"].ap()
        stage = [A.alloc(1024, F32, "wst%d" % i) for i in range(2)]
        SRC = [A.alloc(2 * S, BF16, "src%d" % kv) for kv in range(2)]
        for kv in range(2):
            for g in range(2):
                self.load(SRC[kv], SRC[kv].ap[0:64, g * S:(g + 1) * S], src[kv][g])
        W1 = [A.alloc(32 * 128, BF16, "w1_%d" % kv) for kv in range(2)]
        W2 = [A.alloc(64, BF16, "w2_%d" % kv) for kv in range(2)]
        posT = A.alloc(2 * 32, F32, "posT")
        posTb = A.alloc(2 * 32, BF16, "posTb")
        self.load(posT, posT.ap[0:64, 0:64], self.din["c_posT"].ap()[j])
        self.copy("vector", posTb, posTb.ap[0:64, 0:64], posT, posT.ap[0:64, 0:64])
        i = 0
        for kv in range(2):
            w1 = self.din["nsa_cmp_w1"].ap()[j, kv].rearrange("(l d) h -> d l h", d=64)
            for l4 in range(4):
                st = stage[i % 2]
                i += 1
                self.load(st, st.ap[0:64, 0:1024].rearrange("p (l h) -> p l h", h=128), w1[:, l4 * 8:(l4 + 1) * 8, :])
                self.copy("vector", W1[kv], W1[kv].ap[0:64, l4 * 1024:(l4 + 1) * 1024], st, st.ap[0:64, 0:1024])
            st = stage[i % 2]
            i += 1
            self.load(st, st.ap[:, 0:64], self.din["nsa_cmp_w2"].ap()[j, kv])
            self.copy("vector", W2[kv], W2[kv].ap[:, 0:64], st, st.ap[:, 0:64])
        bias = A.alloc(8, F32, "cbias")
        H = [A.alloc(NCP, BF16, "H%d" % i) for i in range(2)]
        for i in range(2):
            self.memset("vector", H[i], H[i].ap, 0.0)
        ko = [A.alloc(NCP, BF16, "ko%d" % i) for i in range(2)]
        for i in range(2):
            self.memset("vector", ko[i], ko[i].ap, 0.0)
        vo = [A.alloc(64, BF16, "cvo%d" % i) for i in range(2)]
        for kv in range(2):
            pb = self.psum()
            for l in range(32):
                self.mm(pb, pb.ap[:, 0:1], W1[kv], W1[kv].ap[0:64, l * 128:(l + 1) * 128], posTb, posTb.ap[0:64, kv * 32 + l: kv * 32 + l + 1], l == 0, l == 31)
            self.copy("vector", bias, bias.ap[:, kv:kv + 1], pb, pb.ap[:, 0:1])
        n = 0
        for kv in range(2):
            for g in range(2):
                ps = self.psum()
                for l in range(32):
                    rhs = SRC[kv].ap[0:64, g * S + l: g * S + l + 16 * (NC - 1) + 1: 16]
                    self.mm(ps, ps.ap[:, 0:NC], W1[kv], W1[kv].ap[0:64, l * 128:(l + 1) * 128], SRC[kv], rhs, l == 0, l == 31)
                h_ = H[n % 2]
                self.act(h_, h_.ap[:, 0:NC], ps, ps.ap[:, 0:NC], AF.Silu, bias=bias.ap[:, kv:kv + 1], extra_reads=[bias])
                if kv == 0:
                    p2 = self.psum()
                    self.mm(p2, p2.ap[0:64, 0:NC], W2[0], W2[0].ap[:, 0:64], h_, h_.ap[:, 0:NC], True, True)
                    k_ = ko[g]
                    self.copy("vector", k_, k_.ap[0:64, 0:NC], p2, p2.ap[0:64, 0:NC])
                    self.store(kcmp[g], k_, k_.ap[0:64, :])
                else:
                    for ct in range(NCT):
                        p2 = self.psum()
                        self.mm(p2, p2.ap[:, 0:64], h_, h_.ap[:, ct * 128:(ct + 1) * 128], W2[1], W2[1].ap[:, 0:64], True, True)
                        v_ = vo[ct % 2]
                        self.copy("vector", v_, v_.ap[:, 0:64], p2, p2.ap[:, 0:64])
                        self.store(vcmp[g, ct * 128:(ct + 1) * 128, :], v_, v_.ap[:, 0:64])
                n += 1
        self.flush_deferred()
        P.barrier()

    def combine(self, Oacc, gr, b, dst, first, rec_t, recB, tmp_t, bank_lo, bank_hi):
        P = self.P
        r_ = rec_t
        self.ts("vector", r_, r_.ap[64:65, :], Oacc, Oacc.ap[64:65, :], 1e-30, ALU.max)
        P.op("vector", lambda e, o=r_.ap[64:65, :]: e.reciprocal(out=o, in_=o), reads=[r_], writes=[r_])
        if gr is not None:
            self.tt("vector", r_, r_.ap[64:65, :], r_, r_.ap[64:65, :], gr, gr.ap[64:65, b * 512:(b + 1) * 512], ALU.mult)
        pb = self.psum(bank_lo, bank_hi)
        self.mm(pb, pb.ap[0:64, :], self.c_ones_f, self.c_ones_f.ap[64:65, 0:64], r_, r_.ap[64:65, :], True, True)
        self.copy("scalar", recB, recB.ap[0:64, :], pb, pb.ap[0:64, :])
        if first:
            self.tt("vector", dst, dst.ap[0:64, :], Oacc, Oacc.ap[0:64, :], recB, recB.ap[0:64, :], ALU.mult)
        else:
            self.tt("vector", tmp_t, tmp_t.ap[0:64, :], Oacc, Oacc.ap[0:64, :], recB, recB.ap[0:64, :], ALU.mult)
            self.tt("gpsimd", dst, dst.ap[0:64, :], dst, dst.ap[0:64, :], tmp_t, tmp_t.ap[0:64, :], ALU.add)

    def phase_odd_nsa(self, li):
        A, P, S, NT = self.A, self.P, self.S, self.NT
        A.reset(self.base_off)
        NB = S // 64
        NC, NCT, NCP = self.NC, self.NCT, self.NCP
        scr = self.scr
        nsq, ksT, kwT, vsv, vwv, gT = scr["nsq"].ap(), scr["ksT"].ap(), scr["kwT"].ap(), scr["vsv"].ap(), scr["vwv"].ap(), scr["gT"].ap()
        kcmp, vcmp, oT = scr["kcmp"].ap(), scr["vcmp"].ap(), scr["oT"].ap()
        sc = 0.125
        stage = [A.alloc(1024, F32, "wst%d" % i) for i in range(2)]
        KS = A.alloc(2 * S, BF16, "KS")
        VS = A.alloc(2 * NT * 65, BF16, "VS")
        KCM = A.alloc(2 * NCP, BF16, "KCM")
        VCM = A.alloc(2 * NCT * 65, BF16, "VCM")
        self.memset("gpsimd", VS, VS.ap, 1.0)
        self.memset("gpsimd", VCM, VCM.ap, 1.0)
        for g in range(2):
            self.load(KS, KS.ap[0:64, g * S:(g + 1) * S], ksT[g])
            self.load(KS, KS.ap[64:128, g * S:(g + 1) * S], ksT[g])
            self.load(VS, VS.ap.rearrange("p (g t c) -> p g t c", g=2, c=65)[:, g, :, 0:64], vsv[g].rearrange("(t p) d -> p t d", p=128))
            self.load(KCM, KCM.ap[0:64, g * NCP:(g + 1) * NCP], kcmp[g])
            self.load(KCM, KCM.ap[64:128, g * NCP:(g + 1) * NCP], kcmp[g])
            self.load(VCM, VCM.ap.rearrange("p (g t c) -> p g t c", g=2, c=65)[:, g, :, 0:64], vcmp[g].rearrange("(t p) d -> p t d", p=128))
        OV1 = A.alloc(NCT * (NB + 1), BF16, "OV1")
        st = stage[0]
        self.load(st, st.ap[:, 0:NCT * (NB + 1)], self.din["c_ov1"].ap())
        self.copy("vector", OV1, OV1.ap, st, st.ap[:, 0:NCT * (NB + 1)])
        EXP = A.alloc(S, BF16, "EXP")
        for i in range(S // 1024):
            st = stage[(i + 1) % 2]
            self.load(st, st.ap, self.din["c_expand"].ap()[:, i * 1024:(i + 1) * 1024])
            self.copy("vector", EXP, EXP.ap[:, i * 1024:(i + 1) * 1024], st, st.ap)
        MW = A.alloc(4 * 512, F32, "MW")
        MM = A.alloc(5 * 512, F32, "MM")
        self.load(MW, MW.ap, self.din["c_maskw"].ap())
        self.load(MM, MM.ap, self.din["c_maskm"].ap())
        TWW = 2 * (NT - 1) + NB
        TW1 = A.alloc(TWW, F32, "TW1")
        TW2 = A.alloc(TWW, F32, "TW2")
        self.load(TW1, TW1.ap, self.din["c_tw1"].ap())
        self.load(TW2, TW2.ap, self.din["c_tw2"].ap())
        Q8 = A.alloc(4 * 512, BF16, "Q8")
        GR = [A.alloc(3 * 512, F32, "GR%d" % i) for i in range(4)]
        KW = [A.alloc(2 * 1024, BF16, "KW%d" % i) for i in range(2)]
        VW = [A.alloc(2 * 8 * 65, BF16, "VW%d" % i) for i in range(2)]
        for i in range(2):
            self.memset("gpsimd", VW[i], VW[i].ap, 1.0)
        OACC = [A.alloc(512, F32, "OACC%d" % i) for i in range(4)]
        ET = [A.alloc(512, BF16, "ET%d" % i) for i in range(4)]
        pts = [A.alloc(1024, BF16, "pt%d" % i) for i in range(4)]
        ssb = [A.alloc(1024, F32, "ssb%d" % i) for i in range(2)]
        rec = [A.alloc(512, F32, "rec%d" % i) for i in range(2)]
        recB = A.alloc(512, F32, "recB")
        ctmp = A.alloc(512, F32, "ctmp")
        ACC = [A.alloc(4 * NB, F32, "ACC%d" % i) for i in range(2)]
        adj = [A.alloc(NB, F32, "adj%d" % i) for i in range(2)]
        tmpv = A.alloc(NB, F32, "tmpv")
        m8 = A.alloc(16, F32, "m8")
        selb = [A.alloc(NB, BF16, "selb%d" % i) for i in range(2)]
        selT = [A.alloc(512, BF16, "selT%d" % i) for i in range(2)]
        mks = [A.alloc(1024, BF16, "mk%d" % i) for i in range(2)]
        rI = A.alloc(8, F32, "rI")
        ob = [A.alloc(512, BF16, "ob%d" % i) for i in range(2)]
        npt = 0
        ncomb = 0
        nwin = 0
        for qc in range(self.NCH):
            t0 = qc * 512
            for two in range(2):
                self.load(Q8, Q8.ap[two * 64:(two + 1) * 64, :].rearrange("p (h t) -> p h t", h=4),
                          nsq[two::2, :, t0:t0 + 512].rearrange("h r t -> r h t"))
            kw, vw = KW[qc % 2], VW[qc % 2]
            lo = max(t0 - 512, 0)
            nwt = (t0 + 512 - lo) // 128
            slot0 = 8 - nwt
            for g in range(2):
                self.load(kw, kw.ap[0:64, g * 1024 + slot0 * 128:(g + 1) * 1024], kwT[g][:, lo:t0 + 512])
                self.load(kw, kw.ap[64:128, g * 1024 + slot0 * 128:(g + 1) * 1024], kwT[g][:, lo:t0 + 512])
                self.load(vw, vw.ap.rearrange("p (g t c) -> p g t c", g=2, c=65)[:, g, slot0:8, 0:64],
                          vwv[g][lo:t0 + 512, :].rearrange("(t p) d -> p t d", p=128))
            for g in range(2):
                nct = min(NCT, (32 * qc + 31 + 127) // 128)
                for hh in range(4):
                    h = 4 * g + hh
                    gr = GR[hh]
                    self.load(gr, gr.ap[64:65, :].rearrange("p (r t) -> p r t", r=3),
                              gT[3 * h:3 * h + 3, t0:t0 + 512].rearrange("(o r) t -> o r t", o=1))
                    r0 = (h % 2) * 64
                    qh = Q8.ap[r0:r0 + 64, (h // 2) * 512:(h // 2 + 1) * 512]
                    Oc = self.ps[4 + hh % 2]
                    for ct in range(nct):
                        ps = self.psum(0, 4)
                        self.mm(ps, ps.ap, KCM, KCM.ap[r0:r0 + 64, g * NCP + ct * 128: g * NCP + (ct + 1) * 128], Q8, qh, True, True)
                        Dv = 512 * qc - 2048 * ct
                        e_ = ET[ct]
                        if Dv <= 2048:
                            sb = ssb[ct % 2]
                            mi = Dv // 512
                            self.tt("vector", sb, sb.ap[:, 0:512], ps, ps.ap, MM, MM.ap[:, mi * 512:(mi + 1) * 512], ALU.min)
                            self.act(e_, e_.ap, sb, sb.ap[:, 0:512], AF.Exp, scale=sc)
                        else:
                            self.act(e_, e_.ap, ps, ps.ap, AF.Exp, scale=sc)
                        self.mm(Oc, Oc.ap[0:65, :], VCM, VCM.ap[:, (g * NCT + ct) * 65:(g * NCT + ct) * 65 + 65], e_, e_.ap, ct == 0, ct == nct - 1)
                    for s in range(4):
                        ib = self.ps[6 + s // 2]
                        for ct in range(nct):
                            self.mm(ib, ib.ap[:, (s % 2) * (NB + 1):(s % 2 + 1) * (NB + 1)], ET[ct], ET[ct].ap[:, s * 128:(s + 1) * 128],
                                    OV1, OV1.ap[:, ct * (NB + 1):(ct + 1) * (NB + 1)], ct == 0, ct == nct - 1)
                    for bnk in range(2):
                        ib = self.ps[6 + bnk]
                        v3 = ib.ap[:, 0:2 * (NB + 1)].rearrange("p (s n) -> p s n", s=2)
                        ri = rI.ap[:, bnk * 2:(bnk + 1) * 2]
                        self.ts("vector", rI, ri, ib, v3[:, :, NB], 1e-30, ALU.max)
                        P.op("vector", lambda e, o=ri: e.reciprocal(out=o, in_=o), reads=[rI], writes=[rI])
                        for s2 in range(2):
                            s = bnk * 2 + s2
                            dst = ACC[g].ap[:, s * NB:(s + 1) * NB]
                            if hh == 0:
                                self.ts("vector", ACC[g], dst, ib, v3[:, s2, 0:NB], rI.ap[:, s:s + 1], ALU.mult, extra_reads=[rI])
                            else:
                                self.stt(ACC[g], dst, ib, v3[:, s2, 0:NB], rI.ap[:, s:s + 1], ACC[g], dst, ALU.mult, ALU.add, extra_reads=[rI])
                    self.combine(Oc, gr, 0, OACC[hh], True, rec[ncomb % 2], recB, ctmp, 0, 4)
                    ncomb += 1
                psT = self.psum(0, 4)
                psTb = psT.ap.bitcast(BF16)
                for s in range(4):
                    gs = 4 * qc + s
                    off = 2 * (NT - 1) - 2 * gs
                    a_ = adj[s % 2]
                    self.tt("vector", a_, a_.ap, ACC[g], ACC[g].ap[:, s * NB:(s + 1) * NB], TW1, TW1.ap[:, off:off + NB], ALU.mult)
                    self.tt("vector", a_, a_.ap, a_, a_.ap, TW2, TW2.ap[:, off:off + NB], ALU.add)
                    self.memset("vector", a_, a_.ap[:, 0:1], 1.0e4)
                    P.op("vector", lambda e, o=m8.ap[:, 0:8], i=a_.ap: e.max(out=o, in_=i), reads=[a_], writes=[m8])
                    P.op("vector", lambda e, o=tmpv.ap, r=m8.ap[:, 0:8], i=a_.ap: e.match_replace(out=o, in_to_replace=r, in_values=i, imm_value=-2.0e30),
                         reads=[m8, a_], writes=[tmpv])
                    P.op("vector", lambda e, o=m8.ap[:, 8:16], i=tmpv.ap: e.max(out=o, in_=i), reads=[tmpv], writes=[m8])
                    sb_ = selb[s % 2]
                    self.ts("vector", sb_, sb_.ap, a_, a_.ap, m8.ap[:, 15:16], ALU.is_ge, extra_reads=[m8])
                    P.op("tensor", lambda e, o=psTb[0:NB, s * 128:(s + 1) * 128], i=sb_.ap[:, 0:NB]: e.transpose(o, i, self.c_ident_bf.ap[:, 0:128]),
                         reads=[sb_, self.c_ident_bf], writes=[psT])
                self.copy("scalar", selT[g], selT[g].ap[0:NB, :], psT, psTb[0:NB, 0:512])
                OS = [self.ps[4 + hh] for hh in range(4)]
                pend = []
                pairs = self.causal_pairs(qc)
                npr = len(pairs)

                def emit_pv(item):
                    tp_, kt_, pt_, first_, last_ = item
                    vap = VS.ap[:, (g * NT + kt_) * 65:(g * NT + kt_) * 65 + 65]
                    self.mm(OS[2 * tp_], OS[2 * tp_].ap[0:65, :], VS, vap, pt_, pt_.ap[:, 0:512], first_, last_)
                    self.mm(OS[2 * tp_ + 1], OS[2 * tp_ + 1].ap[0:65, :], VS, vap, pt_, pt_.ap[:, 512:1024], first_, last_)
                for pi, (kta, ktb, minfo) in enumerate(pairs):
                    b0 = 2 * (self.nsp % 2)
                    self.nsp += 1
                    ma, mb = self.ps[b0], self.ps[b0 + 1]
                    self.mm(ma, ma.ap, EXP, EXP.ap[0:NB, kta * 128:(kta + 1) * 128], selT[g], selT[g].ap[0:NB, :], True, True)
                    self.mm(mb, mb.ap, EXP, EXP.ap[0:NB, ktb * 128:(ktb + 1) * 128], selT[g], selT[g].ap[0:NB, :], True, True)
                    mk = mks[pi % 2]
                    P.op("scalar", lambda e, o=mk.ap, i=self.psall[:, b0 * 512:(b0 + 2) * 512]: e.copy(out=o, in_=i), reads=[ma, mb], writes=[mk])
                    for half, kt in ((0, kta), (1, ktb)):
                        for tp in range(2):
                            t = 2 * g + tp
                            b0 = 2 * (self.nsp % 2)
                            self.nsp += 1
                            pa, pb = self.ps[b0], self.ps[b0 + 1]
                            self.mm(pa, pa.ap, KS, KS.ap[0:64, g * S + kt * 128: g * S + (kt + 1) * 128], Q8, Q8.ap[0:64, t * 512:(t + 1) * 512], True, True)
                            self.mm(pb, pb.ap, KS, KS.ap[64:128, g * S + kt * 128: g * S + (kt + 1) * 128], Q8, Q8.ap[64:128, t * 512:(t + 1) * 512], True, True)
                            pair_ap = self.psall[:, b0 * 512:(b0 + 2) * 512]
                            pt = pts[npt % 4]
                            npt += 1
                            if minfo is not None:
                                sb = ssb[npt % 2]
                                o3 = sb.ap.rearrange("p (a c) -> p a c", a=2)
                                i3 = pair_ap.rearrange("p (a c) -> p a c", a=2)
                                m3 = minfo[1][:, half * 512:(half + 1) * 512].unsqueeze(1).to_broadcast([128, 2, 512])
                                P.op("vector", lambda e, o=o3, i0=i3, i1=m3: e.tensor_tensor(out=o, in0=i0, in1=i1, op=ALU.min),
                                     reads=[pa, pb, minfo[0]], writes=[sb])
                                P.op("scalar", lambda e, o=pt.ap, i=sb.ap: e.activation(out=o, in_=i, func=AF.Exp, scale=sc), reads=[sb], writes=[pt])
                            else:
                                P.op("scalar", lambda e, o=pt.ap, i=pair_ap: e.activation(out=o, in_=i, func=AF.Exp, scale=sc), reads=[pa, pb], writes=[pt])
                            p3 = pt.ap.rearrange("p (a c) -> p a c", a=2)
                            k3 = mk.ap[:, half * 512:(half + 1) * 512].unsqueeze(1).to_broadcast([128, 2, 512])
                            P.op("vector", lambda e, o=p3, i1=k3: e.tensor_tensor(out=o, in0=o, in1=i1, op=ALU.mult), reads=[pt, mk], writes=[pt])
                            pend.append((tp, kt, pt, pi == 0 and half == 0, pi == npr - 1 and half == 1))
                            if len(pend) > 1:
                                emit_pv(pend.pop(0))
                while pend:
                    emit_pv(pend.pop(0))
                for hh in range(4):
                    self.combine(OS[hh], GR[hh], 1, OACC[hh], False, rec[ncomb % 2], recB, ctmp, 0, 4)
                    ncomb += 1
                for tp in range(2):
                    t = 2 * g + tp
                    wl = []
                    if qc > 0:
                        for o in range(4):
                            wl.append((o, (MW, MW.ap[:, o * 512:(o + 1) * 512], "b")))
                    for o in range(4):
                        wl.append((4 + o, (self.c_maskc, self.c_maskc.ap[:, o * 512:(o + 1) * 512], "b")))
                    pb4 = 4 + 2 * (nwin % 2)
                    nwin += 1
                    Ow = (self.ps[pb4], self.ps[pb4 + 1])
                    self.attn_pass_dual(
                        ((kw, lambda sl, g=g: kw.ap[0:64, g * 1024 + sl * 128: g * 1024 + (sl + 1) * 128]),
                         (kw, lambda sl, g=g: kw.ap[64:128, g * 1024 + sl * 128: g * 1024 + (sl + 1) * 128])),
                        (Q8, Q8.ap[0:64, t * 512:(t + 1) * 512], Q8.ap[64:128, t * 512:(t + 1) * 512]),
                        vw, lambda sl, j_, g=g: vw.ap[:, (g * 8 + sl) * 65:(g * 8 + sl) * 65 + 65], 65, sc, Ow[0], Ow[1], wl,
                        ptiles=pts, ssb=ssb)
                    for two in range(2):
                        hh = 2 * tp + two
                        h = 4 * g + hh
                        self.combine(Ow[two], GR[hh], 2, OACC[hh], False, rec[ncomb % 2], recB, ctmp, 0, 4)
                        ncomb += 1
                        o_ = ob[hh % 2]
                        self.copy("scalar", o_, o_.ap[0:64, :], OACC[hh], OACC[hh].ap[0:64, :])
                        self.store(oT[h // 2, (h % 2) * 64:(h % 2) * 64 + 64, t0:t0 + 512], o_, o_.ap[0:64, :])
        self.flush_deferred()
        P.barrier()

    def aug_head_chunk(self, kt_, v_, q_, rows, scale, qc, Oacc, pts, ssb, r_, recB, o_, dst_ap):
        P = self.P
        kl = self.causal_pairs(qc)
        self.attn_pass(kt_, lambda kt: kt_.ap[0:rows, kt * 128:(kt + 1) * 128], q_, q_.ap[0:rows, :],
                       v_, lambda kt: v_.ap[:, kt * 129: kt * 129 + 65], 65, scale, Oacc, kl, ptiles=pts, ssb=ssb)
        P.op("vector", lambda e, o=r_.ap[64:65, :], i=Oacc.ap[64:65, :]: e.reciprocal(out=o, in_=i), reads=[Oacc], writes=[r_])

        def fin():
            pb = self.ps[6]
            self.mm(pb, pb.ap[0:64, :], self.c_ones_f, self.c_ones_f.ap[64:65, 0:64], r_, r_.ap[64:65, :], True, True)
            self.copy("scalar", recB, recB.ap[0:64, :], pb, pb.ap[0:64, :])
            self.tt("vector", o_, o_.ap[0:64, :], Oacc, Oacc.ap[0:64, :], recB, recB.ap[0:64, :], ALU.mult)
            self.store(dst_ap, o_, o_.ap[0:64, :])
        self.defer(4, fin)

    def phase_odd_mla(self, li):
        A, P, S, NT = self.A, self.P, self.S, self.NT
        A.reset(self.base_off)
        mq, mk, mv, oT = self.scr["mq"].ap(), self.scr["mk"].ap(), self.scr["mv"].ap(), self.scr["oT"].ap()
        KT = [A.alloc(S, BF16, "KT%d" % i) for i in range(2)]
        V = [A.alloc(NT * 129, BF16, "V%d" % i) for i in range(2)]
        Q = [A.alloc(512, BF16, "Q%d" % i) for i in range(2)]
        pts = [A.alloc(1024, BF16, "pt%d" % i) for i in range(4)]
        ssb = [A.alloc(1024, F32, "ssb%d" % i) for i in range(2)]
        rec = [A.alloc(512, F32, "rec%d" % i) for i in range(2)]
        recB = A.alloc(512, F32, "recB")
        ob = [A.alloc(512, BF16, "ob%d" % i) for i in range(2)]
        for i in range(2):
            self.memset("gpsimd", V[i], V[i].ap, 1.0)

        def load_unit(h, slot):
            self.load(KT[slot], KT[slot].ap[0:96, :], mk[h])
            self.load(V[slot], V[slot].ap.rearrange("p (t c) -> p t c", c=129)[:, :, 0:64], mv[h].rearrange("(t p) d -> p t d", p=128))
        load_unit(0, 0)
        n = 0
        for h in range(8):
            slot = h % 2
            if h + 1 < 8:
                load_unit(h + 1, 1 - slot)
            for qc in range(self.NCH):
                q_ = Q[n % 2]
                t0 = qc * 512
                self.load(q_, q_.ap[0:96, :], mq[h, :, t0:t0 + 512])
                self.aug_head_chunk(KT[slot], V[slot], q_, 96, 96.0 ** -0.5, qc, self.ps[4 + n % 2], pts, ssb, rec[n % 2], recB, ob[n % 2],
                                    oT[4 + h // 2, (h % 2) * 64:(h % 2) * 64 + 64, t0:t0 + 512])
                n += 1
        self.flush_deferred()
        P.barrier()

    def build(self):
        S = self.S
        nc = self.nc
        self.inp("x", [S, D])
        self.inp("mem", [MEM, D])
        self.inp("mem_norm", [D])
        for nm in ("norm_mix", "norm_mem", "norm_ffn"):
            self.inp(nm, [DEPTH, D])
        self.inp("ev_w_in", [2, D, EV_W])
        self.inp("ev_b_f", [2, 8])
        self.inp("ev_lam", [2, 4, 64])
        self.inp("ev_subln", [2, 128])
        self.inp("ev_w_out", [2, D, D])
        self.inp("od_w_in", [2, D, OD_W])
        self.inp("nsa_cmp_pos", [2, 2, 32, 64])
        self.inp("nsa_cmp_w1", [2, 2, 2048, 128])
        self.inp("nsa_cmp_w2", [2, 2, 128, 64])
        self.inp("mla_q_norm", [2, 384])
        self.inp("mla_kv_norm", [2, 256])
        self.inp("mla_w_uq", [2, 384, 768])
        self.inp("mla_w_ukv", [2, 256, 1024])
        self.inp("od_w_out", [2, D, D])
        self.inp("xa_wq", [DEPTH, D, 512])
        self.inp("xa_wkv", [DEPTH, D, D])
        self.inp("xa_wo", [DEPTH, 512, D])
        self.inp("ffn_w13", [DEPTH, D, 2 * DFF])
        self.inp("ffn_w2", [DEPTH, DFF, D])
        self.inp("final_norm", [D])
        self.inp("c_gains", [128, (3 * DEPTH + 1) * KC])
        self.inp("c_ident", [128, 128])
        self.inp("c_maskc", [128, 4 * 512])
        self.inp("c_cos64", [128, S])
        self.inp("c_sin64", [128, S])
        NB = S // 64
        self.inp("c_cosm", [96, S])
        self.inp("c_sinm", [96, S])
        self.inp("c_mlan", [2, 128, 5])
        self.inp("c_posT", [2, 64, 64])
        self.inp("c_ov1", [128, self.NCT * (NB + 1)])
        self.inp("c_expand", [128, S])
        self.inp("c_maskw", [128, 4 * 512])
        self.inp("c_maskm", [128, 5 * 512])
        self.inp("c_tw1", [128, 2 * (self.NT - 1) + NB])
        self.inp("c_tw2", [128, 2 * (self.NT - 1) + NB])
        self.dout = nc.dram_tensor("out", [S, D], F32, kind="ExternalOutput")
        self.scratch("xT", [KC, 128, S], F32)
        self.scratch("oT", [KC, 128, S], BF16)
        self.scratch("dq", [4, 128, S], BF16)
        self.scratch("dk", [4, 128, S], BF16)
        self.scratch("dv", [4, S, 128], BF16)
        self.scratch("fq", [8, 70, S], BF16)
        self.scratch("fk", [8, 70, S], BF16)
        self.scratch("fv", [8, S, 64], BF16)
        self.scratch("lfp", [8, S], F32)
        self.scratch("nsq", [8, 64, S], BF16)
        for nm in ("kcT", "vcT", "ksT", "kwT"):
            self.scratch(nm, [2, 64, S], BF16)
        self.scratch("vsv", [2, S, 64], BF16)
        self.scratch("vwv", [2, S, 64], BF16)
        self.scratch("gT", [24, S], F32)
        self.scratch("mq", [8, 96, S], BF16)
        self.scratch("mk", [8, 96, S], BF16)
        self.scratch("mv", [8, S, 64], BF16)
        self.scratch("kcmp", [2, 64, self.NCP], BF16)
        self.scratch("vcmp", [2, self.NCP, 64], BF16)

        A = self.A
        self.c_eps = A.alloc(8, F32, "eps")
        self.c_one = A.alloc(8, F32, "one")
        self.memset("vector", self.c_eps, self.c_eps.ap, EPS)
        self.memset("vector", self.c_one, self.c_one.ap, 1.0)
        self.c_memT = A.alloc(KC * MEM, BF16, "memT")
        self.setup_consts()
        self.phase_init()
        for li in self.layers:
            if li % 2 == 0:
                self.phase_even_in(li)
                self.phase_even_scan(li)
                self.phase_even_attn(li)
                self.phase_out_xattn(li, self.din["ev_w_out"].ap()[li // 2])
            else:
                self.phase_odd_in(li)
                self.phase_odd_cmp(li)
                self.phase_odd_nsa(li)
                self.phase_odd_mla(li)
                self.phase_out_xattn(li, self.din["od_w_out"].ap()[li // 2])
            self.phase_ffn(li)
        self.phase_final()
        self.P.emit()
        return nc


def ones_f_ap(b):
    return b.c_ones_f.ap


def host_gains(inputs):
    cols = []
    for nm in ("norm_mix", "norm_mem", "norm_ffn"):
        g = np.asarray(inputs[nm], dtype=np.float32)
        for li in range(DEPTH):
            cols.append(g[li].reshape(KC, 128).T)
    cols.append(np.asarray(inputs["final_norm"], dtype=np.float32).reshape(KC, 128).T)
    return np.ascontiguousarray(np.concatenate(cols, axis=1))


def host_consts(S):
    c = {}
    c["c_ident"] = np.eye(128, dtype=np.float32)
    m = np.zeros((128, 4, 512), np.float32)
    p = np.arange(128)[:, None]
    q = np.arange(512)[None, :]
    for o in range(4):
        m[:, o, :] = np.where(o * 128 + p <= q, BIG, -BIG)
    c["c_maskc"] = m.reshape(128, 2048)
    cs, sn = rope_tables(S, 128, 32, 2)
    c["c_cos64"] = cs
    c["c_sin64"] = sn
    c32, s32 = rope_tables(S, 32, 16, 1)
    c["c_cosm"] = np.ascontiguousarray(np.concatenate([np.ones((64, S), np.float32), c32], axis=0))
    c["c_sinm"] = np.ascontiguousarray(np.concatenate([np.zeros((64, S), np.float32), s32], axis=0))
    NB = S // 64
    NT = S // 128
    NC = (S - 32) // 16 + 1
    NCT = (NC + 127) // 128
    cidx = np.arange(NCT * 128)
    ov = ((cidx[:, None] * 16 < np.arange(NB)[None, :] * 64 + 64) & (cidx[:, None] * 16 + 32 > np.arange(NB)[None, :] * 64)
          & (cidx[:, None] < NC)).astype(np.float32)
    ov1 = np.concatenate([ov, np.ones((NCT * 128, 1), np.float32)], axis=1)
    c["c_ov1"] = np.ascontiguousarray(ov1.reshape(NCT, 128, NB + 1).transpose(1, 0, 2).reshape(128, NCT * (NB + 1)))
    c["c_expand"] = (np.arange(128)[:, None] == (np.arange(S)[None, :] // 64)).astype(np.float32)
    mw = np.zeros((128, 4, 512), np.float32)
    for o in range(4):
        mw[:, o, :] = np.where(q < 128 * o + p, BIG, -BIG)
    c["c_maskw"] = mw.reshape(128, 2048)
    mm_ = np.zeros((128, 5, 512), np.float32)
    for i in range(5):
        mm_[:, i, :] = np.where(16 * p + 31 - q <= 512 * i, BIG, -BIG)
    c["c_maskm"] = mm_.reshape(128, 2560)
    TWW = 2 * (NT - 1) + NB
    RO = 2 * (NT - 1)
    r = np.arange(TWW)[None, :] - RO
    e = (np.arange(128)[:, None] >= 64).astype(np.int64)
    fut = r > e
    forced = (r == e) | (r == e - 1)
    c["c_tw1"] = np.where(fut | forced, 0.0, 1.0).astype(np.float32)
    c["c_tw2"] = np.where(fut, -BIG, np.where(forced, 1.0e4, 0.0)).astype(np.float32)
    return c


def host_layouts(inputs):
    c = {}
    qn = np.asarray(inputs["mla_q_norm"], dtype=np.float32)
    kn = np.asarray(inputs["mla_kv_norm"], dtype=np.float32)
    ml = np.zeros((2, 128, 5), np.float32)
    for j in range(2):
        ml[j, :, 0:3] = qn[j].reshape(3, 128).T
        ml[j, :, 3:5] = kn[j].reshape(2, 128).T
    c["c_mlan"] = ml
    pos = np.asarray(inputs["nsa_cmp_pos"], dtype=np.float32)
    c["c_posT"] = np.ascontiguousarray(pos.transpose(0, 3, 1, 2).reshape(2, 64, 64))
    return c


_CACHE = {}


def run(inputs, S, layers, n_cores, batch_ids):
    key = (S, tuple(layers))
    if key not in _CACHE:
        import time as _t
        _t0 = _t.time()
        _b = Builder(S, layers)
        _CACHE[key] = _b.build()
        print("[build] %.1fs ninst=%d" % (_t.time() - _t0, _b.P.ninst), {e: len(v) for e, v in _b.P.q.items()}, flush=True)
    nc = _CACHE[key]
    consts = host_consts(S)
    consts["c_gains"] = host_gains(inputs)
    consts.update(host_layouts(inputs))
    in_maps = []
    for b in batch_ids:
        m = {}
        for k, v in inputs.items():
            v = np.asarray(v)
            if k == "x" or k == "mem":
                m[k] = np.ascontiguousarray(v[b], dtype=np.float32)
            else:
                m[k] = np.ascontiguousarray(v, dtype=np.float32)
        m.update(consts)
        in_maps.append(m)
    import time as _t
    _t0 = _t.time()
    res = run_bass_kernel_spmd(nc, in_maps, core_ids=list(range(n_cores)))
    print("[run] spmd launch+compile %.1fs" % (_t.time() - _t0), flush=True)
    return [r["out"] for r in res.results]


def kernel(**inputs):
    x = np.asarray(inputs["x"])
    B, S, _ = x.shape
    outs = run(inputs, S, list(range(DEPTH)), 8, [0, 1, 2, 3, 0, 1, 2, 3])
    return np.stack(outs[:4], axis=0).astype(np.float32)
```

```python
import math
import numpy as np
import ml_dtypes
import concourse.bass as bass
import concourse.mybir as mybir
from concourse.bass_utils import run_bass_kernel_spmd

F32 = mybir.dt.float32
BF16 = mybir.dt.bfloat16
AF = mybir.ActivationFunctionType
ALU = mybir.AluOpType

D = 1024
KC = 8
DEPTH = 4
MEM = 256
EPS = 1e-6
THETA = 10000.0
DFF = 2816
EV_W = 3080
OD_W = 1976
BIG = 1.0e30
KEEPWARM = False

ENGS = ("sync", "scalar", "vector", "gpsimd", "tensor")


class Buf:
    __slots__ = ("name", "w", "r")

    def __init__(self, name):
        self.name = name
        self.w = None
        self.r = []


class Tile:
    __slots__ = ("ap", "buf")

    def __init__(self, ap, name="t"):
        self.ap = ap
        self.buf = Buf(name)

    def __getitem__(self, k):
        return self.ap[k]


class Prog:
    NDMA = 12

    def __init__(self, nc):
        self.nc = nc
        self.q = {e: [] for e in ENGS}
        self.cnt = {e: 0 for e in ENGS}
        self.sem = {e: nc.alloc_semaphore("es_" + e) for e in ENGS}
        self.semid = {e: ("E", e) for e in ENGS}
        self.waited = {e: {} for e in ENGS}
        self.dsem = {}
        self.dval = {}
        self.dnext = {}
        for qn in ("sync", "gpsimd"):
            self.dsem[qn] = [nc.alloc_semaphore("ds_%s_%d" % (qn, i)) for i in range(self.NDMA)]
            self.dval[qn] = [0] * self.NDMA
            self.dnext[qn] = 0
        self.ninst = 0

    def _need(self, eng, tok, kind):
        key, sh, val, teng = tok
        if teng == eng:
            if eng == "tensor" and kind == "waw":
                return
        if self.waited[eng].get(key, 0) >= val:
            return
        self.waited[eng][key] = val
        self.q[eng].append(lambda e, sh=sh, val=val: e.wait_ge(sh, val))

    def _deps(self, eng, reads, writes):
        for t in reads:
            b = t.buf
            if b.w is not None:
                self._need(eng, b.w, "raw")
        for t in writes:
            b = t.buf
            if b.w is not None:
                self._need(eng, b.w, "waw")
            for tok in b.r:
                self._need(eng, tok, "war")

    def _mark(self, tok, reads, writes):
        for t in reads:
            t.buf.r.append(tok)
        for t in writes:
            t.buf.w = tok
            t.buf.r = []

    def op(self, eng, fn, reads=(), writes=()):
        self._deps(eng, reads, writes)
        self.cnt[eng] += 1
        sh = self.sem[eng]
        self.q[eng].append(lambda e, fn=fn, sh=sh: fn(e).then_inc(sh, 1))
        tok = (self.semid[eng], sh, self.cnt[eng], eng)
        self._mark(tok, reads, writes)
        self.ninst += 1

    def dma(self, qn, out, in_, reads=(), writes=()):
        self._deps(qn, reads, writes)
        i = self.dnext[qn]
        self.dnext[qn] = (i + 1) % self.NDMA
        sh = self.dsem[qn][i]
        key = ("D", qn, i)
        prev = self.dval[qn][i]
        if prev > 0 and self.waited[qn].get(key, 0) < prev:
            self.waited[qn][key] = prev
            self.q[qn].append(lambda e, sh=sh, prev=prev: e.wait_ge(sh, prev))
        val = prev + 16
        self.dval[qn][i] = val
        self.q[qn].append(lambda e, out=out, in_=in_, sh=sh: e.dma_start(out=out, in_=in_).then_inc(sh, 16))
        tok = (key, sh, val, None)
        self._mark(tok, reads, writes)
        self.ninst += 1

    def barrier(self):
        toks = []
        for e in ENGS:
            if self.cnt[e] > 0:
                toks.append((self.semid[e], self.sem[e], self.cnt[e], e))
        for qn in self.dsem:
            for i in range(self.NDMA):
                if self.dval[qn][i] > 0:
                    toks.append((("D", qn, i), self.dsem[qn][i], self.dval[qn][i], None))
        for e in ENGS:
            for tok in toks:
                if tok[3] == e:
                    continue
                self._need(e, tok, "raw")

    def emit(self):
        nc = self.nc
        with nc.Block() as block:
            for e in ENGS:
                lst = self.q[e]

                def body(eng, lst=lst):
                    for f in lst:
                        f(eng)
                getattr(block, e)(body)


class Arena:
    def __init__(self, nc, nbytes):
        self.t = nc.alloc_sbuf_tensor("arena", [128, nbytes // 4], F32)
        self.cap = nbytes // 4
        self.off = 0

    def reset(self, off=0):
        self.off = off

    def alloc(self, free_elems, dtype, name="t"):
        nb = free_elems * (2 if dtype == BF16 else 4)
        nw = (nb + 3) // 4
        nw = (nw + 7) // 8 * 8
        assert self.off + nw <= self.cap, ("SBUF arena overflow", name, self.off, nw, self.cap)
        ap = self.t[:, self.off:self.off + nw]
        self.off += nw
        if dtype == BF16:
            ap = ap.bitcast(BF16)[:, 0:free_elems]
        else:
            ap = ap[:, 0:free_elems]
        return Tile(ap, name)


def rope_tables(S, rows, half, pairs_per):
    inv = (1.0 / (np.float32(THETA) ** (np.arange(half, dtype=np.float32) / np.float32(half)))).astype(np.float32)
    pos = np.arange(S, dtype=np.float32)
    ang = (pos[None, :] * inv[:, None]).astype(np.float32)
    c = np.cos(ang).astype(np.float32)
    s = np.sin(ang).astype(np.float32)
    blk_c = np.concatenate([c, c], axis=0)
    blk_s = np.concatenate([-s, s], axis=0)
    reps = rows // (2 * half)
    return np.tile(blk_c, (reps, 1)), np.tile(blk_s, (reps, 1))


class Builder:
    def __init__(self, S, layers, with_final=True):
        self.S = S
        self.NCH = S // 512
        self.NT = S // 128
        self.layers = layers
        self.with_final = with_final
        nc = bass.Bass("TRN2", target_bir_lowering=False)
        self.nc = nc
        self.P = Prog(nc)
        self.A = Arena(nc, 206 * 1024)
        self.psall = nc.alloc_psum_tensor("psall", [128, 8 * 512], F32).ap()
        self.ps = [Tile(self.psall[:, i * 512:(i + 1) * 512], "ps%d" % i) for i in range(8)]
        self.nsp = 0
        self.deferred = []
        self.pass_id = 0
        self.psn = 0
        self.npt = 0
        self.NC = (S - 32) // 16 + 1
        self.NCT = (self.NC + 127) // 128
        self.NCP = self.NCT * 128
        self.din = {}
        self.scr = {}

    def inp(self, name, shape, dt=F32):
        t = self.nc.dram_tensor(name, list(shape), dt, kind="ExternalInput")
        self.din[name] = t
        return t

    def scratch(self, name, shape, dt):
        t = self.nc.dram_tensor(name, list(shape), dt)
        self.scr[name] = t
        return t

    def psum(self, lo=0, hi=8):
        n = hi - lo
        i = lo + (self.psn % n)
        self.psn += 1
        return self.ps[i]

    def defer(self, steps, fn):
        self.deferred.append([steps, fn, self.pass_id])

    def begin_pass(self):
        self.pass_id += 1
        run = [it for it in self.deferred if it[2] <= self.pass_id - 2]
        self.deferred = [it for it in self.deferred if it[2] > self.pass_id - 2]
        for it in run:
            it[1]()

    def tick(self):
        if not self.deferred:
            return
        run = []
        keep = []
        for it in self.deferred:
            it[0] -= 1
            (run if it[0] <= 0 else keep).append(it)
        self.deferred = keep
        for it in run:
            it[1]()

    def flush_deferred(self):
        while self.deferred:
            d = self.deferred
            self.deferred = []
            for it in d:
                it[1]()

    def mm(self, out_t, out_ap, lhsT_t, lhsT_ap, rhs_t, rhs_ap, start, stop):
        self.P.op("tensor",
                  lambda e: e.matmul(out_ap, lhsT=lhsT_ap, rhs=rhs_ap, start=start, stop=stop),
                  reads=[lhsT_t, rhs_t], writes=[out_t])

    def act(self, out_t, out_ap, in_t, in_ap, func, scale=1.0, bias=0.0, extra_reads=()):
        self.P.op("scalar",
                  lambda e: e.activation(out=out_ap, in_=in_ap, func=func, bias=bias, scale=scale),
                  reads=[in_t] + list(extra_reads), writes=[out_t])

    def tt(self, eng, out_t, out_ap, a_t, a_ap, b_t, b_ap, op):
        self.P.op(eng, lambda e: e.tensor_tensor(out=out_ap, in0=a_ap, in1=b_ap, op=op),
                  reads=[a_t, b_t], writes=[out_t])

    def ts(self, eng, out_t, out_ap, a_t, a_ap, s1, op0, s2=None, op1=None, extra_reads=()):
        if op1 is None:
            self.P.op(eng, lambda e: e.tensor_scalar(out=out_ap, in0=a_ap, scalar1=s1, scalar2=None, op0=op0),
                      reads=[a_t] + list(extra_reads), writes=[out_t])
        else:
            self.P.op(eng, lambda e: e.tensor_scalar(out=out_ap, in0=a_ap, scalar1=s1, scalar2=s2, op0=op0, op1=op1),
                      reads=[a_t] + list(extra_reads), writes=[out_t])

    def stt(self, out_t, out_ap, a_t, a_ap, scalar, b_t, b_ap, op0, op1, extra_reads=()):
        self.P.op("vector",
                  lambda e: e.scalar_tensor_tensor(out=out_ap, in0=a_ap, scalar=scalar, in1=b_ap, op0=op0, op1=op1),
                  reads=[a_t, b_t] + list(extra_reads), writes=[out_t])

    def copy(self, eng, out_t, out_ap, in_t, in_ap):
        if eng == "scalar":
            self.P.op("scalar", lambda e: e.copy(out=out_ap, in_=in_ap), reads=[in_t], writes=[out_t])
        else:
            self.P.op(eng, lambda e: e.tensor_copy(out=out_ap, in_=in_ap), reads=[in_t], writes=[out_t])

    def memset(self, eng, t, ap, val):
        self.P.op(eng, lambda e: e.memset(ap, val), reads=[], writes=[t])

    def load(self, t, ap, src):
        self.P.dma("sync", ap, src, reads=[], writes=[t])

    def store(self, dst, t, ap):
        self.P.dma("gpsimd", dst, ap, reads=[t], writes=[])

    def setup_consts(self):
        A = self.A
        S = self.S
        self.c_ones_bf = A.alloc(128, BF16, "ones_bf")
        self.c_ones_f = A.alloc(128, F32, "ones_f")
        self.c_ident_f = A.alloc(128, F32, "ident_f")
        self.c_ident_bf = A.alloc(128, BF16, "ident_bf")
        self.c_maskc = A.alloc(4 * 512, F32, "maskc")
        self.memset("vector", self.c_ones_bf, self.c_ones_bf.ap, 1.0)
        self.memset("vector", self.c_ones_f, self.c_ones_f.ap, 1.0)
        self.load(self.c_ident_f, self.c_ident_f.ap, self.din["c_ident"].ap())
        self.copy("vector", self.c_ident_bf, self.c_ident_bf.ap, self.c_ident_f, self.c_ident_f.ap)
        self.load(self.c_maskc, self.c_maskc.ap, self.din["c_maskc"].ap())
        self.c_gain = A.alloc(3 * DEPTH * KC + 2 * KC, F32, "gains")
        self.load(self.c_gain, self.c_gain.ap[:, 0:(3 * DEPTH + 1) * KC], self.din["c_gains"].ap())
        self.base_off = A.off

    def gain(self, kind, li):
        o = (kind * DEPTH + li) * KC
        return self.c_gain.ap[:, o:o + KC]

    def rmsnorm_fm(self, X, nk, ntok, gain_ap, hT, sq, rstd, nfeat, gain_t=None):
        gt = gain_t if gain_t is not None else self.c_gain
        self.P.op("scalar", lambda e: e.activation(out=sq.ap[:, 0:nk * ntok], in_=X.ap[:, 0:nk * ntok], func=AF.Square),
                  reads=[X], writes=[sq])
        ps = self.psum()
        for k in range(nk):
            self.mm(ps, ps.ap[:, 0:ntok], self.c_ones_bf, self.c_ones_bf.ap[:, 0:128], sq, sq.ap[:, k * ntok:(k + 1) * ntok],
                    start=(k == 0), stop=(k == nk - 1))
        self.act(rstd, rstd.ap[:, 0:ntok], ps, ps.ap[:, 0:ntok], AF.Ln, scale=1.0 / nfeat, bias=self.c_eps.ap[:, 0:1],
                 extra_reads=[self.c_eps])
        self.act(rstd, rstd.ap[:, 0:ntok], rstd, rstd.ap[:, 0:ntok], AF.Exp, scale=-0.5)
        for k in range(nk):
            self.stt(hT, hT.ap[:, k * ntok:(k + 1) * ntok], X, X.ap[:, k * ntok:(k + 1) * ntok], gain_ap[:, k:k + 1],
                     rstd, rstd.ap[:, 0:ntok], ALU.mult, ALU.mult, extra_reads=[gt])

    def load_weight(self, W, wcols, src_ap, nk, ncols, col_off=0, stage=None):
        i = 0
        c0 = 0
        while c0 < ncols:
            cw = min(1024, ncols - c0)
            for k in range(nk):
                st = stage[i % 2]
                i += 1
                self.load(st, st.ap[:, 0:cw], src_ap[k * 128:(k + 1) * 128, c0:c0 + cw])
                dst = W.ap[:, k * wcols + col_off + c0: k * wcols + col_off + c0 + cw]
                eng = ("vector", "scalar")[i % 2]
                self.copy(eng, W, dst, st, st.ap[:, 0:cw])
            c0 += cw

    def phase_init(self):
        A, P, S = self.A, self.P, self.S
        A.reset(self.base_off)
        xin = self.din["x"].ap()
        xT = self.scr["xT"].ap()
        xa = [A.alloc(D, F32, "xin%d" % i) for i in range(2)]
        st = [A.alloc(KC * 512, F32, "xst%d" % i) for i in range(2)]
        for c in range(self.NCH):
            stg = st[c % 2]
            for s in range(4):
                t = c * 4 + s
                xt = xa[t % 2]
                self.load(xt, xt.ap, xin[t * 128:(t + 1) * 128, :])
                for kk in range(2):
                    ps = self.psum()
                    for j in range(4):
                        k = kk * 4 + j
                        P.op("tensor", lambda e, o=ps.ap[:, j * 128:(j + 1) * 128], i=xt.ap[:, k * 128:(k + 1) * 128]:
                             e.transpose(o, i, self.c_ident_f.ap[:, 0:128]), reads=[xt, self.c_ident_f], writes=[ps])
                    dst = stg.ap.rearrange("p (k t) -> p k t", k=KC)[:, kk * 4:(kk + 1) * 4, s * 128:(s + 1) * 128]
                    src = ps.ap.rearrange("p (j t) -> p j t", j=4)
                    self.copy(("vector", "scalar")[kk], stg, dst, ps, src)
            self.store(xT[:, :, c * 512:(c + 1) * 512].rearrange("k p t -> p k t"), stg,
                       stg.ap.rearrange("p (k t) -> p k t", k=KC))
        mem = self.din["mem"].ap()
        gB = A.alloc(D, F32, "memgain")
        self.load(gB, gB.ap, self.din["mem_norm"].ap().partition_broadcast(128))
        for s in range(2):
            mt = xa[s]
            self.load(mt, mt.ap, mem[s * 128:(s + 1) * 128, :])
            sqj = st[0]
            ss = A.alloc(8, F32, "memss%d" % s)
            P.op("scalar", lambda e, o=sqj.ap[:, 0:D], i=mt.ap, a=ss.ap[:, 0:1]: e.activation(out=o, in_=i, func=AF.Square, accum_out=a),
                 reads=[mt], writes=[sqj, ss])
            self.act(ss, ss.ap[:, 1:2], ss, ss.ap[:, 0:1], AF.Ln, scale=1.0 / D, bias=self.c_eps.ap[:, 0:1], extra_reads=[self.c_eps])
            self.act(ss, ss.ap[:, 2:3], ss, ss.ap[:, 1:2], AF.Exp, scale=-0.5)
            mn = st[1]
            self.stt(mn, mn.ap[:, 0:D], mt, mt.ap, ss.ap[:, 2:3], gB, gB.ap, ALU.mult, ALU.mult, extra_reads=[ss])
            for kk in range(2):
                ps = self.psum()
                for j in range(4):
                    k = kk * 4 + j
                    P.op("tensor", lambda e, o=ps.ap[:, j * 128:(j + 1) * 128], i=mn.ap[:, k * 128:(k + 1) * 128]:
                         e.transpose(o, i, self.c_ident_f.ap[:, 0:128]), reads=[mn, self.c_ident_f], writes=[ps])
                dst = self.c_memT.ap.rearrange("p (k t) -> p k t", k=KC)[:, kk * 4:(kk + 1) * 4, s * 128:(s + 1) * 128]
                self.copy("vector", self.c_memT, dst, ps, ps.ap.rearrange("p (j t) -> p j t", j=4))
        self.flush_deferred()
        P.barrier()

    def phase_final(self):
        A, P, S = self.A, self.P, self.S
        A.reset(self.base_off)
        xT = self.scr["xT"].ap()
        out = self.dout.ap()
        X = [A.alloc(KC * 512, F32, "fx%d" % i) for i in range(2)]
        Y = A.alloc(KC * 512, F32, "fy")
        sq = A.alloc(KC * 512, BF16, "fsq")
        rstd = A.alloc(512, F32, "frstd")
        ot = [A.alloc(D, F32, "fo%d" % i) for i in range(2)]
        g = self.c_gain.ap[:, 3 * DEPTH * KC: 3 * DEPTH * KC + KC]
        for c in range(self.NCH):
            x = X[c % 2]
            self.load(x, x.ap.rearrange("p (k t) -> p k t", k=KC), xT[:, :, c * 512:(c + 1) * 512].rearrange("k p t -> p k t"))
            self.rmsnorm_fm(x, KC, 512, g, Y, sq, rstd, D)
            for s in range(4):
                o = ot[s % 2]
                for kk in range(2):
                    ps = self.psum()
                    for j in range(4):
                        k = kk * 4 + j
                        P.op("tensor", lambda e, oo=ps.ap[:, j * 128:(j + 1) * 128], i=Y.ap[:, k * 512 + s * 128: k * 512 + (s + 1) * 128]:
                             e.transpose(oo, i, self.c_ident_f.ap[:, 0:128]), reads=[Y, self.c_ident_f], writes=[ps])
                    self.copy(("vector", "scalar")[kk], o, o.ap[:, kk * 512:(kk + 1) * 512], ps, ps.ap)
                t = c * 4 + s
                self.store(out[t * 128:(t + 1) * 128, :], o, o.ap)
        self.flush_deferred()
        P.barrier()

    def phase_ffn(self, li):
        A, P, S = self.A, self.P, self.S
        A.reset(self.base_off)
        NTK = 256
        NF = DFF // 128
        xT = self.scr["xT"].ap()
        W13 = A.alloc(KC * 2 * DFF, BF16, "w13")
        W2 = A.alloc(NF * D, BF16, "w2")
        stage = [A.alloc(1024, F32, "wst%d" % i) for i in range(2)]
        self.load_weight(W13, 2 * DFF, self.din["ffn_w13"].ap()[li], KC, 2 * DFF, stage=stage)
        self.load_weight(W2, D, self.din["ffn_w2"].ap()[li], NF, D, stage=stage)
        X = [A.alloc(KC * NTK, F32, "x%d" % i) for i in range(2)]
        hTs = [A.alloc(KC * NTK, BF16, "hT%d" % i) for i in range(2)]
        sq = A.alloc(KC * NTK, BF16, "sq")
        rstd = A.alloc(NTK, F32, "rstd")
        act = A.alloc(NF * NTK, BF16, "act")
        sg = [A.alloc(NTK, F32, "sg%d" % i) for i in range(2)]
        g = self.gain(2, li)
        nchunks = S // NTK

        def stage_a(c):
            x = X[c % 2]
            t0 = c * NTK
            self.load(x, x.ap.rearrange("p (k t) -> p k t", k=KC), xT[:, :, t0:t0 + NTK].rearrange("k p t -> p k t"))
            self.rmsnorm_fm(x, KC, NTK, g, hTs[c % 2], sq, rstd, D)

        stage_a(0)
        for c in range(nchunks):
            x = X[c % 2]
            hT = hTs[c % 2]
            t0 = c * NTK
            for f in range(NF):
                pg = self.psum()
                for k in range(KC):
                    self.mm(pg, pg.ap[:, 0:NTK], W13, W13.ap[:, k * 2 * DFF + f * 128: k * 2 * DFF + (f + 1) * 128],
                            hT, hT.ap[:, k * NTK:(k + 1) * NTK], k == 0, k == KC - 1)
                pu = self.psum()
                for k in range(KC):
                    self.mm(pu, pu.ap[:, 0:NTK], W13, W13.ap[:, k * 2 * DFF + DFF + f * 128: k * 2 * DFF + DFF + (f + 1) * 128],
                            hT, hT.ap[:, k * NTK:(k + 1) * NTK], k == 0, k == KC - 1)
                s_ = sg[f % 2]
                self.act(s_, s_.ap[:, 0:NTK], pg, pg.ap[:, 0:NTK], AF.Silu)
                self.tt("vector", act, act.ap[:, f * NTK:(f + 1) * NTK], pu, pu.ap[:, 0:NTK], s_, s_.ap[:, 0:NTK], ALU.mult)
            if c + 1 < nchunks:
                stage_a(c + 1)
            for n in range(KC):
                po = self.psum()
                for f in range(NF):
                    self.mm(po, po.ap[:, 0:NTK], W2, W2.ap[:, f * D + n * 128: f * D + (n + 1) * 128],
                            act, act.ap[:, f * NTK:(f + 1) * NTK], f == 0, f == NF - 1)
                self.tt("vector", x, x.ap[:, n * NTK:(n + 1) * NTK], po, po.ap[:, 0:NTK], x, x.ap[:, n * NTK:(n + 1) * NTK], ALU.add)
            self.store(xT[:, :, t0:t0 + NTK].rearrange("k p t -> p k t"), x, x.ap.rearrange("p (k t) -> p k t", k=KC))
        self.flush_deferred()
        P.barrier()

    def phase_out_xattn(self, li, w_out_ap):
        A, P, S = self.A, self.P, self.S
        A.reset(self.base_off)
        xT = self.scr["xT"].ap()
        oT = self.scr["oT"].ap()
        Wo = A.alloc(KC * D, BF16, "wo")
        Wq = A.alloc(KC * 512, BF16, "wq")
        Wx = A.alloc(4 * D, BF16, "wxo")
        Wkv = A.alloc(KC * D, BF16, "wkv")
        stage = [A.alloc(1024, F32, "wst%d" % i) for i in range(2)]
        self.load_weight(Wo, D, w_out_ap, KC, D, stage=stage)
        self.load_weight(Wq, 512, self.din["xa_wq"].ap()[li], KC, 512, stage=stage)
        self.load_weight(Wx, D, self.din["xa_wo"].ap()[li], 4, D, stage=stage)
        self.load_weight(Wkv, D, self.din["xa_wkv"].ap()[li], KC, D, stage=stage)
        KmT = A.alloc(4 * MEM, BF16, "kmT")
        Vm = A.alloc(2 * 512, BF16, "vm")
        memT = self.c_memT
        for h in range(4):
            ps = self.psum()
            for k in range(KC):
                self.mm(ps, ps.ap[:, 0:MEM], Wkv, Wkv.ap[:, k * D + h * 128: k * D + (h + 1) * 128],
                        memT, memT.ap[:, k * MEM:(k + 1) * MEM], k == 0, k == KC - 1)
            self.copy("vector", KmT, KmT.ap[:, h * MEM:(h + 1) * MEM], ps, ps.ap[:, 0:MEM])
        for s in range(2):
            ps = self.psum()
            for k in range(KC):
                self.mm(ps, ps.ap[:, 0:512], memT, memT.ap[:, k * MEM + s * 128: k * MEM + (s + 1) * 128],
                        Wkv, Wkv.ap[:, k * D + 512: k * D + 1024], k == 0, k == KC - 1)
            self.copy("vector", Vm, Vm.ap[:, s * 512:(s + 1) * 512], ps, ps.ap[:, 0:512])
        X = [A.alloc(KC * 512, F32, "x%d" % i) for i in range(2)]
        O = [A.alloc(KC * 512, BF16, "o%d" % i) for i in range(2)]
        hT = A.alloc(KC * 512, BF16, "hT")
        sq = A.alloc(KC * 512, BF16, "sq")
        rstd = A.alloc(512, F32, "rstd")
        qT = A.alloc(4 * 512, BF16, "qT")
        pT = [A.alloc(512, BF16, "pT%d" % i) for i in range(4)]
        rec = A.alloc(512, F32, "rec")
        xo = A.alloc(4 * 512, BF16, "xo")
        g = self.gain(1, li)
        sc = 128.0 ** -0.5
        for c in range(self.NCH):
            x = X[c % 2]
            o = O[c % 2]
            t0 = c * 512
            self.load(x, x.ap.rearrange("p (k t) -> p k t", k=KC), xT[:, :, t0:t0 + 512].rearrange("k p t -> p k t"))
            self.load(o, o.ap.rearrange("p (k t) -> p k t", k=KC), oT[:, :, t0:t0 + 512].rearrange("k p t -> p k t"))
            for n in range(KC):
                ps = self.psum()
                for k in range(KC):
                    self.mm(ps, ps.ap, Wo, Wo.ap[:, k * D + n * 128: k * D + (n + 1) * 128], o, o.ap[:, k * 512:(k + 1) * 512],
                            k == 0, k == KC - 1)
                self.tt("vector", x, x.ap[:, n * 512:(n + 1) * 512], ps, ps.ap, x, x.ap[:, n * 512:(n + 1) * 512], ALU.add)
            self.rmsnorm_fm(x, KC, 512, g, hT, sq, rstd, D)
            for h in range(4):
                ps = self.psum()
                for k in range(KC):
                    self.mm(ps, ps.ap, Wq, Wq.ap[:, k * 512 + h * 128: k * 512 + (h + 1) * 128], hT, hT.ap[:, k * 512:(k + 1) * 512],
                            k == 0, k == KC - 1)
                self.copy("scalar", qT, qT.ap[:, h * 512:(h + 1) * 512], ps, ps.ap)
            for h in range(4):
                pts = []
                for m in range(2):
                    ps = self.psum()
                    self.mm(ps, ps.ap, KmT, KmT.ap[:, h * MEM + m * 128: h * MEM + (m + 1) * 128], qT, qT.ap[:, h * 512:(h + 1) * 512],
                            True, True)
                    pt = pT[(h * 2 + m) % 4]
                    self.act(pt, pt.ap, ps, ps.ap, AF.Exp, scale=sc)
                    pts.append(pt)
                po = self.psum()
                for m in range(2):
                    self.mm(po, po.ap, Vm, Vm.ap[:, m * 512 + h * 128: m * 512 + (h + 1) * 128], pts[m], pts[m].ap, m == 0, m == 1)
                pz = self.psum()
                for m in range(2):
                    self.mm(pz, pz.ap, self.c_ones_bf, self.c_ones_bf.ap[:, 0:128], pts[m], pts[m].ap, m == 0, m == 1)
                P.op("vector", lambda e, o_=rec.ap, i_=pz.ap: e.reciprocal(out=o_, in_=i_), reads=[pz], writes=[rec])
                self.tt("vector", xo, xo.ap[:, h * 512:(h + 1) * 512], po, po.ap, rec, rec.ap, ALU.mult)
            for n in range(KC):
                ps = self.psum()
                for h in range(4):
                    self.mm(ps, ps.ap, Wx, Wx.ap[:, h * D + n * 128: h * D + (n + 1) * 128], xo, xo.ap[:, h * 512:(h + 1) * 512],
                            h == 0, h == 3)
                self.tt("vector", x, x.ap[:, n * 512:(n + 1) * 512], ps, ps.ap, x, x.ap[:, n * 512:(n + 1) * 512], ALU.add)
            self.store(xT[:, :, t0:t0 + 512].rearrange("k p t -> p k t"), x, x.ap.rearrange("p (k t) -> p k t", k=KC))
        self.flush_deferred()
        P.barrier()

    def attn_pass(self, k_t, k_ap_fn, q_t, q_ap, v_t, v_ap_fn, vcols, scale, Oacc, pairs, accs=None, ptiles=None, ssb=None):
        P = self.P
        self.begin_pass()
        total = 2 * len(pairs)
        done = [0]
        used = [False, False]

        def emit_pv(item):
            kta, ktb, pt = item
            for half, kt in ((0, kta), (1, ktb)):
                self.mm(Oacc, Oacc.ap[0:vcols, :], v_t, v_ap_fn(kt), pt, pt.ap[:, half * 512:(half + 1) * 512], done[0] == 0, done[0] == total - 1)
                done[0] += 1

        pend = None
        for pi, (kta, ktb, minfo) in enumerate(pairs):
            b0 = 2 * (self.nsp % 2)
            self.nsp += 1
            pa, pb = self.ps[b0], self.ps[b0 + 1]
            self.mm(pa, pa.ap, k_t, k_ap_fn(kta), q_t, q_ap, True, True)
            self.mm(pb, pb.ap, k_t, k_ap_fn(ktb), q_t, q_ap, True, True)
            if KEEPWARM:
                self.mm(self.ps[7], self.ps[7].ap, k_t, k_ap_fn(kta), q_t, q_ap, True, True)
            pair_ap = self.psall[:, b0 * 512:(b0 + 2) * 512]
            pt = ptiles[self.npt % len(ptiles)]
            self.npt += 1
            if minfo is not None:
                sb = ssb[self.npt % len(ssb)]
                P.op("vector", lambda e, o=sb.ap, i0=pair_ap, i1=minfo[1]: e.tensor_tensor(out=o, in0=i0, in1=i1, op=ALU.min),
                     reads=[pa, pb, minfo[0]], writes=[sb])
                P.op("scalar", lambda e, o=pt.ap, i=sb.ap: e.activation(out=o, in_=i, func=AF.Exp, scale=scale), reads=[sb], writes=[pt])
            else:
                P.op("scalar", lambda e, o=pt.ap, i=pair_ap: e.activation(out=o, in_=i, func=AF.Exp, scale=scale), reads=[pa, pb], writes=[pt])
            if accs is not None:
                accA, tmpA, accB, tmpB = accs
                if pi % 3 != 2:
                    eng, acc, tmp, ui = "vector", accA, tmpA, 0
                else:
                    eng, acc, tmp, ui = "gpsimd", accB, tmpB, 1
                if not used[ui]:
                    self.tt(eng, acc, acc.ap, pt, pt.ap[:, 0:512], pt, pt.ap[:, 512:1024], ALU.add)
                    used[ui] = True
                else:
                    self.tt(eng, tmp, tmp.ap, pt, pt.ap[:, 0:512], pt, pt.ap[:, 512:1024], ALU.add)
                    self.tt(eng, acc, acc.ap, acc, acc.ap, tmp, tmp.ap, ALU.add)
            if pend is not None:
                emit_pv(pend)
            self.tick()
            pend = (kta, ktb, pt)
        emit_pv(pend)
        return used

    def causal_pairs(self, qc):
        lst = [(2 * i, 2 * i + 1, None) for i in range(2 * qc)]
        lst.append((4 * qc, 4 * qc + 1, (self.c_maskc, self.c_maskc.ap[:, 0:1024])))
        lst.append((4 * qc + 2, 4 * qc + 3, (self.c_maskc, self.c_maskc.ap[:, 1024:2048])))
        return lst

    def attn_pass_dual(self, k_t, q_t, v_t, v_ap_fn, vcols, scale, O0, O1, kts, accs=None, ptiles=None, ssb=None, post=None):
        P = self.P
        self.begin_pass()
        total = len(kts)
        used = [False, False]
        cnt = [0]

        def emit_pv(item):
            kt, pt = item
            first, last = cnt[0] == 0, cnt[0] == total - 1
            self.mm(O0, O0.ap[0:vcols, :], v_t, v_ap_fn(kt, 0), pt, pt.ap[:, 0:512], first, last)
            self.mm(O1, O1.ap[0:vcols, :], v_t, v_ap_fn(kt, 1), pt, pt.ap[:, 512:1024], first, last)
            cnt[0] += 1

        pend = None
        for i, (kt, minfo) in enumerate(kts):
            b0 = 2 * (self.nsp % 2)
            self.nsp += 1
            pa, pb = self.ps[b0], self.ps[b0 + 1]
            ka, kb = k_t[0], k_t[1]
            self.mm(pa, pa.ap, ka[0], ka[1](kt), q_t[0], q_t[1], True, True)
            self.mm(pb, pb.ap, kb[0], kb[1](kt), q_t[0], q_t[2], True, True)
            pair_ap = self.psall[:, b0 * 512:(b0 + 2) * 512]
            pt = ptiles[self.npt % len(ptiles)]
            self.npt += 1
            if minfo is not None:
                sb = ssb[self.npt % len(ssb)]
                if len(minfo) > 2:
                    o3 = sb.ap.rearrange("p (a c) -> p a c", a=2)
                    i3 = pair_ap.rearrange("p (a c) -> p a c", a=2)
                    m3 = minfo[1].unsqueeze(1).to_broadcast([128, 2, 512])
                    P.op("vector", lambda e, o=o3, i0=i3, i1=m3: e.tensor_tensor(out=o, in0=i0, in1=i1, op=ALU.min),
                         reads=[pa, pb, minfo[0]], writes=[sb])
                else:
                    P.op("vector", lambda e, o=sb.ap, i0=pair_ap, i1=minfo[1]: e.tensor_tensor(out=o, in0=i0, in1=i1, op=ALU.min),
                         reads=[pa, pb, minfo[0]], writes=[sb])
                P.op("scalar", lambda e, o=pt.ap, i=sb.ap: e.activation(out=o, in_=i, func=AF.Exp, scale=scale), reads=[sb], writes=[pt])
            else:
                P.op("scalar", lambda e, o=pt.ap, i=pair_ap: e.activation(out=o, in_=i, func=AF.Exp, scale=scale), reads=[pa, pb], writes=[pt])
            if post is not None:
                mt, map512 = post(kt)
                p3 = pt.ap.rearrange("p (a c) -> p a c", a=2)
                m3 = map512.unsqueeze(1).to_broadcast([128, 2, 512])
                P.op("vector", lambda e, o=p3, i1=m3: e.tensor_tensor(out=o, in0=o, in1=i1, op=ALU.mult), reads=[pt, mt], writes=[pt])
            if accs is not None:
                accA, accB = accs
                if i % 3 != 2:
                    eng, acc, ui = "vector", accA, 0
                else:
                    eng, acc, ui = "gpsimd", accB, 1
                if not used[ui]:
                    self.copy(eng, acc, acc.ap, pt, pt.ap)
                    used[ui] = True
                else:
                    self.tt(eng, acc, acc.ap, acc, acc.ap, pt, pt.ap, ALU.add)
            if pend is not None:
                emit_pv(pend)
            self.tick()
            pend = (kt, pt)
        emit_pv(pend)
        return used

    def causal_list(self, qc):
        lst = [(kt, None) for kt in range(4 * qc)]
        for o in range(4):
            lst.append((4 * qc + o, (self.c_maskc, self.c_maskc.ap[:, o * 512:(o + 1) * 512])))
        return lst

    def phase_even_in(self, li):
        A, P, S = self.A, self.P, self.S
        j = li // 2
        A.reset(self.base_off)
        xT = self.scr["xT"].ap()
        w_in = self.din["ev_w_in"].ap()[j]
        W = A.alloc(KC * EV_W, BF16, "w_in")
        Wp = A.alloc(KC * 1024, BF16, "w_perm")
        stage = [A.alloc(1024, F32, "wst%d" % i) for i in range(2)]
        self.load_weight(W, EV_W, w_in, KC, EV_W, stage=stage)
        wv = w_in[:, 0:1024].rearrange("k (b two d) -> k b two d", two=2, d=32)
        i = 0
        for k in range(KC):
            for two in range(2):
                st = stage[i % 2]
                i += 1
                self.load(st, st.ap[:, 0:512].rearrange("p (b d) -> p b d", d=32), wv[k * 128:(k + 1) * 128, :, 1 - two, :])
                dst = Wp.ap[:, k * 1024:(k + 1) * 1024].rearrange("p (b two d) -> p b two d", two=2, d=32)[:, :, two, :]
                self.copy(("vector", "gpsimd")[i % 2], Wp, dst, st, st.ap[:, 0:512].rearrange("p (b d) -> p b d", d=32))
        nbf = A.alloc(8, F32, "nbf")
        self.load(nbf, nbf.ap[0:8, 0:1], self.din["ev_b_f"].ap()[j].rearrange("(h o) -> h o", o=1))
        self.ts("vector", nbf, nbf.ap[0:8, 1:2], nbf, nbf.ap[0:8, 0:1], -1.0, ALU.mult)
        X = [A.alloc(KC * 512, F32, "x%d" % i) for i in range(2)]
        hT = A.alloc(KC * 512, BF16, "hT")
        sq = A.alloc(KC * 512, BF16, "sq")
        rstd = A.alloc(512, F32, "rstd")
        COS = [A.alloc(512, F32, "cos%d" % i) for i in range(2)]
        SIN = [A.alloc(512, F32, "sin%d" % i) for i in range(2)]
        t1 = [A.alloc(512, F32, "t1_%d" % i) for i in range(2)]
        t2 = [A.alloc(512, F32, "t2_%d" % i) for i in range(2)]
        qo = [A.alloc(512, BF16, "qo%d" % i) for i in range(4)]
        vo = [A.alloc(512, BF16, "vo%d" % i) for i in range(2)]
        lf = [A.alloc(512, F32, "lf%d" % i) for i in range(2)]
        g = self.gain(0, li)
        dq, dk, dv = self.scr["dq"].ap(), self.scr["dk"].ap(), self.scr["dv"].ap()
        fq, fk, fv = self.scr["fq"].ap(), self.scr["fk"].ap(), self.scr["fv"].ap()
        lfp = self.scr["lfp"].ap()
        nq = 0
        for c in range(self.NCH):
            x = X[c % 2]
            t0 = c * 512
            self.load(x, x.ap.rearrange("p (k t) -> p k t", k=KC), xT[:, :, t0:t0 + 512].rearrange("k p t -> p k t"))
            cs, sn = COS[c % 2], SIN[c % 2]
            self.load(cs, cs.ap, self.din["c_cos64"].ap()[:, t0:t0 + 512])
            self.load(sn, sn.ap, self.din["c_sin64"].ap()[:, t0:t0 + 512])
            self.rmsnorm_fm(x, KC, 512, g, hT, sq, rstd, D)

            def proj(Wt, wcols, col0, ncols):
                ps = self.psum()
                for k in range(KC):
                    self.mm(ps, ps.ap[0:ncols, :], Wt, Wt.ap[:, k * wcols + col0: k * wcols + col0 + ncols], hT, hT.ap[:, k * 512:(k + 1) * 512],
                            k == 0, k == KC - 1)
                return ps
            for which, dst in ((0, dq), (1, dk)):
                for h in range(4):
                    col = which * 512 + h * 128
                    pa = proj(W, EV_W, col, 128)
                    pb = proj(Wp, 1024, col, 128)
                    a1, a2 = t1[nq % 2], t2[nq % 2]
                    q_ = qo[nq % 4]
                    nq += 1
                    self.tt("vector", a1, a1.ap, pa, pa.ap, cs, cs.ap, ALU.mult)
                    self.tt("vector", a2, a2.ap, pb, pb.ap, sn, sn.ap, ALU.mult)
                    self.tt("gpsimd", q_, q_.ap, a1, a1.ap, a2, a2.ap, ALU.add)
                    self.store(dst[h, :, t0:t0 + 512], q_, q_.ap)
            for which, dst in ((0, fq), (1, fk)):
                for pr in range(4):
                    col = 1536 + which * 512 + pr * 128
                    pa = proj(W, EV_W, col, 128)
                    q_ = qo[nq % 4]
                    nq += 1
                    self.copy("scalar", q_, q_.ap, pa, pa.ap)
                    for two in range(2):
                        self.store(dst[pr * 2 + two, 0:64, t0:t0 + 512], q_, q_.ap[two * 64:(two + 1) * 64, :])
            for s in range(4):
                for which, col, dst in ((0, 1024, dv), (1, 2560, fv)):
                    ps = self.psum()
                    for k in range(KC):
                        self.mm(ps, ps.ap, hT, hT.ap[:, k * 512 + s * 128: k * 512 + (s + 1) * 128], W, W.ap[:, k * EV_W + col: k * EV_W + col + 512],
                                k == 0, k == KC - 1)
                    v_ = vo[which]
                    self.copy(("scalar", "vector")[which], v_, v_.ap, ps, ps.ap)
                    tt0 = t0 + s * 128
                    if which == 0:
                        self.store(dst[:, tt0:tt0 + 128, :].rearrange("h t d -> t h d"), v_, v_.ap.rearrange("p (h d) -> p h d", h=4))
                    else:
                        self.store(dst[:, tt0:tt0 + 128, :].rearrange("h t d -> t h d"), v_, v_.ap.rearrange("p (h d) -> p h d", h=8))
            pz = proj(W, EV_W, 3072, 8)
            l_ = lf[c % 2]
            self.act(l_, l_.ap[0:8, :], pz, pz.ap[0:8, :], AF.Exp, scale=-1.0, bias=nbf.ap[0:8, 1:2], extra_reads=[nbf])
            self.act(l_, l_.ap[0:8, :], l_, l_.ap[0:8, :], AF.Ln, scale=1.0, bias=self.c_one.ap[0:8, 0:1], extra_reads=[self.c_one])
            self.store(lfp[:, t0:t0 + 512], l_, l_.ap[0:8, :])
        self.flush_deferred()
        P.barrier()

    def phase_even_scan(self, li):
        A, P, S = self.A, self.P, self.S
        A.reset(self.base_off)
        PW = min(2048, S)
        npieces = S // PW
        lfp = self.scr["lfp"].ap()
        fq, fk = self.scr["fq"].ap(), self.scr["fk"].ap()
        ones = A.alloc(PW, BF16, "ones")
        self.memset("vector", ones, ones.ap[0:8, :], 1.0)
        self.c_onesrow = A.alloc(PW, F32, "onesrow")
        self.memset("gpsimd", self.c_onesrow, self.c_onesrow.ap[0:8, :], 1.0)
        L = [A.alloc(PW, F32, "L%d" % i) for i in range(2)]
        C = [A.alloc(PW, F32, "C%d" % i) for i in range(2)]
        R = A.alloc(PW, F32, "R")
        hi = [[A.alloc(PW, BF16, "sp%d_%d" % (i, jj)) for jj in range(3)] for i in range(2)]
        ng = [[A.alloc(PW, BF16, "ng%d_%d" % (i, jj)) for jj in range(3)] for i in range(2)]
        zero = A.alloc(8, F32, "zero")
        self.memset("vector", zero, zero.ap[0:8, 0:1], 0.0)
        prev = None
        for pc in range(npieces):
            l_, c_ = L[pc % 2], C[pc % 2]
            t0 = pc * PW
            self.load(l_, l_.ap[0:8, :], lfp[:, t0:t0 + PW])
            init_t = zero if prev is None else prev
            init_ap = zero.ap[0:8, 0:1] if prev is None else prev.ap[0:8, PW - 1:PW]
            P.op("vector", lambda e, o=c_.ap[0:8, :], d1=l_.ap[0:8, :], ia=init_ap, d0=self.c_onesrow.ap[0:8, 0:PW]:
                 e.tensor_tensor_scan(out=o, data0=d0, data1=d1, initial=ia, op0=ALU.mult, op1=ALU.add),
                 reads=[l_, init_t, self.c_onesrow], writes=[c_])
            prev = c_
            h3 = hi[pc % 2]
            n3 = ng[pc % 2]
            self.ts("vector", R, R.ap[0:8, :], c_, c_.ap[0:8, :], 8.0, ALU.mult)
            for jj in range(3):
                self.copy("vector", h3[jj], h3[jj].ap[0:8, :], R, R.ap[0:8, :])
                if jj < 2:
                    self.tt("vector", R, R.ap[0:8, :], R, R.ap[0:8, :], h3[jj], h3[jj].ap[0:8, :], ALU.subtract)
                self.ts("vector", n3[jj], n3[jj].ap[0:8, :], h3[jj], h3[jj].ap[0:8, :], -1.0, ALU.mult)
            for jj in range(3):
                self.store(fq[:, 64 + jj, t0:t0 + PW], n3[jj], n3[jj].ap[0:8, :])
                self.store(fq[:, 67 + jj, t0:t0 + PW], ones, ones.ap[0:8, :])
                self.store(fk[:, 64 + jj, t0:t0 + PW], ones, ones.ap[0:8, :])
                self.store(fk[:, 67 + jj, t0:t0 + PW], h3[jj], h3[jj].ap[0:8, :])
        self.flush_deferred()
        P.barrier()

    def phase_even_attn(self, li):
        A, P, S, NT = self.A, self.P, self.S, self.NT
        j = li // 2
        A.reset(self.base_off)
        lam_init = 0.8 - 0.6 * math.exp(-0.3 * li)
        dq, dk, dv = self.scr["dq"].ap(), self.scr["dk"].ap(), self.scr["dv"].ap()
        fq, fk, fv = self.scr["fq"].ap(), self.scr["fk"].ap(), self.scr["fv"].ap()
        oT = self.scr["oT"].ap()
        KT = [A.alloc(S, BF16, "KT%d" % i) for i in range(2)]
        V = [A.alloc(NT * 129, BF16, "V%d" % i) for i in range(2)]
        Q = [A.alloc(512, BF16, "Q%d" % i) for i in range(2)]
        pts = [A.alloc(1024, BF16, "pt%d" % i) for i in range(4)]
        ssb = [A.alloc(1024, F32, "ssb%d" % i) for i in range(2)]
        rec = [A.alloc(512, F32, "rec%d" % i) for i in range(2)]
        recB = A.alloc(512, F32, "recB")
        on = [A.alloc(512, F32, "on%d" % i) for i in range(2)]
        oa = A.alloc(512, F32, "oa")
        osq = A.alloc(512, BF16, "osq")
        orstd = A.alloc(512, F32, "orstd")
        ob = [A.alloc(512, BF16, "ob%d" % i) for i in range(2)]
        accAs = [A.alloc(1024, F32, "accA%d" % i) for i in range(2)]
        accBs = [A.alloc(1024, F32, "accB%d" % i) for i in range(2)]
        mask2 = A.alloc(4 * 1024, F32, "mask2")
        for o in range(4):
            for two in range(2):
                self.copy("gpsimd", mask2, mask2.ap[:, o * 1024 + two * 512: o * 1024 + (two + 1) * 512], self.c_maskc, self.c_maskc.ap[:, o * 512:(o + 1) * 512])
        lam = A.alloc(4 * 64 + 8, F32, "lam")
        self.load(lam, lam.ap[:, 0:256], self.din["ev_lam"].ap()[j].rearrange("a d -> (a d)").partition_broadcast(128))
        lp = A.alloc(128, F32, "lamp")
        LS = 256
        self.tt("vector", lp, lp.ap[:, 0:64], lam, lam.ap[:, 0:64], lam, lam.ap[:, 64:128], ALU.mult)
        self.tt("vector", lp, lp.ap[:, 64:128], lam, lam.ap[:, 128:192], lam, lam.ap[:, 192:256], ALU.mult)
        P.op("vector", lambda e: e.reduce_sum(out=lam.ap[:, LS:LS + 1], in_=lp.ap[:, 0:64], axis=mybir.AxisListType.X), reads=[lp], writes=[lam])
        P.op("vector", lambda e: e.reduce_sum(out=lam.ap[:, LS + 1:LS + 2], in_=lp.ap[:, 64:128], axis=mybir.AxisListType.X), reads=[lp], writes=[lam])
        self.act(lam, lam.ap[:, LS + 2:LS + 4], lam, lam.ap[:, LS:LS + 2], AF.Exp)
        self.tt("vector", lam, lam.ap[:, LS + 4:LS + 5], lam, lam.ap[:, LS + 2:LS + 3], lam, lam.ap[:, LS + 3:LS + 4], ALU.subtract)
        self.ts("vector", lam, lam.ap[:, LS + 5:LS + 6], lam, lam.ap[:, LS + 4:LS + 5], lam_init, ALU.add, -1.0, ALU.mult)
        neg_lam = lam.ap[:, LS + 5:LS + 6]
        sub = A.alloc(8, F32, "subln")
        self.load(sub, sub.ap[:, 0:1], self.din["ev_subln"].ap()[j].rearrange("(p o) -> p o", o=1))
        self.ts("vector", sub, sub.ap[:, 1:2], sub, sub.ap[:, 0:1], 1.0 - lam_init, ALU.mult)
        for i in range(2):
            self.memset("gpsimd", V[i], V[i].ap, 1.0)
        units = [("f", h) for h in range(8)] + [("d", h) for h in range(4)]

        def load_unit(u, slot):
            kind, h = u
            kt_, v_ = KT[slot], V[slot]
            if kind == "f":
                self.load(kt_, kt_.ap[0:70, :], fk[h])
                self.load(v_, v_.ap.rearrange("p (t c) -> p t c", c=129)[:, :, 0:64], fv[h].rearrange("(t p) d -> p t d", p=128))
            else:
                self.load(kt_, kt_.ap, dk[h])
                self.load(v_, v_.ap.rearrange("p (t c) -> p t c", c=129)[:, :, 0:128], dv[h].rearrange("(t p) d -> p t d", p=128))

        load_unit(units[0], 0)
        nqi = 0
        npass = 0
        for ui, u in enumerate(units):
            kind, h = u
            slot = ui % 2
            if ui + 1 < len(units):
                load_unit(units[ui + 1], 1 - slot)
            kt_, v_ = KT[slot], V[slot]
            for qc in range(self.NCH):
                q_ = Q[nqi % 2]
                nqi += 1
                t0 = qc * 512
                kl = self.causal_pairs(qc)
                if kind == "f":
                    self.load(q_, q_.ap[0:70, :], fq[h, :, t0:t0 + 512])
                    Oacc = self.ps[4 + npass % 2]
                    npass += 1
                    self.attn_pass(kt_, lambda kt, kt_=kt_: kt_.ap[0:70, kt * 128:(kt + 1) * 128], q_, q_.ap[0:70, :],
                                   v_, lambda kt, v_=v_: v_.ap[:, kt * 129: kt * 129 + 65], 65, 0.125, Oacc, kl, ptiles=pts, ssb=ssb)
                    r_ = rec[npass % 2]
                    P.op("vector", lambda e, o=r_.ap[64:65, :], i=Oacc.ap[64:65, :]: e.reciprocal(out=o, in_=i), reads=[Oacc], writes=[r_])
                    o_ = ob[npass % 2]

                    def fin(Oacc=Oacc, r_=r_, o_=o_, dst=oT[4 + h // 2, (h % 2) * 64:(h % 2) * 64 + 64, t0:t0 + 512]):
                        pb = self.ps[6]
                        self.mm(pb, pb.ap[0:64, :], self.c_ones_f, self.c_ones_f.ap[64:65, 0:64], r_, r_.ap[64:65, :], True, True)
                        self.copy("scalar", recB, recB.ap[0:64, :], pb, pb.ap[0:64, :])
                        self.tt("vector", o_, o_.ap[0:64, :], Oacc, Oacc.ap[0:64, :], recB, recB.ap[0:64, :], ALU.mult)
                        self.store(dst, o_, o_.ap[0:64, :])
                    self.defer(4, fin)
                else:
                    self.load(q_, q_.ap, dq[h, :, t0:t0 + 512])
                    if h == 0 and qc == 0:
                        self.flush_deferred()
                    pb4 = 4 + 2 * (npass % 2)
                    npass += 1
                    O0, O1 = self.ps[pb4], self.ps[pb4 + 1]
                    kts = [(kt, None) for kt in range(4 * qc)]
                    for o in range(4):
                        kts.append((4 * qc + o, (mask2, mask2.ap[:, o * 1024:(o + 1) * 1024])))
                    aA, aB = accAs[npass % 2], accBs[npass % 2]
                    used = self.attn_pass_dual(
                        ((kt_, lambda kt, kt_=kt_: kt_.ap[0:64, kt * 128:(kt + 1) * 128]),
                         (kt_, lambda kt, kt_=kt_: kt_.ap[64:128, kt * 128:(kt + 1) * 128])),
                        (q_, q_.ap[0:64, :], q_.ap[64:128, :]),
                        v_, lambda kt, j_, v_=v_: v_.ap[:, kt * 129: kt * 129 + 128], 128, 0.125, O0, O1, kts,
                        accs=(aA, aB), ptiles=pts, ssb=ssb)
                    if used[1]:
                        self.tt("vector", aA, aA.ap, aA, aA.ap, aB, aB.ap, ALU.add)
                    o_ = ob[npass % 2]

                    def fin1(aA=aA):
                        for sub_j in range(2):
                            sacc = self.psum(0, 4)
                            self.mm(sacc, sacc.ap[0:1, :], self.c_ones_f, self.c_ones_f.ap[:, 0:1], aA, aA.ap[:, sub_j * 512:(sub_j + 1) * 512], True, True)
                            r_ = rec[sub_j]
                            P.op("vector", lambda e, o=r_.ap[0:1, :], i=sacc.ap[0:1, :]: e.reciprocal(out=o, in_=i), reads=[sacc], writes=[r_])
                            if sub_j == 1:
                                self.ts("vector", r_, r_.ap[0:1, :], r_, r_.ap[0:1, :], neg_lam[0:1, :], ALU.mult, extra_reads=[lam])

                    def fin2(O0=O0, O1=O1, o_=o_, dst=oT[h, :, t0:t0 + 512]):
                        for sub_j, Oacc in ((0, O0), (1, O1)):
                            r_ = rec[sub_j]
                            pb = self.psum(0, 4)
                            self.mm(pb, pb.ap, self.c_ones_f, self.c_ones_f.ap[0:1, 0:128], r_, r_.ap[0:1, :], True, True)
                            self.copy("scalar", recB, recB.ap, pb, pb.ap)
                            self.tt("vector", on[sub_j], on[sub_j].ap, Oacc, Oacc.ap, recB, recB.ap, ALU.mult)
                        self.tt("gpsimd", oa, oa.ap, on[0], on[0].ap, on[1], on[1].ap, ALU.add)
                        self.act(osq, osq.ap, oa, oa.ap, AF.Square)
                        pz = self.psum(0, 4)
                        self.mm(pz, pz.ap, self.c_ones_bf, self.c_ones_bf.ap[:, 0:128], osq, osq.ap, True, True)
                        self.act(orstd, orstd.ap, pz, pz.ap, AF.Ln, scale=1.0 / 128, bias=self.c_eps.ap[:, 0:1], extra_reads=[self.c_eps])
                        self.act(orstd, orstd.ap, orstd, orstd.ap, AF.Exp, scale=-0.5)
                        self.stt(o_, o_.ap, oa, oa.ap, sub.ap[:, 1:2], orstd, orstd.ap, ALU.mult, ALU.mult, extra_reads=[sub])
                        self.store(dst, o_, o_.ap)
                    self.defer(3, fin1)
                    self.defer(12, fin2)
        self.flush_deferred()
        P.barrier()

    def load_perm(self, Wp, wpcols, dst_col, src_cols_ap, n, half, stage, nk=KC):
        nb = n // (2 * half)
        sv = src_cols_ap.rearrange("k (b two d) -> k b two d", two=2, d=half)
        i = 0
        for k in range(nk):
            for two in range(2):
                st = stage[i % 2]
                i += 1
                self.load(st, st.ap[:, 0:nb * half].rearrange("p (b d) -> p b d", d=half), sv[k * 128:(k + 1) * 128, :, 1 - two, :])
                dst = Wp.ap[:, k * wpcols + dst_col: k * wpcols + dst_col + n].rearrange("p (b two d) -> p b two d", two=2, d=half)[:, :, two, :]
                self.copy(("vector", "gpsimd")[i % 2], Wp, dst, st, st.ap[:, 0:nb * half].rearrange("p (b d) -> p b d", d=half))

    def rope_evac(self, pa, pb, rows, cs, sn, t1, t2, out_t, out_ap):
        self.tt("vector", t1, t1.ap[0:rows, :], pa, pa.ap[0:rows, :], cs, cs.ap[0:rows, :], ALU.mult)
        self.tt("vector", t2, t2.ap[0:rows, :], pb, pb.ap[0:rows, :], sn, sn.ap[0:rows, :], ALU.mult)
        self.tt("gpsimd", out_t, out_ap, t1, t1.ap[0:rows, :], t2, t2.ap[0:rows, :], ALU.add)

    def phase_odd_in(self, li):
        A, P, S = self.A, self.P, self.S
        j = li // 2
        A.reset(self.base_off)
        xT = self.scr["xT"].ap()
        w_in = self.din["od_w_in"].ap()[j]
        NPERM = 928
        W = A.alloc(KC * OD_W, BF16, "w_in")
        Wp = A.alloc(KC * NPERM, BF16, "w_perm")
        stage = [A.alloc(1024, F32, "wst%d" % i) for i in range(2)]
        self.load_weight(W, OD_W, w_in, KC, OD_W, stage=stage)
        self.load_perm(Wp, NPERM, 0, w_in[:, 0:512], 512, 32, stage)
        self.load_perm(Wp, NPERM, 512, w_in[:, 512:640], 128, 32, stage)
        self.load_perm(Wp, NPERM, 640, w_in[:, 768:896], 128, 32, stage)
        self.load_perm(Wp, NPERM, 768, w_in[:, 1024:1152], 128, 32, stage)
        self.load_perm(Wp, NPERM, 896, w_in[:, 1944:1976], 32, 16, stage)
        w_uq = self.din["mla_w_uq"].ap()[j]
        w_ukv = self.din["mla_w_ukv"].ap()[j]
        Wuq = A.alloc(3 * 768, BF16, "wuq")
        Wuqp = A.alloc(3 * 768, BF16, "wuqp")
        self.load_weight(Wuq, 768, w_uq, 3, 768, stage=stage)
        for k in range(3):
            self.copy("gpsimd", Wuqp, Wuqp.ap[:, k * 768:(k + 1) * 768], Wuq, Wuq.ap[:, k * 768:(k + 1) * 768])
        sv = w_uq.rearrange("k (h c) -> k h c", c=96)[:, :, 64:96].rearrange("k h (two d) -> k h two d", two=2)
        i = 0
        for k in range(3):
            for two in range(2):
                st = stage[i % 2]
                i += 1
                self.load(st, st.ap[:, 0:128].rearrange("p (h d) -> p h d", d=16), sv[k * 128:(k + 1) * 128, :, 1 - two, :])
                dst = Wuqp.ap[:, k * 768:(k + 1) * 768].rearrange("p (h c) -> p h c", c=96)[:, :, 64 + two * 16: 64 + two * 16 + 16]
                self.copy("vector", Wuqp, dst, st, st.ap[:, 0:128].rearrange("p (h d) -> p h d", d=16))
        Wkn = A.alloc(2 * 512, BF16, "wkn")
        Wkv = A.alloc(2 * 512, BF16, "wkv")
        kvv = w_ukv.rearrange("k (h two d) -> k h two d", two=2, d=64)
        for k in range(2):
            for two, Wt in ((0, Wkn), (1, Wkv)):
                st = stage[i % 2]
                i += 1
                self.load(st, st.ap[:, 0:512].rearrange("p (h d) -> p h d", d=64), kvv[k * 128:(k + 1) * 128, :, two, :])
                self.copy("vector", Wt, Wt.ap[:, k * 512:(k + 1) * 512], st, st.ap[:, 0:512])
        mlan = A.alloc(8, F32, "mlan")
        self.load(mlan, mlan.ap[:, 0:5], self.din["c_mlan"].ap()[j])
        X = [A.alloc(KC * 512, F32, "x%d" % i) for i in range(2)]
        hT = A.alloc(KC * 512, BF16, "hT")
        sq = A.alloc(KC * 512, BF16, "sq")
        rstd = A.alloc(512, F32, "rstd")
        COS = [A.alloc(512, F32, "cos%d" % i) for i in range(2)]
        SIN = [A.alloc(512, F32, "sin%d" % i) for i in range(2)]
        COSM = [A.alloc(512, F32, "cosm%d" % i) for i in range(2)]
        SINM = [A.alloc(512, F32, "sinm%d" % i) for i in range(2)]
        COSK = [A.alloc(512, F32, "cosk%d" % i) for i in range(2)]
        SINK = [A.alloc(512, F32, "sink%d" % i) for i in range(2)]
        t1 = [A.alloc(512, F32, "t1_%d" % i) for i in range(2)]
        t2 = [A.alloc(512, F32, "t2_%d" % i) for i in range(2)]
        qo = [A.alloc(512, BF16, "qo%d" % i) for i in range(4)]
        vo = [A.alloc(512, BF16, "vo%d" % i) for i in range(2)]
        gt = [A.alloc(512, F32, "gt%d" % i) for i in range(2)]
        CQ = A.alloc(3 * 512, F32, "cq")
        CQn = A.alloc(3 * 512, BF16, "cqn")
        CKV = A.alloc(2 * 512, F32, "ckv")
        CKVn = A.alloc(2 * 512, BF16, "ckvn")
        g = self.gain(0, li)
        scr = self.scr
        nsq, kcT, vcT, ksT, kwT = scr["nsq"].ap(), scr["kcT"].ap(), scr["vcT"].ap(), scr["ksT"].ap(), scr["kwT"].ap()
        vsv, vwv, gT = scr["vsv"].ap(), scr["vwv"].ap(), scr["gT"].ap()
        mq, mk, mv = scr["mq"].ap(), scr["mk"].ap(), scr["mv"].ap()
        nq = 0
        for c in range(self.NCH):
            x = X[c % 2]
            t0 = c * 512
            self.load(x, x.ap.rearrange("p (k t) -> p k t", k=KC), xT[:, :, t0:t0 + 512].rearrange("k p t -> p k t"))
            cs, sn = COS[c % 2], SIN[c % 2]
            csm, snm = COSM[c % 2], SINM[c % 2]
            csk, snk = COSK[c % 2], SINK[c % 2]
            self.load(cs, cs.ap, self.din["c_cos64"].ap()[:, t0:t0 + 512])
            self.load(sn, sn.ap, self.din["c_sin64"].ap()[:, t0:t0 + 512])
            self.load(csm, csm.ap[0:96, :], self.din["c_cosm"].ap()[:, t0:t0 + 512])
            self.load(snm, snm.ap[0:96, :], self.din["c_sinm"].ap()[:, t0:t0 + 512])
            self.load(csk, csk.ap[0:32, :], self.din["c_cosm"].ap()[64:96, t0:t0 + 512])
            self.load(snk, snk.ap[0:32, :], self.din["c_sinm"].ap()[64:96, t0:t0 + 512])
            self.rmsnorm_fm(x, KC, 512, g, hT, sq, rstd, D)

            def proj(Wt, wcols, col0, ncols, rhs=hT, nk=KC):
                ps = self.psum()
                for k in range(nk):
                    self.mm(ps, ps.ap[0:ncols, :], Wt, Wt.ap[:, k * wcols + col0: k * wcols + col0 + ncols], rhs, rhs.ap[:, k * 512:(k + 1) * 512],
                            k == 0, k == nk - 1)
                return ps

            def roped(col, pcol, rows, cst, snt):
                nonlocal nq
                pa = proj(W, OD_W, col, rows)
                pb = proj(Wp, NPERM, pcol, rows)
                q_ = qo[nq % 4]
                a1, a2 = t1[nq % 2], t2[nq % 2]
                nq += 1
                self.rope_evac(pa, pb, rows, cst, snt, a1, a2, q_, q_.ap[0:rows, :])
                return q_
            for t in range(4):
                q_ = roped(128 * t, 128 * t, 128, cs, sn)
                for two in range(2):
                    self.store(nsq[2 * t + two, :, t0:t0 + 512], q_, q_.ap[two * 64:(two + 1) * 64, :])
            for col, pcol, dst in ((512, 512, kcT), (768, 640, ksT), (1024, 768, kwT)):
                q_ = roped(col, pcol, 128, cs, sn)
                for two in range(2):
                    self.store(dst[two, :, t0:t0 + 512], q_, q_.ap[two * 64:(two + 1) * 64, :])
            pa = proj(W, OD_W, 640, 128)
            q_ = qo[nq % 4]
            nq += 1
            self.copy("scalar", q_, q_.ap, pa, pa.ap)
            for two in range(2):
                self.store(vcT[two, :, t0:t0 + 512], q_, q_.ap[two * 64:(two + 1) * 64, :])
            for s in range(4):
                for which, col, dst in ((0, 896, vsv), (1, 1152, vwv)):
                    ps = self.psum()
                    for k in range(KC):
                        self.mm(ps, ps.ap[:, 0:128], hT, hT.ap[:, k * 512 + s * 128: k * 512 + (s + 1) * 128], W, W.ap[:, k * OD_W + col: k * OD_W + col + 128],
                                k == 0, k == KC - 1)
                    v_ = vo[which]
                    self.copy(("scalar", "vector")[which], v_, v_.ap[:, 0:128], ps, ps.ap[:, 0:128])
                    tt0 = t0 + s * 128
                    self.store(dst[:, tt0:tt0 + 128, :].rearrange("g t d -> t g d"), v_, v_.ap[:, 0:128].rearrange("p (g d) -> p g d", g=2))
            pz = proj(W, OD_W, 1280, 24)
            g_ = gt[c % 2]
            self.act(g_, g_.ap[0:24, :], pz, pz.ap[0:24, :], AF.Sigmoid)
            self.store(gT[:, t0:t0 + 512], g_, g_.ap[0:24, :])
            for i3 in range(3):
                pa = proj(W, OD_W, 1304 + 128 * i3, 128)
                self.copy("scalar", CQ, CQ.ap[:, i3 * 512:(i3 + 1) * 512], pa, pa.ap)
            self.rmsnorm_fm(CQ, 3, 512, mlan.ap[:, 0:3], CQn, sq, rstd, 384, gain_t=mlan)
            for i2 in range(2):
                pa = proj(W, OD_W, 1688 + 128 * i2, 128)
                self.copy("scalar", CKV, CKV.ap[:, i2 * 512:(i2 + 1) * 512], pa, pa.ap)
            self.rmsnorm_fm(CKV, 2, 512, mlan.ap[:, 3:5], CKVn, sq, rstd, 256, gain_t=mlan)
            q_ = roped(1944, 896, 32, csk, snk)
            for h in range(8):
                self.store(mk[h, 64:96, t0:t0 + 512], q_, q_.ap[0:32, :])
            for h in range(8):
                pa = proj(Wuq, 768, 96 * h, 96, rhs=CQn, nk=3)
                pb = proj(Wuqp, 768, 96 * h, 96, rhs=CQn, nk=3)
                q_ = qo[nq % 4]
                a1, a2 = t1[nq % 2], t2[nq % 2]
                nq += 1
                self.rope_evac(pa, pb, 96, csm, snm, a1, a2, q_, q_.ap[0:96, :])
                self.store(mq[h, :, t0:t0 + 512], q_, q_.ap[0:96, :])
            for pr in range(4):
                pa = proj(Wkn, 512, 128 * pr, 128, rhs=CKVn, nk=2)
                q_ = qo[nq % 4]
                nq += 1
                self.copy("scalar", q_, q_.ap, pa, pa.ap)
                for two in range(2):
                    self.store(mk[2 * pr + two, 0:64, t0:t0 + 512], q_, q_.ap[two * 64:(two + 1) * 64, :])
            for s in range(4):
                ps = self.psum()
                for k in range(2):
                    self.mm(ps, ps.ap, CKVn, CKVn.ap[:, k * 512 + s * 128: k * 512 + (s + 1) * 128], Wkv, Wkv.ap[:, k * 512:(k + 1) * 512], k == 0, k == 1)
                v_ = vo[s % 2]
                self.copy("vector", v_, v_.ap, ps, ps.ap)
                tt0 = t0 + s * 128
                self.store(mv[:, tt0:tt0 + 128, :].rearrange("h t d -> t h d"), v_, v_.ap.rearrange("p (h d) -> p h d", h=8))
        self.flush_deferred()
        P.barrier()

    def phase_odd_cmp(self, li):
        A, P, S = self.A, self.P, self.S
        j = li // 2
        A.reset(self.base_off)
        NC, NCT, NCP = self.NC, self.NCT, self.NCP
        scr = self.scr
        src = {0: scr["kcT"].ap(), 1: scr["vcT"].ap()}
        kcmp, vcmp = scr["kcmp"].ap(), scr["vcmp"].ap()
        stage = [A.alloc(1024, F32, "wst%d" % i) for i in range(2)]
        SRC = [A.alloc(2 * S, BF16, "src%d" % kv) for kv in range(2)]
        for kv in range(2):
            for g in range(2):
                self.load(SRC[kv], SRC[kv].ap[0:64, g * S:(g + 1) * S], src[kv][g])
        W1 = [A.alloc(32 * 128, BF16, "w1_%d" % kv) for kv in range(2)]
        W2 = [A.alloc(64, BF16, "w2_%d" % kv) for kv in range(2)]
        posT = A.alloc(2 * 32, F32, "posT")
        posTb = A.alloc(2 * 32, BF16, "posTb")
        self.load(posT, posT.ap[0:64, 0:64], self.din["c_posT"].ap()[j])
        self.copy("vector", posTb, posTb.ap[0:64, 0:64], posT, posT.ap[0:64, 0:64])
        i = 0
        for kv in range(2):
            w1 = self.din["nsa_cmp_w1"].ap()[j, kv].rearrange("(l d) h -> d l h", d=64)
            for l4 in range(4):
                st = stage[i % 2]
                i += 1
                self.load(st, st.ap[0:64, 0:1024].rearrange("p (l h) -> p l h", h=128), w1[:, l4 * 8:(l4 + 1) * 8, :])
                self.copy("vector", W1[kv], W1[kv].ap[0:64, l4 * 1024:(l4 + 1) * 1024], st, st.ap[0:64, 0:1024])
            st = stage[i % 2]
            i += 1
            self.load(st, st.ap[:, 0:64], self.din["nsa_cmp_w2"].ap()[j, kv])
            self.copy("vector", W2[kv], W2[kv].ap[:, 0:64], st, st.ap[:, 0:64])
        bias = A.alloc(8, F32, "cbias")
        H = [A.alloc(NCP, BF16, "H%d" % i) for i in range(2)]
        for i in range(2):
            self.memset("vector", H[i], H[i].ap, 0.0)
        ko = [A.alloc(NCP, BF16, "ko%d" % i) for i in range(2)]
        for i in range(2):
            self.memset("vector", ko[i], ko[i].ap, 0.0)
        vo = [A.alloc(64, BF16, "cvo%d" % i) for i in range(2)]
        for kv in range(2):
            pb = self.psum()
            for l in range(32):
                self.mm(pb, pb.ap[:, 0:1], W1[kv], W1[kv].ap[0:64, l * 128:(l + 1) * 128], posTb, posTb.ap[0:64, kv * 32 + l: kv * 32 + l + 1], l == 0, l == 31)
            self.copy("vector", bias, bias.ap[:, kv:kv + 1], pb, pb.ap[:, 0:1])
        n = 0
        for kv in range(2):
            for g in range(2):
                ps = self.psum()
                for l in range(32):
                    rhs = SRC[kv].ap[0:64, g * S + l: g * S + l + 16 * (NC - 1) + 1: 16]
                    self.mm(ps, ps.ap[:, 0:NC], W1[kv], W1[kv].ap[0:64, l * 128:(l + 1) * 128], SRC[kv], rhs, l == 0, l == 31)
                h_ = H[n % 2]
                self.act(h_, h_.ap[:, 0:NC], ps, ps.ap[:, 0:NC], AF.Silu, bias=bias.ap[:, kv:kv + 1], extra_reads=[bias])
                if kv == 0:
                    p2 = self.psum()
                    self.mm(p2, p2.ap[0:64, 0:NC], W2[0], W2[0].ap[:, 0:64], h_, h_.ap[:, 0:NC], True, True)
                    k_ = ko[g]
                    self.copy("vector", k_, k_.ap[0:64, 0:NC], p2, p2.ap[0:64, 0:NC])
                    self.store(kcmp[g], k_, k_.ap[0:64, :])
                else:
                    for ct in range(NCT):
                        p2 = self.psum()
                        self.mm(p2, p2.ap[:, 0:64], h_, h_.ap[:, ct * 128:(ct + 1) * 128], W2[1], W2[1].ap[:, 0:64], True, True)
                        v_ = vo[ct % 2]
                        self.copy("vector", v_, v_.ap[:, 0:64], p2, p2.ap[:, 0:64])
                        self.store(vcmp[g, ct * 128:(ct + 1) * 128, :], v_, v_.ap[:, 0:64])
                n += 1
        self.flush_deferred()
        P.barrier()

    def combine(self, Oacc, gr, b, dst, first, rec_t, recB, tmp_t, bank_lo, bank_hi, defer_steps=0):
        P = self.P
        r_ = rec_t
        self.ts("vector", r_, r_.ap[64:65, :], Oacc, Oacc.ap[64:65, :], 1e-30, ALU.max)
        P.op("vector", lambda e, o=r_.ap[64:65, :]: e.reciprocal(out=o, in_=o), reads=[r_], writes=[r_])
        if gr is not None:
            self.tt("vector", r_, r_.ap[64:65, :], r_, r_.ap[64:65, :], gr, gr.ap[64:65, b * 512:(b + 1) * 512], ALU.mult)

        def stage2():
            pb = self.psum(bank_lo, bank_hi)
            self.mm(pb, pb.ap[0:64, :], self.c_ones_f, self.c_ones_f.ap[64:65, 0:64], r_, r_.ap[64:65, :], True, True)
            self.copy("scalar", recB, recB.ap[0:64, :], pb, pb.ap[0:64, :])
            if first:
                self.tt("vector", dst, dst.ap[0:64, :], Oacc, Oacc.ap[0:64, :], recB, recB.ap[0:64, :], ALU.mult)
            else:
                self.tt("vector", tmp_t, tmp_t.ap[0:64, :], Oacc, Oacc.ap[0:64, :], recB, recB.ap[0:64, :], ALU.mult)
                self.tt("gpsimd", dst, dst.ap[0:64, :], dst, dst.ap[0:64, :], tmp_t, tmp_t.ap[0:64, :], ALU.add)
        if defer_steps > 0:
            self.defer(defer_steps, stage2)
        else:
            stage2()

    def phase_odd_nsa(self, li):
        A, P, S, NT = self.A, self.P, self.S, self.NT
        A.reset(self.base_off)
        NB = S // 64
        NC, NCT, NCP = self.NC, self.NCT, self.NCP
        scr = self.scr
        nsq, ksT, kwT, vsv, vwv, gT = scr["nsq"].ap(), scr["ksT"].ap(), scr["kwT"].ap(), scr["vsv"].ap(), scr["vwv"].ap(), scr["gT"].ap()
        kcmp, vcmp, oT = scr["kcmp"].ap(), scr["vcmp"].ap(), scr["oT"].ap()
        sc = 0.125
        stage = [A.alloc(1024, F32, "wst%d" % i) for i in range(2)]
        KS = A.alloc(2 * S, BF16, "KS")
        VS = A.alloc(2 * NT * 65, BF16, "VS")
        KCM = A.alloc(2 * NCP, BF16, "KCM")
        VCM = A.alloc(2 * NCT * 65, BF16, "VCM")
        self.memset("gpsimd", VS, VS.ap, 1.0)
        self.memset("gpsimd", VCM, VCM.ap, 1.0)
        for g in range(2):
            self.load(KS, KS.ap[0:64, g * S:(g + 1) * S], ksT[g])
            self.load(KS, KS.ap[64:128, g * S:(g + 1) * S], ksT[g])
            self.load(VS, VS.ap.rearrange("p (g t c) -> p g t c", g=2, c=65)[:, g, :, 0:64], vsv[g].rearrange("(t p) d -> p t d", p=128))
            self.load(KCM, KCM.ap[0:64, g * NCP:(g + 1) * NCP], kcmp[g])
            self.load(KCM, KCM.ap[64:128, g * NCP:(g + 1) * NCP], kcmp[g])
            self.load(VCM, VCM.ap.rearrange("p (g t c) -> p g t c", g=2, c=65)[:, g, :, 0:64], vcmp[g].rearrange("(t p) d -> p t d", p=128))
        OV1 = A.alloc(NCT * (NB + 1), BF16, "OV1")
        st = stage[0]
        self.load(st, st.ap[:, 0:NCT * (NB + 1)], self.din["c_ov1"].ap())
        self.copy("vector", OV1, OV1.ap, st, st.ap[:, 0:NCT * (NB + 1)])
        EXP = A.alloc(S, BF16, "EXP")
        for i in range(S // 1024):
            st = stage[(i + 1) % 2]
            self.load(st, st.ap, self.din["c_expand"].ap()[:, i * 1024:(i + 1) * 1024])
            self.copy("vector", EXP, EXP.ap[:, i * 1024:(i + 1) * 1024], st, st.ap)
        MW = A.alloc(4 * 512, F32, "MW")
        MM = A.alloc(5 * 512, F32, "MM")
        self.load(MW, MW.ap, self.din["c_maskw"].ap())
        self.load(MM, MM.ap, self.din["c_maskm"].ap())
        TWW = 2 * (NT - 1) + NB
        TW1 = A.alloc(TWW, F32, "TW1")
        TW2 = A.alloc(TWW, F32, "TW2")
        self.load(TW1, TW1.ap, self.din["c_tw1"].ap())
        self.load(TW2, TW2.ap, self.din["c_tw2"].ap())
        Q8 = A.alloc(4 * 512, BF16, "Q8")
        GR = [A.alloc(3 * 512, F32, "GR%d" % i) for i in range(4)]
        KW = [A.alloc(2 * 1024, BF16, "KW%d" % i) for i in range(2)]
        VW = [A.alloc(2 * 8 * 65, BF16, "VW%d" % i) for i in range(2)]
        for i in range(2):
            self.memset("gpsimd", VW[i], VW[i].ap, 1.0)
        OACC = [A.alloc(512, F32, "OACC%d" % i) for i in range(4)]
        ET = [A.alloc(512, BF16, "ET%d" % i) for i in range(4)]
        pts = [A.alloc(1024, BF16, "pt%d" % i) for i in range(4)]
        ssb = [A.alloc(1024, F32, "ssb%d" % i) for i in range(2)]
        rec = [A.alloc(512, F32, "rec%d" % i) for i in range(2)]
        recB = A.alloc(512, F32, "recB")
        ctmp = A.alloc(512, F32, "ctmp")
        ACC = [A.alloc(4 * NB, F32, "ACC%d" % i) for i in range(2)]
        adj = [A.alloc(NB, F32, "adj%d" % i) for i in range(2)]
        tmpv = A.alloc(NB, F32, "tmpv")
        m8 = A.alloc(16, F32, "m8")
        selb = [A.alloc(NB, BF16, "selb%d" % i) for i in range(2)]
        selT = [A.alloc(512, BF16, "selT%d" % i) for i in range(2)]
        mks = [A.alloc(1024, BF16, "mk%d" % i) for i in range(2)]
        rI = A.alloc(8, F32, "rI")
        ob = [A.alloc(512, BF16, "ob%d" % i) for i in range(2)]
        npt = 0
        ncomb = 0
        nwin = 0
        for qc in range(self.NCH):
            t0 = qc * 512
            for two in range(2):
                self.load(Q8, Q8.ap[two * 64:(two + 1) * 64, :].rearrange("p (h t) -> p h t", h=4),
                          nsq[two::2, :, t0:t0 + 512].rearrange("h r t -> r h t"))
            kw, vw = KW[qc % 2], VW[qc % 2]
            lo = max(t0 - 512, 0)
            nwt = (t0 + 512 - lo) // 128
            slot0 = 8 - nwt
            for g in range(2):
                self.load(kw, kw.ap[0:64, g * 1024 + slot0 * 128:(g + 1) * 1024], kwT[g][:, lo:t0 + 512])
                self.load(kw, kw.ap[64:128, g * 1024 + slot0 * 128:(g + 1) * 1024], kwT[g][:, lo:t0 + 512])
                self.load(vw, vw.ap.rearrange("p (g t c) -> p g t c", g=2, c=65)[:, g, slot0:8, 0:64],
                          vwv[g][lo:t0 + 512, :].rearrange("(t p) d -> p t d", p=128))
            for g in range(2):
                nct = min(NCT, (32 * qc + 31 + 127) // 128)
                for hh in range(4):
                    h = 4 * g + hh
                    gr = GR[hh]
                    self.load(gr, gr.ap[64:65, :].rearrange("p (r t) -> p r t", r=3),
                              gT[3 * h:3 * h + 3, t0:t0 + 512].rearrange("(o r) t -> o r t", o=1))
                    r0 = (h % 2) * 64
                    qh = Q8.ap[r0:r0 + 64, (h // 2) * 512:(h // 2 + 1) * 512]
                    Oc = self.ps[4 + hh % 2]
                    for ct in range(nct):
                        ps = self.psum(0, 4)
                        self.mm(ps, ps.ap, KCM, KCM.ap[r0:r0 + 64, g * NCP + ct * 128: g * NCP + (ct + 1) * 128], Q8, qh, True, True)
                        Dv = 512 * qc - 2048 * ct
                        e_ = ET[ct]
                        if Dv <= 2048:
                            sb = ssb[ct % 2]
                            mi = Dv // 512
                            self.tt("vector", sb, sb.ap[:, 0:512], ps, ps.ap, MM, MM.ap[:, mi * 512:(mi + 1) * 512], ALU.min)
                            self.act(e_, e_.ap, sb, sb.ap[:, 0:512], AF.Exp, scale=sc)
                        else:
                            self.act(e_, e_.ap, ps, ps.ap, AF.Exp, scale=sc)
                        self.mm(Oc, Oc.ap[0:65, :], VCM, VCM.ap[:, (g * NCT + ct) * 65:(g * NCT + ct) * 65 + 65], e_, e_.ap, ct == 0, ct == nct - 1)
                    for s in range(4):
                        ib = self.ps[6 + s // 2]
                        for ct in range(nct):
                            self.mm(ib, ib.ap[:, (s % 2) * (NB + 1):(s % 2 + 1) * (NB + 1)], ET[ct], ET[ct].ap[:, s * 128:(s + 1) * 128],
                                    OV1, OV1.ap[:, ct * (NB + 1):(ct + 1) * (NB + 1)], ct == 0, ct == nct - 1)
                    for bnk in range(2):
                        ib = self.ps[6 + bnk]
                        v3 = ib.ap[:, 0:2 * (NB + 1)].rearrange("p (s n) -> p s n", s=2)
                        ri = rI.ap[:, bnk * 2:(bnk + 1) * 2]
                        self.ts("vector", rI, ri, ib, v3[:, :, NB], 1e-30, ALU.max)
                        P.op("vector", lambda e, o=ri: e.reciprocal(out=o, in_=o), reads=[rI], writes=[rI])
                        for s2 in range(2):
                            s = bnk * 2 + s2
                            dst = ACC[g].ap[:, s * NB:(s + 1) * NB]
                            if hh == 0:
                                self.ts("vector", ACC[g], dst, ib, v3[:, s2, 0:NB], rI.ap[:, s:s + 1], ALU.mult, extra_reads=[rI])
                            else:
                                self.stt(ACC[g], dst, ib, v3[:, s2, 0:NB], rI.ap[:, s:s + 1], ACC[g], dst, ALU.mult, ALU.add, extra_reads=[rI])
                    self.combine(Oc, gr, 0, OACC[hh], True, rec[ncomb % 2], recB, ctmp, 0, 4)
                    ncomb += 1
                psT = self.psum(0, 4)
                psTb = psT.ap.bitcast(BF16)
                for s in range(4):
                    gs = 4 * qc + s
                    off = 2 * (NT - 1) - 2 * gs
                    a_ = adj[s % 2]
                    self.tt("vector", a_, a_.ap, ACC[g], ACC[g].ap[:, s * NB:(s + 1) * NB], TW1, TW1.ap[:, off:off + NB], ALU.mult)
                    self.tt("vector", a_, a_.ap, a_, a_.ap, TW2, TW2.ap[:, off:off + NB], ALU.add)
                    self.memset("vector", a_, a_.ap[:, 0:1], 1.0e4)
                    P.op("vector", lambda e, o=m8.ap[:, 0:8], i=a_.ap: e.max(out=o, in_=i), reads=[a_], writes=[m8])
                    P.op("vector", lambda e, o=tmpv.ap, r=m8.ap[:, 0:8], i=a_.ap: e.match_replace(out=o, in_to_replace=r, in_values=i, imm_value=-2.0e30),
                         reads=[m8, a_], writes=[tmpv])
                    P.op("vector", lambda e, o=m8.ap[:, 8:16], i=tmpv.ap: e.max(out=o, in_=i), reads=[tmpv], writes=[m8])
                    sb_ = selb[s % 2]
                    self.ts("vector", sb_, sb_.ap, a_, a_.ap, m8.ap[:, 15:16], ALU.is_ge, extra_reads=[m8])
                    P.op("tensor", lambda e, o=psTb[0:NB, s * 128:(s + 1) * 128], i=sb_.ap[:, 0:NB]: e.transpose(o, i, self.c_ident_bf.ap[:, 0:128]),
                         reads=[sb_, self.c_ident_bf], writes=[psT])
                self.copy("scalar", selT[g], selT[g].ap[0:NB, :], psT, psTb[0:NB, 0:512])
                OS = [self.ps[4 + hh] for hh in range(4)]
                pend = []
                pairs = self.causal_pairs(qc)
                npr = len(pairs)

                def emit_pv(item):
                    tp_, kt_, pt_, first_, last_ = item
                    vap = VS.ap[:, (g * NT + kt_) * 65:(g * NT + kt_) * 65 + 65]
                    self.mm(OS[2 * tp_], OS[2 * tp_].ap[0:65, :], VS, vap, pt_, pt_.ap[:, 0:512], first_, last_)
                    self.mm(OS[2 * tp_ + 1], OS[2 * tp_ + 1].ap[0:65, :], VS, vap, pt_, pt_.ap[:, 512:1024], first_, last_)
                for pi, (kta, ktb, minfo) in enumerate(pairs):
                    b0 = 2 * (self.nsp % 2)
                    self.nsp += 1
                    ma, mb = self.ps[b0], self.ps[b0 + 1]
                    self.mm(ma, ma.ap, EXP, EXP.ap[0:NB, kta * 128:(kta + 1) * 128], selT[g], selT[g].ap[0:NB, :], True, True)
                    self.mm(mb, mb.ap, EXP, EXP.ap[0:NB, ktb * 128:(ktb + 1) * 128], selT[g], selT[g].ap[0:NB, :], True, True)
                    mk = mks[pi % 2]
                    P.op("scalar", lambda e, o=mk.ap, i=self.psall[:, b0 * 512:(b0 + 2) * 512]: e.copy(out=o, in_=i), reads=[ma, mb], writes=[mk])
                    for half, kt in ((0, kta), (1, ktb)):
                        for tp in range(2):
                            t = 2 * g + tp
                            b0 = 2 * (self.nsp % 2)
                            self.nsp += 1
                            pa, pb = self.ps[b0], self.ps[b0 + 1]
                            self.mm(pa, pa.ap, KS, KS.ap[0:64, g * S + kt * 128: g * S + (kt + 1) * 128], Q8, Q8.ap[0:64, t * 512:(t + 1) * 512], True, True)
                            self.mm(pb, pb.ap, KS, KS.ap[64:128, g * S + kt * 128: g * S + (kt + 1) * 128], Q8, Q8.ap[64:128, t * 512:(t + 1) * 512], True, True)
                            pair_ap = self.psall[:, b0 * 512:(b0 + 2) * 512]
                            pt = pts[npt % 4]
                            npt += 1
                            if minfo is not None:
                                sb = ssb[npt % 2]
                                o3 = sb.ap.rearrange("p (a c) -> p a c", a=2)
                                i3 = pair_ap.rearrange("p (a c) -> p a c", a=2)
                                m3 = minfo[1][:, half * 512:(half + 1) * 512].unsqueeze(1).to_broadcast([128, 2, 512])
                                P.op("vector", lambda e, o=o3, i0=i3, i1=m3: e.tensor_tensor(out=o, in0=i0, in1=i1, op=ALU.min),
                                     reads=[pa, pb, minfo[0]], writes=[sb])
                                P.op("scalar", lambda e, o=pt.ap, i=sb.ap: e.activation(out=o, in_=i, func=AF.Exp, scale=sc), reads=[sb], writes=[pt])
                            else:
                                P.op("scalar", lambda e, o=pt.ap, i=pair_ap: e.activation(out=o, in_=i, func=AF.Exp, scale=sc), reads=[pa, pb], writes=[pt])
                            p3 = pt.ap.rearrange("p (a c) -> p a c", a=2)
                            k3 = mk.ap[:, half * 512:(half + 1) * 512].unsqueeze(1).to_broadcast([128, 2, 512])
                            P.op("vector", lambda e, o=p3, i1=k3: e.tensor_tensor(out=o, in0=o, in1=i1, op=ALU.mult), reads=[pt, mk], writes=[pt])
                            pend.append((tp, kt, pt, pi == 0 and half == 0, pi == npr - 1 and half == 1))
                            if len(pend) > 1:
                                emit_pv(pend.pop(0))
                while pend:
                    emit_pv(pend.pop(0))
                for hh in range(4):
                    self.combine(OS[hh], GR[hh], 1, OACC[hh], False, rec[ncomb % 2], recB, ctmp, 0, 4)
                    ncomb += 1
                for tp in range(2):
                    t = 2 * g + tp
                    wl = []
                    if qc > 0:
                        for o in range(4):
                            wl.append((o, (MW, MW.ap[:, o * 512:(o + 1) * 512], "b")))
                    for o in range(4):
                        wl.append((4 + o, (self.c_maskc, self.c_maskc.ap[:, o * 512:(o + 1) * 512], "b")))
                    pb4 = 4 + 2 * (nwin % 2)
                    nwin += 1
                    Ow = (self.ps[pb4], self.ps[pb4 + 1])
                    self.attn_pass_dual(
                        ((kw, lambda sl, g=g: kw.ap[0:64, g * 1024 + sl * 128: g * 1024 + (sl + 1) * 128]),
                         (kw, lambda sl, g=g: kw.ap[64:128, g * 1024 + sl * 128: g * 1024 + (sl + 1) * 128])),
                        (Q8, Q8.ap[0:64, t * 512:(t + 1) * 512], Q8.ap[64:128, t * 512:(t + 1) * 512]),
                        vw, lambda sl, j_, g=g: vw.ap[:, (g * 8 + sl) * 65:(g * 8 + sl) * 65 + 65], 65, sc, Ow[0], Ow[1], wl,
                        ptiles=pts, ssb=ssb)
                    for two in range(2):
                        hh = 2 * tp + two
                        h = 4 * g + hh
                        self.combine(Ow[two], GR[hh], 2, OACC[hh], False, rec[ncomb % 2], recB, ctmp, 0, 4)
                        ncomb += 1
                        o_ = ob[hh % 2]
                        self.copy("scalar", o_, o_.ap[0:64, :], OACC[hh], OACC[hh].ap[0:64, :])
                        self.store(oT[h // 2, (h % 2) * 64:(h % 2) * 64 + 64, t0:t0 + 512], o_, o_.ap[0:64, :])
        self.flush_deferred()
        P.barrier()

    def aug_head_chunk(self, kt_, v_, q_, rows, scale, qc, Oacc, pts, ssb, r_, recB, o_, dst_ap):
        P = self.P
        kl = self.causal_pairs(qc)
        self.attn_pass(kt_, lambda kt: kt_.ap[0:rows, kt * 128:(kt + 1) * 128], q_, q_.ap[0:rows, :],
                       v_, lambda kt: v_.ap[:, kt * 129: kt * 129 + 65], 65, scale, Oacc, kl, ptiles=pts, ssb=ssb)
        P.op("vector", lambda e, o=r_.ap[64:65, :], i=Oacc.ap[64:65, :]: e.reciprocal(out=o, in_=i), reads=[Oacc], writes=[r_])

        def fin():
            pb = self.ps[6]
            self.mm(pb, pb.ap[0:64, :], self.c_ones_f, self.c_ones_f.ap[64:65, 0:64], r_, r_.ap[64:65, :], True, True)
            self.copy("scalar", recB, recB.ap[0:64, :], pb, pb.ap[0:64, :])
            self.tt("vector", o_, o_.ap[0:64, :], Oacc, Oacc.ap[0:64, :], recB, recB.ap[0:64, :], ALU.mult)
            self.store(dst_ap, o_, o_.ap[0:64, :])
        self.defer(4, fin)

    def phase_odd_mla(self, li):
        A, P, S, NT = self.A, self.P, self.S, self.NT
        A.reset(self.base_off)
        mq, mk, mv, oT = self.scr["mq"].ap(), self.scr["mk"].ap(), self.scr["mv"].ap(), self.scr["oT"].ap()
        KT = [A.alloc(S, BF16, "KT%d" % i) for i in range(2)]
        V = [A.alloc(NT * 129, BF16, "V%d" % i) for i in range(2)]
        Q = [A.alloc(512, BF16, "Q%d" % i) for i in range(2)]
        pts = [A.alloc(1024, BF16, "pt%d" % i) for i in range(4)]
        ssb = [A.alloc(1024, F32, "ssb%d" % i) for i in range(2)]
        rec = [A.alloc(512, F32, "rec%d" % i) for i in range(2)]
        recB = A.alloc(512, F32, "recB")
        ob = [A.alloc(512, BF16, "ob%d" % i) for i in range(2)]
        for i in range(2):
            self.memset("gpsimd", V[i], V[i].ap, 1.0)

        def load_unit(h, slot):
            self.load(KT[slot], KT[slot].ap[0:96, :], mk[h])
            self.load(V[slot], V[slot].ap.rearrange("p (t c) -> p t c", c=129)[:, :, 0:64], mv[h].rearrange("(t p) d -> p t d", p=128))
        load_unit(0, 0)
        n = 0
        for h in range(8):
            slot = h % 2
            if h + 1 < 8:
                load_unit(h + 1, 1 - slot)
            for qc in range(self.NCH):
                q_ = Q[n % 2]
                t0 = qc * 512
                self.load(q_, q_.ap[0:96, :], mq[h, :, t0:t0 + 512])
                self.aug_head_chunk(KT[slot], V[slot], q_, 96, 96.0 ** -0.5, qc, self.ps[4 + n % 2], pts, ssb, rec[n % 2], recB, ob[n % 2],
                                    oT[4 + h // 2, (h % 2) * 64:(h % 2) * 64 + 64, t0:t0 + 512])
                n += 1
        self.flush_deferred()
        P.barrier()

    def build(self):
        S = self.S
        nc = self.nc
        self.inp("x", [S, D])
        self.inp("mem", [MEM, D])
        self.inp("mem_norm", [D])
        for nm in ("norm_mix", "norm_mem", "norm_ffn"):
            self.inp(nm, [DEPTH, D])
        self.inp("ev_w_in", [2, D, EV_W])
        self.inp("ev_b_f", [2, 8])
        self.inp("ev_lam", [2, 4, 64])
        self.inp("ev_subln", [2, 128])
        self.inp("ev_w_out", [2, D, D])
        self.inp("od_w_in", [2, D, OD_W])
        self.inp("nsa_cmp_pos", [2, 2, 32, 64])
        self.inp("nsa_cmp_w1", [2, 2, 2048, 128])
        self.inp("nsa_cmp_w2", [2, 2, 128, 64])
        self.inp("mla_q_norm", [2, 384])
        self.inp("mla_kv_norm", [2, 256])
        self.inp("mla_w_uq", [2, 384, 768])
        self.inp("mla_w_ukv", [2, 256, 1024])
        self.inp("od_w_out", [2, D, D])
        self.inp("xa_wq", [DEPTH, D, 512])
        self.inp("xa_wkv", [DEPTH, D, D])
        self.inp("xa_wo", [DEPTH, 512, D])
        self.inp("ffn_w13", [DEPTH, D, 2 * DFF])
        self.inp("ffn_w2", [DEPTH, DFF, D])
        self.inp("final_norm", [D])
        self.inp("c_gains", [128, (3 * DEPTH + 1) * KC])
        self.inp("c_ident", [128, 128])
        self.inp("c_maskc", [128, 4 * 512])
        self.inp("c_cos64", [128, S])
        self.inp("c_sin64", [128, S])
        NB = S // 64
        self.inp("c_cosm", [96, S])
        self.inp("c_sinm", [96, S])
        self.inp("c_mlan", [2, 128, 5])
        self.inp("c_posT", [2, 64, 64])
        self.inp("c_ov1", [128, self.NCT * (NB + 1)])
        self.inp("c_expand", [128, S])
        self.inp("c_maskw", [128, 4 * 512])
        self.inp("c_maskm", [128, 5 * 512])
        self.inp("c_tw1", [128, 2 * (self.NT - 1) + NB])
        self.inp("c_tw2", [128, 2 * (self.NT - 1) + NB])
        self.dout = nc.dram_tensor("out", [S, D], F32, kind="ExternalOutput")
        self.scratch("xT", [KC, 128, S], F32)
        self.scratch("oT", [KC, 128, S], BF16)
        self.scratch("dq", [4, 128, S], BF16)
        self.scratch("dk", [4, 128, S], BF16)
        self.scratch("dv", [4, S, 128], BF16)
        self.scratch("fq", [8, 70, S], BF16)
        self.scratch("fk", [8, 70, S], BF16)
        self.scratch("fv", [8, S, 64], BF16)
        self.scratch("lfp", [8, S], F32)
        self.scratch("nsq", [8, 64, S], BF16)
        for nm in ("kcT", "vcT", "ksT", "kwT"):
            self.scratch(nm, [2, 64, S], BF16)
        self.scratch("vsv", [2, S, 64], BF16)
        self.scratch("vwv", [2, S, 64], BF16)
        self.scratch("gT", [24, S], F32)
        self.scratch("mq", [8, 96, S], BF16)
        self.scratch("mk", [8, 96, S], BF16)
        self.scratch("mv", [8, S, 64], BF16)
        self.scratch("kcmp", [2, 64, self.NCP], BF16)
        self.scratch("vcmp", [2, self.NCP, 64], BF16)

        A = self.A
        self.c_eps = A.alloc(8, F32, "eps")
        self.c_one = A.alloc(8, F32, "one")
        self.memset("vector", self.c_eps, self.c_eps.ap, EPS)
        self.memset("vector", self.c_one, self.c_one.ap, 1.0)
        self.c_memT = A.alloc(KC * MEM, BF16, "memT")
        self.setup_consts()
        self.phase_init()
        for li in self.layers:
            if li % 2 == 0:
                self.phase_even_in(li)
                self.phase_even_scan(li)
                self.phase_even_attn(li)
                self.phase_out_xattn(li, self.din["ev_w_out"].ap()[li // 2])
            else:
                self.phase_odd_in(li)
                self.phase_odd_cmp(li)
                self.phase_odd_nsa(li)
                self.phase_odd_mla(li)
                self.phase_out_xattn(li, self.din["od_w_out"].ap()[li // 2])
            self.phase_ffn(li)
        self.phase_final()
        self.P.emit()
        return nc


def ones_f_ap(b):
    return b.c_ones_f.ap


def host_gains(inputs):
    cols = []
    for nm in ("norm_mix", "norm_mem", "norm_ffn"):
        g = np.asarray(inputs[nm], dtype=np.float32)
        for li in range(DEPTH):
            cols.append(g[li].reshape(KC, 128).T)
    cols.append(np.asarray(inputs["final_norm"], dtype=np.float32).reshape(KC, 128).T)
    return np.ascontiguousarray(np.concatenate(cols, axis=1))


def host_consts(S):
    c = {}
    c["c_ident"] = np.eye(128, dtype=np.float32)
    m = np.zeros((128, 4, 512), np.float32)
    p = np.arange(128)[:, None]
    q = np.arange(512)[None, :]
    for o in range(4):
        m[:, o, :] = np.where(o * 128 + p <= q, BIG, -BIG)
    c["c_maskc"] = m.reshape(128, 2048)
    cs, sn = rope_tables(S, 128, 32, 2)
    c["c_cos64"] = cs
    c["c_sin64"] = sn
    c32, s32 = rope_tables(S, 32, 16, 1)
    c["c_cosm"] = np.ascontiguousarray(np.concatenate([np.ones((64, S), np.float32), c32], axis=0))
    c["c_sinm"] = np.ascontiguousarray(np.concatenate([np.zeros((64, S), np.float32), s32], axis=0))
    NB = S // 64
    NT = S // 128
    NC = (S - 32) // 16 + 1
    NCT = (NC + 127) // 128
    cidx = np.arange(NCT * 128)
    ov = ((cidx[:, None] * 16 < np.arange(NB)[None, :] * 64 + 64) & (cidx[:, None] * 16 + 32 > np.arange(NB)[None, :] * 64)
          & (cidx[:, None] < NC)).astype(np.float32)
    ov1 = np.concatenate([ov, np.ones((NCT * 128, 1), np.float32)], axis=1)
    c["c_ov1"] = np.ascontiguousarray(ov1.reshape(NCT, 128, NB + 1).transpose(1, 0, 2).reshape(128, NCT * (NB + 1)))
    c["c_expand"] = (np.arange(128)[:, None] == (np.arange(S)[None, :] // 64)).astype(np.float32)
    mw = np.zeros((128, 4, 512), np.float32)
    for o in range(4):
        mw[:, o, :] = np.where(q < 128 * o + p, BIG, -BIG)
    c["c_maskw"] = mw.reshape(128, 2048)
    mm_ = np.zeros((128, 5, 512), np.float32)
    for i in range(5):
        mm_[:, i, :] = np.where(16 * p + 31 - q <= 512 * i, BIG, -BIG)
    c["c_maskm"] = mm_.reshape(128, 2560)
    TWW = 2 * (NT - 1) + NB
    RO = 2 * (NT - 1)
    r = np.arange(TWW)[None, :] - RO
    e = (np.arange(128)[:, None] >= 64).astype(np.int64)
    fut = r > e
    forced = (r == e) | (r == e - 1)
    c["c_tw1"] = np.where(fut | forced, 0.0, 1.0).astype(np.float32)
    c["c_tw2"] = np.where(fut, -BIG, np.where(forced, 1.0e4, 0.0)).astype(np.float32)
    return c


def host_layouts(inputs):
    c = {}
    qn = np.asarray(inputs["mla_q_norm"], dtype=np.float32)
    kn = np.asarray(inputs["mla_kv_norm"], dtype=np.float32)
    ml = np.zeros((2, 128, 5), np.float32)
    for j in range(2):
        ml[j, :, 0:3] = qn[j].reshape(3, 128).T
        ml[j, :, 3:5] = kn[j].reshape(2, 128).T
    c["c_mlan"] = ml
    pos = np.asarray(inputs["nsa_cmp_pos"], dtype=np.float32)
    c["c_posT"] = np.ascontiguousarray(pos.transpose(0, 3, 1, 2).reshape(2, 64, 64))
    return c


_CACHE = {}


def run(inputs, S, layers, n_cores, batch_ids):
    key = (S, tuple(layers))
    if key not in _CACHE:
        import time as _t
        _t0 = _t.time()
        _b = Builder(S, layers)
        _CACHE[key] = _b.build()
        print("[build] %.1fs ninst=%d" % (_t.time() - _t0, _b.P.ninst), {e: len(v) for e, v in _b.P.q.items()}, flush=True)
    nc = _CACHE[key]
    consts = host_consts(S)
    consts["c_gains"] = host_gains(inputs)
    consts.update(host_layouts(inputs))
    in_maps = []
    for b in batch_ids:
        m = {}
        for k, v in inputs.items():
            v = np.asarray(v)
            if k == "x" or k == "mem":
                m[k] = np.ascontiguousarray(v[b], dtype=np.float32)
            else:
                m[k] = np.ascontiguousarray(v, dtype=np.float32)
        m.update(consts)
        in_maps.append(m)
    import time as _t
    _t0 = _t.time()
    res = run_bass_kernel_spmd(nc, in_maps, core_ids=list(range(n_cores)))
    print("[run] spmd launch+compile %.1fs" % (_t.time() - _t0), flush=True)
    return [r["out"] for r in res.results]


def kernel(**inputs):
    x = np.asarray(inputs["x"])
    B, S, _ = x.shape
    outs = run(inputs, S, list(range(DEPTH)), 8, [0, 1, 2, 3, 0, 1, 2, 3])
    return np.stack(outs[:4], axis=0).astype(np.float32)
```
